# Optimizing a Trainium2 kernel written in Bass

```python
import math
import jax, jax.numpy as jnp
from jax import lax
import numpy as np

D_MODEL = 2048
BATCH = 4
SEQ = 2048
DEPTH = 1

PLE_DIM = 256
EPS = 1e-6
NEG = -1e30
POOL_WIDTH = D_MODEL // 2
POOL_WINDOWS = (2, 4, 8, 16)
POOL_GROUPS = len(POOL_WINDOWS)
POOL_GROUP_DIM = POOL_WIDTH // POOL_GROUPS
HEAD_DIM = 128
N_HEADS = (D_MODEL // 2) // HEAD_DIM
N_KV_GROUPS = 2
HEADS_PER_GROUP = N_HEADS // N_KV_GROUPS
NSA_WIDTH = N_HEADS * HEAD_DIM
KV_WIDTH = N_KV_GROUPS * HEAD_DIM
N_BRANCH = 3
IN_WIDTH = POOL_WIDTH + NSA_WIDTH + 6 * KV_WIDTH + N_HEADS * N_BRANCH
ATTN_SCALE = HEAD_DIM ** -0.5
CMP_BLOCK = 32
CMP_STRIDE = 16
CMP_HIDDEN = 128
SEL_BLOCK = 64
N_SELECT = 16
SEL_QCHUNK = 32
WINDOW = 512
WIN_QBLOCK = 128
ROPE_THETA = 500000.0
ROPE_DIM = HEAD_DIM // 4
D_FF = -(-8 * D_MODEL // (3 * 256)) * 256

kernel_name = "hymba_pool_nsa_swiglu_ple"


def rmsnorm(x, g):
    xf = x.astype(jnp.float32)
    y = xf * lax.rsqrt(jnp.mean(xf * xf, axis=-1, keepdims=True) + EPS)
    return (y * g.astype(jnp.float32)).astype(x.dtype)


def rope_tables(T):
    pos = jnp.arange(T, dtype=jnp.float32)
    inv_freq = ROPE_THETA ** (-jnp.arange(0, ROPE_DIM, 2, dtype=jnp.float32) / ROPE_DIM)
    ang = pos[:, None] * inv_freq[None, :]
    return jnp.cos(ang)[:, None, :], jnp.sin(ang)[:, None, :]


def partial_rope(x, cos, sin):
    half = ROPE_DIM // 2
    cos = cos.astype(x.dtype)
    sin = sin.astype(x.dtype)
    x1 = x[..., :half]
    x2 = x[..., half:ROPE_DIM]
    return jnp.concatenate([x1 * cos - x2 * sin, x2 * cos + x1 * sin, x[..., ROPE_DIM:]], axis=-1)


def pool_mixer(u, pool_w, pool_scale):
    B, T, _ = u.shape
    uf = u.astype(jnp.float32).reshape(B, T, POOL_GROUPS, POOL_GROUP_DIM)
    cs = jnp.concatenate([jnp.zeros_like(uf[:, :1]), jnp.cumsum(uf, axis=1)], axis=1)
    t = jnp.arange(T)
    outs = []
    for gi, w in enumerate(POOL_WINDOWS):
        start = jnp.maximum(t + 1 - w, 0)
        win_sum = cs[:, t + 1, gi] - cs[:, start, gi]
        count = (t + 1 - start).astype(jnp.float32)[None, :, None]
        outs.append(win_sum / count - uf[:, :, gi])
    pooled = jnp.stack(outs, axis=2).astype(u.dtype)
    y = jnp.einsum('btgc,gcd->btgd', pooled, pool_w).reshape(B, T, POOL_WIDTH)
    return y * pool_scale


def compress_blocks(k, pe, w1, w2):
    B, T, G, D = k.shape
    n_cmp = (T - CMP_BLOCK) // CMP_STRIDE + 1
    idx = CMP_STRIDE * jnp.arange(n_cmp)[:, None] + jnp.arange(CMP_BLOCK)[None, :]
    blocks = k[:, idx] + pe[None, None, :, None, :]
    flat = blocks.transpose(0, 1, 3, 2, 4).reshape(B, n_cmp, G, CMP_BLOCK * D)
    return jax.nn.gelu(flat @ w1) @ w2


def nsa_mixer(q, kc, vc, ks, vs, kw, vw, gates,
              cmp_k_pe, cmp_k_w1, cmp_k_w2, cmp_v_pe, cmp_v_w1, cmp_v_w2):
    B, T, H, D = q.shape
    G, R = N_KV_GROUPS, HEADS_PER_GROUP
    dt = q.dtype
    t = jnp.arange(T)
    qg = q.reshape(B, T, G, R, D)

    kcmp = compress_blocks(kc, cmp_k_pe, cmp_k_w1, cmp_k_w2)
    vcmp = compress_blocks(vc, cmp_v_pe, cmp_v_w1, cmp_v_w2)
    n_cmp = kcmp.shape[1]
    s = jnp.einsum('btgrd,bngd->bgrtn', qg, kcmp).astype(jnp.float32) * ATTN_SCALE
    blk_end = CMP_STRIDE * jnp.arange(n_cmp) + CMP_BLOCK - 1
    valid = blk_end[None, :] <= t[:, None]
    p_cmp = jax.nn.softmax(jnp.where(valid, s, NEG), axis=-1) * valid
    o_cmp = jnp.einsum('bgrtn,bngd->btgrd', p_cmp.astype(dt), vcmp)

    n_sel = T // SEL_BLOCK
    cstart = CMP_STRIDE * jnp.arange(n_cmp)
    sstart = SEL_BLOCK * jnp.arange(n_sel)
    overlap = ((cstart[:, None] < sstart[None, :] + SEL_BLOCK) &
               (cstart[:, None] + CMP_BLOCK > sstart[None, :])).astype(jnp.float32)
    imp = jnp.einsum('bgtn,ns->bgts', p_cmp.sum(axis=2), overlap)
    j = jnp.arange(n_sel)[None, :]
    cur = (t // SEL_BLOCK)[:, None]
    forced = (j == 0) | (j == cur) | (j == cur - 1)
    imp = jnp.where(j > cur, -jnp.inf, jnp.where(forced, jnp.inf, imp))
    k_top = min(N_SELECT, n_sel)
    _, sel_idx = lax.top_k(imp, k_top)

    ksb = ks.transpose(0, 2, 1, 3).reshape(B, G, n_sel, SEL_BLOCK, D)
    vsb = vs.transpose(0, 2, 1, 3).reshape(B, G, n_sel, SEL_BLOCK, D)
    n_chunk = T // SEL_QCHUNK
    q_ch = qg.transpose(0, 2, 3, 1, 4).reshape(B, G, R, n_chunk, SEL_QCHUNK, D).transpose(3, 0, 1, 2, 4, 5)
    idx_ch = sel_idx.reshape(B, G, n_chunk, SEL_QCHUNK, k_top).transpose(2, 0, 1, 3, 4)
    t_ch = t.reshape(n_chunk, SEL_QCHUNK)
    bi = jnp.arange(B)[:, None, None, None]
    gi = jnp.arange(G)[None, :, None, None]
    in_blk = jnp.arange(SEL_BLOCK)

    def sel_chunk(args):
        qc, ic, tc = args
        kb = ksb[bi, gi, ic]
        vb = vsb[bi, gi, ic]
        sc = jnp.einsum('bgrqd,bgqksd->bgrqks', qc, kb).astype(jnp.float32) * ATTN_SCALE
        kpos = ic[..., None] * SEL_BLOCK + in_blk
        mask = kpos <= tc[None, None, :, None, None]
        sc = jnp.where(mask[:, :, None], sc, NEG)
        pr = jax.nn.softmax(sc.reshape(B, G, R, SEL_QCHUNK, -1), axis=-1).reshape(sc.shape)
        return jnp.einsum('bgrqks,bgqksd->bgrqd', pr.astype(dt), vb)

    o_sel = lax.map(sel_chunk, (q_ch, idx_ch, t_ch))
    o_sel = o_sel.transpose(1, 0, 4, 2, 3, 5).reshape(B, T, G, R, D)

    nb = T // WIN_QBLOCK
    nw = WINDOW // WIN_QBLOCK
    band = jnp.arange(nb)[:, None] + jnp.arange(nw + 1)[None, :]
    pad = ((0, 0), (0, 0), (WINDOW, 0), (0, 0))
    kwp = jnp.pad(kw.transpose(0, 2, 1, 3), pad).reshape(B, G, nb + nw, WIN_QBLOCK, D)
    vwp = jnp.pad(vw.transpose(0, 2, 1, 3), pad).reshape(B, G, nb + nw, WIN_QBLOCK, D)
    kband = kwp[:, :, band].reshape(B, G, nb, (nw + 1) * WIN_QBLOCK, D)
    vband = vwp[:, :, band].reshape(B, G, nb, (nw + 1) * WIN_QBLOCK, D)
    qb = qg.transpose(0, 2, 3, 1, 4).reshape(B, G, R, nb, WIN_QBLOCK, D)
    sw = jnp.einsum('bgrnqd,bgnkd->bgrnqk', qb, kband).astype(jnp.float32) * ATTN_SCALE
    qpos = jnp.arange(nb)[:, None] * WIN_QBLOCK + jnp.arange(WIN_QBLOCK)[None, :]
    kpos = jnp.arange(nb)[:, None] * WIN_QBLOCK - WINDOW + jnp.arange((nw + 1) * WIN_QBLOCK)[None, :]
    diff = qpos[:, :, None] - kpos[:, None, :]
    wmask = (diff >= 0) & (diff < WINDOW) & (kpos[:, None, :] >= 0)
    pw = jax.nn.softmax(jnp.where(wmask, sw, NEG), axis=-1)
    o_win = jnp.einsum('bgrnqk,bgnkd->bgrnqd', pw.astype(dt), vband)
    o_win = o_win.transpose(0, 3, 4, 1, 2, 5).reshape(B, T, G, R, D)

    g = jax.nn.sigmoid(gates.astype(jnp.float32)).astype(dt).reshape(B, T, G, R, N_BRANCH, 1)
    o = g[..., 0, :] * o_cmp + g[..., 1, :] * o_sel + g[..., 2, :] * o_win
    return o.reshape(B, T, NSA_WIDTH)


def hybrid_mixer(a, w_in, pool_w, pool_scale, cmp_k_pe, cmp_k_w1, cmp_k_w2,
                 cmp_v_pe, cmp_v_w1, cmp_v_w2, w_out, cos, sin):
    B, T, _ = a.shape
    proj = a @ w_in
    offs = [int(o) for o in np.cumsum([POOL_WIDTH, NSA_WIDTH] + [KV_WIDTH] * 6)]
    u, q, kc, vc, ks, vs, kw, vw, gts = jnp.split(proj, offs, axis=-1)
    kvshape = (B, T, N_KV_GROUPS, HEAD_DIM)
    q = partial_rope(q.reshape(B, T, N_HEADS, HEAD_DIM), cos, sin)
    kc = partial_rope(kc.reshape(kvshape), cos, sin)
    ks = partial_rope(ks.reshape(kvshape), cos, sin)
    kw = partial_rope(kw.reshape(kvshape), cos, sin)
    y_nsa = nsa_mixer(q, kc, vc.reshape(kvshape), ks, vs.reshape(kvshape), kw, vw.reshape(kvshape),
                      gts.reshape(B, T, N_HEADS, N_BRANCH),
                      cmp_k_pe, cmp_k_w1, cmp_k_w2, cmp_v_pe, cmp_v_w1, cmp_v_w2)
    y_pool = pool_mixer(u, pool_w, pool_scale)
    return jnp.concatenate([y_pool, y_nsa], axis=-1) @ w_out


def swiglu(x, w_gate, w_up, w_down):
    return (jax.nn.silu(x @ w_gate) * (x @ w_up)) @ w_down


def setup_inputs(seed: int = 0) -> dict:
    key = jax.random.key(seed)
    ks = jax.random.split(key, 24)
    f32 = jnp.float32
    L = DEPTH

    def nrm(k, shape, scale):
        return jax.random.normal(k, shape, f32) * scale

    def gain(k, shape):
        return 1.0 + 0.02 * jax.random.normal(k, shape, f32)

    return {
        'x': jax.random.normal(ks[0], (BATCH, SEQ, D_MODEL), f32),
        'p': jax.random.normal(ks[1], (DEPTH, BATCH, SEQ, PLE_DIM), f32),
        'in_norm_g': gain(ks[2], (L, D_MODEL)),
        'w_in': nrm(ks[3], (L, D_MODEL, IN_WIDTH), D_MODEL ** -0.5),
        'pool_w': nrm(ks[4], (L, POOL_GROUPS, POOL_GROUP_DIM, POOL_GROUP_DIM), POOL_GROUP_DIM ** -0.5),
        'pool_scale': gain(ks[5], (L, POOL_WIDTH)),
        'cmp_k_pe': nrm(ks[6], (L, CMP_BLOCK, HEAD_DIM), 0.02),
        'cmp_k_w1': nrm(ks[7], (L, CMP_BLOCK * HEAD_DIM, CMP_HIDDEN), (CMP_BLOCK * HEAD_DIM) ** -0.5),
        'cmp_k_w2': nrm(ks[8], (L, CMP_HIDDEN, HEAD_DIM), CMP_HIDDEN ** -0.5),
        'cmp_v_pe': nrm(ks[9], (L, CMP_BLOCK, HEAD_DIM), 0.02),
        'cmp_v_w1': nrm(ks[10], (L, CMP_BLOCK * HEAD_DIM, CMP_HIDDEN), (CMP_BLOCK * HEAD_DIM) ** -0.5),
        'cmp_v_w2': nrm(ks[11], (L, CMP_HIDDEN, HEAD_DIM), CMP_HIDDEN ** -0.5),
        'w_out': nrm(ks[12], (L, D_MODEL, D_MODEL), D_MODEL ** -0.5),
        'ffn_norm_g': gain(ks[13], (L, D_MODEL)),
        'w_gate': nrm(ks[14], (L, D_MODEL, D_FF), D_MODEL ** -0.5),
        'w_up': nrm(ks[15], (L, D_MODEL, D_FF), D_MODEL ** -0.5),
        'w_down': nrm(ks[16], (L, D_FF, D_MODEL), D_FF ** -0.5),
        'ple_norm_g': gain(ks[17], (L, D_MODEL)),
        'w_ple_gate': nrm(ks[18], (L, D_MODEL, D_MODEL), D_MODEL ** -0.5),
        'w_ple_proj': nrm(ks[19], (L, PLE_DIM, D_MODEL), PLE_DIM ** -0.5),
        'final_norm_g': gain(ks[20], (D_MODEL,)),
    }


def reference(x, p, in_norm_g, w_in, pool_w, pool_scale, cmp_k_pe, cmp_k_w1, cmp_k_w2,
              cmp_v_pe, cmp_v_w1, cmp_v_w2, w_out, ffn_norm_g, w_gate, w_up, w_down,
              ple_norm_g, w_ple_gate, w_ple_proj, final_norm_g):
    T = x.shape[1]
    cos, sin = rope_tables(T)
    h = x
    for i in range(DEPTH):
        a = rmsnorm(h, in_norm_g[i])
        h = h + hybrid_mixer(a, w_in[i], pool_w[i], pool_scale[i], cmp_k_pe[i], cmp_k_w1[i], cmp_k_w2[i],
                             cmp_v_pe[i], cmp_v_w1[i], cmp_v_w2[i], w_out[i], cos, sin)
        h = h + swiglu(rmsnorm(h, ffn_norm_g[i]), w_gate[i], w_up[i], w_down[i])
        gate = jax.nn.sigmoid((rmsnorm(h, ple_norm_g[i]) @ w_ple_gate[i]).astype(jnp.float32)).astype(h.dtype)
        h = h + (p[i] @ w_ple_proj[i]) * gate
    return rmsnorm(h, final_norm_g)
```

```python
import numpy as np
from contextlib import ExitStack
import concourse.bass as bass
import concourse.mybir as mybir
from concourse.bass_utils import run_bass_kernel_spmd

F32 = mybir.dt.float32
BF16 = mybir.dt.bfloat16
AF = mybir.ActivationFunctionType
ALU = mybir.AluOpType
AX = mybir.AxisListType

ENGS = ("pe", "act", "dve", "pool", "sp")
BIG = 1.0e30
SCALE = 128.0 ** -0.5


class Buf:
    __slots__ = ("name", "last_w", "readers", "excl")

    def __init__(self, name, excl=False):
        self.name = name
        self.last_w = None
        self.readers = []
        self.excl = excl


class Op:
    __slots__ = ("eng", "fn", "deps", "signal", "sigval", "dma_key", "dma_val", "idx")


class Sched:
    def __init__(self):
        self.ops = {e: [] for e in ENGS}
        self.dma_cnt = {}
        self.nbuf = 0

    def buf(self, name=None):
        self.nbuf += 1
        return Buf(name or f"b{self.nbuf}")

    def bufs(self, n, name="b"):
        return [self.buf(f"{name}{i}") for i in range(n)]

    def _add(self, eng, fn, reads, writes, dma_key=None):
        op = Op()
        op.eng = eng
        op.fn = fn
        op.signal = False
        op.sigval = None
        op.dma_key = dma_key
        op.dma_val = None
        op.idx = len(self.ops[eng])
        deps = []
        for b in reads:
            if b.last_w is not None:
                deps.append(b.last_w)
            if b.excl:
                deps.extend(t for t in b.readers if t[1] != eng)
        for b in writes:
            if b.last_w is not None:
                deps.append(b.last_w)
            deps.extend(b.readers)
        if dma_key is not None:
            self.dma_cnt[dma_key] = self.dma_cnt.get(dma_key, 0) + 16
            op.dma_val = self.dma_cnt[dma_key]
            tok = ("dma", dma_key, op.dma_val)
        else:
            tok = ("eng", eng, op.idx)
        op.deps = [d for d in set(deps)
                   if not (d[0] == "eng" and d[1] == "pe" and eng == "pe" and dma_key is None)]
        for b in reads:
            b.readers.append(tok)
        for b in writes:
            b.last_w = tok
            b.readers = []
        self.ops[eng].append(op)
        return op

    def op(self, eng, fn, reads=(), writes=()):
        return self._add(eng, fn, list(reads), list(writes))

    def dma(self, eng, fn, key, reads=(), writes=()):
        return self._add(eng, fn, list(reads), list(writes), dma_key=key)

    def barrier(self, bufs):
        toks = []
        for e in ENGS:
            for o in reversed(self.ops[e]):
                if o.dma_key is None:
                    toks.append(("eng", e, o.idx))
                    break
        for k, v in self.dma_cnt.items():
            toks.append(("dma", k, v))
        for b in bufs:
            b.readers.extend(toks)

    def emit(self, nc, stack):
        for e in ENGS:
            for o in self.ops[e]:
                for d in o.deps:
                    if d[0] == "eng":
                        self.ops[d[1]][d[2]].signal = True
        esem = {}
        for e in ENGS:
            c = 0
            for o in self.ops[e]:
                if o.signal and o.dma_key is None:
                    c += 1
                    o.sigval = c
            if c > 0:
                esem[e] = stack.enter_context(nc.semaphore(f"s_{e}"))
        dsem = {k: stack.enter_context(nc.semaphore(f"d_{k}")) for k in self.dma_cnt}
        self.nsem = len(esem) + len(dsem)
        block = stack.enter_context(nc.Block())
        ops = self.ops
        stats = {}

        def run(e, engobj):
            waited = {}
            nw = 0
            for o in ops[e]:
                need = {}
                for d in o.deps:
                    if d[0] == "eng":
                        sem = esem[d[1]]
                        val = ops[d[1]][d[2]].sigval
                        key = ("e", d[1])
                    else:
                        sem = dsem[d[1]]
                        val = d[2]
                        key = ("d", d[1])
                    if need.get(key, (None, -1))[1] < val:
                        need[key] = (sem, val)
                for key, (sem, val) in need.items():
                    if waited.get(key, -1) >= val:
                        continue
                    waited[key] = val
                    engobj.wait_ge(sem, val)
                    nw += 1
                ins = o.fn(engobj)
                if o.dma_key is not None:
                    ins.then_inc(dsem[o.dma_key], 16)
                elif o.signal:
                    ins.then_inc(esem[e], 1)
            stats[e] = (len(ops[e]), nw)

        if ops["sp"]:
            block.sync(lambda eng: run("sp", eng))
        if ops["pe"]:
            block.tensor(lambda eng: run("pe", eng))
        if ops["act"]:
            block.scalar(lambda eng: run("act", eng))
        if ops["dve"]:
            block.vector(lambda eng: run("dve", eng))
        if ops["pool"]:
            block.gpsimd(lambda eng: run("pool", eng))
        self.stats = stats


class Arena:
    def __init__(self, nc, nbytes):
        self.nbytes = nbytes
        self.t = nc.alloc_sbuf_tensor("arena", [128, nbytes // 4], F32)

    def view(self, off, shape, dtype=F32, parts=128):
        shape = list(shape)
        n = int(np.prod(shape))
        nb = n * (2 if dtype == BF16 else 4)
        assert off % 4 == 0 and nb % 4 == 0, (off, nb)
        assert off + nb <= self.nbytes, ("arena overflow", off, nb)
        ap = self.t[0:parts, off // 4:(off + nb) // 4]
        if dtype == BF16:
            ap = ap.bitcast(BF16)
        if len(shape) == 2:
            ap = ap.rearrange("p (a b) -> p a b", a=shape[0])
        elif len(shape) == 3:
            ap = ap.rearrange("p (a b c) -> p a b c", a=shape[0], b=shape[1])
        elif len(shape) == 4:
            ap = ap.rearrange("p (a b c d) -> p a b c d", a=shape[0], b=shape[1], c=shape[2])
        return ap


class Carver:
    def __init__(self, arena, start, end, name=""):
        self.a = arena
        self.p = start
        self.end = end
        self.name = name

    def take(self, shape, dtype=F32, parts=128):
        n = int(np.prod(shape)) * (2 if dtype == BF16 else 4)
        n = (n + 31) // 32 * 32
        off = self.p
        self.p += n
        assert self.p <= self.end, ("carver overflow", self.name, self.p, self.end)
        return self.a.view(off, shape, dtype, parts)


ARENA_BYTES = 212000
NQ_FF = [(0, 12), (12, 22), (22, 34), (34, 44)]


def build(stop_after=None, dbg=()):
    nc = bass.Bass("TRN2", target_bir_lowering=False)

    def din(name, shape):
        return nc.dram_tensor(name, list(shape), F32, kind="ExternalInput").ap()

    x_d = din("x", [2048, 2048])
    p_d = din("p", [1024, 256])
    g_in_d = din("g_in", [2048])
    g_ffn_d = din("g_ffn", [2048])
    g_ple_d = din("g_ple", [2048])
    g_fin_d = din("g_fin", [2048])
    wU_d = din("wU", [4, 128, 16 * 256])
    wT_d = din("wT", [10, 128, 16 * 256])
    wG_d = din("wG", [128, 16 * 24])
    poolw_d = din("poolw", [128, 4 * 2 * 256])
    pscale_d = din("pscale", [128, 8])
    w1k_d = din("w1k", [128, 32 * 128])
    w1v_d = din("w1v", [128, 32 * 128])
    w2k_d = din("w2k", [128, 128])
    w2v_d = din("w2v", [128, 128])
    pek_d = din("pekT", [128, 32])
    pev_d = din("pevT", [128, 32])
    wout_d = din("wout", [4, 128, 16 * 512])
    wg_d = din("wgate", [22, 128, 16 * 256])
    wu_d = din("wup", [22, 128, 16 * 256])
    wd_d = din("wdown", [16, 128, 12 * 512])
    wpg_d = din("wpg", [4, 128, 16 * 512])
    wpp_d = din("wpp", [4, 128, 2 * 512])
    c_cos_d = din("c_cos", [128, 16 * 32])
    c_sin_d = din("c_sin", [128, 16 * 32])
    c_cvalid_d = din("c_cvalid", [128, 1024])
    c_overlap_d = din("c_overlap", [128, 32])
    c_selbias_d = din("c_selbias", [128, 8 * 32])
    c_E_d = din("c_E", [32, 16 * 128])
    c_tri_d = din("c_tri", [128, 3 * 128])
    c_wvalid_d = din("c_wvalid", [128, 40])
    c_invc_d = din("c_invc", [128, 4 * 16])
    out_d = nc.dram_tensor("out", [1024, 2048], F32, kind="ExternalOutput").ap()
    dbg_d = {}
    for name, shape in dbg:
        dbg_d[name] = nc.dram_tensor("dbg_" + name, list(shape), F32, kind="ExternalOutput").ap()

    S = Sched()
    st = ExitStack()
    A = Arena(nc, ARENA_BYTES)
    ps = [st.enter_context(nc.psum_tensor(f"ps{i}", [128, 512], F32)) for i in range(8)]
    psb = [p[:].bitcast(BF16) for p in ps]
    PB = [Buf(f"psum{i}", excl=True) for i in range(8)]

    def PE(fn, r=(), w=()):
        return S.op("pe", fn, r, w)

    def ACT(fn, r=(), w=()):
        return S.op("act", fn, r, w)

    def DVE(fn, r=(), w=()):
        return S.op("dve", fn, r, w)

    def POOL(fn, r=(), w=()):
        return S.op("pool", fn, r, w)

    out_bufs = []

    dbg_state = [ARENA_BYTES - 8192, 2048]
    B_dbgstg = S.buf("dbgstg")

    def dump(name, ap, buf, parts=128):
        if name not in dbg_d:
            return
        d = dbg_d[name]
        n = ap.shape[1]
        stg = A.view(dbg_state[0], [dbg_state[1]], F32)
        bl = buf if isinstance(buf, list) else [buf]
        for c0 in range(0, n, dbg_state[1]):
            c1 = min(n, c0 + dbg_state[1])
            ACT(lambda e, c0=c0, c1=c1: e.activation(out=stg[0:parts, 0:c1 - c0], in_=ap[:, c0:c1], func=AF.Copy), bl, [B_dbgstg])
            db = S.buf("dbgd_" + name)
            S.dma("sp", lambda e, c0=c0, c1=c1: e.dma_start(out=d[0:parts, c0:c1], in_=stg[0:parts, 0:c1 - c0]), "dbg", [B_dbgstg], [db])
            out_bufs.append(db)

    CONST_END = 26624
    CC = Carver(A, 0, CONST_END, "const")
    tri = CC.take([3, 128], BF16)
    ident = tri[:, 0, :]
    diag = tri[:, 1, :]
    upst = tri[:, 2, :]
    cvalid = CC.take([1024], BF16)
    Emat = CC.take([16, 128], BF16, parts=32)
    selbias = CC.take([8, 32])
    invc = CC.take([4, 16])
    cos2 = CC.take([16, 32])
    sin2 = CC.take([16, 32])
    wvalid = CC.take([40])
    gsig = CC.take([8, 24])
    pscale = CC.take([8])
    gfull = CC.take([2048])
    stat = CC.take([16, 8])
    B_const = S.buf("const")
    B_gfull = S.buf("gfull")
    B_gsig = S.bufs(8, "gsig")

    S.dma("sp", lambda e: e.dma_start(out=selbias.rearrange("p a b -> p (a b)"), in_=c_selbias_d), "cst", [], [B_const])
    S.dma("sp", lambda e: e.dma_start(out=invc.rearrange("p a b -> p (a b)"), in_=c_invc_d), "cst", [], [B_const])
    S.dma("sp", lambda e: e.dma_start(out=cos2.rearrange("p a b -> p (a b)"), in_=c_cos_d), "cst", [], [B_const])
    S.dma("sp", lambda e: e.dma_start(out=sin2.rearrange("p a b -> p (a b)"), in_=c_sin_d), "cst", [], [B_const])
    S.dma("sp", lambda e: e.dma_start(out=wvalid, in_=c_wvalid_d), "cst", [], [B_const])
    S.dma("sp", lambda e: e.dma_start(out=pscale, in_=pscale_d), "cst", [], [B_const])
    S.dma("pool", lambda e: e.dma_start(out=tri.rearrange("p a b -> p (a b)"), in_=c_tri_d), "cstp", [], [B_const])
    S.dma("pool", lambda e: e.dma_start(out=cvalid, in_=c_cvalid_d), "cstp", [], [B_const])
    S.dma("pool", lambda e: e.dma_start(out=Emat.rearrange("p a b -> p (a b)"), in_=c_E_d), "cstp", [], [B_const])

    def load_g(g_d):
        S.dma("sp", lambda e: e.dma_start(out=gfull, in_=g_d.partition_broadcast(128)), "gld", [], [B_gfull])

    PC = Carver(A, CONST_END, ARENA_BYTES, "persist")
    qT = PC.take([2, 8, 4, 128], BF16)
    kT = PC.take([3, 2, 2048], BF16)
    vcT = PC.take([2, 2048], BF16)
    v1 = PC.take([4, 16, 132], BF16)
    YMIX_OFF = PC.p
    ymixT = PC.take([16, 1024], BF16)
    uhalo = PC.take([8, 16])
    TRANS0 = PC.p
    B_qT = [[S.buf(f"qT{g}_{o}") for o in range(8)] for g in range(2)]
    B_kT = [[[S.buf(f"kT{k}_{g}_{l}") for l in range(16)] for g in range(2)] for k in range(3)]
    B_vcT = [[S.buf(f"vcT{g}_{l}") for l in range(16)] for g in range(2)]
    B_v1 = [[S.buf(f"v1{k}_{l}") for l in range(16)] for k in range(4)]
    B_ymix = [S.buf(f"ymix{c}") for c in range(16)]
    B_uhalo = S.buf("uhalo")
    B_v1init = S.buf("v1init")
    POOL(lambda e: e.memset(v1[:, :, :, 128:132], 1.0), [], [B_v1init])

    nstat = [0]

    def rms_to_bf16(src, B_src, dst_bf, B_dst, junk, B_junk):
        k = nstat[0] % 16
        nstat[0] += 1
        s = stat[:, k, :]
        sb = S.buf()
        ACT(lambda e: e.activation(out=junk, in_=src, func=AF.Square, accum_out=s[:, 0:1]), [B_src], [B_junk, sb])
        DVE(lambda e: e.tensor_scalar(out=s[:, 1:2], in0=s[:, 0:1], scalar1=1.0 / 2048, scalar2=1e-6, op0=ALU.mult, op1=ALU.add), [sb], [sb])
        ACT(lambda e: e.activation(out=s[:, 2:3], in_=s[:, 1:2], func=AF.Sqrt), [sb], [sb])
        DVE(lambda e: e.reciprocal(out=s[:, 3:4], in_=s[:, 2:3]), [sb], [sb])
        DVE(lambda e: e.scalar_tensor_tensor(out=dst_bf, in0=src, scalar=s[:, 3:4], in1=gfull, op0=ALU.mult, op1=ALU.mult),
            [sb, B_src, B_gfull], [B_dst])

    def transpose_tile(a_bf, B_a, aT_dst, B_aT, col0):
        for hb in range(2):
            bank = 6 + hb
            for j in range(8):
                dc = hb * 8 + j
                PE(lambda e, dc=dc, j=j, bank=bank: e.transpose(out=psb[bank][:, j * 128:(j + 1) * 128], in_=a_bf[:, dc * 128:(dc + 1) * 128], identity=ident),
                   [B_a, B_const], [PB[bank]])
            eng = ACT if hb == 0 else DVE
            if hb == 0:
                ACT(lambda e, bank=bank, hb=hb: e.activation(out=aT_dst[:, hb * 8:(hb + 1) * 8, col0:col0 + 128],
                                                           in_=psb[bank].rearrange("p (a b) -> p a b", a=8), func=AF.Copy), [PB[bank]], [B_aT])
            else:
                DVE(lambda e, bank=bank, hb=hb: e.tensor_copy(out=aT_dst[:, hb * 8:(hb + 1) * 8, col0:col0 + 128],
                                                            in_=psb[bank].rearrange("p (a b) -> p a b", a=8)), [PB[bank]], [B_aT])

    TC = Carver(A, TRANS0, ARENA_BYTES - 8192, "AB")
    aT = TC.take([16, 1024], BF16)
    wsl = [TC.take([16, 256], BF16) for _ in range(2)]
    AB_SHARED = TC.p
    xt = [TC.take([2048]) for _ in range(2)]
    abf = [TC.take([2048], BF16) for _ in range(2)]
    stage = [TC.take([2, 128], BF16) for _ in range(2)]
    rtmp = [TC.take([2, 32]) for _ in range(2)]
    rtmp2 = [TC.take([2, 32]) for _ in range(2)]
    wgates = TC.take([16, 24], BF16)
    B_aT = [S.buf(f"aT{i}") for i in range(8)]
    B_xt = S.bufs(2, "xt")
    B_abf = S.bufs(2, "abf")
    B_wsl = S.bufs(2, "wsl")
    B_stage = S.bufs(2, "stage")
    B_rtmp = S.bufs(2, "rtmp")
    B_wgates = S.buf("wgates")
    wcount = [0]
    scount = [0]

    load_g(g_in_d)
    S.dma("pool", lambda e: e.dma_start(out=wgates.rearrange("p a b -> p (a b)"), in_=wG_d), "wgl", [], [B_wgates])

    def load_w(src_ap):
        i = wcount[0] % 2
        wcount[0] += 1
        S.dma("pool", lambda e: e.dma_start(out=wsl[i].rearrange("p a b -> p (a b)"), in_=src_ap), f"wsl{i}", [], [B_wsl[i]])
        return i

    def norm_pass(pas):
        for i in range(8):
            lt = pas * 8 + i
            sl = lt % 2
            S.dma("sp", lambda e, lt=lt, sl=sl: e.dma_start(out=xt[sl], in_=x_d[lt * 128:(lt + 1) * 128, :]), f"xt{sl}", [], [B_xt[sl]])
            rms_to_bf16(xt[sl], B_xt[sl], abf[sl], B_abf[sl], abf[sl], B_abf[sl])
            transpose_tile(abf[sl], B_abf[sl], aT, B_aT[i], i * 128)

    def rope(pbank, stg, B_stg, lt, sl):
        src = ps[pbank][:, 0:256].rearrange("p (h d) -> p h d", h=2)
        c2 = cos2[:, lt, :].unsqueeze(1).to_broadcast([128, 2, 32])
        sA = sin2[:, lt, 0:16].unsqueeze(1).to_broadcast([128, 2, 16])
        sB = sin2[:, lt, 16:32].unsqueeze(1).to_broadcast([128, 2, 16])
        t1 = rtmp[sl]
        t2 = rtmp2[sl]
        DVE(lambda e: e.tensor_tensor(out=t1, in0=src[:, :, 0:32], in1=c2, op=ALU.mult), [PB[pbank], B_const], [B_rtmp[sl]])
        DVE(lambda e: e.tensor_tensor(out=t2[:, :, 0:16], in0=src[:, :, 16:32], in1=sA, op=ALU.mult), [PB[pbank], B_const], [B_rtmp[sl]])
        DVE(lambda e: e.tensor_tensor(out=t2[:, :, 16:32], in0=src[:, :, 0:16], in1=sB, op=ALU.mult), [PB[pbank], B_const], [B_rtmp[sl]])
        DVE(lambda e: e.tensor_tensor(out=stg[:, :, 0:32], in0=t1, in1=t2, op=ALU.add), [B_rtmp[sl]], [B_stg])

    def tok_half(hg, pas, i, wi, pbank):
        lt = pas * 8 + i
        for dc in range(16):
            PE(lambda e, dc=dc: e.matmul(ps[pbank][:, 0:256], lhsT=aT[:, dc, i * 128:(i + 1) * 128], rhs=wsl[wi][:, dc, :], start=(dc == 0), stop=(dc == 15)),
               [B_aT[i], B_wsl[wi]], [PB[pbank]])
        src = ps[pbank][:, 0:256].rearrange("p (h d) -> p h d", h=2)
        if hg in (7, 9):
            vk = 0 if hg == 7 else 2
            ACT(lambda e: e.activation(out=v1[:, vk:vk + 2, lt, 0:128], in_=src, func=AF.Copy), [PB[pbank], B_v1init], [B_v1[vk][lt], B_v1[vk + 1][lt]])
            return
        sl = scount[0] % 2
        scount[0] += 1
        stg = stage[sl]
        tb = 6 + (scount[0] % 2)
        ACT(lambda e: e.activation(out=stg, in_=src, func=AF.Copy), [PB[pbank]], [B_stage[sl]])
        if hg != 5:
            rope(pbank, stg, B_stage[sl], lt, sl)
        for r in range(2):
            PE(lambda e, r=r: e.transpose(out=psb[tb][:, r * 128:(r + 1) * 128], in_=stg[:, r, :], identity=ident), [B_stage[sl], B_const], [PB[tb]])
        pin = psb[tb][:, 0:256].rearrange("p (g t) -> p g t", g=2)
        if hg < 4:
            g, r0, ot = hg // 2, (hg % 2) * 2, i
            DVE(lambda e: e.tensor_copy(out=qT[:, g, ot, r0:r0 + 2, :], in_=pin), [PB[tb]], [B_qT[g][ot]])
        elif hg == 5:
            DVE(lambda e: e.tensor_copy(out=vcT[:, :, lt * 128:(lt + 1) * 128], in_=pin), [PB[tb]], [B_vcT[0][lt], B_vcT[1][lt]])
        else:
            kind = (hg - 4) // 2
            DVE(lambda e: e.tensor_copy(out=kT[:, kind, :, lt * 128:(lt + 1) * 128], in_=pin), [PB[tb]], [B_kT[kind][0][lt], B_kT[kind][1][lt]])

    pbrot = [0]

    def next_pb():
        b = pbrot[0] % 6
        pbrot[0] += 1
        return b

    if stop_after == "C0":
        dump("gfull", gfull, [B_gfull, B_const])
        return _finish(nc, S, st, out_bufs)
    for pas in range(2):
        norm_pass(pas)
        if stop_after == "N0":
            dump("aT", aT.rearrange("p a b -> p (a b)"), B_aT)
            return _finish(nc, S, st, out_bufs)
        hgs = [4, 5, 6, 7, 8, 9] if pas == 0 else list(range(10))
        for hg in hgs:
            wi = load_w(wT_d[hg])
            for i in range(8):
                tok_half(hg, pas, i, wi, next_pb())
        if pas == 0:
            for ug in range(4):
                wi = load_w(wU_d[ug])
                for cc in range(2):
                    c = ug * 2 + cc
                    pb = next_pb()
                    for dc in range(16):
                        PE(lambda e, dc=dc, cc=cc, wi=wi, pb=pb: e.matmul(ps[pb][:, 0:16], lhsT=wsl[wi][:, dc, cc * 128:(cc + 1) * 128], rhs=aT[:, dc, 1008:1024],
                                                                        start=(dc == 0), stop=(dc == 15)), [B_aT[7], B_wsl[wi]], [PB[pb]])
                    ACT(lambda e, c=c, pb=pb: e.activation(out=uhalo[:, c, :], in_=ps[pb][:, 0:16], func=AF.Copy), [PB[pb]], [B_uhalo])
            if stop_after == "P0":
                dump("kT", kT.rearrange("p a b c -> p (a b c)"), [B_kT[k][g][l] for k in range(3) for g in range(2) for l in range(16)])
                return _finish(nc, S, st, out_bufs)
        else:
            for i in range(8):
                pb = next_pb()
                for dc in range(16):
                    PE(lambda e, dc=dc, i=i, pb=pb: e.matmul(ps[pb][:, 0:24], lhsT=aT[:, dc, i * 128:(i + 1) * 128], rhs=wgates[:, dc, :],
                                                             start=(dc == 0), stop=(dc == 15)), [B_aT[i], B_wgates], [PB[pb]])
                ACT(lambda e, i=i, pb=pb: e.activation(out=gsig[:, i, :], in_=ps[pb][:, 0:24], func=AF.Sigmoid), [PB[pb]], [B_gsig[i]])

    dump("kT", kT.rearrange("p a b c -> p (a b c)"), [B_kT[k][g][l] for k in range(3) for g in range(2) for l in range(16)])
    dump("qT", qT.rearrange("p a b c d -> p (a b c d)"), [B_qT[g][o] for g in range(2) for o in range(8)])
    dump("v1", v1.rearrange("p a b c -> p (a b c)"), [B_v1[k][l] for k in range(4) for l in range(16)] + [B_v1init])
    dump("vcT", vcT.rearrange("p a b -> p (a b)"), [B_vcT[g][l] for g in range(2) for l in range(16)])
    dump("gsig", gsig.rearrange("p a b -> p (a b)"), B_gsig)

    if stop_after == "P1":
        return _finish(nc, S, st, out_bufs)
    UC = Carver(A, AB_SHARED, ARENA_BYTES - 8192, "pool")
    ubuf = UC.take([1040])
    sbufA = UC.take([1040])
    sbufB = UC.take([1040])
    ptmp = UC.take([16])
    pooled = [UC.take([2, 1024], BF16) for _ in range(2)]
    poolw = UC.take([4, 2, 256], BF16)
    B_ubuf = S.buf("ubuf")
    B_sA = S.buf("sA")
    B_sB = S.buf("sB")
    B_ptmp = S.buf("ptmp")
    B_pooled = S.bufs(2, "pooled")
    B_poolw = S.buf("poolw")
    S.barrier([B_ubuf, B_sA, B_sB, B_ptmp, B_poolw] + B_pooled)
    S.dma("pool", lambda e: e.dma_start(out=poolw.rearrange("p a b c -> p (a b c)"), in_=poolw_d), "poolw", [], [B_poolw])
    for gi in range(4):
        w = 2 << gi
        wi = load_w(wU_d[gi])
        psl = gi % 2
        for cc in range(2):
            c = gi * 2 + cc
            pbs = [next_pb(), next_pb()]
            for th2 in range(2):
                for dc in range(16):
                    PE(lambda e, dc=dc, cc=cc, wi=wi, th2=th2, pb=pbs[th2]: e.matmul(ps[pb][:], lhsT=wsl[wi][:, dc, cc * 128:(cc + 1) * 128],
                                                                                   rhs=aT[:, dc, th2 * 512:(th2 + 1) * 512], start=(dc == 0), stop=(dc == 15)),
                       [B_aT[th2 * 4 + k] for k in range(4)] + [B_wsl[wi]], [PB[pbs[th2]]])
            ACT(lambda e, c=c: e.activation(out=ubuf[:, 0:16], in_=uhalo[:, c, :], func=AF.Copy), [B_uhalo], [B_ubuf])
            ACT(lambda e, pb=pbs[0]: e.activation(out=ubuf[:, 16:528], in_=ps[pb][:], func=AF.Copy), [PB[pbs[0]]], [B_ubuf])
            ACT(lambda e, pb=pbs[1]: e.activation(out=ubuf[:, 528:1040], in_=ps[pb][:], func=AF.Copy), [PB[pbs[1]]], [B_ubuf])
            cur, Bcur = ubuf, B_ubuf
            nxt = [(sbufA, B_sA), (sbufB, B_sB)]
            sh = 1
            k = 0
            while sh < w:
                dst, Bd = nxt[k % 2]
                DVE(lambda e, cur=cur, dst=dst, sh=sh: e.tensor_tensor(out=dst[:, sh:1040], in0=cur[:, sh:1040], in1=cur[:, 0:1040 - sh], op=ALU.add), [Bcur], [Bd])
                cur, Bcur = dst, Bd
                sh *= 2
                k += 1
            pl = pooled[psl][:, cc, :]
            DVE(lambda e, cur=cur, pl=pl, w=w: e.scalar_tensor_tensor(out=pl[:, 16:1024], in0=cur[:, 32:1040], scalar=1.0 / w, in1=ubuf[:, 32:1040], op0=ALU.mult, op1=ALU.subtract),
                [Bcur, B_ubuf], [B_pooled[psl]])
            DVE(lambda e, cur=cur, gi=gi: e.tensor_tensor(out=ptmp, in0=cur[:, 16:32], in1=invc[:, gi, :], op=ALU.mult), [Bcur, B_const], [B_ptmp])
            DVE(lambda e, pl=pl: e.tensor_tensor(out=pl[:, 0:16], in0=ptmp, in1=ubuf[:, 16:32], op=ALU.subtract), [B_ptmp, B_ubuf], [B_pooled[psl]])
        for dch in range(2):
            for th2 in range(2):
                pb = next_pb()
                for cc in range(2):
                    PE(lambda e, cc=cc, dch=dch, th2=th2, pb=pb, gi=gi, psl=psl: e.matmul(ps[pb][:], lhsT=poolw[:, gi, cc, dch * 128:(dch + 1) * 128],
                                                                                        rhs=pooled[psl][:, cc, th2 * 512:(th2 + 1) * 512], start=(cc == 0), stop=(cc == 1)),
                       [B_poolw, B_pooled[psl]], [PB[pb]])
                ch = gi * 2 + dch
                ACT(lambda e, ch=ch, th2=th2, pb=pb: e.activation(out=ymixT[:, ch, th2 * 512:(th2 + 1) * 512], in_=ps[pb][:], func=AF.Copy, scale=pscale[:, ch:ch + 1]),
                    [PB[pb], B_const], [B_ymix[ch]])
    dump("ypool", ymixT[:, 0:8, :].rearrange("p a b -> p (a b)"), B_ymix[0:8])

    if stop_after == "B":
        return _finish(nc, S, st, out_bufs)

    AC = Carver(A, TRANS0, ARENA_BYTES - 8192, "attn")
    w1 = [AC.take([32, 128], BF16) for _ in range(2)]
    w2 = [AC.take([128], BF16) for _ in range(2)]
    peT = [AC.take([32], BF16) for _ in range(2)]
    hb_ = AC.take([4, 128])
    hx = AC.take([4, 128])
    hy = AC.take([4, 128])
    cbias = AC.take([2, 2])
    geluT = AC.take([4, 128], BF16)
    kcmpT = AC.take([2, 128], BF16)
    vcmp1 = AC.take([2, 164], BF16)
    PT = [AC.take([4, 128], BF16) for _ in range(3)]
    msk = [AC.take([128], BF16) for _ in range(4)]
    oacc = [AC.take([4, 128]) for _ in range(2)]
    obf = [AC.take([4, 128], BF16) for _ in range(2)]
    impb = [AC.take([32]) for _ in range(2)]
    impw = [AC.take([32]) for _ in range(2)]
    m16 = [AC.take([16]) for _ in range(2)]
    selbf = [AC.take([32], BF16) for _ in range(2)]
    selT = AC.take([2, 8, 128], BF16, parts=32)
    rs = [AC.take([12, 4]) for _ in range(2)]
    B_w1 = S.bufs(2, "w1")
    B_w2 = S.bufs(2, "w2")
    B_peT = S.bufs(2, "peT")
    B_h = S.bufs(4, "hid")
    B_cb = S.buf("cbias")
    B_gelu = S.bufs(4, "gelu")
    B_kcmp = S.bufs(2, "kcmp")
    B_vcmp = S.bufs(2, "vcmp")
    B_PT = S.bufs(3, "PT")
    B_msk = S.bufs(4, "msk")
    B_mx = [PB[2]] * 4
    B_oacc = S.bufs(2, "oacc")
    B_obf = S.bufs(2, "obf")
    B_imp = S.bufs(2, "imp")
    B_selbf = S.bufs(2, "selbf")
    B_selT = [[S.buf(f"selT{g}_{o}") for o in range(8)] for g in range(2)]
    B_rs = S.bufs(2, "rs")
    S.barrier(B_w1 + B_w2 + B_peT + B_h + [B_cb] + B_gelu + B_kcmp + B_vcmp + B_PT + B_msk + B_oacc + B_obf + B_imp + B_selbf
              + [b for l in B_selT for b in l] + B_rs)
    for kv, (w1d, w2d, ped) in enumerate([(w1k_d, w2k_d, pek_d), (w1v_d, w2v_d, pev_d)]):
        S.dma("pool", lambda e, kv=kv, w1d=w1d: e.dma_start(out=w1[kv].rearrange("p a b -> p (a b)"), in_=w1d), f"w1_{kv}", [], [B_w1[kv]])
        S.dma("pool", lambda e, kv=kv, w2d=w2d: e.dma_start(out=w2[kv], in_=w2d), f"w2_{kv}", [], [B_w2[kv]])
        S.dma("pool", lambda e, kv=kv, ped=ped: e.dma_start(out=peT[kv], in_=ped), f"pe_{kv}", [], [B_peT[kv]])
    POOL(lambda e: e.memset(vcmp1, 0.0), [], B_vcmp)
    for g in range(2):
        S.dma("pool", lambda e, g=g: e.dma_start(out=vcmp1[:, g, 129:161], in_=c_overlap_d), f"ovl{g}", [], [B_vcmp[g]])
        POOL(lambda e, g=g: e.memset(vcmp1[0:127, g, 128:129], 1.0), [], [B_vcmp[g]])

    CB = 7
    for kv in range(2):
        for l in range(32):
            PE(lambda e, l=l, kv=kv: e.matmul(ps[CB][:, 500 + kv:501 + kv], lhsT=w1[kv][:, l, :], rhs=peT[kv][:, l:l + 1], start=(l == 0), stop=(l == 31)),
               [B_w1[kv], B_peT[kv]], [PB[CB]])
        DVE(lambda e, kv=kv: e.tensor_copy(out=cbias[:, kv, 0:1], in_=ps[CB][:, 500 + kv:501 + kv]), [PB[CB]], [B_cb])
    for kv in range(2):
        for g in range(2):
            idx = kv * 2 + g
            srcT = kT[:, 0, g, :] if kv == 0 else vcT[:, g, :]
            Bsrc = [B_kT[0][g][l] for l in range(16)] if kv == 0 else [B_vcT[g][l] for l in range(16)]
            for l in range(32):
                PE(lambda e, l=l, kv=kv, idx=idx, srcT=srcT: e.matmul(ps[CB][:, idx * 128:idx * 128 + 127], lhsT=w1[kv][:, l, :], rhs=srcT[:, l:l + 16 * 126 + 1:16],
                                                                  start=(l == 0), stop=(l == 31)), Bsrc + [B_w1[kv]], [PB[CB]])
            x_ = hb_[:, idx, 0:127]
            DVE(lambda e, idx=idx, kv=kv, x_=x_: e.tensor_scalar(out=x_, in0=ps[CB][:, idx * 128:idx * 128 + 127], scalar1=cbias[:, kv, 0:1], scalar2=None, op0=ALU.add),
                [PB[CB], B_cb], [B_h[idx]])
            DVE(lambda e, idx=idx, x_=x_: e.tensor_tensor(out=hx[:, idx, 0:127], in0=x_, in1=x_, op=ALU.mult), [B_h[idx]], [B_h[idx]])
            DVE(lambda e, idx=idx: e.tensor_scalar(out=hx[:, idx, 0:127], in0=hx[:, idx, 0:127], scalar1=0.044715, scalar2=1.0, op0=ALU.mult, op1=ALU.add), [B_h[idx]], [B_h[idx]])
            DVE(lambda e, idx=idx, x_=x_: e.tensor_tensor(out=hx[:, idx, 0:127], in0=hx[:, idx, 0:127], in1=x_, op=ALU.mult), [B_h[idx]], [B_h[idx]])
            ACT(lambda e, idx=idx: e.activation(out=hy[:, idx, 0:127], in_=hx[:, idx, 0:127], func=AF.Sigmoid, scale=1.5957691216057308), [B_h[idx]], [B_h[idx]])
            DVE(lambda e, idx=idx, x_=x_: e.tensor_tensor(out=geluT[:, idx, 0:127], in0=hy[:, idx, 0:127], in1=x_, op=ALU.mult), [B_h[idx]], [B_gelu[idx]])
    POOL(lambda e: e.memset(kcmpT, 0.0), [], B_kcmp)
    for g in range(2):
        PE(lambda e, g=g: e.matmul(ps[CB][:, 0:127], lhsT=w2[0], rhs=geluT[:, g, 0:127], start=True, stop=True), [B_w2[0], B_gelu[g]], [PB[CB]])
        DVE(lambda e, g=g: e.tensor_copy(out=kcmpT[:, g, 0:127], in_=ps[CB][:, 0:127]), [PB[CB]], [B_kcmp[g]])
        PE(lambda e, g=g: e.matmul(ps[CB][0:127, 128:256], lhsT=geluT[:, 2 + g, 0:127], rhs=w2[1], start=True, stop=True), [B_w2[1], B_gelu[2 + g]], [PB[CB]])
        DVE(lambda e, g=g: e.tensor_copy(out=vcmp1[0:127, g, 0:128], in_=ps[CB][0:127, 128:256]), [PB[CB]], [B_vcmp[g]])
    dump("kcmpT", kcmpT.rearrange("p a b -> p (a b)"), B_kcmp)
    dump("vcmp1", vcmp1.rearrange("p a b -> p (a b)"), B_vcmp)

    cnt = {"S": 0, "PT": 0, "msk": 0, "O": 0, "m": 0, "oa": 0}

    def evac_branch(obanks, width, br, g, ot, oa, first, with_imp=None):
        r_ = rs[oa]
        for r in range(4):
            bank = obanks[r // 2]
            c0 = (r % 2) * 256
            h = g * 4 + r
            col = br * 4 + r
            DVE(lambda e, bank=bank, c0=c0, col=col: e.tensor_scalar(out=r_[:, col, 0:1], in0=ps[bank][:, c0 + 128:c0 + 129], scalar1=1e-30, scalar2=None, op0=ALU.add),
                [PB[bank]], [B_rs[oa]])
            DVE(lambda e, col=col: e.reciprocal(out=r_[:, col, 1:2], in_=r_[:, col, 0:1]), [B_rs[oa]], [B_rs[oa]])
            DVE(lambda e, col=col, h=h, br=br: e.tensor_tensor(out=r_[:, col, 2:3], in0=r_[:, col, 1:2], in1=gsig[:, ot, h * 3 + br:h * 3 + br + 1], op=ALU.mult),
                [B_rs[oa], B_gsig[ot]], [B_rs[oa]])
            if first:
                DVE(lambda e, bank=bank, c0=c0, col=col, r=r: e.tensor_scalar(out=oacc[oa][:, r, :], in0=ps[bank][:, c0:c0 + 128], scalar1=r_[:, col, 2:3], scalar2=None, op0=ALU.mult),
                    [PB[bank], B_rs[oa]], [B_oacc[oa]])
            else:
                DVE(lambda e, bank=bank, c0=c0, col=col, r=r: e.scalar_tensor_tensor(out=oacc[oa][:, r, :], in0=ps[bank][:, c0:c0 + 128], scalar=r_[:, col, 2:3], in1=oacc[oa][:, r, :],
                                                                                   op0=ALU.mult, op1=ALU.add), [PB[bank], B_rs[oa]], [B_oacc[oa]])
            if with_imp is not None:
                ib, Bi = with_imp
                if r == 0:
                    DVE(lambda e, bank=bank, c0=c0, col=col: e.tensor_scalar(out=ib, in0=ps[bank][:, c0 + 129:c0 + 161], scalar1=r_[:, col, 1:2], scalar2=None, op0=ALU.mult),
                        [PB[bank], B_rs[oa]], [Bi])
                else:
                    DVE(lambda e, bank=bank, c0=c0, col=col: e.scalar_tensor_tensor(out=ib, in0=ps[bank][:, c0 + 129:c0 + 161], scalar=r_[:, col, 1:2], in1=ib, op0=ALU.mult, op1=ALU.add),
                        [PB[bank], B_rs[oa]], [Bi])

    def osets():
        i = cnt["O"] % 2
        cnt["O"] += 1
        return (3, 4) if i == 0 else (5, 6)

    def score_exp(lhsT_ap, B_l, g, ot):
        sb = cnt["S"] % 2
        cnt["S"] += 1
        pi = cnt["PT"] % 3
        cnt["PT"] += 1
        PE(lambda e: e.matmul(ps[sb][:], lhsT=lhsT_ap, rhs=qT[:, g, ot].rearrange("p r t -> p (r t)"), start=True, stop=True), B_l + [B_qT[g][ot]], [PB[sb]])
        ACT(lambda e: e.activation(out=PT[pi].rearrange("p r t -> p (r t)"), in_=ps[sb][:], func=AF.Exp, scale=SCALE), [PB[sb]], [B_PT[pi]])
        return pi

    def pv(pi, rhs_ap, B_r, width, obanks, first, last):
        for r in range(4):
            bank = obanks[r // 2]
            c0 = (r % 2) * 256
            PE(lambda e, r=r, bank=bank, c0=c0: e.matmul(ps[bank][:, c0:c0 + width], lhsT=PT[pi][:, r, :], rhs=rhs_ap, start=(first and r % 2 == 0), stop=last,
                                                         skip_group_check=True),
               [B_PT[pi]] + B_r, [PB[bank]])

    for g in range(2):
        for ot in range(8):
            ltq = 8 + ot
            oa = cnt["oa"] % 2
            cnt["oa"] += 1
            pi = score_exp(kcmpT[:, g, :], [B_kcmp[g]], g, ot)
            DVE(lambda e, pi=pi, ot=ot: e.tensor_tensor(out=PT[pi], in0=PT[pi], in1=cvalid[:, ot * 128:(ot + 1) * 128].unsqueeze(1).to_broadcast([128, 4, 128]), op=ALU.mult),
                [B_PT[pi], B_const], [B_PT[pi]])
            ob = osets()
            pv(pi, vcmp1[:, g, 0:161], [B_vcmp[g]], 161, ob, True, True)
            ii = cnt["m"] % 2
            cnt["m"] += 1
            evac_branch(ob, 161, 0, g, ot, oa, True, with_imp=(impb[ii], B_imp[ii]))
            DVE(lambda e, ii=ii, ot=ot: e.tensor_tensor(out=impb[ii], in0=impb[ii], in1=selbias[:, ot, :], op=ALU.add), [B_imp[ii], B_const], [B_imp[ii]])
            DVE(lambda e, ii=ii: e.max(out=m16[ii][:, 0:8], in_=impb[ii]), [B_imp[ii]], [B_imp[ii]])
            DVE(lambda e, ii=ii: e.match_replace(out=impw[ii], in_to_replace=m16[ii][:, 0:8], in_values=impb[ii], imm_value=-3.0e30), [B_imp[ii]], [B_imp[ii]])
            DVE(lambda e, ii=ii: e.max(out=m16[ii][:, 8:16], in_=impw[ii]), [B_imp[ii]], [B_imp[ii]])
            DVE(lambda e, ii=ii: e.tensor_scalar(out=m16[ii][:, 15:16], in0=m16[ii][:, 15:16], scalar1=-1.0e29, scalar2=None, op0=ALU.max), [B_imp[ii]], [B_imp[ii]])
            DVE(lambda e, ii=ii: e.tensor_scalar(out=selbf[ii], in0=impb[ii], scalar1=m16[ii][:, 15:16], scalar2=None, op0=ALU.is_ge), [B_imp[ii]], [B_selbf[ii]])
            PE(lambda e, ii=ii: e.transpose(out=psb[7][0:32, 0:128], in_=selbf[ii], identity=ident), [B_selbf[ii], B_const], [PB[7]])
            DVE(lambda e, g=g, ot=ot: e.tensor_copy(out=selT[:, g, ot, :], in_=psb[7][0:32, 0:128]), [PB[7]], [B_selT[g][ot]])
            ob = osets()
            for kt in range(ltq + 1):
                pi = score_exp(kT[:, 1, g, kt * 128:(kt + 1) * 128], [B_kT[1][g][kt]], g, ot)
                mslot = cnt["msk"] % 4
                cnt["msk"] += 1
                mcol = mslot * 128
                PE(lambda e, kt=kt, g=g, ot=ot, mcol=mcol: e.matmul(ps[2][:, mcol:mcol + 128], lhsT=Emat[:, kt, :], rhs=selT[:, g, ot, :], start=True, stop=True),
                   [B_const, B_selT[g][ot]], [B_mx[mslot]])
                if kt == ltq:
                    DVE(lambda e, mslot=mslot, mcol=mcol: e.tensor_tensor(out=msk[mslot], in0=ps[2][:, mcol:mcol + 128], in1=diag, op=ALU.mult), [B_mx[mslot], B_const], [B_msk[mslot]])
                else:
                    ACT(lambda e, mslot=mslot, mcol=mcol: e.activation(out=msk[mslot], in_=ps[2][:, mcol:mcol + 128], func=AF.Copy), [B_mx[mslot]], [B_msk[mslot]])
                POOL(lambda e, pi=pi, mslot=mslot: e.tensor_tensor(out=PT[pi], in0=PT[pi], in1=msk[mslot].unsqueeze(1).to_broadcast([128, 4, 128]), op=ALU.mult),
                     [B_PT[pi], B_msk[mslot]], [B_PT[pi]])
                pv(pi, v1[:, 0 + g, kt, 0:129], [B_v1[0 + g][kt], B_v1init], 129, ob, kt == 0, kt == ltq)
            evac_branch(ob, 129, 1, g, ot, oa, False)
            ob = osets()
            for off in range(5):
                kt = ltq - 4 + off
                pi = score_exp(kT[:, 2, g, kt * 128:(kt + 1) * 128], [B_kT[2][g][kt]], g, ot)
                wv = wvalid[:, ot * 5 + off:ot * 5 + off + 1]
                if off == 0:
                    DVE(lambda e, pi=pi, wv=wv: e.scalar_tensor_tensor(out=PT[pi], in0=PT[pi], scalar=wv, in1=upst.unsqueeze(1).to_broadcast([128, 4, 128]), op0=ALU.mult, op1=ALU.mult),
                        [B_PT[pi], B_const], [B_PT[pi]])
                elif off == 4:
                    DVE(lambda e, pi=pi: e.tensor_tensor(out=PT[pi], in0=PT[pi], in1=diag.unsqueeze(1).to_broadcast([128, 4, 128]), op=ALU.mult), [B_PT[pi], B_const], [B_PT[pi]])
                elif kt < 8:
                    POOL(lambda e, pi=pi, wv=wv: e.tensor_scalar(out=PT[pi], in0=PT[pi], scalar1=wv, scalar2=None, op0=ALU.mult), [B_PT[pi], B_const], [B_PT[pi]])
                pv(pi, v1[:, 2 + g, kt, 0:129], [B_v1[2 + g][kt], B_v1init], 129, ob, off == 0, off == 4)
            evac_branch(ob, 129, 2, g, ot, oa, False)
            ACT(lambda e, oa=oa: e.activation(out=obf[oa], in_=oacc[oa], func=AF.Copy), [B_oacc[oa]], [B_obf[oa]])
            for r in range(4):
                PE(lambda e, r=r, oa=oa: e.transpose(out=psb[7][:, 256 + r * 128:256 + (r + 1) * 128], in_=obf[oa][:, r, :], identity=ident), [B_obf[oa], B_const], [PB[7]])
            DVE(lambda e, g=g, ot=ot: e.tensor_copy(out=ymixT[:, 8 + g * 4:12 + g * 4, ot * 128:(ot + 1) * 128], in_=psb[7][:, 256:768].rearrange("p (r t) -> p r t", r=4)),
                [PB[7]], [B_ymix[8 + g * 4 + r] for r in range(4)])
    dump("ynsa", ymixT[:, 8:16, :].rearrange("p a b -> p (a b)"), B_ymix[8:16])
    dump("selT", selT.rearrange("p a b c -> p (a b c)"), [b for l in B_selT for b in l], parts=32)

    if stop_after == "C":
        return _finish(nc, S, st, out_bufs)

    hres = A.view(CONST_END, [8, 2048], F32)
    assert CONST_END + 65536 <= YMIX_OFF
    D_END = CONST_END + 65536
    DC = Carver(A, TRANS0, ARENA_BYTES - 8192, "D")
    wo = [DC.take([16, 512], BF16) for _ in range(2)]
    B_h = [[S.buf(f"h{ot}_{c}") for c in range(4)] for ot in range(8)]
    B_wo = S.bufs(2, "wo")
    S.barrier([b for l in B_h for b in l] + B_wo)
    for ot in range(8):
        S.dma("sp", lambda e, ot=ot: e.dma_start(out=hres[:, ot, :], in_=x_d[1024 + ot * 128:1024 + (ot + 1) * 128, :]), f"hres{ot}", [], B_h[ot])
    pb8 = [0]

    def npb():
        b = pb8[0] % 8
        pb8[0] += 1
        return b

    for dmc in range(4):
        sl = dmc % 2
        S.dma("pool", lambda e, dmc=dmc, sl=sl: e.dma_start(out=wo[sl].rearrange("p a b -> p (a b)"), in_=wout_d[dmc]), f"wo{sl}", [], [B_wo[sl]])
        for ot in range(8):
            pb = npb()
            for c in range(16):
                PE(lambda e, c=c, ot=ot, sl=sl, pb=pb: e.matmul(ps[pb][:], lhsT=ymixT[:, c, ot * 128:(ot + 1) * 128], rhs=wo[sl][:, c, :], start=(c == 0), stop=(c == 15)),
                   [B_ymix[c], B_wo[sl]], [PB[pb]])
            DVE(lambda e, ot=ot, dmc=dmc, pb=pb: e.tensor_tensor(out=hres[:, ot, dmc * 512:(dmc + 1) * 512], in0=ps[pb][:], in1=hres[:, ot, dmc * 512:(dmc + 1) * 512], op=ALU.add),
                [PB[pb], B_h[ot][dmc]], [B_h[ot][dmc]])
    dump("h1", hres.rearrange("p a b -> p (a b)"), [b for l in B_h for b in l])
    if stop_after == "D":
        return _finish(nc, S, st, out_bufs)

    EC = Carver(A, D_END, ARENA_BYTES, "E")
    a2T = EC.take([16, 1024], BF16)
    E_A2T_END = EC.p
    hT = EC.take([12, 1024], BF16)
    wgu = [EC.take([16, 256], BF16) for _ in range(4)]
    wdn = [EC.take([12, 512], BF16) for _ in range(2)]
    sgt = [EC.take([512]) for _ in range(2)]
    EC2 = Carver(A, 768, 768 + 2048 + 4096 + 1024 + 256 + 2048 + 2048, "Ealias")
    abf2 = [EC2.take([2048], BF16), EC2.take([2048], BF16)]
    dbg_state[0] = EC2.p
    dbg_state[1] = 512
    S.barrier([B_dbgstg])
    B_a2T = [S.buf(f"a2T{i}") for i in range(8)]
    B_hT = [[S.buf(f"hT{c}_{t}") for t in range(2)] for c in range(12)]
    B_wgu = S.bufs(4, "wgu")
    B_wdn = S.bufs(2, "wdn")
    B_sgt = S.bufs(2, "sgt")
    B_abf2 = S.bufs(2, "abf2")
    S.barrier(B_a2T + [b for l in B_hT for b in l] + B_wgu + B_wdn + B_sgt + B_abf2)

    def rms_h(ot, dst, B_dst):
        k = nstat[0] % 16
        nstat[0] += 1
        s_ = stat[:, k, :]
        sb = S.buf()
        src = hres[:, ot, :]
        ACT(lambda e: e.activation(out=dst, in_=src, func=AF.Square, accum_out=s_[:, 0:1]), B_h[ot], [B_dst, sb])
        DVE(lambda e: e.tensor_scalar(out=s_[:, 1:2], in0=s_[:, 0:1], scalar1=1.0 / 2048, scalar2=1e-6, op0=ALU.mult, op1=ALU.add), [sb], [sb])
        ACT(lambda e: e.activation(out=s_[:, 2:3], in_=s_[:, 1:2], func=AF.Sqrt), [sb], [sb])
        DVE(lambda e: e.reciprocal(out=s_[:, 3:4], in_=s_[:, 2:3]), [sb], [sb])
        DVE(lambda e: e.scalar_tensor_tensor(out=dst, in0=src, scalar=s_[:, 3:4], in1=gfull, op0=ALU.mult, op1=ALU.mult),
            [sb, B_gfull] + B_h[ot], [B_dst])

    def norm_hres(g_d, dstT, B_dstT):
        load_g(g_d)
        for ot in range(8):
            sl = ot % 2
            rms_h(ot, abf2[sl], B_abf2[sl])
            transpose_tile(abf2[sl], B_abf2[sl], dstT, B_dstT[ot], ot * 128)

    norm_hres(g_ffn_d, a2T, B_a2T)
    gcount = [0]
    dcount = [0]
    for fq, (c0, c1) in enumerate(NQ_FF):
        ncq = c1 - c0
        for grp in range(c0 // 2, c1 // 2):
            sl = (gcount[0] % 2) * 2
            gcount[0] += 1
            S.dma("pool", lambda e, grp=grp, sl=sl: e.dma_start(out=wgu[sl].rearrange("p a b -> p (a b)"), in_=wg_d[grp]), f"wgu{sl}", [], [B_wgu[sl]])
            S.dma("pool", lambda e, grp=grp, sl=sl: e.dma_start(out=wgu[sl + 1].rearrange("p a b -> p (a b)"), in_=wu_d[grp]), f"wgu{sl + 1}", [], [B_wgu[sl + 1]])
            for half in range(2):
                ci = grp * 2 + half - c0
                for th2 in range(2):
                    pg, pu = npb(), npb()
                    for dc in range(16):
                        PE(lambda e, dc=dc, sl=sl, half=half, th2=th2, pg=pg: e.matmul(ps[pg][:], lhsT=wgu[sl][:, dc, half * 128:(half + 1) * 128], rhs=a2T[:, dc, th2 * 512:(th2 + 1) * 512],
                                                                                     start=(dc == 0), stop=(dc == 15)), [B_wgu[sl]] + B_a2T[th2 * 4:th2 * 4 + 4], [PB[pg]])
                    for dc in range(16):
                        PE(lambda e, dc=dc, sl=sl, half=half, th2=th2, pu=pu: e.matmul(ps[pu][:], lhsT=wgu[sl + 1][:, dc, half * 128:(half + 1) * 128], rhs=a2T[:, dc, th2 * 512:(th2 + 1) * 512],
                                                                                     start=(dc == 0), stop=(dc == 15)), [B_wgu[sl + 1]] + B_a2T[th2 * 4:th2 * 4 + 4], [PB[pu]])
                    ss = th2
                    ACT(lambda e, pg=pg, ss=ss: e.activation(out=sgt[ss], in_=ps[pg][:], func=AF.Silu), [PB[pg]], [B_sgt[ss]])
                    DVE(lambda e, pu=pu, ss=ss, ci=ci, th2=th2: e.tensor_tensor(out=hT[:, ci, th2 * 512:(th2 + 1) * 512], in0=ps[pu][:], in1=sgt[ss], op=ALU.mult),
                        [PB[pu], B_sgt[ss]], [B_hT[ci][th2]])
        for dmc in range(4):
            sl = dcount[0] % 2
            dcount[0] += 1
            S.dma("pool", lambda e, fq=fq, dmc=dmc, sl=sl: e.dma_start(out=wdn[sl].rearrange("p a b -> p (a b)"), in_=wd_d[fq * 4 + dmc]), f"wdn{sl}", [], [B_wdn[sl]])
            for ot in range(8):
                pb = npb()
                for ci in range(ncq):
                    PE(lambda e, ci=ci, ot=ot, sl=sl, pb=pb: e.matmul(ps[pb][:], lhsT=hT[:, ci, ot * 128:(ot + 1) * 128], rhs=wdn[sl][:, ci, :], start=(ci == 0), stop=(ci == ncq - 1)),
                       [B_hT[ci][ot // 4], B_wdn[sl]], [PB[pb]])
                DVE(lambda e, ot=ot, dmc=dmc, pb=pb: e.tensor_tensor(out=hres[:, ot, dmc * 512:(dmc + 1) * 512], in0=ps[pb][:], in1=hres[:, ot, dmc * 512:(dmc + 1) * 512], op=ALU.add),
                    [PB[pb], B_h[ot][dmc]], [B_h[ot][dmc]])
    dump("h2", hres.rearrange("p a b -> p (a b)"), [b for l in B_h for b in l])
    if stop_after == "E":
        return _finish(nc, S, st, out_bufs)

    FC = Carver(A, E_A2T_END, ARENA_BYTES - 8192, "F")
    a3T = a2T
    B_a3T = B_a2T
    wpg = [FC.take([16, 512], BF16) for _ in range(2)]
    wpp = [FC.take([2, 512], BF16) for _ in range(2)]
    pT = FC.take([2, 1024], BF16)
    pf = [FC.take([256]) for _ in range(2)]
    pbf = [FC.take([256], BF16) for _ in range(2)]
    gt = [FC.take([512]) for _ in range(2)]
    outt = [FC.take([2048]) for _ in range(2)]
    B_wpg = S.bufs(2, "wpg")
    B_wpp = S.bufs(2, "wpp")
    B_pT = S.bufs(8, "pT")
    B_pf = S.bufs(2, "pf")
    B_pbf = S.bufs(2, "pbf")
    B_gt = S.bufs(2, "gt")
    B_outt = S.bufs(2, "outt")
    S.barrier(B_wpg + B_wpp + B_pT + B_pf + B_pbf + B_gt + B_outt)
    norm_hres(g_ple_d, a3T, B_a3T)
    for ot in range(8):
        sl = ot % 2
        S.dma("sp", lambda e, ot=ot, sl=sl: e.dma_start(out=pf[sl], in_=p_d[ot * 128:(ot + 1) * 128, :]), f"pf{sl}", [], [B_pf[sl]])
        ACT(lambda e, sl=sl: e.activation(out=pbf[sl], in_=pf[sl], func=AF.Copy), [B_pf[sl]], [B_pbf[sl]])
        for c2 in range(2):
            PE(lambda e, c2=c2, sl=sl: e.transpose(out=psb[7][:, c2 * 128:(c2 + 1) * 128], in_=pbf[sl][:, c2 * 128:(c2 + 1) * 128], identity=ident), [B_pbf[sl], B_const], [PB[7]])
        DVE(lambda e, ot=ot: e.tensor_copy(out=pT[:, :, ot * 128:(ot + 1) * 128], in_=psb[7][:, 0:256].rearrange("p (a b) -> p a b", a=2)), [PB[7]], [B_pT[ot]])
    pb7 = [0]

    def npb7():
        b = pb7[0] % 7
        pb7[0] += 1
        return b

    for dmc in range(4):
        sl = dmc % 2
        S.dma("pool", lambda e, dmc=dmc, sl=sl: e.dma_start(out=wpg[sl].rearrange("p a b -> p (a b)"), in_=wpg_d[dmc]), f"wpg{sl}", [], [B_wpg[sl]])
        S.dma("pool", lambda e, dmc=dmc, sl=sl: e.dma_start(out=wpp[sl].rearrange("p a b -> p (a b)"), in_=wpp_d[dmc]), f"wpp{sl}", [], [B_wpp[sl]])
        for ot in range(8):
            p1, p2 = npb7(), npb7()
            for dc in range(16):
                PE(lambda e, dc=dc, ot=ot, sl=sl, p1=p1: e.matmul(ps[p1][:], lhsT=a3T[:, dc, ot * 128:(ot + 1) * 128], rhs=wpg[sl][:, dc, :], start=(dc == 0), stop=(dc == 15)),
                   [B_a3T[ot], B_wpg[sl]], [PB[p1]])
            for c2 in range(2):
                PE(lambda e, c2=c2, ot=ot, sl=sl, p2=p2: e.matmul(ps[p2][:], lhsT=pT[:, c2, ot * 128:(ot + 1) * 128], rhs=wpp[sl][:, c2, :], start=(c2 == 0), stop=(c2 == 1)),
                   [B_pT[ot], B_wpp[sl]], [PB[p2]])
            gs = (dmc * 8 + ot) % 2
            ACT(lambda e, p1=p1, gs=gs: e.activation(out=gt[gs], in_=ps[p1][:], func=AF.Sigmoid), [PB[p1]], [B_gt[gs]])
            DVE(lambda e, p2=p2, gs=gs: e.tensor_tensor(out=gt[gs], in0=ps[p2][:], in1=gt[gs], op=ALU.mult), [PB[p2], B_gt[gs]], [B_gt[gs]])
            POOL(lambda e, ot=ot, dmc=dmc, gs=gs: e.tensor_tensor(out=hres[:, ot, dmc * 512:(dmc + 1) * 512], in0=gt[gs], in1=hres[:, ot, dmc * 512:(dmc + 1) * 512], op=ALU.add),
                 [B_gt[gs], B_h[ot][dmc]], [B_h[ot][dmc]])
    load_g(g_fin_d)
    for ot in range(8):
        sl = ot % 2
        rms_h(ot, outt[sl], B_outt[sl])
        ob = S.buf()
        S.dma("sp", lambda e, ot=ot, sl=sl: e.dma_start(out=out_d[ot * 128:(ot + 1) * 128, :], in_=outt[sl]), f"ost{sl}", [B_outt[sl]], [ob])
        out_bufs.append(ob)
    return _finish(nc, S, st, out_bufs)


def _finish(nc, S, st, out_bufs):
    S.op("sp", lambda e: e.nop(), out_bufs, [])
    S.emit(nc, st)
    st.close()
    return nc, S


def _tile_cols(W, c0, n):
    return np.ascontiguousarray(W[:, c0:c0 + n].reshape(16, 128, n).transpose(1, 0, 2).reshape(128, 16 * n))


def _consts(th):
    f32 = np.float32
    L = np.arange(2048)
    tg = L - 1024 + 1024 * th
    pos = np.maximum(tg, 0).astype(f32)
    inv_freq = (f32(500000.0) ** (-np.arange(0, 32, 2, dtype=f32) / f32(32))).astype(f32)
    ang = (pos[:, None] * inv_freq[None, :]).astype(f32)
    cos, sin = np.cos(ang).astype(f32), np.sin(ang).astype(f32)
    cos2 = np.concatenate([cos, cos], 1)
    sin2 = np.concatenate([-sin, sin], 1)

    def tokmaj(a):
        w = a.shape[1]
        return np.ascontiguousarray(a.reshape(16, 128, w).transpose(1, 0, 2).reshape(128, 16 * w))

    n = np.arange(128)
    Lo = 1024 + np.arange(1024)
    cvalid = ((16 * n[:, None] + 31) <= Lo[None, :]) & ((th == 1) | (n[:, None] >= 64))
    s = np.arange(32)
    overlap = ((16 * n[:, None]) < (64 * s[None, :] + 64)) & ((16 * n[:, None] + 32) > 64 * s[None, :])
    overlap = overlap & (n[:, None] < 127)
    cur = Lo // 64
    valid = (s[None, :] <= cur[:, None]) & ((th == 1) | (s[None, :] >= 16))
    forced = (s[None, :] == cur[:, None]) | (s[None, :] == cur[:, None] - 1) | (s[None, :] == 16 * (1 - th))
    selbias = np.where(valid, np.where(forced, BIG, 0.0), -BIG).astype(f32)
    selbias = np.ascontiguousarray(selbias.reshape(8, 128, 32).transpose(1, 0, 2).reshape(128, 256))
    k = np.arange(128)
    E = np.zeros((32, 16, 128), f32)
    for kt in range(16):
        E[2 * kt + k // 64, kt, k] = 1.0
    ident = np.eye(128, dtype=f32)
    diag = (k[:, None] <= k[None, :]).astype(f32)
    upst = (k[:, None] > k[None, :]).astype(f32)
    tri = np.concatenate([ident, diag, upst], 1)
    wvalid = np.zeros((8, 5), f32)
    for ot in range(8):
        for off in range(5):
            kt = 8 + ot - 4 + off
            wvalid[ot, off] = 1.0 if (th == 1 or kt >= 8) else 0.0
    wvalid = np.broadcast_to(wvalid.reshape(1, 40), (128, 40))
    invc = np.zeros((4, 16), f32)
    for gi in range(4):
        w = 2 << gi
        for i in range(16):
            invc[gi, i] = 1.0 / w if th == 1 else 1.0 / min(i + 1, w)
    invc = np.broadcast_to(invc.reshape(1, 64), (128, 64))
    c = lambda a: np.ascontiguousarray(a, dtype=f32)
    return {
        "c_cos": tokmaj(cos2), "c_sin": tokmaj(sin2), "c_cvalid": c(cvalid), "c_overlap": c(overlap), "c_selbias": c(selbias),
        "c_E": c(E.reshape(32, 2048)), "c_tri": c(tri), "c_wvalid": c(wvalid), "c_invc": c(invc),
    }


def _prep_shared(w_in, pool_w, pool_scale, cmp_k_pe, cmp_k_w1, cmp_k_w2, cmp_v_pe, cmp_v_w1, cmp_v_w2, w_out,
                 w_gate, w_up, w_down, w_ple_gate, w_ple_proj, in_norm_g, ffn_norm_g, ple_norm_g, final_norm_g):
    c = lambda a: np.ascontiguousarray(a, dtype=np.float32)
    d = {}
    w_in = w_in[0]
    d["wU"] = np.stack([_tile_cols(w_in, i * 256, 256) for i in range(4)])
    d["wT"] = np.stack([_tile_cols(w_in, 1024 + i * 256, 256) for i in range(10)])
    d["wG"] = _tile_cols(w_in, 3584, 24)
    d["poolw"] = c(pool_w[0].reshape(4, 2, 128, 256).transpose(2, 0, 1, 3).reshape(128, 2048))
    d["pscale"] = c(pool_scale[0].reshape(8, 128).T)
    d["w1k"] = c(cmp_k_w1[0].reshape(32, 128, 128).transpose(1, 0, 2).reshape(128, 4096))
    d["w1v"] = c(cmp_v_w1[0].reshape(32, 128, 128).transpose(1, 0, 2).reshape(128, 4096))
    d["w2k"] = c(cmp_k_w2[0])
    d["w2v"] = c(cmp_v_w2[0])
    d["pekT"] = c(cmp_k_pe[0].T)
    d["pevT"] = c(cmp_v_pe[0].T)
    d["wout"] = np.stack([_tile_cols(w_out[0], i * 512, 512) for i in range(4)])
    d["wgate"] = np.stack([_tile_cols(w_gate[0], i * 256, 256) for i in range(22)])
    d["wup"] = np.stack([_tile_cols(w_up[0], i * 256, 256) for i in range(22)])
    wd = np.zeros((16, 128, 12 * 512), np.float32)
    for fq, (c0, c1) in enumerate(NQ_FF):
        for dmc in range(4):
            blk = w_down[0][c0 * 128:c1 * 128, dmc * 512:(dmc + 1) * 512].reshape(c1 - c0, 128, 512).transpose(1, 0, 2)
            wd[fq * 4 + dmc, :, :(c1 - c0) * 512] = blk.reshape(128, -1)
    d["wdown"] = wd
    d["wpg"] = np.stack([_tile_cols(w_ple_gate[0], i * 512, 512) for i in range(4)])
    wpp = w_ple_proj[0]
    d["wpp"] = np.stack([c(wpp[:, i * 512:(i + 1) * 512].reshape(2, 128, 512).transpose(1, 0, 2).reshape(128, 1024)) for i in range(4)])
    d["g_in"] = c(in_norm_g[0])
    d["g_ffn"] = c(ffn_norm_g[0])
    d["g_ple"] = c(ple_norm_g[0])
    d["g_fin"] = c(final_norm_g)
    return d


def make_in_maps(x, p, **w):
    shared = _prep_shared(**w)
    cst = [_consts(0), _consts(1)]
    in_maps = []
    for b in range(4):
        for th in range(2):
            m = dict(shared)
            m.update(cst[th])
            if th == 1:
                xl = x[b]
            else:
                xl = np.concatenate([np.zeros((1024, 2048), np.float32), x[b, :1024]], 0)
            m["x"] = np.ascontiguousarray(xl, dtype=np.float32)
            m["p"] = np.ascontiguousarray(p[0, b, th * 1024:(th + 1) * 1024], dtype=np.float32)
            in_maps.append(m)
    return in_maps


_NC_CACHE = {}


def kernel(x, p, in_norm_g, w_in, pool_w, pool_scale, cmp_k_pe, cmp_k_w1, cmp_k_w2, cmp_v_pe, cmp_v_w1, cmp_v_w2,
           w_out, ffn_norm_g, w_gate, w_up, w_down, ple_norm_g, w_ple_gate, w_ple_proj, final_norm_g):
    args = dict(locals())
    args = {k: np.asarray(v) for k, v in args.items()}
    x = args.pop("x")
    p = args.pop("p")
    in_maps = make_in_maps(x, p, **args)
    if "nc" not in _NC_CACHE:
        _NC_CACHE["nc"] = build()[0]
    nc = _NC_CACHE["nc"]
    res = run_bass_kernel_spmd(nc, in_maps, core_ids=list(range(8)))
    out = np.zeros((4, 2048, 2048), np.float32)
    for b in range(4):
        for th in range(2):
            out[b, th * 1024:(th + 1) * 1024] = res.results[b * 2 + th]["out"]
    return out
```

```python
import numpy as np
from contextlib import ExitStack
import concourse.bass as bass
import concourse.mybir as mybir
from concourse.bass_utils import run_bass_kernel_spmd

F32 = mybir.dt.float32
BF16 = mybir.dt.bfloat16
AF = mybir.ActivationFunctionType
ALU = mybir.AluOpType
AX = mybir.AxisListType

ENGS = ("pe", "act", "dve", "pool", "sp")
BIG = 1.0e30
SCALE = 128.0 ** -0.5


class Buf:
    __slots__ = ("name", "last_w", "readers", "excl")

    def __init__(self, name, excl=False):
        self.name = name
        self.last_w = None
        self.readers = []
        self.excl = excl


class Op:
    __slots__ = ("eng", "fn", "deps", "signal", "sigval", "dma_key", "dma_val", "idx")


class Sched:
    def __init__(self):
        self.ops = {e: [] for e in ENGS}
        self.dma_cnt = {}
        self.nbuf = 0

    def buf(self, name=None):
        self.nbuf += 1
        return Buf(name or f"b{self.nbuf}")

    def bufs(self, n, name="b"):
        return [self.buf(f"{name}{i}") for i in range(n)]

    def _add(self, eng, fn, reads, writes, dma_key=None):
        op = Op()
        op.eng = eng
        op.fn = fn
        op.signal = False
        op.sigval = None
        op.dma_key = dma_key
        op.dma_val = None
        op.idx = len(self.ops[eng])
        deps = []
        for b in reads:
            if b.last_w is not None:
                deps.append(b.last_w)
            if b.excl:
                deps.extend(t for t in b.readers if t[1] != eng)
        for b in writes:
            if b.last_w is not None:
                deps.append(b.last_w)
            deps.extend(b.readers)
        if dma_key is not None:
            self.dma_cnt[dma_key] = self.dma_cnt.get(dma_key, 0) + 16
            op.dma_val = self.dma_cnt[dma_key]
            tok = ("dma", dma_key, op.dma_val)
        else:
            tok = ("eng", eng, op.idx)
        op.deps = [d for d in set(deps)
                   if not (d[0] == "eng" and d[1] == "pe" and eng == "pe" and dma_key is None)]
        for b in reads:
            b.readers.append(tok)
        for b in writes:
            b.last_w = tok
            b.readers = []
        self.ops[eng].append(op)
        return op

    def op(self, eng, fn, reads=(), writes=()):
        return self._add(eng, fn, list(reads), list(writes))

    def dma(self, eng, fn, key, reads=(), writes=()):
        return self._add(eng, fn, list(reads), list(writes), dma_key=key)

    def barrier(self, bufs):
        toks = []
        for e in ENGS:
            for o in reversed(self.ops[e]):
                if o.dma_key is None:
                    toks.append(("eng", e, o.idx))
                    break
        for k, v in self.dma_cnt.items():
            toks.append(("dma", k, v))
        for b in bufs:
            b.readers.extend(toks)

    def emit(self, nc, stack):
        for e in ENGS:
            for o in self.ops[e]:
                for d in o.deps:
                    if d[0] == "eng":
                        self.ops[d[1]][d[2]].signal = True
        esem = {}
        for e in ENGS:
            c = 0
            for o in self.ops[e]:
                if o.signal and o.dma_key is None:
                    c += 1
                    o.sigval = c
            if c > 0:
                esem[e] = stack.enter_context(nc.semaphore(f"s_{e}"))
        dsem = {k: stack.enter_context(nc.semaphore(f"d_{k}")) for k in self.dma_cnt}
        self.nsem = len(esem) + len(dsem)
        block = stack.enter_context(nc.Block())
        ops = self.ops
        stats = {}

        def run(e, engobj):
            waited = {}
            nw = 0
            for o in ops[e]:
                need = {}
                for d in o.deps:
                    if d[0] == "eng":
                        sem = esem[d[1]]
                        val = ops[d[1]][d[2]].sigval
                        key = ("e", d[1])
                    else:
                        sem = dsem[d[1]]
                        val = d[2]
                        key = ("d", d[1])
                    if need.get(key, (None, -1))[1] < val:
                        need[key] = (sem, val)
                for key, (sem, val) in need.items():
                    if waited.get(key, -1) >= val:
                        continue
                    waited[key] = val
                    engobj.wait_ge(sem, val)
                    nw += 1
                ins = o.fn(engobj)
                if o.dma_key is not None:
                    ins.then_inc(dsem[o.dma_key], 16)
                elif o.signal:
                    ins.then_inc(esem[e], 1)
            stats[e] = (len(ops[e]), nw)

        if ops["sp"]:
            block.sync(lambda eng: run("sp", eng))
        if ops["pe"]:
            block.tensor(lambda eng: run("pe", eng))
        if ops["act"]:
            block.scalar(lambda eng: run("act", eng))
        if ops["dve"]:
            block.vector(lambda eng: run("dve", eng))
        if ops["pool"]:
            block.gpsimd(lambda eng: run("pool", eng))
        self.stats = stats


class Arena:
    def __init__(self, nc, nbytes):
        self.nbytes = nbytes
        self.t = nc.alloc_sbuf_tensor("arena", [128, nbytes // 4], F32)

    def view(self, off, shape, dtype=F32, parts=128):
        shape = list(shape)
        n = int(np.prod(shape))
        nb = n * (2 if dtype == BF16 else 4)
        assert off % 4 == 0 and nb % 4 == 0, (off, nb)
        assert off + nb <= self.nbytes, ("arena overflow", off, nb)
        ap = self.t[0:parts, off // 4:(off + nb) // 4]
        if dtype == BF16:
            ap = ap.bitcast(BF16)
        if len(shape) == 2:
            ap = ap.rearrange("p (a b) -> p a b", a=shape[0])
        elif len(shape) == 3:
            ap = ap.rearrange("p (a b c) -> p a b c", a=shape[0], b=shape[1])
        elif len(shape) == 4:
            ap = ap.rearrange("p (a b c d) -> p a b c d", a=shape[0], b=shape[1], c=shape[2])
        return ap


class Carver:
    def __init__(self, arena, start, end, name=""):
        self.a = arena
        self.p = start
        self.end = end
        self.name = name

    def take(self, shape, dtype=F32, parts=128):
        n = int(np.prod(shape)) * (2 if dtype == BF16 else 4)
        n = (n + 31) // 32 * 32
        off = self.p
        self.p += n
        assert self.p <= self.end, ("carver overflow", self.name, self.p, self.end)
        return self.a.view(off, shape, dtype, parts)


ARENA_BYTES = 212000
NQ_FF = [(0, 12), (12, 22), (22, 34), (34, 44)]


def build(stop_after=None, dbg=()):
    nc = bass.Bass("TRN2", target_bir_lowering=False)

    def din(name, shape):
        return nc.dram_tensor(name, list(shape), F32, kind="ExternalInput").ap()

    x_d = din("x", [2048, 2048])
    p_d = din("p", [1024, 256])
    g_in_d = din("g_in", [2048])
    g_ffn_d = din("g_ffn", [2048])
    g_ple_d = din("g_ple", [2048])
    g_fin_d = din("g_fin", [2048])
    wU_d = din("wU", [4, 128, 16 * 256])
    wT_d = din("wT", [10, 128, 16 * 256])
    wG_d = din("wG", [128, 16 * 24])
    poolw_d = din("poolw", [128, 4 * 2 * 256])
    pscale_d = din("pscale", [128, 8])
    w1k_d = din("w1k", [128, 32 * 128])
    w1v_d = din("w1v", [128, 32 * 128])
    w2k_d = din("w2k", [128, 128])
    w2v_d = din("w2v", [128, 128])
    pek_d = din("pekT", [128, 32])
    pev_d = din("pevT", [128, 32])
    wout_d = din("wout", [4, 128, 16 * 512])
    wg_d = din("wgate", [22, 128, 16 * 256])
    wu_d = din("wup", [22, 128, 16 * 256])
    wd_d = din("wdown", [16, 128, 12 * 512])
    wpg_d = din("wpg", [4, 128, 16 * 512])
    wpp_d = din("wpp", [4, 128, 2 * 512])
    c_cos_d = din("c_cos", [128, 16 * 32])
    c_sin_d = din("c_sin", [128, 16 * 32])
    c_cvalid_d = din("c_cvalid", [128, 1024])
    c_overlap_d = din("c_overlap", [128, 32])
    c_selbias_d = din("c_selbias", [128, 8 * 32])
    c_E_d = din("c_E", [32, 16 * 128])
    c_tri_d = din("c_tri", [128, 3 * 128])
    c_wvalid_d = din("c_wvalid", [128, 40])
    c_invc_d = din("c_invc", [128, 4 * 16])
    out_d = nc.dram_tensor("out", [1024, 2048], F32, kind="ExternalOutput").ap()
    dbg_d = {}
    for name, shape in dbg:
        dbg_d[name] = nc.dram_tensor("dbg_" + name, list(shape), F32, kind="ExternalOutput").ap()

    S = Sched()
    st = ExitStack()
    A = Arena(nc, ARENA_BYTES)
    ps = [st.enter_context(nc.psum_tensor(f"ps{i}", [128, 512], F32)) for i in range(8)]
    psb = [p[:].bitcast(BF16) for p in ps]
    PB = [Buf(f"psum{i}", excl=True) for i in range(8)]

    def PE(fn, r=(), w=()):
        return S.op("pe", fn, r, w)

    def ACT(fn, r=(), w=()):
        return S.op("act", fn, r, w)

    def DVE(fn, r=(), w=()):
        return S.op("dve", fn, r, w)

    def POOL(fn, r=(), w=()):
        return S.op("pool", fn, r, w)

    out_bufs = []

    dbg_state = [ARENA_BYTES - 8192, 2048]
    B_dbgstg = S.buf("dbgstg")

    def dump(name, ap, buf, parts=128):
        if name not in dbg_d:
            return
        d = dbg_d[name]
        n = ap.shape[1]
        stg = A.view(dbg_state[0], [dbg_state[1]], F32)
        bl = buf if isinstance(buf, list) else [buf]
        for c0 in range(0, n, dbg_state[1]):
            c1 = min(n, c0 + dbg_state[1])
            ACT(lambda e, c0=c0, c1=c1: e.activation(out=stg[0:parts, 0:c1 - c0], in_=ap[:, c0:c1], func=AF.Copy), bl, [B_dbgstg])
            db = S.buf("dbgd_" + name)
            S.dma("sp", lambda e, c0=c0, c1=c1: e.dma_start(out=d[0:parts, c0:c1], in_=stg[0:parts, 0:c1 - c0]), "dbg", [B_dbgstg], [db])
            out_bufs.append(db)

    CONST_END = 26624
    CC = Carver(A, 0, CONST_END, "const")
    tri = CC.take([3, 128], BF16)
    ident = tri[:, 0, :]
    diag = tri[:, 1, :]
    upst = tri[:, 2, :]
    cvalid = CC.take([1024], BF16)
    Emat = CC.take([16, 128], BF16, parts=32)
    selbias = CC.take([8, 32])
    invc = CC.take([4, 16])
    cos2 = CC.take([16, 32])
    sin2 = CC.take([16, 32])
    wvalid = CC.take([40])
    gsig = CC.take([8, 24])
    pscale = CC.take([8])
    gfull = CC.take([2048])
    stat = CC.take([16, 8])
    B_const = S.buf("const")
    B_gfull = S.buf("gfull")
    B_gsig = S.bufs(8, "gsig")

    S.dma("sp", lambda e: e.dma_start(out=selbias.rearrange("p a b -> p (a b)"), in_=c_selbias_d), "cst", [], [B_const])
    S.dma("sp", lambda e: e.dma_start(out=invc.rearrange("p a b -> p (a b)"), in_=c_invc_d), "cst", [], [B_const])
    S.dma("sp", lambda e: e.dma_start(out=cos2.rearrange("p a b -> p (a b)"), in_=c_cos_d), "cst", [], [B_const])
    S.dma("sp", lambda e: e.dma_start(out=sin2.rearrange("p a b -> p (a b)"), in_=c_sin_d), "cst", [], [B_const])
    S.dma("sp", lambda e: e.dma_start(out=wvalid, in_=c_wvalid_d), "cst", [], [B_const])
    S.dma("sp", lambda e: e.dma_start(out=pscale, in_=pscale_d), "cst", [], [B_const])
    S.dma("pool", lambda e: e.dma_start(out=tri.rearrange("p a b -> p (a b)"), in_=c_tri_d), "cstp", [], [B_const])
    S.dma("pool", lambda e: e.dma_start(out=cvalid, in_=c_cvalid_d), "cstp", [], [B_const])
    S.dma("pool", lambda e: e.dma_start(out=Emat.rearrange("p a b -> p (a b)"), in_=c_E_d), "cstp", [], [B_const])

    def load_g(g_d):
        S.dma("sp", lambda e: e.dma_start(out=gfull, in_=g_d.partition_broadcast(128)), "gld", [], [B_gfull])

    PC = Carver(A, CONST_END, ARENA_BYTES, "persist")
    qT = PC.take([2, 8, 4, 128], BF16)
    kT = PC.take([3, 2, 2048], BF16)
    vcT = PC.take([2, 2048], BF16)
    v1 = PC.take([4, 16, 132], BF16)
    YMIX_OFF = PC.p
    ymixT = PC.take([16, 1024], BF16)
    uhalo = PC.take([8, 16])
    TRANS0 = PC.p
    B_qT = [[S.buf(f"qT{g}_{o}") for o in range(8)] for g in range(2)]
    B_kT = [[[S.buf(f"kT{k}_{g}_{l}") for l in range(16)] for g in range(2)] for k in range(3)]
    B_vcT = [[S.buf(f"vcT{g}_{l}") for l in range(16)] for g in range(2)]
    B_v1 = [[S.buf(f"v1{k}_{l}") for l in range(16)] for k in range(4)]
    B_ymix = [S.buf(f"ymix{c}") for c in range(16)]
    B_uhalo = S.buf("uhalo")
    B_v1init = S.buf("v1init")
    POOL(lambda e: e.memset(v1[:, :, :, 128:132], 1.0), [], [B_v1init])

    nstat = [0]

    def rms_to_bf16(src, B_src, dst_bf, B_dst, junk, B_junk):
        k = nstat[0] % 16
        nstat[0] += 1
        s = stat[:, k, :]
        sb = S.buf()
        ACT(lambda e: e.activation(out=junk, in_=src, func=AF.Square, accum_out=s[:, 0:1]), [B_src], [B_junk, sb])
        DVE(lambda e: e.tensor_scalar(out=s[:, 1:2], in0=s[:, 0:1], scalar1=1.0 / 2048, scalar2=1e-6, op0=ALU.mult, op1=ALU.add), [sb], [sb])
        ACT(lambda e: e.activation(out=s[:, 2:3], in_=s[:, 1:2], func=AF.Sqrt), [sb], [sb])
        DVE(lambda e: e.reciprocal(out=s[:, 3:4], in_=s[:, 2:3]), [sb], [sb])
        DVE(lambda e: e.scalar_tensor_tensor(out=dst_bf, in0=src, scalar=s[:, 3:4], in1=gfull, op0=ALU.mult, op1=ALU.mult),
            [sb, B_src, B_gfull], [B_dst])

    def transpose_tile(a_bf, B_a, aT_dst, B_aT, col0):
        for hb in range(2):
            bank = 6 + hb
            for j in range(8):
                dc = hb * 8 + j
                PE(lambda e, dc=dc, j=j, bank=bank: e.transpose(out=psb[bank][:, j * 128:(j + 1) * 128], in_=a_bf[:, dc * 128:(dc + 1) * 128], identity=ident),
                   [B_a, B_const], [PB[bank]])
            eng = ACT if hb == 0 else DVE
            if hb == 0:
                ACT(lambda e, bank=bank, hb=hb: e.activation(out=aT_dst[:, hb * 8:(hb + 1) * 8, col0:col0 + 128],
                                                           in_=psb[bank].rearrange("p (a b) -> p a b", a=8), func=AF.Copy), [PB[bank]], [B_aT])
            else:
                DVE(lambda e, bank=bank, hb=hb: e.tensor_copy(out=aT_dst[:, hb * 8:(hb + 1) * 8, col0:col0 + 128],
                                                            in_=psb[bank].rearrange("p (a b) -> p a b", a=8)), [PB[bank]], [B_aT])

    TC = Carver(A, TRANS0, ARENA_BYTES - 8192, "AB")
    aT = TC.take([16, 1024], BF16)
    wsl = [TC.take([16, 256], BF16) for _ in range(2)]
    AB_SHARED = TC.p
    xt = [TC.take([2048]) for _ in range(2)]
    abf = [TC.take([2048], BF16) for _ in range(2)]
    stage = [TC.take([2, 128], BF16) for _ in range(2)]
    rtmp = [TC.take([2, 32]) for _ in range(2)]
    rtmp2 = [TC.take([2, 32]) for _ in range(2)]
    wgates = TC.take([16, 24], BF16)
    B_aT = [S.buf(f"aT{i}") for i in range(8)]
    B_xt = S.bufs(2, "xt")
    B_abf = S.bufs(2, "abf")
    B_wsl = S.bufs(2, "wsl")
    B_stage = S.bufs(2, "stage")
    B_rtmp = S.bufs(2, "rtmp")
    B_wgates = S.buf("wgates")
    wcount = [0]
    scount = [0]

    load_g(g_in_d)
    S.dma("pool", lambda e: e.dma_start(out=wgates.rearrange("p a b -> p (a b)"), in_=wG_d), "wgl", [], [B_wgates])

    def load_w(src_ap):
        i = wcount[0] % 2
        wcount[0] += 1
        S.dma("pool", lambda e: e.dma_start(out=wsl[i].rearrange("p a b -> p (a b)"), in_=src_ap), f"wsl{i}", [], [B_wsl[i]])
        return i

    def norm_pass(pas):
        for i in range(8):
            lt = pas * 8 + i
            sl = lt % 2
            S.dma("sp", lambda e, lt=lt, sl=sl: e.dma_start(out=xt[sl], in_=x_d[lt * 128:(lt + 1) * 128, :]), f"xt{sl}", [], [B_xt[sl]])
            rms_to_bf16(xt[sl], B_xt[sl], abf[sl], B_abf[sl], abf[sl], B_abf[sl])
            transpose_tile(abf[sl], B_abf[sl], aT, B_aT[i], i * 128)

    def rope(pbank, stg, B_stg, lt, sl):
        src = ps[pbank][:, 0:256].rearrange("p (h d) -> p h d", h=2)
        c2 = cos2[:, lt, :].unsqueeze(1).to_broadcast([128, 2, 32])
        sA = sin2[:, lt, 0:16].unsqueeze(1).to_broadcast([128, 2, 16])
        sB = sin2[:, lt, 16:32].unsqueeze(1).to_broadcast([128, 2, 16])
        t1 = rtmp[sl]
        t2 = rtmp2[sl]
        DVE(lambda e: e.tensor_tensor(out=t1, in0=src[:, :, 0:32], in1=c2, op=ALU.mult), [PB[pbank], B_const], [B_rtmp[sl]])
        DVE(lambda e: e.tensor_tensor(out=t2[:, :, 0:16], in0=src[:, :, 16:32], in1=sA, op=ALU.mult), [PB[pbank], B_const], [B_rtmp[sl]])
        DVE(lambda e: e.tensor_tensor(out=t2[:, :, 16:32], in0=src[:, :, 0:16], in1=sB, op=ALU.mult), [PB[pbank], B_const], [B_rtmp[sl]])
        DVE(lambda e: e.tensor_tensor(out=stg[:, :, 0:32], in0=t1, in1=t2, op=ALU.add), [B_rtmp[sl]], [B_stg])

    def tok_half(hg, pas, i, wi, pbank):
        lt = pas * 8 + i
        for dc in range(16):
            PE(lambda e, dc=dc: e.matmul(ps[pbank][:, 0:256], lhsT=aT[:, dc, i * 128:(i + 1) * 128], rhs=wsl[wi][:, dc, :], start=(dc == 0), stop=(dc == 15)),
               [B_aT[i], B_wsl[wi]], [PB[pbank]])

        def evac():
            src = ps[pbank][:, 0:256].rearrange("p (h d) -> p h d", h=2)
            if hg in (7, 9):
                vk = 0 if hg == 7 else 2
                ACT(lambda e: e.activation(out=v1[:, vk:vk + 2, lt, 0:128], in_=src, func=AF.Copy), [PB[pbank], B_v1init], [B_v1[vk][lt], B_v1[vk + 1][lt]])
                return
            sl = scount[0] % 2
            scount[0] += 1
            stg = stage[sl]
            tb = 6 + (scount[0] % 2)
            ACT(lambda e: e.activation(out=stg, in_=src, func=AF.Copy), [PB[pbank]], [B_stage[sl]])
            if hg != 5:
                rope(pbank, stg, B_stage[sl], lt, sl)
            for r in range(2):
                PE(lambda e, r=r: e.transpose(out=psb[tb][:, r * 128:(r + 1) * 128], in_=stg[:, r, :], identity=ident), [B_stage[sl], B_const], [PB[tb]])
            pin = psb[tb][:, 0:256].rearrange("p (g t) -> p g t", g=2)
            if hg < 4:
                g, r0, ot = hg // 2, (hg % 2) * 2, i
                DVE(lambda e: e.tensor_copy(out=qT[:, g, ot, r0:r0 + 2, :], in_=pin), [PB[tb]], [B_qT[g][ot]])
            elif hg == 5:
                DVE(lambda e: e.tensor_copy(out=vcT[:, :, lt * 128:(lt + 1) * 128], in_=pin), [PB[tb]], [B_vcT[0][lt], B_vcT[1][lt]])
            else:
                kind = (hg - 4) // 2
                DVE(lambda e: e.tensor_copy(out=kT[:, kind, :, lt * 128:(lt + 1) * 128], in_=pin), [PB[tb]], [B_kT[kind][0][lt], B_kT[kind][1][lt]])
        return evac

    pbrot = [0]

    def next_pb():
        b = pbrot[0] % 6
        pbrot[0] += 1
        return b

    if stop_after == "C0":
        dump("gfull", gfull, [B_gfull, B_const])
        return _finish(nc, S, st, out_bufs)
    for pas in range(2):
        norm_pass(pas)
        if stop_after == "N0":
            dump("aT", aT.rearrange("p a b -> p (a b)"), B_aT)
            return _finish(nc, S, st, out_bufs)
        hgs = [4, 5, 6, 7, 8, 9] if pas == 0 else list(range(10))
        pend = []
        for hg in hgs:
            wi = load_w(wT_d[hg])
            for i in range(8):
                pend.append(tok_half(hg, pas, i, wi, next_pb()))
                if len(pend) > 2:
                    pend.pop(0)()
        for ev in pend:
            ev()
        if pas == 0:
            for ug in range(4):
                wi = load_w(wU_d[ug])
                for cc in range(2):
                    c = ug * 2 + cc
                    pb = next_pb()
                    for dc in range(16):
                        PE(lambda e, dc=dc, cc=cc, wi=wi, pb=pb: e.matmul(ps[pb][:, 0:16], lhsT=wsl[wi][:, dc, cc * 128:(cc + 1) * 128], rhs=aT[:, dc, 1008:1024],
                                                                        start=(dc == 0), stop=(dc == 15)), [B_aT[7], B_wsl[wi]], [PB[pb]])
                    ACT(lambda e, c=c, pb=pb: e.activation(out=uhalo[:, c, :], in_=ps[pb][:, 0:16], func=AF.Copy), [PB[pb]], [B_uhalo])
            if stop_after == "P0":
                dump("kT", kT.rearrange("p a b c -> p (a b c)"), [B_kT[k][g][l] for k in range(3) for g in range(2) for l in range(16)])
                return _finish(nc, S, st, out_bufs)
        else:
            for i in range(8):
                pb = next_pb()
                for dc in range(16):
                    PE(lambda e, dc=dc, i=i, pb=pb: e.matmul(ps[pb][:, 0:24], lhsT=aT[:, dc, i * 128:(i + 1) * 128], rhs=wgates[:, dc, :],
                                                             start=(dc == 0), stop=(dc == 15)), [B_aT[i], B_wgates], [PB[pb]])
                ACT(lambda e, i=i, pb=pb: e.activation(out=gsig[:, i, :], in_=ps[pb][:, 0:24], func=AF.Sigmoid), [PB[pb]], [B_gsig[i]])

    dump("kT", kT.rearrange("p a b c -> p (a b c)"), [B_kT[k][g][l] for k in range(3) for g in range(2) for l in range(16)])
    dump("qT", qT.rearrange("p a b c d -> p (a b c d)"), [B_qT[g][o] for g in range(2) for o in range(8)])
    dump("v1", v1.rearrange("p a b c -> p (a b c)"), [B_v1[k][l] for k in range(4) for l in range(16)] + [B_v1init])
    dump("vcT", vcT.rearrange("p a b -> p (a b)"), [B_vcT[g][l] for g in range(2) for l in range(16)])
    dump("gsig", gsig.rearrange("p a b -> p (a b)"), B_gsig)

    if stop_after == "P1":
        return _finish(nc, S, st, out_bufs)
    UC = Carver(A, AB_SHARED, ARENA_BYTES - 8192, "pool")
    ubuf = UC.take([1040])
    sbufA = UC.take([1040])
    sbufB = UC.take([1040])
    ptmp = UC.take([16])
    pooled = [UC.take([2, 1024], BF16) for _ in range(2)]
    poolw = UC.take([4, 2, 256], BF16)
    B_ubuf = S.buf("ubuf")
    B_sA = S.buf("sA")
    B_sB = S.buf("sB")
    B_ptmp = S.buf("ptmp")
    B_pooled = S.bufs(2, "pooled")
    B_poolw = S.buf("poolw")
    S.barrier([B_ubuf, B_sA, B_sB, B_ptmp, B_poolw] + B_pooled)
    S.dma("pool", lambda e: e.dma_start(out=poolw.rearrange("p a b c -> p (a b c)"), in_=poolw_d), "poolw", [], [B_poolw])
    for gi in range(4):
        w = 2 << gi
        wi = load_w(wU_d[gi])
        psl = gi % 2
        for cc in range(2):
            c = gi * 2 + cc
            pbs = [next_pb(), next_pb()]
            for th2 in range(2):
                for dc in range(16):
                    PE(lambda e, dc=dc, cc=cc, wi=wi, th2=th2, pb=pbs[th2]: e.matmul(ps[pb][:], lhsT=wsl[wi][:, dc, cc * 128:(cc + 1) * 128],
                                                                                   rhs=aT[:, dc, th2 * 512:(th2 + 1) * 512], start=(dc == 0), stop=(dc == 15)),
                       [B_aT[th2 * 4 + k] for k in range(4)] + [B_wsl[wi]], [PB[pbs[th2]]])
            ACT(lambda e, c=c: e.activation(out=ubuf[:, 0:16], in_=uhalo[:, c, :], func=AF.Copy), [B_uhalo], [B_ubuf])
            ACT(lambda e, pb=pbs[0]: e.activation(out=ubuf[:, 16:528], in_=ps[pb][:], func=AF.Copy), [PB[pbs[0]]], [B_ubuf])
            ACT(lambda e, pb=pbs[1]: e.activation(out=ubuf[:, 528:1040], in_=ps[pb][:], func=AF.Copy), [PB[pbs[1]]], [B_ubuf])
            cur, Bcur = ubuf, B_ubuf
            nxt = [(sbufA, B_sA), (sbufB, B_sB)]
            sh = 1
            k = 0
            while sh < w:
                dst, Bd = nxt[k % 2]
                DVE(lambda e, cur=cur, dst=dst, sh=sh: e.tensor_tensor(out=dst[:, sh:1040], in0=cur[:, sh:1040], in1=cur[:, 0:1040 - sh], op=ALU.add), [Bcur], [Bd])
                cur, Bcur = dst, Bd
                sh *= 2
                k += 1
            pl = pooled[psl][:, cc, :]
            DVE(lambda e, cur=cur, pl=pl, w=w: e.scalar_tensor_tensor(out=pl[:, 16:1024], in0=cur[:, 32:1040], scalar=1.0 / w, in1=ubuf[:, 32:1040], op0=ALU.mult, op1=ALU.subtract),
                [Bcur, B_ubuf], [B_pooled[psl]])
            DVE(lambda e, cur=cur, gi=gi: e.tensor_tensor(out=ptmp, in0=cur[:, 16:32], in1=invc[:, gi, :], op=ALU.mult), [Bcur, B_const], [B_ptmp])
            DVE(lambda e, pl=pl: e.tensor_tensor(out=pl[:, 0:16], in0=ptmp, in1=ubuf[:, 16:32], op=ALU.subtract), [B_ptmp, B_ubuf], [B_pooled[psl]])
        for dch in range(2):
            for th2 in range(2):
                pb = next_pb()
                for cc in range(2):
                    PE(lambda e, cc=cc, dch=dch, th2=th2, pb=pb, gi=gi, psl=psl: e.matmul(ps[pb][:], lhsT=poolw[:, gi, cc, dch * 128:(dch + 1) * 128],
                                                                                        rhs=pooled[psl][:, cc, th2 * 512:(th2 + 1) * 512], start=(cc == 0), stop=(cc == 1)),
                       [B_poolw, B_pooled[psl]], [PB[pb]])
                ch = gi * 2 + dch
                ACT(lambda e, ch=ch, th2=th2, pb=pb: e.activation(out=ymixT[:, ch, th2 * 512:(th2 + 1) * 512], in_=ps[pb][:], func=AF.Copy, scale=pscale[:, ch:ch + 1]),
                    [PB[pb], B_const], [B_ymix[ch]])
    dump("ypool", ymixT[:, 0:8, :].rearrange("p a b -> p (a b)"), B_ymix[0:8])

    if stop_after == "B":
        return _finish(nc, S, st, out_bufs)

    AC = Carver(A, TRANS0, ARENA_BYTES - 8192, "attn")
    w1 = [AC.take([32, 128], BF16) for _ in range(2)]
    w2 = [AC.take([128], BF16) for _ in range(2)]
    peT = [AC.take([32], BF16) for _ in range(2)]
    hb_ = AC.take([4, 128])
    hx = AC.take([4, 128])
    hy = AC.take([4, 128])
    cbias = AC.take([2, 2])
    geluT = AC.take([4, 128], BF16)
    kcmpT = AC.take([2, 128], BF16)
    vcmp1 = AC.take([2, 164], BF16)
    PT = [AC.take([4, 128], BF16) for _ in range(4)]
    oaccs = [AC.take([4, 128]) for _ in range(16)]
    obf = [AC.take([4, 128], BF16) for _ in range(2)]
    impb = [AC.take([32]) for _ in range(2)]
    impw = [AC.take([32]) for _ in range(2)]
    m16 = [AC.take([16]) for _ in range(2)]
    selbf = [AC.take([32], BF16) for _ in range(2)]
    negT = AC.take([2, 8, 128], BF16, parts=32)
    rs = [AC.take([3, 4]) for _ in range(2)]
    B_w1 = S.bufs(2, "w1")
    B_w2 = S.bufs(2, "w2")
    B_peT = S.bufs(2, "peT")
    B_h = S.bufs(4, "hid")
    B_cb = S.buf("cbias")
    B_gelu = S.bufs(4, "gelu")
    B_kcmp = S.bufs(2, "kcmp")
    B_vcmp = S.bufs(2, "vcmp")
    B_PT = S.bufs(4, "PT")
    B_oaccs = S.bufs(16, "oacc")
    B_obf = S.bufs(2, "obf")
    B_imp = S.bufs(2, "imp")
    B_selbf = S.bufs(2, "selbf")
    B_negT = [[S.buf(f"negT{g}_{o}") for o in range(8)] for g in range(2)]
    B_rs = S.bufs(2, "rs")
    S.barrier(B_w1 + B_w2 + B_peT + B_h + [B_cb] + B_gelu + B_kcmp + B_vcmp + B_PT + B_oaccs + B_obf + B_imp + B_selbf
              + [b for l in B_negT for b in l] + B_rs)
    for kv, (w1d, w2d, ped) in enumerate([(w1k_d, w2k_d, pek_d), (w1v_d, w2v_d, pev_d)]):
        S.dma("pool", lambda e, kv=kv, w1d=w1d: e.dma_start(out=w1[kv].rearrange("p a b -> p (a b)"), in_=w1d), f"w1_{kv}", [], [B_w1[kv]])
        S.dma("pool", lambda e, kv=kv, w2d=w2d: e.dma_start(out=w2[kv], in_=w2d), f"w2_{kv}", [], [B_w2[kv]])
        S.dma("pool", lambda e, kv=kv, ped=ped: e.dma_start(out=peT[kv], in_=ped), f"pe_{kv}", [], [B_peT[kv]])
    POOL(lambda e: e.memset(vcmp1, 0.0), [], B_vcmp)
    for g in range(2):
        S.dma("pool", lambda e, g=g: e.dma_start(out=vcmp1[:, g, 129:161], in_=c_overlap_d), f"ovl{g}", [], [B_vcmp[g]])
        POOL(lambda e, g=g: e.memset(vcmp1[0:127, g, 128:129], 1.0), [], [B_vcmp[g]])

    CB = 7
    for kv in range(2):
        for l in range(32):
            PE(lambda e, l=l, kv=kv: e.matmul(ps[CB][:, 500 + kv:501 + kv], lhsT=w1[kv][:, l, :], rhs=peT[kv][:, l:l + 1], start=(l == 0), stop=(l == 31)),
               [B_w1[kv], B_peT[kv]], [PB[CB]])
        DVE(lambda e, kv=kv: e.tensor_copy(out=cbias[:, kv, 0:1], in_=ps[CB][:, 500 + kv:501 + kv]), [PB[CB]], [B_cb])
    for kv in range(2):
        for g in range(2):
            idx = kv * 2 + g
            srcT = kT[:, 0, g, :] if kv == 0 else vcT[:, g, :]
            Bsrc = [B_kT[0][g][l] for l in range(16)] if kv == 0 else [B_vcT[g][l] for l in range(16)]
            for l in range(32):
                PE(lambda e, l=l, kv=kv, idx=idx, srcT=srcT: e.matmul(ps[CB][:, idx * 128:idx * 128 + 127], lhsT=w1[kv][:, l, :], rhs=srcT[:, l:l + 16 * 126 + 1:16],
                                                                  start=(l == 0), stop=(l == 31)), Bsrc + [B_w1[kv]], [PB[CB]])
            x_ = hb_[:, idx, 0:127]
            DVE(lambda e, idx=idx, kv=kv, x_=x_: e.tensor_scalar(out=x_, in0=ps[CB][:, idx * 128:idx * 128 + 127], scalar1=cbias[:, kv, 0:1], scalar2=None, op0=ALU.add),
                [PB[CB], B_cb], [B_h[idx]])
            DVE(lambda e, idx=idx, x_=x_: e.tensor_tensor(out=hx[:, idx, 0:127], in0=x_, in1=x_, op=ALU.mult), [B_h[idx]], [B_h[idx]])
            DVE(lambda e, idx=idx: e.tensor_scalar(out=hx[:, idx, 0:127], in0=hx[:, idx, 0:127], scalar1=0.044715, scalar2=1.0, op0=ALU.mult, op1=ALU.add), [B_h[idx]], [B_h[idx]])
            DVE(lambda e, idx=idx, x_=x_: e.tensor_tensor(out=hx[:, idx, 0:127], in0=hx[:, idx, 0:127], in1=x_, op=ALU.mult), [B_h[idx]], [B_h[idx]])
            ACT(lambda e, idx=idx: e.activation(out=hy[:, idx, 0:127], in_=hx[:, idx, 0:127], func=AF.Sigmoid, scale=1.5957691216057308), [B_h[idx]], [B_h[idx]])
            DVE(lambda e, idx=idx, x_=x_: e.tensor_tensor(out=geluT[:, idx, 0:127], in0=hy[:, idx, 0:127], in1=x_, op=ALU.mult), [B_h[idx]], [B_gelu[idx]])
    POOL(lambda e: e.memset(kcmpT, 0.0), [], B_kcmp)
    for g in range(2):
        PE(lambda e, g=g: e.matmul(ps[CB][:, 0:127], lhsT=w2[0], rhs=geluT[:, g, 0:127], start=True, stop=True), [B_w2[0], B_gelu[g]], [PB[CB]])
        DVE(lambda e, g=g: e.tensor_copy(out=kcmpT[:, g, 0:127], in_=ps[CB][:, 0:127]), [PB[CB]], [B_kcmp[g]])
        PE(lambda e, g=g: e.matmul(ps[CB][0:127, 128:256], lhsT=geluT[:, 2 + g, 0:127], rhs=w2[1], start=True, stop=True), [B_w2[1], B_gelu[2 + g]], [PB[CB]])
        DVE(lambda e, g=g: e.tensor_copy(out=vcmp1[0:127, g, 0:128], in_=ps[CB][0:127, 128:256]), [PB[CB]], [B_vcmp[g]])
    dump("kcmpT", kcmpT.rearrange("p a b -> p (a b)"), B_kcmp)
    dump("vcmp1", vcmp1.rearrange("p a b -> p (a b)"), B_vcmp)

    cnt = {"S": 0, "PT": 0, "O": 0, "rs": 0, "ob": 0}
    SB = [0, 1, 2]

    def evac_branch(obanks, br, g, ot, oa, B_oa, first, imp=None):
        k = cnt["rs"] % 2
        cnt["rs"] += 1
        r_ = rs[k]
        Br = B_rs[k]
        for bi, bank in enumerate(obanks):
            DVE(lambda e, bi=bi, bank=bank: e.tensor_scalar(out=r_[:, 0, 2 * bi:2 * bi + 2], in0=ps[bank][:, 128:385:256], scalar1=1e-30, scalar2=None, op0=ALU.add),
                [PB[bank]], [Br])
        DVE(lambda e: e.reciprocal(out=r_[:, 1, :], in_=r_[:, 0, :]), [Br], [Br])
        s0 = g * 12 + br
        DVE(lambda e: e.tensor_tensor(out=r_[:, 2, :], in0=r_[:, 1, :], in1=gsig[:, ot, s0:s0 + 10:3], op=ALU.mult), [Br, B_gsig[ot]], [Br])
        for r in range(4):
            bank = obanks[r // 2]
            c0 = (r % 2) * 256
            if first:
                DVE(lambda e, bank=bank, c0=c0, r=r: e.tensor_scalar(out=oa[:, r, :], in0=ps[bank][:, c0:c0 + 128], scalar1=r_[:, 2, r:r + 1], scalar2=None, op0=ALU.mult),
                    [PB[bank], Br], [B_oa])
            else:
                DVE(lambda e, bank=bank, c0=c0, r=r: e.scalar_tensor_tensor(out=oa[:, r, :], in0=ps[bank][:, c0:c0 + 128], scalar=r_[:, 2, r:r + 1], in1=oa[:, r, :],
                                                                          op0=ALU.mult, op1=ALU.add), [PB[bank], Br], [B_oa])
            if imp is not None:
                ib, Bi = imp
                if r == 0:
                    DVE(lambda e, bank=bank, c0=c0, r=r: e.tensor_scalar(out=ib, in0=ps[bank][:, c0 + 129:c0 + 161], scalar1=r_[:, 1, r:r + 1], scalar2=None, op0=ALU.mult),
                        [PB[bank], Br], [Bi])
                else:
                    DVE(lambda e, bank=bank, c0=c0, r=r: e.scalar_tensor_tensor(out=ib, in0=ps[bank][:, c0 + 129:c0 + 161], scalar=r_[:, 1, r:r + 1], in1=ib, op0=ALU.mult, op1=ALU.add),
                        [PB[bank], Br], [Bi])

    def osets():
        i = cnt["O"] % 2
        cnt["O"] += 1
        return (3, 4) if i == 0 else (5, 6)

    def score_exp(lhsT_ap, B_l, g, ot, bias=None):
        sb = SB[cnt["S"] % 3]
        cnt["S"] += 1
        pi = cnt["PT"] % 4
        cnt["PT"] += 1
        rhs_q = qT[:, g, ot].rearrange("p r t -> p (r t)")
        PE(lambda e: e.matmul(ps[sb][:], lhsT=lhsT_ap, rhs=rhs_q, start=True, stop=(bias is None)), B_l + [B_qT[g][ot]], [PB[sb]])
        if bias is not None:
            kt = bias
            PE(lambda e: e.matmul(ps[sb][:].rearrange("p (r t) -> p r t", r=4), lhsT=Emat[:, kt, :], rhs=negT[:, g, ot, :].unsqueeze(1).to_broadcast([32, 4, 128]),
                                  start=False, stop=True), [B_const, B_negT[g][ot]], [PB[sb]])
        ACT(lambda e: e.activation(out=PT[pi].rearrange("p r t -> p (r t)"), in_=ps[sb][:], func=AF.Exp, scale=SCALE), [PB[sb]], [B_PT[pi]])
        return pi

    def pv(pi, rhs_ap, B_r, width, obanks, first, last):
        for r in range(4):
            bank = obanks[r // 2]
            c0 = (r % 2) * 256
            PE(lambda e, r=r, bank=bank, c0=c0: e.matmul(ps[bank][:, c0:c0 + width], lhsT=PT[pi][:, r, :], rhs=rhs_ap, start=(first and r % 2 == 0), stop=last,
                                                         skip_group_check=True),
               [B_PT[pi]] + B_r, [PB[bank]])

    def bcast4(m):
        return m.unsqueeze(1).to_broadcast([128, 4, 128])

    def run_pipeline(steps, la):
        stt = {}
        n = len(steps)
        for i in range(n + la):
            if i < n:
                stt[i] = steps[i][0]()
            j = i - la
            if j >= 0:
                steps[j][1](stt.pop(j))

    c1 = []
    for g in range(2):
        for ot in range(8):
            def stepA(g=g, ot=ot):
                pi = score_exp(kcmpT[:, g, :], [B_kcmp[g]], g, ot)
                DVE(lambda e: e.tensor_tensor(out=PT[pi], in0=PT[pi], in1=bcast4(cvalid[:, ot * 128:(ot + 1) * 128]), op=ALU.mult), [B_PT[pi], B_const], [B_PT[pi]])
                return pi

            def stepB(pi, g=g, ot=ot):
                oi = g * 8 + ot
                ob = osets()
                pv(pi, vcmp1[:, g, 0:161], [B_vcmp[g]], 161, ob, True, True)
                ii = oi % 2
                evac_branch(ob, 0, g, ot, oaccs[oi], B_oaccs[oi], True, imp=(impb[ii], B_imp[ii]))
                DVE(lambda e: e.tensor_tensor(out=impb[ii], in0=impb[ii], in1=selbias[:, ot, :], op=ALU.add), [B_imp[ii], B_const], [B_imp[ii]])
                DVE(lambda e: e.max(out=m16[ii][:, 0:8], in_=impb[ii]), [B_imp[ii]], [B_imp[ii]])
                DVE(lambda e: e.match_replace(out=impw[ii], in_to_replace=m16[ii][:, 0:8], in_values=impb[ii], imm_value=-3.0e30), [B_imp[ii]], [B_imp[ii]])
                DVE(lambda e: e.max(out=m16[ii][:, 8:16], in_=impw[ii]), [B_imp[ii]], [B_imp[ii]])
                DVE(lambda e: e.tensor_scalar(out=m16[ii][:, 15:16], in0=m16[ii][:, 15:16], scalar1=-1.0e29, scalar2=None, op0=ALU.max), [B_imp[ii]], [B_imp[ii]])
                DVE(lambda e: e.tensor_scalar(out=impw[ii], in0=impb[ii], scalar1=m16[ii][:, 15:16], scalar2=None, op0=ALU.is_ge), [B_imp[ii]], [B_imp[ii]])
                DVE(lambda e: e.tensor_scalar(out=selbf[ii], in0=impw[ii], scalar1=-1.0, scalar2=30000.0, op0=ALU.add, op1=ALU.mult), [B_imp[ii]], [B_selbf[ii]])
                PE(lambda e: e.transpose(out=psb[7][0:32, 0:128], in_=selbf[ii], identity=ident), [B_selbf[ii], B_const], [PB[7]])
                DVE(lambda e: e.tensor_copy(out=negT[:, g, ot, :], in_=psb[7][0:32, 0:128]), [PB[7]], [B_negT[g][ot]])
            c1.append((stepA, stepB))
    run_pipeline(c1, 1)

    c2 = []
    for g in range(2):
        for ot in range(8):
            ltq = 8 + ot
            oi = g * 8 + ot
            for br in (1, 2):
                nst = ltq + 1 if br == 1 else 5
                holder = {}
                for si in range(nst):
                    kt = si if br == 1 else ltq - 4 + si

                    def stepA(g=g, ot=ot, br=br, si=si, kt=kt, ltq=ltq):
                        if br == 1:
                            pi = score_exp(kT[:, 1, g, kt * 128:(kt + 1) * 128], [B_kT[1][g][kt]], g, ot, bias=kt)
                            if kt == ltq:
                                DVE(lambda e: e.tensor_tensor(out=PT[pi], in0=PT[pi], in1=bcast4(diag), op=ALU.mult), [B_PT[pi], B_const], [B_PT[pi]])
                        else:
                            pi = score_exp(kT[:, 2, g, kt * 128:(kt + 1) * 128], [B_kT[2][g][kt]], g, ot)
                            wv = wvalid[:, ot * 5 + si:ot * 5 + si + 1]
                            if si == 0:
                                DVE(lambda e: e.scalar_tensor_tensor(out=PT[pi], in0=PT[pi], scalar=wv, in1=bcast4(upst), op0=ALU.mult, op1=ALU.mult), [B_PT[pi], B_const], [B_PT[pi]])
                            elif si == 4:
                                DVE(lambda e: e.tensor_tensor(out=PT[pi], in0=PT[pi], in1=bcast4(diag), op=ALU.mult), [B_PT[pi], B_const], [B_PT[pi]])
                            elif kt < 8:
                                DVE(lambda e: e.tensor_scalar(out=PT[pi], in0=PT[pi], scalar1=wv, scalar2=None, op0=ALU.mult), [B_PT[pi], B_const], [B_PT[pi]])
                        return pi

                    def stepB(pi, g=g, ot=ot, br=br, si=si, kt=kt, nst=nst, oi=oi, holder=holder):
                        if si == 0:
                            holder["ob"] = osets()
                        ob = holder["ob"]
                        vk = (0 if br == 1 else 2) + g
                        pv(pi, v1[:, vk, kt, 0:129], [B_v1[vk][kt], B_v1init], 129, ob, si == 0, si == nst - 1)
                        if si == nst - 1:
                            evac_branch(ob, br, g, ot, oaccs[oi], B_oaccs[oi], False)
                            if br == 2:
                                k = cnt["ob"] % 2
                                cnt["ob"] += 1
                                ACT(lambda e: e.activation(out=obf[k], in_=oaccs[oi], func=AF.Copy), [B_oaccs[oi]], [B_obf[k]])
                                for r in range(4):
                                    PE(lambda e, r=r: e.transpose(out=psb[7][:, 256 + r * 128:256 + (r + 1) * 128], in_=obf[k][:, r, :], identity=ident), [B_obf[k], B_const], [PB[7]])
                                DVE(lambda e: e.tensor_copy(out=ymixT[:, 8 + g * 4:12 + g * 4, ot * 128:(ot + 1) * 128], in_=psb[7][:, 256:768].rearrange("p (r t) -> p r t", r=4)),
                                    [PB[7]], [B_ymix[8 + g * 4 + r] for r in range(4)])
                    c2.append((stepA, stepB))
    run_pipeline(c2, 2)
    dump("ynsa", ymixT[:, 8:16, :].rearrange("p a b -> p (a b)"), B_ymix[8:16])

    if stop_after == "C":
        return _finish(nc, S, st, out_bufs)

    hres = A.view(CONST_END, [8, 2048], F32)
    assert CONST_END + 65536 <= YMIX_OFF
    D_END = CONST_END + 65536
    DC = Carver(A, TRANS0, ARENA_BYTES - 8192, "D")
    wo = [DC.take([16, 512], BF16) for _ in range(2)]
    B_h = [[S.buf(f"h{ot}_{c}") for c in range(4)] for ot in range(8)]
    B_wo = S.bufs(2, "wo")
    S.barrier([b for l in B_h for b in l] + B_wo)
    for ot in range(8):
        S.dma("sp", lambda e, ot=ot: e.dma_start(out=hres[:, ot, :], in_=x_d[1024 + ot * 128:1024 + (ot + 1) * 128, :]), f"hres{ot}", [], B_h[ot])
    pb8 = [0]

    def npb():
        b = pb8[0] % 8
        pb8[0] += 1
        return b

    for dmc in range(4):
        sl = dmc % 2
        S.dma("pool", lambda e, dmc=dmc, sl=sl: e.dma_start(out=wo[sl].rearrange("p a b -> p (a b)"), in_=wout_d[dmc]), f"wo{sl}", [], [B_wo[sl]])
        for ot in range(8):
            pb = npb()
            for c in range(16):
                PE(lambda e, c=c, ot=ot, sl=sl, pb=pb: e.matmul(ps[pb][:], lhsT=ymixT[:, c, ot * 128:(ot + 1) * 128], rhs=wo[sl][:, c, :], start=(c == 0), stop=(c == 15)),
                   [B_ymix[c], B_wo[sl]], [PB[pb]])
            DVE(lambda e, ot=ot, dmc=dmc, pb=pb: e.tensor_tensor(out=hres[:, ot, dmc * 512:(dmc + 1) * 512], in0=ps[pb][:], in1=hres[:, ot, dmc * 512:(dmc + 1) * 512], op=ALU.add),
                [PB[pb], B_h[ot][dmc]], [B_h[ot][dmc]])
    dump("h1", hres.rearrange("p a b -> p (a b)"), [b for l in B_h for b in l])
    if stop_after == "D":
        return _finish(nc, S, st, out_bufs)

    EC = Carver(A, D_END, ARENA_BYTES, "E")
    a2T = EC.take([16, 1024], BF16)
    E_A2T_END = EC.p
    hT = EC.take([12, 1024], BF16)
    wgu = [EC.take([16, 256], BF16) for _ in range(4)]
    wdn = [EC.take([12, 512], BF16) for _ in range(2)]
    sgt = [EC.take([512]) for _ in range(2)]
    EC2 = Carver(A, 768, 768 + 2048 + 4096 + 1024 + 256 + 2048 + 2048, "Ealias")
    abf2 = [EC2.take([2048], BF16), EC2.take([2048], BF16)]
    dbg_state[0] = EC2.p
    dbg_state[1] = 512
    S.barrier([B_dbgstg])
    B_a2T = [S.buf(f"a2T{i}") for i in range(8)]
    B_hT = [[S.buf(f"hT{c}_{t}") for t in range(2)] for c in range(12)]
    B_wgu = S.bufs(4, "wgu")
    B_wdn = S.bufs(2, "wdn")
    B_sgt = S.bufs(2, "sgt")
    B_abf2 = S.bufs(2, "abf2")
    S.barrier(B_a2T + [b for l in B_hT for b in l] + B_wgu + B_wdn + B_sgt + B_abf2)

    def rms_h(ot, dst, B_dst):
        k = nstat[0] % 16
        nstat[0] += 1
        s_ = stat[:, k, :]
        sb = S.buf()
        src = hres[:, ot, :]
        ACT(lambda e: e.activation(out=dst, in_=src, func=AF.Square, accum_out=s_[:, 0:1]), B_h[ot], [B_dst, sb])
        DVE(lambda e: e.tensor_scalar(out=s_[:, 1:2], in0=s_[:, 0:1], scalar1=1.0 / 2048, scalar2=1e-6, op0=ALU.mult, op1=ALU.add), [sb], [sb])
        ACT(lambda e: e.activation(out=s_[:, 2:3], in_=s_[:, 1:2], func=AF.Sqrt), [sb], [sb])
        DVE(lambda e: e.reciprocal(out=s_[:, 3:4], in_=s_[:, 2:3]), [sb], [sb])
        DVE(lambda e: e.scalar_tensor_tensor(out=dst, in0=src, scalar=s_[:, 3:4], in1=gfull, op0=ALU.mult, op1=ALU.mult),
            [sb, B_gfull] + B_h[ot], [B_dst])

    def norm_hres(g_d, dstT, B_dstT):
        load_g(g_d)
        for ot in range(8):
            sl = ot % 2
            rms_h(ot, abf2[sl], B_abf2[sl])
            transpose_tile(abf2[sl], B_abf2[sl], dstT, B_dstT[ot], ot * 128)

    norm_hres(g_ffn_d, a2T, B_a2T)
    gcount = [0]
    dcount = [0]
    for fq, (c0, c1) in enumerate(NQ_FF):
        ncq = c1 - c0
        for grp in range(c0 // 2, c1 // 2):
            sl = (gcount[0] % 2) * 2
            gcount[0] += 1
            S.dma("pool", lambda e, grp=grp, sl=sl: e.dma_start(out=wgu[sl].rearrange("p a b -> p (a b)"), in_=wg_d[grp]), f"wgu{sl}", [], [B_wgu[sl]])
            S.dma("pool", lambda e, grp=grp, sl=sl: e.dma_start(out=wgu[sl + 1].rearrange("p a b -> p (a b)"), in_=wu_d[grp]), f"wgu{sl + 1}", [], [B_wgu[sl + 1]])
            for half in range(2):
                ci = grp * 2 + half - c0
                for th2 in range(2):
                    pg, pu = npb(), npb()
                    for dc in range(16):
                        PE(lambda e, dc=dc, sl=sl, half=half, th2=th2, pg=pg: e.matmul(ps[pg][:], lhsT=wgu[sl][:, dc, half * 128:(half + 1) * 128], rhs=a2T[:, dc, th2 * 512:(th2 + 1) * 512],
                                                                                     start=(dc == 0), stop=(dc == 15)), [B_wgu[sl]] + B_a2T[th2 * 4:th2 * 4 + 4], [PB[pg]])
                    for dc in range(16):
                        PE(lambda e, dc=dc, sl=sl, half=half, th2=th2, pu=pu: e.matmul(ps[pu][:], lhsT=wgu[sl + 1][:, dc, half * 128:(half + 1) * 128], rhs=a2T[:, dc, th2 * 512:(th2 + 1) * 512],
                                                                                     start=(dc == 0), stop=(dc == 15)), [B_wgu[sl + 1]] + B_a2T[th2 * 4:th2 * 4 + 4], [PB[pu]])
                    ss = th2
                    ACT(lambda e, pg=pg, ss=ss: e.activation(out=sgt[ss], in_=ps[pg][:], func=AF.Silu), [PB[pg]], [B_sgt[ss]])
                    DVE(lambda e, pu=pu, ss=ss, ci=ci, th2=th2: e.tensor_tensor(out=hT[:, ci, th2 * 512:(th2 + 1) * 512], in0=ps[pu][:], in1=sgt[ss], op=ALU.mult),
                        [PB[pu], B_sgt[ss]], [B_hT[ci][th2]])
        for dmc in range(4):
            sl = dcount[0] % 2
            dcount[0] += 1
            S.dma("pool", lambda e, fq=fq, dmc=dmc, sl=sl: e.dma_start(out=wdn[sl].rearrange("p a b -> p (a b)"), in_=wd_d[fq * 4 + dmc]), f"wdn{sl}", [], [B_wdn[sl]])
            for ot in range(8):
                pb = npb()
                for ci in range(ncq):
                    PE(lambda e, ci=ci, ot=ot, sl=sl, pb=pb: e.matmul(ps[pb][:], lhsT=hT[:, ci, ot * 128:(ot + 1) * 128], rhs=wdn[sl][:, ci, :], start=(ci == 0), stop=(ci == ncq - 1)),
                       [B_hT[ci][ot // 4], B_wdn[sl]], [PB[pb]])
                DVE(lambda e, ot=ot, dmc=dmc, pb=pb: e.tensor_tensor(out=hres[:, ot, dmc * 512:(dmc + 1) * 512], in0=ps[pb][:], in1=hres[:, ot, dmc * 512:(dmc + 1) * 512], op=ALU.add),
                    [PB[pb], B_h[ot][dmc]], [B_h[ot][dmc]])
    dump("h2", hres.rearrange("p a b -> p (a b)"), [b for l in B_h for b in l])
    if stop_after == "E":
        return _finish(nc, S, st, out_bufs)

    FC = Carver(A, E_A2T_END, ARENA_BYTES - 8192, "F")
    a3T = a2T
    B_a3T = B_a2T
    wpg = [FC.take([16, 512], BF16) for _ in range(2)]
    wpp = [FC.take([2, 512], BF16) for _ in range(2)]
    pT = FC.take([2, 1024], BF16)
    pf = [FC.take([256]) for _ in range(2)]
    pbf = [FC.take([256], BF16) for _ in range(2)]
    gt = [FC.take([512]) for _ in range(2)]
    outt = [FC.take([2048]) for _ in range(2)]
    B_wpg = S.bufs(2, "wpg")
    B_wpp = S.bufs(2, "wpp")
    B_pT = S.bufs(8, "pT")
    B_pf = S.bufs(2, "pf")
    B_pbf = S.bufs(2, "pbf")
    B_gt = S.bufs(2, "gt")
    B_outt = S.bufs(2, "outt")
    S.barrier(B_wpg + B_wpp + B_pT + B_pf + B_pbf + B_gt + B_outt)
    norm_hres(g_ple_d, a3T, B_a3T)
    for ot in range(8):
        sl = ot % 2
        S.dma("sp", lambda e, ot=ot, sl=sl: e.dma_start(out=pf[sl], in_=p_d[ot * 128:(ot + 1) * 128, :]), f"pf{sl}", [], [B_pf[sl]])
        ACT(lambda e, sl=sl: e.activation(out=pbf[sl], in_=pf[sl], func=AF.Copy), [B_pf[sl]], [B_pbf[sl]])
        for c2 in range(2):
            PE(lambda e, c2=c2, sl=sl: e.transpose(out=psb[7][:, c2 * 128:(c2 + 1) * 128], in_=pbf[sl][:, c2 * 128:(c2 + 1) * 128], identity=ident), [B_pbf[sl], B_const], [PB[7]])
        DVE(lambda e, ot=ot: e.tensor_copy(out=pT[:, :, ot * 128:(ot + 1) * 128], in_=psb[7][:, 0:256].rearrange("p (a b) -> p a b", a=2)), [PB[7]], [B_pT[ot]])
    pb7 = [0]

    def npb7():
        b = pb7[0] % 7
        pb7[0] += 1
        return b

    for dmc in range(4):
        sl = dmc % 2
        S.dma("pool", lambda e, dmc=dmc, sl=sl: e.dma_start(out=wpg[sl].rearrange("p a b -> p (a b)"), in_=wpg_d[dmc]), f"wpg{sl}", [], [B_wpg[sl]])
        S.dma("pool", lambda e, dmc=dmc, sl=sl: e.dma_start(out=wpp[sl].rearrange("p a b -> p (a b)"), in_=wpp_d[dmc]), f"wpp{sl}", [], [B_wpp[sl]])
        for ot in range(8):
            p1, p2 = npb7(), npb7()
            for dc in range(16):
                PE(lambda e, dc=dc, ot=ot, sl=sl, p1=p1: e.matmul(ps[p1][:], lhsT=a3T[:, dc, ot * 128:(ot + 1) * 128], rhs=wpg[sl][:, dc, :], start=(dc == 0), stop=(dc == 15)),
                   [B_a3T[ot], B_wpg[sl]], [PB[p1]])
            for c2 in range(2):
                PE(lambda e, c2=c2, ot=ot, sl=sl, p2=p2: e.matmul(ps[p2][:], lhsT=pT[:, c2, ot * 128:(ot + 1) * 128], rhs=wpp[sl][:, c2, :], start=(c2 == 0), stop=(c2 == 1)),
                   [B_pT[ot], B_wpp[sl]], [PB[p2]])
            gs = (dmc * 8 + ot) % 2
            ACT(lambda e, p1=p1, gs=gs: e.activation(out=gt[gs], in_=ps[p1][:], func=AF.Sigmoid), [PB[p1]], [B_gt[gs]])
            DVE(lambda e, p2=p2, gs=gs: e.tensor_tensor(out=gt[gs], in0=ps[p2][:], in1=gt[gs], op=ALU.mult), [PB[p2], B_gt[gs]], [B_gt[gs]])
            POOL(lambda e, ot=ot, dmc=dmc, gs=gs: e.tensor_tensor(out=hres[:, ot, dmc * 512:(dmc + 1) * 512], in0=gt[gs], in1=hres[:, ot, dmc * 512:(dmc + 1) * 512], op=ALU.add),
                 [B_gt[gs], B_h[ot][dmc]], [B_h[ot][dmc]])
    load_g(g_fin_d)
    for ot in range(8):
        sl = ot % 2
        rms_h(ot, outt[sl], B_outt[sl])
        ob = S.buf()
        S.dma("sp", lambda e, ot=ot, sl=sl: e.dma_start(out=out_d[ot * 128:(ot + 1) * 128, :], in_=outt[sl]), f"ost{sl}", [B_outt[sl]], [ob])
        out_bufs.append(ob)
    return _finish(nc, S, st, out_bufs)


def _finish(nc, S, st, out_bufs):
    S.op("sp", lambda e: e.nop(), out_bufs, [])
    S.emit(nc, st)
    st.close()
    return nc, S


def _tile_cols(W, c0, n):
    return np.ascontiguousarray(W[:, c0:c0 + n].reshape(16, 128, n).transpose(1, 0, 2).reshape(128, 16 * n))


def _consts(th):
    f32 = np.float32
    L = np.arange(2048)
    tg = L - 1024 + 1024 * th
    pos = np.maximum(tg, 0).astype(f32)
    inv_freq = (f32(500000.0) ** (-np.arange(0, 32, 2, dtype=f32) / f32(32))).astype(f32)
    ang = (pos[:, None] * inv_freq[None, :]).astype(f32)
    cos, sin = np.cos(ang).astype(f32), np.sin(ang).astype(f32)
    cos2 = np.concatenate([cos, cos], 1)
    sin2 = np.concatenate([-sin, sin], 1)

    def tokmaj(a):
        w = a.shape[1]
        return np.ascontiguousarray(a.reshape(16, 128, w).transpose(1, 0, 2).reshape(128, 16 * w))

    n = np.arange(128)
    Lo = 1024 + np.arange(1024)
    cvalid = ((16 * n[:, None] + 31) <= Lo[None, :]) & ((th == 1) | (n[:, None] >= 64))
    s = np.arange(32)
    overlap = ((16 * n[:, None]) < (64 * s[None, :] + 64)) & ((16 * n[:, None] + 32) > 64 * s[None, :])
    overlap = overlap & (n[:, None] < 127)
    cur = Lo // 64
    valid = (s[None, :] <= cur[:, None]) & ((th == 1) | (s[None, :] >= 16))
    forced = (s[None, :] == cur[:, None]) | (s[None, :] == cur[:, None] - 1) | (s[None, :] == 16 * (1 - th))
    selbias = np.where(valid, np.where(forced, BIG, 0.0), -BIG).astype(f32)
    selbias = np.ascontiguousarray(selbias.reshape(8, 128, 32).transpose(1, 0, 2).reshape(128, 256))
    k = np.arange(128)
    E = np.zeros((32, 16, 128), f32)
    for kt in range(16):
        E[2 * kt + k // 64, kt, k] = 1.0
    ident = np.eye(128, dtype=f32)
    diag = (k[:, None] <= k[None, :]).astype(f32)
    upst = (k[:, None] > k[None, :]).astype(f32)
    tri = np.concatenate([ident, diag, upst], 1)
    wvalid = np.zeros((8, 5), f32)
    for ot in range(8):
        for off in range(5):
            kt = 8 + ot - 4 + off
            wvalid[ot, off] = 1.0 if (th == 1 or kt >= 8) else 0.0
    wvalid = np.broadcast_to(wvalid.reshape(1, 40), (128, 40))
    invc = np.zeros((4, 16), f32)
    for gi in range(4):
        w = 2 << gi
        for i in range(16):
            invc[gi, i] = 1.0 / w if th == 1 else 1.0 / min(i + 1, w)
    invc = np.broadcast_to(invc.reshape(1, 64), (128, 64))
    c = lambda a: np.ascontiguousarray(a, dtype=f32)
    return {
        "c_cos": tokmaj(cos2), "c_sin": tokmaj(sin2), "c_cvalid": c(cvalid), "c_overlap": c(overlap), "c_selbias": c(selbias),
        "c_E": c(E.reshape(32, 2048)), "c_tri": c(tri), "c_wvalid": c(wvalid), "c_invc": c(invc),
    }


def _prep_shared(w_in, pool_w, pool_scale, cmp_k_pe, cmp_k_w1, cmp_k_w2, cmp_v_pe, cmp_v_w1, cmp_v_w2, w_out,
                 w_gate, w_up, w_down, w_ple_gate, w_ple_proj, in_norm_g, ffn_norm_g, ple_norm_g, final_norm_g):
    c = lambda a: np.ascontiguousarray(a, dtype=np.float32)
    d = {}
    w_in = w_in[0]
    d["wU"] = np.stack([_tile_cols(w_in, i * 256, 256) for i in range(4)])
    d["wT"] = np.stack([_tile_cols(w_in, 1024 + i * 256, 256) for i in range(10)])
    d["wG"] = _tile_cols(w_in, 3584, 24)
    d["poolw"] = c(pool_w[0].reshape(4, 2, 128, 256).transpose(2, 0, 1, 3).reshape(128, 2048))
    d["pscale"] = c(pool_scale[0].reshape(8, 128).T)
    d["w1k"] = c(cmp_k_w1[0].reshape(32, 128, 128).transpose(1, 0, 2).reshape(128, 4096))
    d["w1v"] = c(cmp_v_w1[0].reshape(32, 128, 128).transpose(1, 0, 2).reshape(128, 4096))
    d["w2k"] = c(cmp_k_w2[0])
    d["w2v"] = c(cmp_v_w2[0])
    d["pekT"] = c(cmp_k_pe[0].T)
    d["pevT"] = c(cmp_v_pe[0].T)
    d["wout"] = np.stack([_tile_cols(w_out[0], i * 512, 512) for i in range(4)])
    d["wgate"] = np.stack([_tile_cols(w_gate[0], i * 256, 256) for i in range(22)])
    d["wup"] = np.stack([_tile_cols(w_up[0], i * 256, 256) for i in range(22)])
    wd = np.zeros((16, 128, 12 * 512), np.float32)
    for fq, (c0, c1) in enumerate(NQ_FF):
        for dmc in range(4):
            blk = w_down[0][c0 * 128:c1 * 128, dmc * 512:(dmc + 1) * 512].reshape(c1 - c0, 128, 512).transpose(1, 0, 2)
            wd[fq * 4 + dmc, :, :(c1 - c0) * 512] = blk.reshape(128, -1)
    d["wdown"] = wd
    d["wpg"] = np.stack([_tile_cols(w_ple_gate[0], i * 512, 512) for i in range(4)])
    wpp = w_ple_proj[0]
    d["wpp"] = np.stack([c(wpp[:, i * 512:(i + 1) * 512].reshape(2, 128, 512).transpose(1, 0, 2).reshape(128, 1024)) for i in range(4)])
    d["g_in"] = c(in_norm_g[0])
    d["g_ffn"] = c(ffn_norm_g[0])
    d["g_ple"] = c(ple_norm_g[0])
    d["g_fin"] = c(final_norm_g)
    return d


def make_in_maps(x, p, **w):
    shared = _prep_shared(**w)
    cst = [_consts(0), _consts(1)]
    in_maps = []
    for b in range(4):
        for th in range(2):
            m = dict(shared)
            m.update(cst[th])
            if th == 1:
                xl = x[b]
            else:
                xl = np.concatenate([np.zeros((1024, 2048), np.float32), x[b, :1024]], 0)
            m["x"] = np.ascontiguousarray(xl, dtype=np.float32)
            m["p"] = np.ascontiguousarray(p[0, b, th * 1024:(th + 1) * 1024], dtype=np.float32)
            in_maps.append(m)
    return in_maps


_NC_CACHE = {}


def kernel(x, p, in_norm_g, w_in, pool_w, pool_scale, cmp_k_pe, cmp_k_w1, cmp_k_w2, cmp_v_pe, cmp_v_w1, cmp_v_w2,
           w_out, ffn_norm_g, w_gate, w_up, w_down, ple_norm_g, w_ple_gate, w_ple_proj, final_norm_g):
    args = dict(locals())
    args = {k: np.asarray(v) for k, v in args.items()}
    x = args.pop("x")
    p = args.pop("p")
    in_maps = make_in_maps(x, p, **args)
    if "nc" not in _NC_CACHE:
        _NC_CACHE["nc"] = build()[0]
    nc = _NC_CACHE["nc"]
    res = run_bass_kernel_spmd(nc, in_maps, core_ids=list(range(8)))
    out = np.zeros((4, 2048, 2048), np.float32)
    for b in range(4):
        for th in range(2):
            out[b, th * 1024:(th + 1) * 1024] = res.results[b * 2 + th]["out"]
    return out
```

```python
import numpy as np
from contextlib import ExitStack
import concourse.bass as bass
import concourse.mybir as mybir
from concourse.bass_utils import run_bass_kernel_spmd

F32 = mybir.dt.float32
BF16 = mybir.dt.bfloat16
AF = mybir.ActivationFunctionType
ALU = mybir.AluOpType
AX = mybir.AxisListType

ENGS = ("pe", "act", "dve", "pool", "sp")
BIG = 1.0e30
SCALE = 128.0 ** -0.5


class Buf:
    __slots__ = ("name", "last_w", "readers", "excl")

    def __init__(self, name, excl=False):
        self.name = name
        self.last_w = None
        self.readers = []
        self.excl = excl


class Op:
    __slots__ = ("eng", "fn", "deps", "signal", "sigval", "dma_key", "dma_val", "idx")


class Sched:
    def __init__(self):
        self.ops = {e: [] for e in ENGS}
        self.dma_cnt = {}
        self.nbuf = 0

    def buf(self, name=None):
        self.nbuf += 1
        return Buf(name or f"b{self.nbuf}")

    def bufs(self, n, name="b"):
        return [self.buf(f"{name}{i}") for i in range(n)]

    def _add(self, eng, fn, reads, writes, dma_key=None):
        op = Op()
        op.eng = eng
        op.fn = fn
        op.signal = False
        op.sigval = None
        op.dma_key = dma_key
        op.dma_val = None
        op.idx = len(self.ops[eng])
        deps = []
        for b in reads:
            if b.last_w is not None:
                deps.append(b.last_w)
            if b.excl:
                deps.extend(t for t in b.readers if t[1] != eng)
        for b in writes:
            if b.last_w is not None:
                deps.append(b.last_w)
            deps.extend(b.readers)
        if dma_key is not None:
            self.dma_cnt[dma_key] = self.dma_cnt.get(dma_key, 0) + 16
            op.dma_val = self.dma_cnt[dma_key]
            tok = ("dma", dma_key, op.dma_val)
        else:
            tok = ("eng", eng, op.idx)
        op.deps = [d for d in set(deps)
                   if not (d[0] == "eng" and d[1] == "pe" and eng == "pe" and dma_key is None)]
        for b in reads:
            b.readers.append(tok)
        for b in writes:
            b.last_w = tok
            b.readers = []
        self.ops[eng].append(op)
        return op

    def op(self, eng, fn, reads=(), writes=()):
        return self._add(eng, fn, list(reads), list(writes))

    def dma(self, eng, fn, key, reads=(), writes=()):
        return self._add(eng, fn, list(reads), list(writes), dma_key=key)

    def barrier(self, bufs):
        toks = []
        for e in ENGS:
            for o in reversed(self.ops[e]):
                if o.dma_key is None:
                    toks.append(("eng", e, o.idx))
                    break
        for k, v in self.dma_cnt.items():
            toks.append(("dma", k, v))
        for b in bufs:
            b.readers.extend(toks)

    def emit(self, nc, stack):
        for e in ENGS:
            for o in self.ops[e]:
                for d in o.deps:
                    if d[0] == "eng":
                        self.ops[d[1]][d[2]].signal = True
        esem = {}
        for e in ENGS:
            c = 0
            for o in self.ops[e]:
                if o.signal and o.dma_key is None:
                    c += 1
                    o.sigval = c
            if c > 0:
                esem[e] = stack.enter_context(nc.semaphore(f"s_{e}"))
        dsem = {k: stack.enter_context(nc.semaphore(f"d_{k}")) for k in self.dma_cnt}
        self.nsem = len(esem) + len(dsem)
        block = stack.enter_context(nc.Block())
        ops = self.ops
        stats = {}

        def run(e, engobj):
            waited = {}
            nw = 0
            for o in ops[e]:
                need = {}
                for d in o.deps:
                    if d[0] == "eng":
                        sem = esem[d[1]]
                        val = ops[d[1]][d[2]].sigval
                        key = ("e", d[1])
                    else:
                        sem = dsem[d[1]]
                        val = d[2]
                        key = ("d", d[1])
                    if need.get(key, (None, -1))[1] < val:
                        need[key] = (sem, val)
                for key, (sem, val) in need.items():
                    if waited.get(key, -1) >= val:
                        continue
                    waited[key] = val
                    engobj.wait_ge(sem, val)
                    nw += 1
                ins = o.fn(engobj)
                if o.dma_key is not None:
                    ins.then_inc(dsem[o.dma_key], 16)
                elif o.signal:
                    ins.then_inc(esem[e], 1)
            stats[e] = (len(ops[e]), nw)

        if ops["sp"]:
            block.sync(lambda eng: run("sp", eng))
        if ops["pe"]:
            block.tensor(lambda eng: run("pe", eng))
        if ops["act"]:
            block.scalar(lambda eng: run("act", eng))
        if ops["dve"]:
            block.vector(lambda eng: run("dve", eng))
        if ops["pool"]:
            block.gpsimd(lambda eng: run("pool", eng))
        self.stats = stats


class Arena:
    def __init__(self, nc, nbytes):
        self.nbytes = nbytes
        self.t = nc.alloc_sbuf_tensor("arena", [128, nbytes // 4], F32)

    def view(self, off, shape, dtype=F32, parts=128):
        shape = list(shape)
        n = int(np.prod(shape))
        nb = n * (2 if dtype == BF16 else 4)
        assert off % 4 == 0 and nb % 4 == 0, (off, nb)
        assert off + nb <= self.nbytes, ("arena overflow", off, nb)
        ap = self.t[0:parts, off // 4:(off + nb) // 4]
        if dtype == BF16:
            ap = ap.bitcast(BF16)
        if len(shape) == 2:
            ap = ap.rearrange("p (a b) -> p a b", a=shape[0])
        elif len(shape) == 3:
            ap = ap.rearrange("p (a b c) -> p a b c", a=shape[0], b=shape[1])
        elif len(shape) == 4:
            ap = ap.rearrange("p (a b c d) -> p a b c d", a=shape[0], b=shape[1], c=shape[2])
        return ap


class Carver:
    def __init__(self, arena, start, end, name=""):
        self.a = arena
        self.p = start
        self.end = end
        self.name = name

    def take(self, shape, dtype=F32, parts=128):
        n = int(np.prod(shape)) * (2 if dtype == BF16 else 4)
        n = (n + 31) // 32 * 32
        off = self.p
        self.p += n
        assert self.p <= self.end, ("carver overflow", self.name, self.p, self.end)
        return self.a.view(off, shape, dtype, parts)


ARENA_BYTES = 212000
NQ_FF = [(0, 12), (12, 22), (22, 34), (34, 44)]


def build(stop_after=None, dbg=()):
    nc = bass.Bass("TRN2", target_bir_lowering=False)

    def din(name, shape):
        return nc.dram_tensor(name, list(shape), F32, kind="ExternalInput").ap()

    x_d = din("x", [2048, 2048])
    p_d = din("p", [1024, 256])
    g_in_d = din("g_in", [2048])
    g_ffn_d = din("g_ffn", [2048])
    g_ple_d = din("g_ple", [2048])
    g_fin_d = din("g_fin", [2048])
    wU_d = din("wU", [4, 128, 16 * 256])
    wT_d = din("wT", [10, 128, 16 * 256])
    wG_d = din("wG", [128, 16 * 24])
    poolw_d = din("poolw", [128, 4 * 2 * 256])
    pscale_d = din("pscale", [128, 8])
    w1k_d = din("w1k", [128, 32 * 128])
    w1v_d = din("w1v", [128, 32 * 128])
    w2k_d = din("w2k", [128, 128])
    w2v_d = din("w2v", [128, 128])
    pek_d = din("pekT", [128, 32])
    pev_d = din("pevT", [128, 32])
    wout_d = din("wout", [4, 128, 16 * 512])
    wg_d = din("wgate", [22, 128, 16 * 256])
    wu_d = din("wup", [22, 128, 16 * 256])
    wd_d = din("wdown", [16, 128, 12 * 512])
    wpg_d = din("wpg", [4, 128, 16 * 512])
    wpp_d = din("wpp", [4, 128, 2 * 512])
    c_cos_d = din("c_cos", [128, 16 * 32])
    c_sin_d = din("c_sin", [128, 16 * 32])
    c_cvalid_d = din("c_cvalid", [128, 1024])
    c_overlap_d = din("c_overlap", [128, 32])
    c_selbias_d = din("c_selbias", [128, 8 * 32])
    c_E_d = din("c_E", [128, 16 * 128])
    c_tri_d = din("c_tri", [128, 3 * 128])
    c_wvalid_d = din("c_wvalid", [128, 40])
    c_invc_d = din("c_invc", [128, 4 * 16])
    out_d = nc.dram_tensor("out", [1024, 2048], F32, kind="ExternalOutput").ap()
    dbg_d = {}
    for name, shape in dbg:
        dbg_d[name] = nc.dram_tensor("dbg_" + name, list(shape), F32, kind="ExternalOutput").ap()

    S = Sched()
    st = ExitStack()
    A = Arena(nc, ARENA_BYTES)
    ps = [st.enter_context(nc.psum_tensor(f"ps{i}", [128, 512], F32)) for i in range(8)]
    psb = [p[:].bitcast(BF16) for p in ps]
    PB = [Buf(f"psum{i}", excl=True) for i in range(8)]

    def PE(fn, r=(), w=()):
        return S.op("pe", fn, r, w)

    def ACT(fn, r=(), w=()):
        return S.op("act", fn, r, w)

    def DVE(fn, r=(), w=()):
        return S.op("dve", fn, r, w)

    def POOL(fn, r=(), w=()):
        return S.op("pool", fn, r, w)

    out_bufs = []

    dbg_state = [ARENA_BYTES - 8192, 2048]
    B_dbgstg = S.buf("dbgstg")

    def dump(name, ap, buf, parts=128):
        if name not in dbg_d:
            return
        d = dbg_d[name]
        n = ap.shape[1]
        stg = A.view(dbg_state[0], [dbg_state[1]], F32)
        bl = buf if isinstance(buf, list) else [buf]
        for c0 in range(0, n, dbg_state[1]):
            c1 = min(n, c0 + dbg_state[1])
            ACT(lambda e, c0=c0, c1=c1: e.activation(out=stg[0:parts, 0:c1 - c0], in_=ap[:, c0:c1], func=AF.Copy), bl, [B_dbgstg])
            db = S.buf("dbgd_" + name)
            S.dma("sp", lambda e, c0=c0, c1=c1: e.dma_start(out=d[0:parts, c0:c1], in_=stg[0:parts, 0:c1 - c0]), "dbg", [B_dbgstg], [db])
            out_bufs.append(db)

    CONST_END = 26624
    CC = Carver(A, 0, CONST_END, "const")
    tri = CC.take([3, 128], BF16)
    ident = tri[:, 0, :]
    diag = tri[:, 1, :]
    upst = tri[:, 2, :]
    cvalid = CC.take([1024], BF16)
    Emat = CC.take([16, 128], BF16)
    selbias = CC.take([8, 32])
    invc = CC.take([4, 16])
    cos2 = CC.take([16, 32])
    sin2 = CC.take([16, 32])
    wvalid = CC.take([40])
    gsig = CC.take([8, 24])
    pscale = CC.take([8])
    gfull = CC.take([2048])
    stat = CC.take([16, 8])
    B_const = S.buf("const")
    B_gfull = S.buf("gfull")
    B_gsig = S.bufs(8, "gsig")

    S.dma("sp", lambda e: e.dma_start(out=selbias.rearrange("p a b -> p (a b)"), in_=c_selbias_d), "cst", [], [B_const])
    S.dma("sp", lambda e: e.dma_start(out=invc.rearrange("p a b -> p (a b)"), in_=c_invc_d), "cst", [], [B_const])
    S.dma("sp", lambda e: e.dma_start(out=cos2.rearrange("p a b -> p (a b)"), in_=c_cos_d), "cst", [], [B_const])
    S.dma("sp", lambda e: e.dma_start(out=sin2.rearrange("p a b -> p (a b)"), in_=c_sin_d), "cst", [], [B_const])
    S.dma("sp", lambda e: e.dma_start(out=wvalid, in_=c_wvalid_d), "cst", [], [B_const])
    S.dma("sp", lambda e: e.dma_start(out=pscale, in_=pscale_d), "cst", [], [B_const])
    S.dma("pool", lambda e: e.dma_start(out=tri.rearrange("p a b -> p (a b)"), in_=c_tri_d), "cstp", [], [B_const])
    S.dma("pool", lambda e: e.dma_start(out=cvalid, in_=c_cvalid_d), "cstp", [], [B_const])
    S.dma("pool", lambda e: e.dma_start(out=Emat.rearrange("p a b -> p (a b)"), in_=c_E_d), "cstp", [], [B_const])

    def load_g(g_d):
        S.dma("sp", lambda e: e.dma_start(out=gfull, in_=g_d.partition_broadcast(128)), "gld", [], [B_gfull])

    PC = Carver(A, CONST_END, ARENA_BYTES, "persist")
    qT = PC.take([2, 8, 4, 128], BF16)
    kT = PC.take([3, 2, 2048], BF16)
    vcT = PC.take([2, 2048], BF16)
    v1 = PC.take([4, 16, 132], BF16)
    YMIX_OFF = PC.p
    ymixT = PC.take([16, 1024], BF16)
    uhalo = PC.take([8, 16])
    TRANS0 = PC.p
    B_qT = [[S.buf(f"qT{g}_{o}") for o in range(8)] for g in range(2)]
    B_kT = [[[S.buf(f"kT{k}_{g}_{l}") for l in range(16)] for g in range(2)] for k in range(3)]
    B_vcT = [[S.buf(f"vcT{g}_{l}") for l in range(16)] for g in range(2)]
    B_v1 = [[S.buf(f"v1{k}_{l}") for l in range(16)] for k in range(4)]
    B_ymix = [S.buf(f"ymix{c}") for c in range(16)]
    B_uhalo = S.buf("uhalo")
    B_v1init = S.buf("v1init")
    POOL(lambda e: e.memset(v1[:, :, :, 128:132], 1.0), [], [B_v1init])

    nstat = [0]

    def rms_to_bf16(src, B_src, dst_bf, B_dst, junk, B_junk):
        k = nstat[0] % 16
        nstat[0] += 1
        s = stat[:, k, :]
        sb = S.buf()
        ACT(lambda e: e.activation(out=junk, in_=src, func=AF.Square, accum_out=s[:, 0:1]), [B_src], [B_junk, sb])
        DVE(lambda e: e.tensor_scalar(out=s[:, 1:2], in0=s[:, 0:1], scalar1=1.0 / 2048, scalar2=1e-6, op0=ALU.mult, op1=ALU.add), [sb], [sb])
        ACT(lambda e: e.activation(out=s[:, 2:3], in_=s[:, 1:2], func=AF.Sqrt), [sb], [sb])
        DVE(lambda e: e.reciprocal(out=s[:, 3:4], in_=s[:, 2:3]), [sb], [sb])
        DVE(lambda e: e.scalar_tensor_tensor(out=dst_bf, in0=src, scalar=s[:, 3:4], in1=gfull, op0=ALU.mult, op1=ALU.mult),
            [sb, B_src, B_gfull], [B_dst])

    def transpose_tile(a_bf, B_a, aT_dst, B_aT, col0):
        for hb in range(2):
            bank = 6 + hb
            for j in range(8):
                dc = hb * 8 + j
                PE(lambda e, dc=dc, j=j, bank=bank: e.transpose(out=psb[bank][:, j * 128:(j + 1) * 128], in_=a_bf[:, dc * 128:(dc + 1) * 128], identity=ident),
                   [B_a, B_const], [PB[bank]])
            eng = ACT if hb == 0 else DVE
            if hb == 0:
                ACT(lambda e, bank=bank, hb=hb: e.activation(out=aT_dst[:, hb * 8:(hb + 1) * 8, col0:col0 + 128],
                                                           in_=psb[bank].rearrange("p (a b) -> p a b", a=8), func=AF.Copy), [PB[bank]], [B_aT])
            else:
                DVE(lambda e, bank=bank, hb=hb: e.tensor_copy(out=aT_dst[:, hb * 8:(hb + 1) * 8, col0:col0 + 128],
                                                            in_=psb[bank].rearrange("p (a b) -> p a b", a=8)), [PB[bank]], [B_aT])

    TC = Carver(A, TRANS0, ARENA_BYTES - 8192, "AB")
    aT = TC.take([16, 1024], BF16)
    wsl = [TC.take([16, 256], BF16) for _ in range(2)]
    AB_SHARED = TC.p
    xt = [TC.take([2048]) for _ in range(2)]
    abf = [TC.take([2048], BF16) for _ in range(2)]
    stage = [TC.take([2, 128], BF16) for _ in range(2)]
    rtmp = [TC.take([2, 32]) for _ in range(2)]
    rtmp2 = [TC.take([2, 32]) for _ in range(2)]
    wgates = TC.take([16, 24], BF16)
    B_aT = [S.buf(f"aT{i}") for i in range(8)]
    B_xt = S.bufs(2, "xt")
    B_abf = S.bufs(2, "abf")
    B_wsl = S.bufs(2, "wsl")
    B_stage = S.bufs(2, "stage")
    B_rtmp = S.bufs(2, "rtmp")
    B_wgates = S.buf("wgates")
    wcount = [0]
    scount = [0]

    load_g(g_in_d)
    S.dma("pool", lambda e: e.dma_start(out=wgates.rearrange("p a b -> p (a b)"), in_=wG_d), "wgl", [], [B_wgates])

    def load_w(src_ap):
        i = wcount[0] % 2
        wcount[0] += 1
        S.dma("pool", lambda e: e.dma_start(out=wsl[i].rearrange("p a b -> p (a b)"), in_=src_ap), f"wsl{i}", [], [B_wsl[i]])
        return i

    def norm_pass(pas):
        for i in range(8):
            lt = pas * 8 + i
            sl = lt % 2
            S.dma("sp", lambda e, lt=lt, sl=sl: e.dma_start(out=xt[sl], in_=x_d[lt * 128:(lt + 1) * 128, :]), f"xt{sl}", [], [B_xt[sl]])
            rms_to_bf16(xt[sl], B_xt[sl], abf[sl], B_abf[sl], abf[sl], B_abf[sl])
            transpose_tile(abf[sl], B_abf[sl], aT, B_aT[i], i * 128)

    def rope(pbank, stg, B_stg, lt, sl):
        src = ps[pbank][:, 0:256].rearrange("p (h d) -> p h d", h=2)
        c2 = cos2[:, lt, :].unsqueeze(1).to_broadcast([128, 2, 32])
        sA = sin2[:, lt, 0:16].unsqueeze(1).to_broadcast([128, 2, 16])
        sB = sin2[:, lt, 16:32].unsqueeze(1).to_broadcast([128, 2, 16])
        t1 = rtmp[sl]
        t2 = rtmp2[sl]
        DVE(lambda e: e.tensor_tensor(out=t1, in0=src[:, :, 0:32], in1=c2, op=ALU.mult), [PB[pbank], B_const], [B_rtmp[sl]])
        DVE(lambda e: e.tensor_tensor(out=t2[:, :, 0:16], in0=src[:, :, 16:32], in1=sA, op=ALU.mult), [PB[pbank], B_const], [B_rtmp[sl]])
        DVE(lambda e: e.tensor_tensor(out=t2[:, :, 16:32], in0=src[:, :, 0:16], in1=sB, op=ALU.mult), [PB[pbank], B_const], [B_rtmp[sl]])
        DVE(lambda e: e.tensor_tensor(out=stg[:, :, 0:32], in0=t1, in1=t2, op=ALU.add), [B_rtmp[sl]], [B_stg])

    def tok_half(hg, pas, i, wi, pbank):
        lt = pas * 8 + i
        for dc in range(16):
            PE(lambda e, dc=dc: e.matmul(ps[pbank][:, 0:256], lhsT=aT[:, dc, i * 128:(i + 1) * 128], rhs=wsl[wi][:, dc, :], start=(dc == 0), stop=(dc == 15)),
               [B_aT[i], B_wsl[wi]], [PB[pbank]])

        def evac():
            src = ps[pbank][:, 0:256].rearrange("p (h d) -> p h d", h=2)
            if hg in (7, 9):
                vk = 0 if hg == 7 else 2
                ACT(lambda e: e.activation(out=v1[:, vk:vk + 2, lt, 0:128], in_=src, func=AF.Copy), [PB[pbank], B_v1init], [B_v1[vk][lt], B_v1[vk + 1][lt]])
                return
            sl = scount[0] % 2
            scount[0] += 1
            stg = stage[sl]
            tb = 6 + (scount[0] % 2)
            ACT(lambda e: e.activation(out=stg, in_=src, func=AF.Copy), [PB[pbank]], [B_stage[sl]])
            if hg != 5:
                rope(pbank, stg, B_stage[sl], lt, sl)
            for r in range(2):
                PE(lambda e, r=r: e.transpose(out=psb[tb][:, r * 128:(r + 1) * 128], in_=stg[:, r, :], identity=ident), [B_stage[sl], B_const], [PB[tb]])
            pin = psb[tb][:, 0:256].rearrange("p (g t) -> p g t", g=2)
            if hg < 4:
                g, r0, ot = hg // 2, (hg % 2) * 2, i
                DVE(lambda e: e.tensor_copy(out=qT[:, g, ot, r0:r0 + 2, :], in_=pin), [PB[tb]], [B_qT[g][ot]])
            elif hg == 5:
                DVE(lambda e: e.tensor_copy(out=vcT[:, :, lt * 128:(lt + 1) * 128], in_=pin), [PB[tb]], [B_vcT[0][lt], B_vcT[1][lt]])
            else:
                kind = (hg - 4) // 2
                DVE(lambda e: e.tensor_copy(out=kT[:, kind, :, lt * 128:(lt + 1) * 128], in_=pin), [PB[tb]], [B_kT[kind][0][lt], B_kT[kind][1][lt]])
        return evac

    pbrot = [0]

    def next_pb():
        b = pbrot[0] % 6
        pbrot[0] += 1
        return b

    if stop_after == "C0":
        dump("gfull", gfull, [B_gfull, B_const])
        return _finish(nc, S, st, out_bufs)
    for pas in range(2):
        norm_pass(pas)
        if stop_after == "N0":
            dump("aT", aT.rearrange("p a b -> p (a b)"), B_aT)
            return _finish(nc, S, st, out_bufs)
        hgs = [4, 5, 6, 7, 8, 9] if pas == 0 else list(range(10))
        pend = []
        for hg in hgs:
            wi = load_w(wT_d[hg])
            for i in range(8):
                pend.append(tok_half(hg, pas, i, wi, next_pb()))
                if len(pend) > 2:
                    pend.pop(0)()
        for ev in pend:
            ev()
        if pas == 0:
            for ug in range(4):
                wi = load_w(wU_d[ug])
                for cc in range(2):
                    c = ug * 2 + cc
                    pb = next_pb()
                    for dc in range(16):
                        PE(lambda e, dc=dc, cc=cc, wi=wi, pb=pb: e.matmul(ps[pb][:, 0:16], lhsT=wsl[wi][:, dc, cc * 128:(cc + 1) * 128], rhs=aT[:, dc, 1008:1024],
                                                                        start=(dc == 0), stop=(dc == 15)), [B_aT[7], B_wsl[wi]], [PB[pb]])
                    ACT(lambda e, c=c, pb=pb: e.activation(out=uhalo[:, c, :], in_=ps[pb][:, 0:16], func=AF.Copy), [PB[pb]], [B_uhalo])
            if stop_after == "P0":
                dump("kT", kT.rearrange("p a b c -> p (a b c)"), [B_kT[k][g][l] for k in range(3) for g in range(2) for l in range(16)])
                return _finish(nc, S, st, out_bufs)
        else:
            for i in range(8):
                pb = next_pb()
                for dc in range(16):
                    PE(lambda e, dc=dc, i=i, pb=pb: e.matmul(ps[pb][:, 0:24], lhsT=aT[:, dc, i * 128:(i + 1) * 128], rhs=wgates[:, dc, :],
                                                             start=(dc == 0), stop=(dc == 15)), [B_aT[i], B_wgates], [PB[pb]])
                ACT(lambda e, i=i, pb=pb: e.activation(out=gsig[:, i, :], in_=ps[pb][:, 0:24], func=AF.Sigmoid), [PB[pb]], [B_gsig[i]])

    dump("kT", kT.rearrange("p a b c -> p (a b c)"), [B_kT[k][g][l] for k in range(3) for g in range(2) for l in range(16)])
    dump("qT", qT.rearrange("p a b c d -> p (a b c d)"), [B_qT[g][o] for g in range(2) for o in range(8)])
    dump("v1", v1.rearrange("p a b c -> p (a b c)"), [B_v1[k][l] for k in range(4) for l in range(16)] + [B_v1init])
    dump("vcT", vcT.rearrange("p a b -> p (a b)"), [B_vcT[g][l] for g in range(2) for l in range(16)])
    dump("gsig", gsig.rearrange("p a b -> p (a b)"), B_gsig)

    if stop_after == "P1":
        return _finish(nc, S, st, out_bufs)
    UC = Carver(A, AB_SHARED, ARENA_BYTES - 8192, "pool")
    ubuf = UC.take([1040])
    sbufA = UC.take([1040])
    sbufB = UC.take([1040])
    ptmp = UC.take([16])
    pooled = [UC.take([2, 1024], BF16) for _ in range(2)]
    poolw = UC.take([4, 2, 256], BF16)
    B_ubuf = S.buf("ubuf")
    B_sA = S.buf("sA")
    B_sB = S.buf("sB")
    B_ptmp = S.buf("ptmp")
    B_pooled = S.bufs(2, "pooled")
    B_poolw = S.buf("poolw")
    S.barrier([B_ubuf, B_sA, B_sB, B_ptmp, B_poolw] + B_pooled)
    S.dma("pool", lambda e: e.dma_start(out=poolw.rearrange("p a b c -> p (a b c)"), in_=poolw_d), "poolw", [], [B_poolw])
    for gi in range(4):
        w = 2 << gi
        wi = load_w(wU_d[gi])
        psl = gi % 2
        for cc in range(2):
            c = gi * 2 + cc
            pbs = [next_pb(), next_pb()]
            for th2 in range(2):
                for dc in range(16):
                    PE(lambda e, dc=dc, cc=cc, wi=wi, th2=th2, pb=pbs[th2]: e.matmul(ps[pb][:], lhsT=wsl[wi][:, dc, cc * 128:(cc + 1) * 128],
                                                                                   rhs=aT[:, dc, th2 * 512:(th2 + 1) * 512], start=(dc == 0), stop=(dc == 15)),
                       [B_aT[th2 * 4 + k] for k in range(4)] + [B_wsl[wi]], [PB[pbs[th2]]])
            ACT(lambda e, c=c: e.activation(out=ubuf[:, 0:16], in_=uhalo[:, c, :], func=AF.Copy), [B_uhalo], [B_ubuf])
            ACT(lambda e, pb=pbs[0]: e.activation(out=ubuf[:, 16:528], in_=ps[pb][:], func=AF.Copy), [PB[pbs[0]]], [B_ubuf])
            ACT(lambda e, pb=pbs[1]: e.activation(out=ubuf[:, 528:1040], in_=ps[pb][:], func=AF.Copy), [PB[pbs[1]]], [B_ubuf])
            cur, Bcur = ubuf, B_ubuf
            nxt = [(sbufA, B_sA), (sbufB, B_sB)]
            sh = 1
            k = 0
            while sh < w:
                dst, Bd = nxt[k % 2]
                DVE(lambda e, cur=cur, dst=dst, sh=sh: e.tensor_tensor(out=dst[:, sh:1040], in0=cur[:, sh:1040], in1=cur[:, 0:1040 - sh], op=ALU.add), [Bcur], [Bd])
                cur, Bcur = dst, Bd
                sh *= 2
                k += 1
            pl = pooled[psl][:, cc, :]
            DVE(lambda e, cur=cur, pl=pl, w=w: e.scalar_tensor_tensor(out=pl[:, 16:1024], in0=cur[:, 32:1040], scalar=1.0 / w, in1=ubuf[:, 32:1040], op0=ALU.mult, op1=ALU.subtract),
                [Bcur, B_ubuf], [B_pooled[psl]])
            DVE(lambda e, cur=cur, gi=gi: e.tensor_tensor(out=ptmp, in0=cur[:, 16:32], in1=invc[:, gi, :], op=ALU.mult), [Bcur, B_const], [B_ptmp])
            DVE(lambda e, pl=pl: e.tensor_tensor(out=pl[:, 0:16], in0=ptmp, in1=ubuf[:, 16:32], op=ALU.subtract), [B_ptmp, B_ubuf], [B_pooled[psl]])
        for dch in range(2):
            for th2 in range(2):
                pb = next_pb()
                for cc in range(2):
                    PE(lambda e, cc=cc, dch=dch, th2=th2, pb=pb, gi=gi, psl=psl: e.matmul(ps[pb][:], lhsT=poolw[:, gi, cc, dch * 128:(dch + 1) * 128],
                                                                                        rhs=pooled[psl][:, cc, th2 * 512:(th2 + 1) * 512], start=(cc == 0), stop=(cc == 1)),
                       [B_poolw, B_pooled[psl]], [PB[pb]])
                ch = gi * 2 + dch
                ACT(lambda e, ch=ch, th2=th2, pb=pb: e.activation(out=ymixT[:, ch, th2 * 512:(th2 + 1) * 512], in_=ps[pb][:], func=AF.Copy, scale=pscale[:, ch:ch + 1]),
                    [PB[pb], B_const], [B_ymix[ch]])
    dump("ypool", ymixT[:, 0:8, :].rearrange("p a b -> p (a b)"), B_ymix[0:8])

    if stop_after == "B":
        return _finish(nc, S, st, out_bufs)

    AC = Carver(A, TRANS0, ARENA_BYTES - 8192, "attn")
    w1 = [AC.take([32, 128], BF16) for _ in range(2)]
    w2 = [AC.take([128], BF16) for _ in range(2)]
    peT = [AC.take([32], BF16) for _ in range(2)]
    hb_ = AC.take([4, 128])
    hx = AC.take([4, 128])
    hy = AC.take([4, 128])
    cbias = AC.take([2, 2])
    geluT = AC.take([4, 128], BF16)
    kcmpT = AC.take([2, 128], BF16)
    vcmp1 = AC.take([2, 164], BF16)
    PT = [AC.take([4, 128], BF16) for _ in range(4)]
    oaccs = [AC.take([4, 128]) for _ in range(4)] * 4
    obf = [AC.take([4, 128], BF16) for _ in range(2)]
    impb = [AC.take([32]) for _ in range(2)]
    impw = [AC.take([32]) for _ in range(2)]
    m16 = [AC.take([16]) for _ in range(2)]
    selbf = [AC.take([32], BF16) for _ in range(2)]
    negT = AC.take([2, 8, 4, 128], BF16)
    PTc = [AC.take([4, 128], BF16) for _ in range(2)]
    rs = [AC.take([3, 4]) for _ in range(2)]
    B_w1 = S.bufs(2, "w1")
    B_w2 = S.bufs(2, "w2")
    B_peT = S.bufs(2, "peT")
    B_h = S.bufs(4, "hid")
    B_cb = S.buf("cbias")
    B_gelu = S.bufs(4, "gelu")
    B_kcmp = S.bufs(2, "kcmp")
    B_vcmp = S.bufs(2, "vcmp")
    B_PT = S.bufs(4, "PT")
    B_PTc = S.bufs(2, "PTc")
    B_oaccs = S.bufs(4, "oacc") * 4
    B_obf = S.bufs(2, "obf")
    B_imp = S.bufs(2, "imp")
    B_selbf = S.bufs(2, "selbf")
    B_negT = [[S.buf(f"negT{g}_{o}") for o in range(8)] for g in range(2)]
    B_rs = S.bufs(2, "rs")
    S.barrier(B_w1 + B_w2 + B_peT + B_h + [B_cb] + B_gelu + B_kcmp + B_vcmp + B_PT + B_PTc + B_oaccs + B_obf + B_imp + B_selbf
              + [b for l in B_negT for b in l] + B_rs)
    for kv, (w1d, w2d, ped) in enumerate([(w1k_d, w2k_d, pek_d), (w1v_d, w2v_d, pev_d)]):
        S.dma("pool", lambda e, kv=kv, w1d=w1d: e.dma_start(out=w1[kv].rearrange("p a b -> p (a b)"), in_=w1d), f"w1_{kv}", [], [B_w1[kv]])
        S.dma("pool", lambda e, kv=kv, w2d=w2d: e.dma_start(out=w2[kv], in_=w2d), f"w2_{kv}", [], [B_w2[kv]])
        S.dma("pool", lambda e, kv=kv, ped=ped: e.dma_start(out=peT[kv], in_=ped), f"pe_{kv}", [], [B_peT[kv]])
    POOL(lambda e: e.memset(vcmp1, 0.0), [], B_vcmp)
    POOL(lambda e: e.memset(negT.rearrange("p a b c d -> p (a b c d)"), 0.0), [], [b for l in B_negT for b in l])
    for g in range(2):
        S.dma("pool", lambda e, g=g: e.dma_start(out=vcmp1[:, g, 129:161], in_=c_overlap_d), f"ovl{g}", [], [B_vcmp[g]])
        POOL(lambda e, g=g: e.memset(vcmp1[0:127, g, 128:129], 1.0), [], [B_vcmp[g]])

    CB = 7
    for kv in range(2):
        for l in range(32):
            PE(lambda e, l=l, kv=kv: e.matmul(ps[CB][:, 500 + kv:501 + kv], lhsT=w1[kv][:, l, :], rhs=peT[kv][:, l:l + 1], start=(l == 0), stop=(l == 31)),
               [B_w1[kv], B_peT[kv]], [PB[CB]])
        DVE(lambda e, kv=kv: e.tensor_copy(out=cbias[:, kv, 0:1], in_=ps[CB][:, 500 + kv:501 + kv]), [PB[CB]], [B_cb])
    for kv in range(2):
        for g in range(2):
            idx = kv * 2 + g
            srcT = kT[:, 0, g, :] if kv == 0 else vcT[:, g, :]
            Bsrc = [B_kT[0][g][l] for l in range(16)] if kv == 0 else [B_vcT[g][l] for l in range(16)]
            for l in range(32):
                PE(lambda e, l=l, kv=kv, idx=idx, srcT=srcT: e.matmul(ps[CB][:, idx * 128:idx * 128 + 127], lhsT=w1[kv][:, l, :], rhs=srcT[:, l:l + 16 * 126 + 1:16],
                                                                  start=(l == 0), stop=(l == 31)), Bsrc + [B_w1[kv]], [PB[CB]])
            x_ = hb_[:, idx, 0:127]
            DVE(lambda e, idx=idx, kv=kv, x_=x_: e.tensor_scalar(out=x_, in0=ps[CB][:, idx * 128:idx * 128 + 127], scalar1=cbias[:, kv, 0:1], scalar2=None, op0=ALU.add),
                [PB[CB], B_cb], [B_h[idx]])
            DVE(lambda e, idx=idx, x_=x_: e.tensor_tensor(out=hx[:, idx, 0:127], in0=x_, in1=x_, op=ALU.mult), [B_h[idx]], [B_h[idx]])
            DVE(lambda e, idx=idx: e.tensor_scalar(out=hx[:, idx, 0:127], in0=hx[:, idx, 0:127], scalar1=0.044715, scalar2=1.0, op0=ALU.mult, op1=ALU.add), [B_h[idx]], [B_h[idx]])
            DVE(lambda e, idx=idx, x_=x_: e.tensor_tensor(out=hx[:, idx, 0:127], in0=hx[:, idx, 0:127], in1=x_, op=ALU.mult), [B_h[idx]], [B_h[idx]])
            ACT(lambda e, idx=idx: e.activation(out=hy[:, idx, 0:127], in_=hx[:, idx, 0:127], func=AF.Sigmoid, scale=1.5957691216057308), [B_h[idx]], [B_h[idx]])
            DVE(lambda e, idx=idx, x_=x_: e.tensor_tensor(out=geluT[:, idx, 0:127], in0=hy[:, idx, 0:127], in1=x_, op=ALU.mult), [B_h[idx]], [B_gelu[idx]])
    POOL(lambda e: e.memset(kcmpT, 0.0), [], B_kcmp)
    for g in range(2):
        PE(lambda e, g=g: e.matmul(ps[CB][:, 0:127], lhsT=w2[0], rhs=geluT[:, g, 0:127], start=True, stop=True), [B_w2[0], B_gelu[g]], [PB[CB]])
        DVE(lambda e, g=g: e.tensor_copy(out=kcmpT[:, g, 0:127], in_=ps[CB][:, 0:127]), [PB[CB]], [B_kcmp[g]])
        PE(lambda e, g=g: e.matmul(ps[CB][0:127, 128:256], lhsT=geluT[:, 2 + g, 0:127], rhs=w2[1], start=True, stop=True), [B_w2[1], B_gelu[2 + g]], [PB[CB]])
        DVE(lambda e, g=g: e.tensor_copy(out=vcmp1[0:127, g, 0:128], in_=ps[CB][0:127, 128:256]), [PB[CB]], [B_vcmp[g]])
    dump("kcmpT", kcmpT.rearrange("p a b -> p (a b)"), B_kcmp)
    dump("vcmp1", vcmp1.rearrange("p a b -> p (a b)"), B_vcmp)

    cnt = {"S": 0, "PT": 0, "O": 0, "rs": 0, "ob": 0}
    SB = [0, 1, 2]

    def evac_branch(obanks, br, g, ot, oa, B_oa, first, imp=None):
        k = cnt["rs"] % 2
        cnt["rs"] += 1
        r_ = rs[k]
        Br = B_rs[k]
        for bi, bank in enumerate(obanks):
            DVE(lambda e, bi=bi, bank=bank: e.tensor_scalar(out=r_[:, 0, 2 * bi:2 * bi + 2], in0=ps[bank][:, 128:385:256], scalar1=1e-30, scalar2=None, op0=ALU.add),
                [PB[bank]], [Br])
        DVE(lambda e: e.reciprocal(out=r_[:, 1, :], in_=r_[:, 0, :]), [Br], [Br])
        s0 = g * 12 + br
        DVE(lambda e: e.tensor_tensor(out=r_[:, 2, :], in0=r_[:, 1, :], in1=gsig[:, ot, s0:s0 + 10:3], op=ALU.mult), [Br, B_gsig[ot]], [Br])
        for r in range(4):
            bank = obanks[r // 2]
            c0 = (r % 2) * 256
            if first:
                DVE(lambda e, bank=bank, c0=c0, r=r: e.tensor_scalar(out=oa[:, r, :], in0=ps[bank][:, c0:c0 + 128], scalar1=r_[:, 2, r:r + 1], scalar2=None, op0=ALU.mult),
                    [PB[bank], Br], [B_oa])
            else:
                DVE(lambda e, bank=bank, c0=c0, r=r: e.scalar_tensor_tensor(out=oa[:, r, :], in0=ps[bank][:, c0:c0 + 128], scalar=r_[:, 2, r:r + 1], in1=oa[:, r, :],
                                                                          op0=ALU.mult, op1=ALU.add), [PB[bank], Br], [B_oa])
            if imp is not None:
                ib, Bi = imp
                if r == 0:
                    DVE(lambda e, bank=bank, c0=c0, r=r: e.tensor_scalar(out=ib, in0=ps[bank][:, c0 + 129:c0 + 161], scalar1=r_[:, 1, r:r + 1], scalar2=None, op0=ALU.mult),
                        [PB[bank], Br], [Bi])
                else:
                    DVE(lambda e, bank=bank, c0=c0, r=r: e.scalar_tensor_tensor(out=ib, in0=ps[bank][:, c0 + 129:c0 + 161], scalar=r_[:, 1, r:r + 1], in1=ib, op0=ALU.mult, op1=ALU.add),
                        [PB[bank], Br], [Bi])

    def osets():
        i = cnt["O"] % 2
        cnt["O"] += 1
        return (3, 4) if i == 0 else (5, 6)

    def score_exp(lhsT_ap, B_l, g, ot, bias=None):
        sb = SB[cnt["S"] % 3]
        cnt["S"] += 1
        pi = cnt["PT"] % 4
        cnt["PT"] += 1
        rhs_q = qT[:, g, ot].rearrange("p r t -> p (r t)")
        PE(lambda e: e.matmul(ps[sb][:], lhsT=lhsT_ap, rhs=rhs_q, start=True, stop=(bias is None)), B_l + [B_qT[g][ot]], [PB[sb]])
        if bias is not None:
            kt = bias
            PE(lambda e: e.matmul(ps[sb][:], lhsT=Emat[:, kt, :], rhs=negT[:, g, ot].rearrange("p r t -> p (r t)"), start=False, stop=True),
               [B_const, B_negT[g][ot]], [PB[sb]])
        ACT(lambda e: e.activation(out=PT[pi].rearrange("p r t -> p (r t)"), in_=ps[sb][:], func=AF.Exp, scale=SCALE), [PB[sb]], [B_PT[pi]])
        return pi

    def pv(pi, rhs_ap, B_r, width, obanks, first, last):
        for r in range(4):
            bank = obanks[r // 2]
            c0 = (r % 2) * 256
            PE(lambda e, r=r, bank=bank, c0=c0: e.matmul(ps[bank][:, c0:c0 + width], lhsT=PT[pi][:, r, :], rhs=rhs_ap, start=(first and r % 2 == 0), stop=last,
                                                         skip_group_check=True),
               [B_PT[pi]] + B_r, [PB[bank]])

    def bcast4(m):
        return m.unsqueeze(1).to_broadcast([128, 4, 128])

    deferred = []

    def run_pipeline(steps, la):
        stt = {}
        n = len(steps)
        for i in range(n + la):
            if i < n:
                if steps[i][2] is not None:
                    steps[i][2]()
                stt[i] = steps[i][0]()
            j = i - la
            if j >= 0:
                steps[j][1](stt.pop(j))
            for dfr in list(deferred):
                dfr[0] -= 1
                if dfr[0] <= 0:
                    deferred.remove(dfr)
                    dfr[1]()

    def c1_stages(g, ot):
        oi = g * 8 + ot
        ii = oi % 2
        pt = PTc[ii]
        Bpt = B_PTc[ii]
        hold = {}

        def s0():
            sb = SB[cnt["S"] % 3]
            cnt["S"] += 1
            PE(lambda e: e.matmul(ps[sb][:], lhsT=kcmpT[:, g, :], rhs=qT[:, g, ot].rearrange("p r t -> p (r t)"), start=True, stop=True), [B_kcmp[g], B_qT[g][ot]], [PB[sb]])
            ACT(lambda e: e.activation(out=pt.rearrange("p r t -> p (r t)"), in_=ps[sb][:], func=AF.Exp, scale=SCALE), [PB[sb]], [Bpt])
            DVE(lambda e: e.tensor_tensor(out=pt, in0=pt, in1=bcast4(cvalid[:, ot * 128:(ot + 1) * 128]), op=ALU.mult), [Bpt, B_const], [Bpt])

        def s1():
            ob = (5, 6)
            for r in range(4):
                bank = ob[r // 2]
                c0 = (r % 2) * 256
                PE(lambda e, r=r, bank=bank, c0=c0: e.matmul(ps[bank][:, c0:c0 + 161], lhsT=pt[:, r, :], rhs=vcmp1[:, g, 0:161], start=(r % 2 == 0), stop=True, skip_group_check=True),
                   [Bpt, B_vcmp[g]], [PB[bank]])
            evac_branch(ob, 0, g, ot, oaccs[oi], B_oaccs[oi], True, imp=(impb[ii], B_imp[ii]))

        def s2():
            DVE(lambda e: e.tensor_tensor(out=impb[ii], in0=impb[ii], in1=selbias[:, ot, :], op=ALU.add), [B_imp[ii], B_const], [B_imp[ii]])
            DVE(lambda e: e.max(out=m16[ii][:, 0:8], in_=impb[ii]), [B_imp[ii]], [B_imp[ii]])
            DVE(lambda e: e.match_replace(out=impw[ii], in_to_replace=m16[ii][:, 0:8], in_values=impb[ii], imm_value=-3.0e30), [B_imp[ii]], [B_imp[ii]])
            DVE(lambda e: e.max(out=m16[ii][:, 8:16], in_=impw[ii]), [B_imp[ii]], [B_imp[ii]])
            DVE(lambda e: e.tensor_scalar(out=m16[ii][:, 15:16], in0=m16[ii][:, 15:16], scalar1=-1.0e29, scalar2=None, op0=ALU.max), [B_imp[ii]], [B_imp[ii]])
            DVE(lambda e: e.tensor_scalar(out=impw[ii], in0=impb[ii], scalar1=m16[ii][:, 15:16], scalar2=None, op0=ALU.is_ge), [B_imp[ii]], [B_imp[ii]])
            DVE(lambda e: e.tensor_scalar(out=selbf[ii], in0=impw[ii], scalar1=-1.0, scalar2=30000.0, op0=ALU.add, op1=ALU.mult), [B_imp[ii]], [B_selbf[ii]])

        def s3():
            PE(lambda e: e.transpose(out=psb[7][0:32, 0:128], in_=selbf[ii], identity=ident), [B_selbf[ii], B_const], [PB[7]])
            DVE(lambda e: e.tensor_copy(out=negT[0:32, g, ot], in_=psb[7][0:32, 0:128].unsqueeze(1).to_broadcast([32, 4, 128])), [PB[7]], [B_negT[g][ot]])
        return [s0, s1, s2, s3]

    chains = [(g, ot) for g in range(2) for ot in range(8)]
    c1s = [c1_stages(g, ot) for (g, ot) in chains]
    for f in c1s[0]:
        f()

    c2 = []
    for g in range(2):
        for ot in range(8):
            ltq = 8 + ot
            oi = g * 8 + ot
            lstep = [0]
            for br in (1, 2):
                nst = ltq + 1 if br == 1 else 5
                holder = {}
                for si in range(nst):
                    kt = si if br == 1 else ltq - 4 + si

                    def stepA(g=g, ot=ot, br=br, si=si, kt=kt, ltq=ltq):
                        if br == 1:
                            pi = score_exp(kT[:, 1, g, kt * 128:(kt + 1) * 128], [B_kT[1][g][kt]], g, ot, bias=kt)
                            if kt == ltq:
                                DVE(lambda e: e.tensor_tensor(out=PT[pi], in0=PT[pi], in1=bcast4(diag), op=ALU.mult), [B_PT[pi], B_const], [B_PT[pi]])
                        else:
                            pi = score_exp(kT[:, 2, g, kt * 128:(kt + 1) * 128], [B_kT[2][g][kt]], g, ot)
                            wv = wvalid[:, ot * 5 + si:ot * 5 + si + 1]
                            if si == 0:
                                DVE(lambda e: e.scalar_tensor_tensor(out=PT[pi], in0=PT[pi], scalar=wv, in1=bcast4(upst), op0=ALU.mult, op1=ALU.mult), [B_PT[pi], B_const], [B_PT[pi]])
                            elif si == 4:
                                DVE(lambda e: e.tensor_tensor(out=PT[pi], in0=PT[pi], in1=bcast4(diag), op=ALU.mult), [B_PT[pi], B_const], [B_PT[pi]])
                            elif kt < 8:
                                DVE(lambda e: e.tensor_scalar(out=PT[pi], in0=PT[pi], scalar1=wv, scalar2=None, op0=ALU.mult), [B_PT[pi], B_const], [B_PT[pi]])
                        return pi

                    def stepB(pi, g=g, ot=ot, br=br, si=si, kt=kt, nst=nst, oi=oi, holder=holder):
                        ob = (3, 4) if br == 1 else (5, 6)
                        vk = (0 if br == 1 else 2) + g
                        pv(pi, v1[:, vk, kt, 0:129], [B_v1[vk][kt], B_v1init], 129, ob, si == 0, si == nst - 1)
                        if si == nst - 1:
                            evac_branch(ob, br, g, ot, oaccs[oi], B_oaccs[oi], False)
                            if br == 2:
                                def combine(g=g, ot=ot, oi=oi):
                                    k = cnt["ob"] % 2
                                    cnt["ob"] += 1
                                    ACT(lambda e: e.activation(out=obf[k], in_=oaccs[oi], func=AF.Copy), [B_oaccs[oi]], [B_obf[k]])

                                    def tr():
                                        for r in range(4):
                                            PE(lambda e, r=r: e.transpose(out=psb[7][:, 256 + r * 128:256 + (r + 1) * 128], in_=obf[k][:, r, :], identity=ident), [B_obf[k], B_const], [PB[7]])
                                        DVE(lambda e: e.tensor_copy(out=ymixT[:, 8 + g * 4:12 + g * 4, ot * 128:(ot + 1) * 128], in_=psb[7][:, 256:768].rearrange("p (r t) -> p r t", r=4)),
                                            [PB[7]], [B_ymix[8 + g * 4 + r] for r in range(4)])
                                    deferred.append([6, tr])
                                deferred.append([4, combine])
                    pre = None
                    if oi + 1 < 16 and lstep[0] in (0, 2, 5, 9):
                        pre = c1s[oi + 1][(0, 2, 5, 9).index(lstep[0])]
                    lstep[0] += 1
                    c2.append((stepA, stepB, pre))
    run_pipeline(c2, 2)
    while deferred:
        deferred.pop(0)[1]()
    dump("ynsa", ymixT[:, 8:16, :].rearrange("p a b -> p (a b)"), B_ymix[8:16])

    if stop_after == "C":
        return _finish(nc, S, st, out_bufs)

    hres = A.view(CONST_END, [8, 2048], F32)
    assert CONST_END + 65536 <= YMIX_OFF
    D_END = CONST_END + 65536
    DC = Carver(A, TRANS0, ARENA_BYTES - 8192, "D")
    wo = [DC.take([16, 512], BF16) for _ in range(2)]
    B_h = [[S.buf(f"h{ot}_{c}") for c in range(4)] for ot in range(8)]
    B_wo = S.bufs(2, "wo")
    S.barrier([b for l in B_h for b in l] + B_wo)
    for ot in range(8):
        S.dma("sp", lambda e, ot=ot: e.dma_start(out=hres[:, ot, :], in_=x_d[1024 + ot * 128:1024 + (ot + 1) * 128, :]), f"hres{ot}", [], B_h[ot])
    pb8 = [0]

    def npb():
        b = pb8[0] % 8
        pb8[0] += 1
        return b

    for dmc in range(4):
        sl = dmc % 2
        S.dma("pool", lambda e, dmc=dmc, sl=sl: e.dma_start(out=wo[sl].rearrange("p a b -> p (a b)"), in_=wout_d[dmc]), f"wo{sl}", [], [B_wo[sl]])
        for ot in range(8):
            pb = npb()
            for c in range(16):
                PE(lambda e, c=c, ot=ot, sl=sl, pb=pb: e.matmul(ps[pb][:], lhsT=ymixT[:, c, ot * 128:(ot + 1) * 128], rhs=wo[sl][:, c, :], start=(c == 0), stop=(c == 15)),
                   [B_ymix[c], B_wo[sl]], [PB[pb]])
            DVE(lambda e, ot=ot, dmc=dmc, pb=pb: e.tensor_tensor(out=hres[:, ot, dmc * 512:(dmc + 1) * 512], in0=ps[pb][:], in1=hres[:, ot, dmc * 512:(dmc + 1) * 512], op=ALU.add),
                [PB[pb], B_h[ot][dmc]], [B_h[ot][dmc]])
    dump("h1", hres.rearrange("p a b -> p (a b)"), [b for l in B_h for b in l])
    if stop_after == "D":
        return _finish(nc, S, st, out_bufs)

    EC = Carver(A, D_END, ARENA_BYTES, "E")
    a2T = EC.take([16, 1024], BF16)
    E_A2T_END = EC.p
    hT = EC.take([12, 1024], BF16)
    wgu = [EC.take([16, 256], BF16) for _ in range(4)]
    wdn = [EC.take([12, 512], BF16) for _ in range(2)]
    sgt = [EC.take([512]) for _ in range(2)]
    EC2 = Carver(A, 768, 768 + 2048 + 4096 + 1024 + 256 + 2048 + 2048, "Ealias")
    abf2 = [EC2.take([2048], BF16), EC2.take([2048], BF16)]
    dbg_state[0] = EC2.p
    dbg_state[1] = 512
    S.barrier([B_dbgstg])
    B_a2T = [S.buf(f"a2T{i}") for i in range(8)]
    B_hT = [[S.buf(f"hT{c}_{t}") for t in range(2)] for c in range(12)]
    B_wgu = S.bufs(4, "wgu")
    B_wdn = S.bufs(2, "wdn")
    B_sgt = S.bufs(2, "sgt")
    B_abf2 = S.bufs(2, "abf2")
    S.barrier(B_a2T + [b for l in B_hT for b in l] + B_wgu + B_wdn + B_sgt + B_abf2)

    def rms_h(ot, dst, B_dst):
        k = nstat[0] % 16
        nstat[0] += 1
        s_ = stat[:, k, :]
        sb = S.buf()
        src = hres[:, ot, :]
        ACT(lambda e: e.activation(out=dst, in_=src, func=AF.Square, accum_out=s_[:, 0:1]), B_h[ot], [B_dst, sb])
        DVE(lambda e: e.tensor_scalar(out=s_[:, 1:2], in0=s_[:, 0:1], scalar1=1.0 / 2048, scalar2=1e-6, op0=ALU.mult, op1=ALU.add), [sb], [sb])
        ACT(lambda e: e.activation(out=s_[:, 2:3], in_=s_[:, 1:2], func=AF.Sqrt), [sb], [sb])
        DVE(lambda e: e.reciprocal(out=s_[:, 3:4], in_=s_[:, 2:3]), [sb], [sb])
        DVE(lambda e: e.scalar_tensor_tensor(out=dst, in0=src, scalar=s_[:, 3:4], in1=gfull, op0=ALU.mult, op1=ALU.mult),
            [sb, B_gfull] + B_h[ot], [B_dst])

    def norm_hres(g_d, dstT, B_dstT):
        load_g(g_d)
        for ot in range(8):
            sl = ot % 2
            rms_h(ot, abf2[sl], B_abf2[sl])
            transpose_tile(abf2[sl], B_abf2[sl], dstT, B_dstT[ot], ot * 128)

    norm_hres(g_ffn_d, a2T, B_a2T)
    gcount = [0]
    dcount = [0]
    for fq, (c0, c1) in enumerate(NQ_FF):
        ncq = c1 - c0
        for grp in range(c0 // 2, c1 // 2):
            sl = (gcount[0] % 2) * 2
            gcount[0] += 1
            S.dma("pool", lambda e, grp=grp, sl=sl: e.dma_start(out=wgu[sl].rearrange("p a b -> p (a b)"), in_=wg_d[grp]), f"wgu{sl}", [], [B_wgu[sl]])
            S.dma("pool", lambda e, grp=grp, sl=sl: e.dma_start(out=wgu[sl + 1].rearrange("p a b -> p (a b)"), in_=wu_d[grp]), f"wgu{sl + 1}", [], [B_wgu[sl + 1]])
            for half in range(2):
                ci = grp * 2 + half - c0
                for th2 in range(2):
                    pg, pu = npb(), npb()
                    for dc in range(16):
                        PE(lambda e, dc=dc, sl=sl, half=half, th2=th2, pg=pg: e.matmul(ps[pg][:], lhsT=wgu[sl][:, dc, half * 128:(half + 1) * 128], rhs=a2T[:, dc, th2 * 512:(th2 + 1) * 512],
                                                                                     start=(dc == 0), stop=(dc == 15)), [B_wgu[sl]] + B_a2T[th2 * 4:th2 * 4 + 4], [PB[pg]])
                    for dc in range(16):
                        PE(lambda e, dc=dc, sl=sl, half=half, th2=th2, pu=pu: e.matmul(ps[pu][:], lhsT=wgu[sl + 1][:, dc, half * 128:(half + 1) * 128], rhs=a2T[:, dc, th2 * 512:(th2 + 1) * 512],
                                                                                     start=(dc == 0), stop=(dc == 15)), [B_wgu[sl + 1]] + B_a2T[th2 * 4:th2 * 4 + 4], [PB[pu]])
                    ss = th2
                    ACT(lambda e, pg=pg, ss=ss: e.activation(out=sgt[ss], in_=ps[pg][:], func=AF.Silu), [PB[pg]], [B_sgt[ss]])
                    DVE(lambda e, pu=pu, ss=ss, ci=ci, th2=th2: e.tensor_tensor(out=hT[:, ci, th2 * 512:(th2 + 1) * 512], in0=ps[pu][:], in1=sgt[ss], op=ALU.mult),
                        [PB[pu], B_sgt[ss]], [B_hT[ci][th2]])
        for dmc in range(4):
            sl = dcount[0] % 2
            dcount[0] += 1
            S.dma("pool", lambda e, fq=fq, dmc=dmc, sl=sl: e.dma_start(out=wdn[sl].rearrange("p a b -> p (a b)"), in_=wd_d[fq * 4 + dmc]), f"wdn{sl}", [], [B_wdn[sl]])
            for ot in range(8):
                pb = npb()
                for ci in range(ncq):
                    PE(lambda e, ci=ci, ot=ot, sl=sl, pb=pb: e.matmul(ps[pb][:], lhsT=hT[:, ci, ot * 128:(ot + 1) * 128], rhs=wdn[sl][:, ci, :], start=(ci == 0), stop=(ci == ncq - 1)),
                       [B_hT[ci][ot // 4], B_wdn[sl]], [PB[pb]])
                DVE(lambda e, ot=ot, dmc=dmc, pb=pb: e.tensor_tensor(out=hres[:, ot, dmc * 512:(dmc + 1) * 512], in0=ps[pb][:], in1=hres[:, ot, dmc * 512:(dmc + 1) * 512], op=ALU.add),
                    [PB[pb], B_h[ot][dmc]], [B_h[ot][dmc]])
    dump("h2", hres.rearrange("p a b -> p (a b)"), [b for l in B_h for b in l])
    if stop_after == "E":
        return _finish(nc, S, st, out_bufs)

    FC = Carver(A, E_A2T_END, ARENA_BYTES - 8192, "F")
    a3T = a2T
    B_a3T = B_a2T
    wpg = [FC.take([16, 512], BF16) for _ in range(2)]
    wpp = [FC.take([2, 512], BF16) for _ in range(2)]
    pT = FC.take([2, 1024], BF16)
    pf = [FC.take([256]) for _ in range(2)]
    pbf = [FC.take([256], BF16) for _ in range(2)]
    gt = [FC.take([512]) for _ in range(2)]
    outt = [FC.take([2048]) for _ in range(2)]
    B_wpg = S.bufs(2, "wpg")
    B_wpp = S.bufs(2, "wpp")
    B_pT = S.bufs(8, "pT")
    B_pf = S.bufs(2, "pf")
    B_pbf = S.bufs(2, "pbf")
    B_gt = S.bufs(2, "gt")
    B_outt = S.bufs(2, "outt")
    S.barrier(B_wpg + B_wpp + B_pT + B_pf + B_pbf + B_gt + B_outt)
    norm_hres(g_ple_d, a3T, B_a3T)
    for ot in range(8):
        sl = ot % 2
        S.dma("sp", lambda e, ot=ot, sl=sl: e.dma_start(out=pf[sl], in_=p_d[ot * 128:(ot + 1) * 128, :]), f"pf{sl}", [], [B_pf[sl]])
        ACT(lambda e, sl=sl: e.activation(out=pbf[sl], in_=pf[sl], func=AF.Copy), [B_pf[sl]], [B_pbf[sl]])
        for c2 in range(2):
            PE(lambda e, c2=c2, sl=sl: e.transpose(out=psb[7][:, c2 * 128:(c2 + 1) * 128], in_=pbf[sl][:, c2 * 128:(c2 + 1) * 128], identity=ident), [B_pbf[sl], B_const], [PB[7]])
        DVE(lambda e, ot=ot: e.tensor_copy(out=pT[:, :, ot * 128:(ot + 1) * 128], in_=psb[7][:, 0:256].rearrange("p (a b) -> p a b", a=2)), [PB[7]], [B_pT[ot]])
    pb7 = [0]

    def npb7():
        b = pb7[0] % 7
        pb7[0] += 1
        return b

    for dmc in range(4):
        sl = dmc % 2
        S.dma("pool", lambda e, dmc=dmc, sl=sl: e.dma_start(out=wpg[sl].rearrange("p a b -> p (a b)"), in_=wpg_d[dmc]), f"wpg{sl}", [], [B_wpg[sl]])
        S.dma("pool", lambda e, dmc=dmc, sl=sl: e.dma_start(out=wpp[sl].rearrange("p a b -> p (a b)"), in_=wpp_d[dmc]), f"wpp{sl}", [], [B_wpp[sl]])
        for ot in range(8):
            p1, p2 = npb7(), npb7()
            for dc in range(16):
                PE(lambda e, dc=dc, ot=ot, sl=sl, p1=p1: e.matmul(ps[p1][:], lhsT=a3T[:, dc, ot * 128:(ot + 1) * 128], rhs=wpg[sl][:, dc, :], start=(dc == 0), stop=(dc == 15)),
                   [B_a3T[ot], B_wpg[sl]], [PB[p1]])
            for c2 in range(2):
                PE(lambda e, c2=c2, ot=ot, sl=sl, p2=p2: e.matmul(ps[p2][:], lhsT=pT[:, c2, ot * 128:(ot + 1) * 128], rhs=wpp[sl][:, c2, :], start=(c2 == 0), stop=(c2 == 1)),
                   [B_pT[ot], B_wpp[sl]], [PB[p2]])
            gs = (dmc * 8 + ot) % 2
            ACT(lambda e, p1=p1, gs=gs: e.activation(out=gt[gs], in_=ps[p1][:], func=AF.Sigmoid), [PB[p1]], [B_gt[gs]])
            DVE(lambda e, p2=p2, gs=gs: e.tensor_tensor(out=gt[gs], in0=ps[p2][:], in1=gt[gs], op=ALU.mult), [PB[p2], B_gt[gs]], [B_gt[gs]])
            POOL(lambda e, ot=ot, dmc=dmc, gs=gs: e.tensor_tensor(out=hres[:, ot, dmc * 512:(dmc + 1) * 512], in0=gt[gs], in1=hres[:, ot, dmc * 512:(dmc + 1) * 512], op=ALU.add),
                 [B_gt[gs], B_h[ot][dmc]], [B_h[ot][dmc]])
    load_g(g_fin_d)
    for ot in range(8):
        sl = ot % 2
        rms_h(ot, outt[sl], B_outt[sl])
        ob = S.buf()
        S.dma("sp", lambda e, ot=ot, sl=sl: e.dma_start(out=out_d[ot * 128:(ot + 1) * 128, :], in_=outt[sl]), f"ost{sl}", [B_outt[sl]], [ob])
        out_bufs.append(ob)
    return _finish(nc, S, st, out_bufs)


def _finish(nc, S, st, out_bufs):
    S.op("sp", lambda e: e.nop(), out_bufs, [])
    S.emit(nc, st)
    st.close()
    return nc, S


def _tile_cols(W, c0, n):
    return np.ascontiguousarray(W[:, c0:c0 + n].reshape(16, 128, n).transpose(1, 0, 2).reshape(128, 16 * n))


def _consts(th):
    f32 = np.float32
    L = np.arange(2048)
    tg = L - 1024 + 1024 * th
    pos = np.maximum(tg, 0).astype(f32)
    inv_freq = (f32(500000.0) ** (-np.arange(0, 32, 2, dtype=f32) / f32(32))).astype(f32)
    ang = (pos[:, None] * inv_freq[None, :]).astype(f32)
    cos, sin = np.cos(ang).astype(f32), np.sin(ang).astype(f32)
    cos2 = np.concatenate([cos, cos], 1)
    sin2 = np.concatenate([-sin, sin], 1)

    def tokmaj(a):
        w = a.shape[1]
        return np.ascontiguousarray(a.reshape(16, 128, w).transpose(1, 0, 2).reshape(128, 16 * w))

    n = np.arange(128)
    Lo = 1024 + np.arange(1024)
    cvalid = ((16 * n[:, None] + 31) <= Lo[None, :]) & ((th == 1) | (n[:, None] >= 64))
    s = np.arange(32)
    overlap = ((16 * n[:, None]) < (64 * s[None, :] + 64)) & ((16 * n[:, None] + 32) > 64 * s[None, :])
    overlap = overlap & (n[:, None] < 127)
    cur = Lo // 64
    valid = (s[None, :] <= cur[:, None]) & ((th == 1) | (s[None, :] >= 16))
    forced = (s[None, :] == cur[:, None]) | (s[None, :] == cur[:, None] - 1) | (s[None, :] == 16 * (1 - th))
    selbias = np.where(valid, np.where(forced, BIG, 0.0), -BIG).astype(f32)
    selbias = np.ascontiguousarray(selbias.reshape(8, 128, 32).transpose(1, 0, 2).reshape(128, 256))
    k = np.arange(128)
    E = np.zeros((128, 16, 128), f32)
    for kt in range(16):
        E[2 * kt + k // 64, kt, k] = 1.0
    ident = np.eye(128, dtype=f32)
    diag = (k[:, None] <= k[None, :]).astype(f32)
    upst = (k[:, None] > k[None, :]).astype(f32)
    tri = np.concatenate([ident, diag, upst], 1)
    wvalid = np.zeros((8, 5), f32)
    for ot in range(8):
        for off in range(5):
            kt = 8 + ot - 4 + off
            wvalid[ot, off] = 1.0 if (th == 1 or kt >= 8) else 0.0
    wvalid = np.broadcast_to(wvalid.reshape(1, 40), (128, 40))
    invc = np.zeros((4, 16), f32)
    for gi in range(4):
        w = 2 << gi
        for i in range(16):
            invc[gi, i] = 1.0 / w if th == 1 else 1.0 / min(i + 1, w)
    invc = np.broadcast_to(invc.reshape(1, 64), (128, 64))
    c = lambda a: np.ascontiguousarray(a, dtype=f32)
    return {
        "c_cos": tokmaj(cos2), "c_sin": tokmaj(sin2), "c_cvalid": c(cvalid), "c_overlap": c(overlap), "c_selbias": c(selbias),
        "c_E": c(E.reshape(128, 2048)), "c_tri": c(tri), "c_wvalid": c(wvalid), "c_invc": c(invc),
    }


def _prep_shared(w_in, pool_w, pool_scale, cmp_k_pe, cmp_k_w1, cmp_k_w2, cmp_v_pe, cmp_v_w1, cmp_v_w2, w_out,
                 w_gate, w_up, w_down, w_ple_gate, w_ple_proj, in_norm_g, ffn_norm_g, ple_norm_g, final_norm_g):
    c = lambda a: np.ascontiguousarray(a, dtype=np.float32)
    d = {}
    w_in = w_in[0]
    d["wU"] = np.stack([_tile_cols(w_in, i * 256, 256) for i in range(4)])
    d["wT"] = np.stack([_tile_cols(w_in, 1024 + i * 256, 256) for i in range(10)])
    d["wG"] = _tile_cols(w_in, 3584, 24)
    d["poolw"] = c(pool_w[0].reshape(4, 2, 128, 256).transpose(2, 0, 1, 3).reshape(128, 2048))
    d["pscale"] = c(pool_scale[0].reshape(8, 128).T)
    d["w1k"] = c(cmp_k_w1[0].reshape(32, 128, 128).transpose(1, 0, 2).reshape(128, 4096))
    d["w1v"] = c(cmp_v_w1[0].reshape(32, 128, 128).transpose(1, 0, 2).reshape(128, 4096))
    d["w2k"] = c(cmp_k_w2[0])
    d["w2v"] = c(cmp_v_w2[0])
    d["pekT"] = c(cmp_k_pe[0].T)
    d["pevT"] = c(cmp_v_pe[0].T)
    d["wout"] = np.stack([_tile_cols(w_out[0], i * 512, 512) for i in range(4)])
    d["wgate"] = np.stack([_tile_cols(w_gate[0], i * 256, 256) for i in range(22)])
    d["wup"] = np.stack([_tile_cols(w_up[0], i * 256, 256) for i in range(22)])
    wd = np.zeros((16, 128, 12 * 512), np.float32)
    for fq, (c0, c1) in enumerate(NQ_FF):
        for dmc in range(4):
            blk = w_down[0][c0 * 128:c1 * 128, dmc * 512:(dmc + 1) * 512].reshape(c1 - c0, 128, 512).transpose(1, 0, 2)
            wd[fq * 4 + dmc, :, :(c1 - c0) * 512] = blk.reshape(128, -1)
    d["wdown"] = wd
    d["wpg"] = np.stack([_tile_cols(w_ple_gate[0], i * 512, 512) for i in range(4)])
    wpp = w_ple_proj[0]
    d["wpp"] = np.stack([c(wpp[:, i * 512:(i + 1) * 512].reshape(2, 128, 512).transpose(1, 0, 2).reshape(128, 1024)) for i in range(4)])
    d["g_in"] = c(in_norm_g[0])
    d["g_ffn"] = c(ffn_norm_g[0])
    d["g_ple"] = c(ple_norm_g[0])
    d["g_fin"] = c(final_norm_g)
    return d


def make_in_maps(x, p, **w):
    shared = _prep_shared(**w)
    cst = [_consts(0), _consts(1)]
    in_maps = []
    for b in range(4):
        for th in range(2):
            m = dict(shared)
            m.update(cst[th])
            if th == 1:
                xl = x[b]
            else:
                xl = np.concatenate([np.zeros((1024, 2048), np.float32), x[b, :1024]], 0)
            m["x"] = np.ascontiguousarray(xl, dtype=np.float32)
            m["p"] = np.ascontiguousarray(p[0, b, th * 1024:(th + 1) * 1024], dtype=np.float32)
            in_maps.append(m)
    return in_maps


_NC_CACHE = {}


def kernel(x, p, in_norm_g, w_in, pool_w, pool_scale, cmp_k_pe, cmp_k_w1, cmp_k_w2, cmp_v_pe, cmp_v_w1, cmp_v_w2,
           w_out, ffn_norm_g, w_gate, w_up, w_down, ple_norm_g, w_ple_gate, w_ple_proj, final_norm_g):
    args = dict(locals())
    args = {k: np.asarray(v) for k, v in args.items()}
    x = args.pop("x")
    p = args.pop("p")
    in_maps = make_in_maps(x, p, **args)
    if "nc" not in _NC_CACHE:
        _NC_CACHE["nc"] = build()[0]
    nc = _NC_CACHE["nc"]
    res = run_bass_kernel_spmd(nc, in_maps, core_ids=list(range(8)))
    out = np.zeros((4, 2048, 2048), np.float32)
    for b in range(4):
        for th in range(2):
            out[b, th * 1024:(th + 1) * 1024] = res.results[b * 2 + th]["out"]
    return out
```

```python
import numpy as np
from contextlib import ExitStack
import concourse.bass as bass
import concourse.mybir as mybir
from concourse.bass_utils import run_bass_kernel_spmd

F32 = mybir.dt.float32
BF16 = mybir.dt.bfloat16
AF = mybir.ActivationFunctionType
ALU = mybir.AluOpType
AX = mybir.AxisListType

ENGS = ("pe", "act", "dve", "pool", "sp")
BIG = 1.0e30
SCALE = 128.0 ** -0.5


class Buf:
    __slots__ = ("name", "last_w", "readers", "excl")

    def __init__(self, name, excl=False):
        self.name = name
        self.last_w = None
        self.readers = []
        self.excl = excl


class Op:
    __slots__ = ("eng", "fn", "deps", "signal", "sigval", "dma_key", "dma_val", "idx")


class Sched:
    def __init__(self):
        self.ops = {e: [] for e in ENGS}
        self.dma_cnt = {}
        self.nbuf = 0

    def buf(self, name=None):
        self.nbuf += 1
        return Buf(name or f"b{self.nbuf}")

    def bufs(self, n, name="b"):
        return [self.buf(f"{name}{i}") for i in range(n)]

    def _add(self, eng, fn, reads, writes, dma_key=None):
        op = Op()
        op.eng = eng
        op.fn = fn
        op.signal = False
        op.sigval = None
        op.dma_key = dma_key
        op.dma_val = None
        op.idx = len(self.ops[eng])
        deps = []
        for b in reads:
            if b.last_w is not None:
                deps.append(b.last_w)
            if b.excl:
                deps.extend(t for t in b.readers if t[1] != eng)
        for b in writes:
            if b.last_w is not None:
                deps.append(b.last_w)
            deps.extend(b.readers)
        if dma_key is not None:
            self.dma_cnt[dma_key] = self.dma_cnt.get(dma_key, 0) + 16
            op.dma_val = self.dma_cnt[dma_key]
            tok = ("dma", dma_key, op.dma_val)
        else:
            tok = ("eng", eng, op.idx)
        op.deps = [d for d in set(deps)
                   if not (d[0] == "eng" and d[1] == "pe" and eng == "pe" and dma_key is None)]
        for b in reads:
            b.readers.append(tok)
        for b in writes:
            b.last_w = tok
            b.readers = []
        self.ops[eng].append(op)
        return op

    def op(self, eng, fn, reads=(), writes=()):
        return self._add(eng, fn, list(reads), list(writes))

    def dma(self, eng, fn, key, reads=(), writes=()):
        return self._add(eng, fn, list(reads), list(writes), dma_key=key)

    def barrier(self, bufs):
        toks = []
        for e in ENGS:
            for o in reversed(self.ops[e]):
                if o.dma_key is None:
                    toks.append(("eng", e, o.idx))
                    break
        for k, v in self.dma_cnt.items():
            toks.append(("dma", k, v))
        for b in bufs:
            b.readers.extend(toks)

    def emit(self, nc, stack):
        for e in ENGS:
            for o in self.ops[e]:
                for d in o.deps:
                    if d[0] == "eng":
                        self.ops[d[1]][d[2]].signal = True
        esem = {}
        for e in ENGS:
            c = 0
            for o in self.ops[e]:
                if o.signal and o.dma_key is None:
                    c += 1
                    o.sigval = c
            if c > 0:
                esem[e] = stack.enter_context(nc.semaphore(f"s_{e}"))
        dsem = {k: stack.enter_context(nc.semaphore(f"d_{k}")) for k in self.dma_cnt}
        self.nsem = len(esem) + len(dsem)
        block = stack.enter_context(nc.Block())
        ops = self.ops
        stats = {}

        def run(e, engobj):
            waited = {}
            nw = 0
            for o in ops[e]:
                need = {}
                for d in o.deps:
                    if d[0] == "eng":
                        sem = esem[d[1]]
                        val = ops[d[1]][d[2]].sigval
                        key = ("e", d[1])
                    else:
                        sem = dsem[d[1]]
                        val = d[2]
                        key = ("d", d[1])
                    if need.get(key, (None, -1))[1] < val:
                        need[key] = (sem, val)
                for key, (sem, val) in need.items():
                    if waited.get(key, -1) >= val:
                        continue
                    waited[key] = val
                    engobj.wait_ge(sem, val)
                    nw += 1
                ins = o.fn(engobj)
                if o.dma_key is not None:
                    ins.then_inc(dsem[o.dma_key], 16)
                elif o.signal:
                    ins.then_inc(esem[e], 1)
            stats[e] = (len(ops[e]), nw)

        if ops["sp"]:
            block.sync(lambda eng: run("sp", eng))
        if ops["pe"]:
            block.tensor(lambda eng: run("pe", eng))
        if ops["act"]:
            block.scalar(lambda eng: run("act", eng))
        if ops["dve"]:
            block.vector(lambda eng: run("dve", eng))
        if ops["pool"]:
            block.gpsimd(lambda eng: run("pool", eng))
        self.stats = stats


class Arena:
    def __init__(self, nc, nbytes):
        self.nbytes = nbytes
        self.t = nc.alloc_sbuf_tensor("arena", [128, nbytes // 4], F32)

    def view(self, off, shape, dtype=F32, parts=128):
        shape = list(shape)
        n = int(np.prod(shape))
        nb = n * (2 if dtype == BF16 else 4)
        assert off % 4 == 0 and nb % 4 == 0, (off, nb)
        assert off + nb <= self.nbytes, ("arena overflow", off, nb)
        ap = self.t[0:parts, off // 4:(off + nb) // 4]
        if dtype == BF16:
            ap = ap.bitcast(BF16)
        if len(shape) == 2:
            ap = ap.rearrange("p (a b) -> p a b", a=shape[0])
        elif len(shape) == 3:
            ap = ap.rearrange("p (a b c) -> p a b c", a=shape[0], b=shape[1])
        elif len(shape) == 4:
            ap = ap.rearrange("p (a b c d) -> p a b c d", a=shape[0], b=shape[1], c=shape[2])
        return ap


class Carver:
    def __init__(self, arena, start, end, name=""):
        self.a = arena
        self.p = start
        self.end = end
        self.name = name

    def take(self, shape, dtype=F32, parts=128):
        n = int(np.prod(shape)) * (2 if dtype == BF16 else 4)
        n = (n + 31) // 32 * 32
        off = self.p
        self.p += n
        assert self.p <= self.end, ("carver overflow", self.name, self.p, self.end)
        return self.a.view(off, shape, dtype, parts)


ARENA_BYTES = 212000
NQ_FF = [(0, 12), (12, 22), (22, 34), (34, 44)]


def build(stop_after=None, dbg=()):
    nc = bass.Bass("TRN2", target_bir_lowering=False)

    def din(name, shape):
        return nc.dram_tensor(name, list(shape), F32, kind="ExternalInput").ap()

    x_d = din("x", [2048, 2048])
    p_d = din("p", [1024, 256])
    g_in_d = din("g_in", [2048])
    g_ffn_d = din("g_ffn", [2048])
    g_ple_d = din("g_ple", [2048])
    g_fin_d = din("g_fin", [2048])
    wU_d = din("wU", [4, 128, 16 * 256])
    wT_d = din("wT", [10, 128, 16 * 256])
    wG_d = din("wG", [128, 16 * 24])
    poolw_d = din("poolw", [128, 4 * 2 * 256])
    pscale_d = din("pscale", [128, 8])
    w1k_d = din("w1k", [128, 32 * 128])
    w1v_d = din("w1v", [128, 32 * 128])
    w2k_d = din("w2k", [128, 128])
    w2v_d = din("w2v", [128, 128])
    pek_d = din("pekT", [128, 32])
    pev_d = din("pevT", [128, 32])
    wout_d = din("wout", [4, 128, 16 * 512])
    wg_d = din("wgate", [22, 128, 16 * 256])
    wu_d = din("wup", [22, 128, 16 * 256])
    wd_d = din("wdown", [16, 128, 12 * 512])
    wpg_d = din("wpg", [4, 128, 16 * 512])
    wpp_d = din("wpp", [4, 128, 2 * 512])
    c_cos_d = din("c_cos", [128, 16 * 32])
    c_sin_d = din("c_sin", [128, 16 * 32])
    c_cvalid_d = din("c_cvalid", [128, 1024])
    c_overlap_d = din("c_overlap", [128, 32])
    c_selbias_d = din("c_selbias", [128, 8 * 32])
    c_E_d = din("c_E", [128, 16 * 128])
    c_tri_d = din("c_tri", [128, 3 * 128])
    c_wvalid_d = din("c_wvalid", [128, 40])
    c_invc_d = din("c_invc", [128, 4 * 16])
    out_d = nc.dram_tensor("out", [1024, 2048], F32, kind="ExternalOutput").ap()
    dbg_d = {}
    for name, shape in dbg:
        dbg_d[name] = nc.dram_tensor("dbg_" + name, list(shape), F32, kind="ExternalOutput").ap()

    S = Sched()
    st = ExitStack()
    A = Arena(nc, ARENA_BYTES)
    ps = [st.enter_context(nc.psum_tensor(f"ps{i}", [128, 512], F32)) for i in range(8)]
    psb = [p[:].bitcast(BF16) for p in ps]
    PB = [Buf(f"psum{i}", excl=True) for i in range(8)]

    def PE(fn, r=(), w=()):
        return S.op("pe", fn, r, w)

    def ACT(fn, r=(), w=()):
        return S.op("act", fn, r, w)

    def DVE(fn, r=(), w=()):
        return S.op("dve", fn, r, w)

    def POOL(fn, r=(), w=()):
        return S.op("pool", fn, r, w)

    out_bufs = []

    dbg_state = [ARENA_BYTES - 8192, 2048]
    B_dbgstg = S.buf("dbgstg")

    def dump(name, ap, buf, parts=128):
        if name not in dbg_d:
            return
        d = dbg_d[name]
        n = ap.shape[1]
        stg = A.view(dbg_state[0], [dbg_state[1]], F32)
        bl = buf if isinstance(buf, list) else [buf]
        for c0 in range(0, n, dbg_state[1]):
            c1 = min(n, c0 + dbg_state[1])
            ACT(lambda e, c0=c0, c1=c1: e.activation(out=stg[0:parts, 0:c1 - c0], in_=ap[:, c0:c1], func=AF.Copy), bl, [B_dbgstg])
            db = S.buf("dbgd_" + name)
            S.dma("sp", lambda e, c0=c0, c1=c1: e.dma_start(out=d[0:parts, c0:c1], in_=stg[0:parts, 0:c1 - c0]), "dbg", [B_dbgstg], [db])
            out_bufs.append(db)

    CONST_END = 26624
    CC = Carver(A, 0, CONST_END, "const")
    tri = CC.take([3, 128], BF16)
    ident = tri[:, 0, :]
    diag = tri[:, 1, :]
    upst = tri[:, 2, :]
    cvalid = CC.take([1024], BF16)
    Emat = CC.take([16, 128], BF16)
    selbias = CC.take([8, 32])
    invc = CC.take([4, 16])
    cos2 = CC.take([16, 32])
    sin2 = CC.take([16, 32])
    wvalid = CC.take([40])
    gsig = CC.take([8, 24])
    pscale = CC.take([8])
    gfull = CC.take([2048])
    stat = CC.take([16, 8])
    B_const = S.buf("const")
    B_gfull = S.buf("gfull")
    B_gsig = S.bufs(8, "gsig")

    cbufs = S.bufs(9, "cst")
    S.dma("sp", lambda e: e.dma_start(out=selbias.rearrange("p a b -> p (a b)"), in_=c_selbias_d), "cst", [], [cbufs[0]])
    S.dma("sp", lambda e: e.dma_start(out=invc.rearrange("p a b -> p (a b)"), in_=c_invc_d), "cst", [], [cbufs[1]])
    S.dma("sp", lambda e: e.dma_start(out=cos2.rearrange("p a b -> p (a b)"), in_=c_cos_d), "cst", [], [cbufs[2]])
    S.dma("sp", lambda e: e.dma_start(out=sin2.rearrange("p a b -> p (a b)"), in_=c_sin_d), "cst", [], [cbufs[3]])
    S.dma("sp", lambda e: e.dma_start(out=wvalid, in_=c_wvalid_d), "cst", [], [cbufs[4]])
    S.dma("sp", lambda e: e.dma_start(out=pscale, in_=pscale_d), "cst", [], [cbufs[5]])
    S.dma("pool", lambda e: e.dma_start(out=tri.rearrange("p a b -> p (a b)"), in_=c_tri_d), "cstp", [], [cbufs[6]])
    S.dma("pool", lambda e: e.dma_start(out=cvalid, in_=c_cvalid_d), "cstp", [], [cbufs[7]])
    S.dma("pool", lambda e: e.dma_start(out=Emat.rearrange("p a b -> p (a b)"), in_=c_E_d), "cstp", [], [cbufs[8]])
    POOL(lambda e: e.memset(stat[:, 15, 7:8], 0.0), cbufs, [B_const])

    def load_g(g_d):
        S.dma("sp", lambda e: e.dma_start(out=gfull, in_=g_d.partition_broadcast(128)), "gld", [], [B_gfull])

    PC = Carver(A, CONST_END, ARENA_BYTES, "persist")
    qT = PC.take([2, 8, 4, 128], BF16)
    kT = PC.take([3, 2, 2048], BF16)
    vcT = PC.take([2, 2048], BF16)
    v1 = PC.take([4, 16, 132], BF16)
    YMIX_OFF = PC.p
    ymixT = PC.take([16, 1024], BF16)
    uhalo = PC.take([8, 16])
    TRANS0 = PC.p
    B_qT = [[S.buf(f"qT{g}_{o}") for o in range(8)] for g in range(2)]
    B_kT = [[[S.buf(f"kT{k}_{g}_{l}") for l in range(16)] for g in range(2)] for k in range(3)]
    B_vcT = [[S.buf(f"vcT{g}_{l}") for l in range(16)] for g in range(2)]
    B_v1 = [[S.buf(f"v1{k}_{l}") for l in range(16)] for k in range(4)]
    B_ymix = [S.buf(f"ymix{c}") for c in range(16)]
    B_uhalo = S.buf("uhalo")
    B_v1init = S.buf("v1init")
    POOL(lambda e: e.memset(v1[:, :, :, 128:132], 1.0), [], [B_v1init])

    nstat = [0]

    def rms_to_bf16(src, B_src, dst_bf, B_dst, junk, B_junk):
        k = nstat[0] % 16
        nstat[0] += 1
        s = stat[:, k, :]
        sb = S.buf()
        ACT(lambda e: e.activation(out=junk, in_=src, func=AF.Square, accum_out=s[:, 0:1]), [B_src], [B_junk, sb])
        DVE(lambda e: e.tensor_scalar(out=s[:, 1:2], in0=s[:, 0:1], scalar1=1.0 / 2048, scalar2=1e-6, op0=ALU.mult, op1=ALU.add), [sb], [sb])
        ACT(lambda e: e.activation(out=s[:, 2:3], in_=s[:, 1:2], func=AF.Sqrt), [sb], [sb])
        DVE(lambda e: e.reciprocal(out=s[:, 3:4], in_=s[:, 2:3]), [sb], [sb])
        DVE(lambda e: e.scalar_tensor_tensor(out=dst_bf, in0=src, scalar=s[:, 3:4], in1=gfull, op0=ALU.mult, op1=ALU.mult),
            [sb, B_src, B_gfull], [B_dst])

    def transpose_tile(a_bf, B_a, aT_dst, B_aT, col0):
        for hb in range(2):
            bank = 6 + hb
            for j in range(8):
                dc = hb * 8 + j
                PE(lambda e, dc=dc, j=j, bank=bank: e.transpose(out=psb[bank][:, j * 128:(j + 1) * 128], in_=a_bf[:, dc * 128:(dc + 1) * 128], identity=ident),
                   [B_a, B_const], [PB[bank]])
            eng = ACT if hb == 0 else DVE
            if hb == 0:
                ACT(lambda e, bank=bank, hb=hb: e.activation(out=aT_dst[:, hb * 8:(hb + 1) * 8, col0:col0 + 128],
                                                           in_=psb[bank].rearrange("p (a b) -> p a b", a=8), func=AF.Copy), [PB[bank]], [B_aT])
            else:
                DVE(lambda e, bank=bank, hb=hb: e.tensor_copy(out=aT_dst[:, hb * 8:(hb + 1) * 8, col0:col0 + 128],
                                                            in_=psb[bank].rearrange("p (a b) -> p a b", a=8)), [PB[bank]], [B_aT])

    TC = Carver(A, TRANS0, ARENA_BYTES - 8192, "AB")
    aT = TC.take([16, 1024], BF16)
    wsl = [TC.take([16, 256], BF16) for _ in range(2)]
    AB_SHARED = TC.p
    xt = [TC.take([2048]) for _ in range(2)]
    abf = [TC.take([2048], BF16) for _ in range(2)]
    stage = [TC.take([2, 128], BF16) for _ in range(2)]
    rtmp = [TC.take([2, 32]) for _ in range(2)]
    rtmp2 = [TC.take([2, 32]) for _ in range(2)]
    wgates = TC.take([16, 24], BF16)
    B_aT = [S.buf(f"aT{i}") for i in range(8)]
    B_xt = S.bufs(2, "xt")
    B_abf = S.bufs(2, "abf")
    B_wsl = S.bufs(2, "wsl")
    B_stage = S.bufs(2, "stage")
    B_rtmp = S.bufs(2, "rtmp")
    B_wgates = S.buf("wgates")
    wcount = [0]
    scount = [0]

    load_g(g_in_d)
    S.dma("pool", lambda e: e.dma_start(out=wgates.rearrange("p a b -> p (a b)"), in_=wG_d), "wgl", [], [B_wgates])

    def load_w(src_ap):
        i = wcount[0] % 2
        wcount[0] += 1
        S.dma("pool", lambda e: e.dma_start(out=wsl[i].rearrange("p a b -> p (a b)"), in_=src_ap), f"wsl{i}", [], [B_wsl[i]])
        return i

    def norm_pass(pas):
        def s1(i):
            lt = pas * 8 + i
            sl = lt % 2
            S.dma("sp", lambda e, lt=lt, sl=sl: e.dma_start(out=xt[sl], in_=x_d[lt * 128:(lt + 1) * 128, :]), f"xt{sl}", [], [B_xt[sl]])
            rms_to_bf16(xt[sl], B_xt[sl], abf[sl], B_abf[sl], abf[sl], B_abf[sl])

        def s2(i):
            sl = (pas * 8 + i) % 2
            transpose_tile(abf[sl], B_abf[sl], aT, B_aT[i], i * 128)
        s1(0)
        for i in range(8):
            if i + 1 < 8:
                s1(i + 1)
            s2(i)

    def rope(pbank, stg, B_stg, lt, sl):
        src = ps[pbank][:, 0:256].rearrange("p (h d) -> p h d", h=2)
        c2 = cos2[:, lt, :].unsqueeze(1).to_broadcast([128, 2, 32])
        sA = sin2[:, lt, 0:16].unsqueeze(1).to_broadcast([128, 2, 16])
        sB = sin2[:, lt, 16:32].unsqueeze(1).to_broadcast([128, 2, 16])
        t1 = rtmp[sl]
        t2 = rtmp2[sl]
        DVE(lambda e: e.tensor_tensor(out=t1, in0=src[:, :, 0:32], in1=c2, op=ALU.mult), [PB[pbank], B_const], [B_rtmp[sl]])
        DVE(lambda e: e.tensor_tensor(out=t2[:, :, 0:16], in0=src[:, :, 16:32], in1=sA, op=ALU.mult), [PB[pbank], B_const], [B_rtmp[sl]])
        DVE(lambda e: e.tensor_tensor(out=t2[:, :, 16:32], in0=src[:, :, 0:16], in1=sB, op=ALU.mult), [PB[pbank], B_const], [B_rtmp[sl]])
        DVE(lambda e: e.tensor_tensor(out=stg[:, :, 0:32], in0=t1, in1=t2, op=ALU.add), [B_rtmp[sl]], [B_stg])

    def tok_half(hg, pas, i, wi, pbank):
        lt = pas * 8 + i
        for dc in range(16):
            PE(lambda e, dc=dc: e.matmul(ps[pbank][:, 0:256], lhsT=aT[:, dc, i * 128:(i + 1) * 128], rhs=wsl[wi][:, dc, :], start=(dc == 0), stop=(dc == 15)),
               [B_aT[i], B_wsl[wi]], [PB[pbank]])

        def evac():
            src = ps[pbank][:, 0:256].rearrange("p (h d) -> p h d", h=2)
            if hg in (7, 9):
                vk = 0 if hg == 7 else 2
                ACT(lambda e: e.activation(out=v1[:, vk:vk + 2, lt, 0:128], in_=src, func=AF.Copy), [PB[pbank], B_v1init], [B_v1[vk][lt], B_v1[vk + 1][lt]])
                return
            sl = scount[0] % 2
            scount[0] += 1
            stg = stage[sl]
            tb = 6 + (scount[0] % 2)
            ACT(lambda e: e.activation(out=stg, in_=src, func=AF.Copy), [PB[pbank]], [B_stage[sl]])
            if hg != 5:
                rope(pbank, stg, B_stage[sl], lt, sl)
            for r in range(2):
                PE(lambda e, r=r: e.transpose(out=psb[tb][:, r * 128:(r + 1) * 128], in_=stg[:, r, :], identity=ident), [B_stage[sl], B_const], [PB[tb]])
            pin = psb[tb][:, 0:256].rearrange("p (g t) -> p g t", g=2)
            if hg < 4:
                g, r0, ot = hg // 2, (hg % 2) * 2, i
                DVE(lambda e: e.tensor_copy(out=qT[:, g, ot, r0:r0 + 2, :], in_=pin), [PB[tb]], [B_qT[g][ot]])
            elif hg == 5:
                DVE(lambda e: e.tensor_copy(out=vcT[:, :, lt * 128:(lt + 1) * 128], in_=pin), [PB[tb]], [B_vcT[0][lt], B_vcT[1][lt]])
            else:
                kind = (hg - 4) // 2
                DVE(lambda e: e.tensor_copy(out=kT[:, kind, :, lt * 128:(lt + 1) * 128], in_=pin), [PB[tb]], [B_kT[kind][0][lt], B_kT[kind][1][lt]])
        return evac

    pbrot = [0]

    def next_pb():
        b = pbrot[0] % 6
        pbrot[0] += 1
        return b

    if stop_after == "C0":
        dump("gfull", gfull, [B_gfull, B_const])
        return _finish(nc, S, st, out_bufs)
    for pas in range(2):
        norm_pass(pas)
        if stop_after == "N0":
            dump("aT", aT.rearrange("p a b -> p (a b)"), B_aT)
            return _finish(nc, S, st, out_bufs)
        hgs = [4, 5, 6, 7, 8, 9] if pas == 0 else list(range(10))
        pend = []
        for hg in hgs:
            wi = load_w(wT_d[hg])
            for i in range(8):
                pend.append(tok_half(hg, pas, i, wi, next_pb()))
                if len(pend) > 2:
                    pend.pop(0)()
        for ev in pend:
            ev()
        if pas == 0:
            for ug in range(4):
                wi = load_w(wU_d[ug])
                for cc in range(2):
                    c = ug * 2 + cc
                    pb = next_pb()
                    for dc in range(16):
                        PE(lambda e, dc=dc, cc=cc, wi=wi, pb=pb: e.matmul(ps[pb][:, 0:16], lhsT=wsl[wi][:, dc, cc * 128:(cc + 1) * 128], rhs=aT[:, dc, 1008:1024],
                                                                        start=(dc == 0), stop=(dc == 15)), [B_aT[7], B_wsl[wi]], [PB[pb]])
                    ACT(lambda e, c=c, pb=pb: e.activation(out=uhalo[:, c, :], in_=ps[pb][:, 0:16], func=AF.Copy), [PB[pb]], [B_uhalo])
            if stop_after == "P0":
                dump("kT", kT.rearrange("p a b c -> p (a b c)"), [B_kT[k][g][l] for k in range(3) for g in range(2) for l in range(16)])
                return _finish(nc, S, st, out_bufs)
        else:
            for i in range(8):
                pb = next_pb()
                for dc in range(16):
                    PE(lambda e, dc=dc, i=i, pb=pb: e.matmul(ps[pb][:, 0:24], lhsT=aT[:, dc, i * 128:(i + 1) * 128], rhs=wgates[:, dc, :],
                                                             start=(dc == 0), stop=(dc == 15)), [B_aT[i], B_wgates], [PB[pb]])
                ACT(lambda e, i=i, pb=pb: e.activation(out=gsig[:, i, :], in_=ps[pb][:, 0:24], func=AF.Sigmoid), [PB[pb]], [B_gsig[i]])

    dump("kT", kT.rearrange("p a b c -> p (a b c)"), [B_kT[k][g][l] for k in range(3) for g in range(2) for l in range(16)])
    dump("qT", qT.rearrange("p a b c d -> p (a b c d)"), [B_qT[g][o] for g in range(2) for o in range(8)])
    dump("v1", v1.rearrange("p a b c -> p (a b c)"), [B_v1[k][l] for k in range(4) for l in range(16)] + [B_v1init])
    dump("vcT", vcT.rearrange("p a b -> p (a b)"), [B_vcT[g][l] for g in range(2) for l in range(16)])
    dump("gsig", gsig.rearrange("p a b -> p (a b)"), B_gsig)

    if stop_after == "P1":
        return _finish(nc, S, st, out_bufs)
    UC = Carver(A, AB_SHARED, ARENA_BYTES - 8192, "pool")
    ubuf = UC.take([1040])
    sbufA = UC.take([1040])
    sbufB = UC.take([1040])
    ptmp = UC.take([16])
    pooled = [UC.take([2, 1024], BF16) for _ in range(2)]
    poolw = UC.take([4, 2, 256], BF16)
    B_ubuf = S.buf("ubuf")
    B_sA = S.buf("sA")
    B_sB = S.buf("sB")
    B_ptmp = S.buf("ptmp")
    B_pooled = S.bufs(2, "pooled")
    B_poolw = S.buf("poolw")
    S.barrier([B_ubuf, B_sA, B_sB, B_ptmp, B_poolw] + B_pooled)
    S.dma("pool", lambda e: e.dma_start(out=poolw.rearrange("p a b c -> p (a b c)"), in_=poolw_d), "poolw", [], [B_poolw])
    for gi in range(4):
        w = 2 << gi
        wi = load_w(wU_d[gi])
        psl = gi % 2
        for cc in range(2):
            c = gi * 2 + cc
            pbs = [next_pb(), next_pb()]
            for th2 in range(2):
                for dc in range(16):
                    PE(lambda e, dc=dc, cc=cc, wi=wi, th2=th2, pb=pbs[th2]: e.matmul(ps[pb][:], lhsT=wsl[wi][:, dc, cc * 128:(cc + 1) * 128],
                                                                                   rhs=aT[:, dc, th2 * 512:(th2 + 1) * 512], start=(dc == 0), stop=(dc == 15)),
                       [B_aT[th2 * 4 + k] for k in range(4)] + [B_wsl[wi]], [PB[pbs[th2]]])
            ACT(lambda e, c=c: e.activation(out=ubuf[:, 0:16], in_=uhalo[:, c, :], func=AF.Copy), [B_uhalo], [B_ubuf])
            ACT(lambda e, pb=pbs[0]: e.activation(out=ubuf[:, 16:528], in_=ps[pb][:], func=AF.Copy), [PB[pbs[0]]], [B_ubuf])
            ACT(lambda e, pb=pbs[1]: e.activation(out=ubuf[:, 528:1040], in_=ps[pb][:], func=AF.Copy), [PB[pbs[1]]], [B_ubuf])
            cur, Bcur = ubuf, B_ubuf
            nxt = [(sbufA, B_sA), (sbufB, B_sB)]
            sh = 1
            k = 0
            while sh < w:
                dst, Bd = nxt[k % 2]
                DVE(lambda e, cur=cur, dst=dst, sh=sh: e.tensor_tensor(out=dst[:, sh:1040], in0=cur[:, sh:1040], in1=cur[:, 0:1040 - sh], op=ALU.add), [Bcur], [Bd])
                cur, Bcur = dst, Bd
                sh *= 2
                k += 1
            pl = pooled[psl][:, cc, :]
            DVE(lambda e, cur=cur, pl=pl, w=w: e.scalar_tensor_tensor(out=pl[:, 16:1024], in0=cur[:, 32:1040], scalar=1.0 / w, in1=ubuf[:, 32:1040], op0=ALU.mult, op1=ALU.subtract),
                [Bcur, B_ubuf], [B_pooled[psl]])
            DVE(lambda e, cur=cur, gi=gi: e.tensor_tensor(out=ptmp, in0=cur[:, 16:32], in1=invc[:, gi, :], op=ALU.mult), [Bcur, B_const], [B_ptmp])
            DVE(lambda e, pl=pl: e.tensor_tensor(out=pl[:, 0:16], in0=ptmp, in1=ubuf[:, 16:32], op=ALU.subtract), [B_ptmp, B_ubuf], [B_pooled[psl]])
        for dch in range(2):
            for th2 in range(2):
                pb = next_pb()
                for cc in range(2):
                    PE(lambda e, cc=cc, dch=dch, th2=th2, pb=pb, gi=gi, psl=psl: e.matmul(ps[pb][:], lhsT=poolw[:, gi, cc, dch * 128:(dch + 1) * 128],
                                                                                        rhs=pooled[psl][:, cc, th2 * 512:(th2 + 1) * 512], start=(cc == 0), stop=(cc == 1)),
                       [B_poolw, B_pooled[psl]], [PB[pb]])
                ch = gi * 2 + dch
                ACT(lambda e, ch=ch, th2=th2, pb=pb: e.activation(out=ymixT[:, ch, th2 * 512:(th2 + 1) * 512], in_=ps[pb][:], func=AF.Copy, scale=pscale[:, ch:ch + 1]),
                    [PB[pb], B_const], [B_ymix[ch]])
    dump("ypool", ymixT[:, 0:8, :].rearrange("p a b -> p (a b)"), B_ymix[0:8])

    if stop_after == "B":
        return _finish(nc, S, st, out_bufs)

    AC = Carver(A, TRANS0, ARENA_BYTES - 8192, "attn")
    w1 = [AC.take([32, 128], BF16) for _ in range(2)]
    w2 = [AC.take([128], BF16) for _ in range(2)]
    peT = [AC.take([32], BF16) for _ in range(2)]
    hb_ = AC.take([4, 128])
    hx = AC.take([4, 128])
    hy = AC.take([4, 128])
    cbias = AC.take([2, 2])
    geluT = AC.take([4, 128], BF16)
    kcmpT = AC.take([2, 128], BF16)
    vcmp1 = AC.take([2, 164], BF16)
    PT = [AC.take([4, 128], BF16) for _ in range(4)]
    oaccs = [AC.take([4, 128]) for _ in range(4)] * 4
    obf = [AC.take([4, 128], BF16) for _ in range(2)]
    etmp = [AC.take([2, 128]) for _ in range(2)]
    impb = [AC.take([32]) for _ in range(2)]
    impw = [AC.take([32]) for _ in range(2)]
    m16 = [AC.take([16]) for _ in range(2)]
    selbf = [AC.take([32], BF16) for _ in range(2)]
    negT = AC.take([2, 8, 4, 128], BF16)
    PTc = [AC.take([4, 128], BF16) for _ in range(2)]
    rs = [AC.take([3, 4]) for _ in range(2)]
    B_w1 = S.bufs(2, "w1")
    B_w2 = S.bufs(2, "w2")
    B_peT = S.bufs(2, "peT")
    B_h = S.bufs(4, "hid")
    B_cb = S.buf("cbias")
    B_gelu = S.bufs(4, "gelu")
    B_kcmp = S.bufs(2, "kcmp")
    B_vcmp = S.bufs(2, "vcmp")
    B_PT = S.bufs(4, "PT")
    B_PTc = S.bufs(2, "PTc")
    B_oaccs = S.bufs(4, "oacc") * 4
    B_obf = S.bufs(2, "obf")
    B_etmp = S.bufs(2, "etmp")
    B_imp = S.bufs(2, "imp")
    B_selbf = S.bufs(2, "selbf")
    B_negT = [[S.buf(f"negT{g}_{o}") for o in range(8)] for g in range(2)]
    B_rs = S.bufs(2, "rs")
    S.barrier(B_w1 + B_w2 + B_peT + B_h + [B_cb] + B_gelu + B_kcmp + B_vcmp + B_PT + B_PTc + B_oaccs + B_obf + B_etmp + B_imp + B_selbf
              + [b for l in B_negT for b in l] + B_rs)
    for kv, (w1d, w2d, ped) in enumerate([(w1k_d, w2k_d, pek_d), (w1v_d, w2v_d, pev_d)]):
        S.dma("pool", lambda e, kv=kv, w1d=w1d: e.dma_start(out=w1[kv].rearrange("p a b -> p (a b)"), in_=w1d), f"w1_{kv}", [], [B_w1[kv]])
        S.dma("pool", lambda e, kv=kv, w2d=w2d: e.dma_start(out=w2[kv], in_=w2d), f"w2_{kv}", [], [B_w2[kv]])
        S.dma("pool", lambda e, kv=kv, ped=ped: e.dma_start(out=peT[kv], in_=ped), f"pe_{kv}", [], [B_peT[kv]])
    POOL(lambda e: e.memset(vcmp1, 0.0), [], B_vcmp)
    POOL(lambda e: e.memset(negT.rearrange("p a b c d -> p (a b c d)"), 0.0), [], [b for l in B_negT for b in l])
    for g in range(2):
        S.dma("pool", lambda e, g=g: e.dma_start(out=vcmp1[:, g, 129:161], in_=c_overlap_d), f"ovl{g}", [], [B_vcmp[g]])
        POOL(lambda e, g=g: e.memset(vcmp1[0:127, g, 128:129], 1.0), [], [B_vcmp[g]])

    CB = 7
    for kv in range(2):
        for l in range(32):
            PE(lambda e, l=l, kv=kv: e.matmul(ps[CB][:, 500 + kv:501 + kv], lhsT=w1[kv][:, l, :], rhs=peT[kv][:, l:l + 1], start=(l == 0), stop=(l == 31)),
               [B_w1[kv], B_peT[kv]], [PB[CB]])
        DVE(lambda e, kv=kv: e.tensor_copy(out=cbias[:, kv, 0:1], in_=ps[CB][:, 500 + kv:501 + kv]), [PB[CB]], [B_cb])
    for kv in range(2):
        for g in range(2):
            idx = kv * 2 + g
            srcT = kT[:, 0, g, :] if kv == 0 else vcT[:, g, :]
            Bsrc = [B_kT[0][g][l] for l in range(16)] if kv == 0 else [B_vcT[g][l] for l in range(16)]
            for l in range(32):
                PE(lambda e, l=l, kv=kv, idx=idx, srcT=srcT: e.matmul(ps[CB][:, idx * 128:idx * 128 + 127], lhsT=w1[kv][:, l, :], rhs=srcT[:, l:l + 16 * 126 + 1:16],
                                                                  start=(l == 0), stop=(l == 31)), Bsrc + [B_w1[kv]], [PB[CB]])
            x_ = hb_[:, idx, 0:127]
            DVE(lambda e, idx=idx, kv=kv, x_=x_: e.tensor_scalar(out=x_, in0=ps[CB][:, idx * 128:idx * 128 + 127], scalar1=cbias[:, kv, 0:1], scalar2=None, op0=ALU.add),
                [PB[CB], B_cb], [B_h[idx]])
            DVE(lambda e, idx=idx, x_=x_: e.tensor_tensor(out=hx[:, idx, 0:127], in0=x_, in1=x_, op=ALU.mult), [B_h[idx]], [B_h[idx]])
            DVE(lambda e, idx=idx: e.tensor_scalar(out=hx[:, idx, 0:127], in0=hx[:, idx, 0:127], scalar1=0.044715, scalar2=1.0, op0=ALU.mult, op1=ALU.add), [B_h[idx]], [B_h[idx]])
            DVE(lambda e, idx=idx, x_=x_: e.tensor_tensor(out=hx[:, idx, 0:127], in0=hx[:, idx, 0:127], in1=x_, op=ALU.mult), [B_h[idx]], [B_h[idx]])
            ACT(lambda e, idx=idx: e.activation(out=hy[:, idx, 0:127], in_=hx[:, idx, 0:127], func=AF.Sigmoid, scale=1.5957691216057308), [B_h[idx]], [B_h[idx]])
            DVE(lambda e, idx=idx, x_=x_: e.tensor_tensor(out=geluT[:, idx, 0:127], in0=hy[:, idx, 0:127], in1=x_, op=ALU.mult), [B_h[idx]], [B_gelu[idx]])
    POOL(lambda e: e.memset(kcmpT, 0.0), [], B_kcmp)
    for g in range(2):
        PE(lambda e, g=g: e.matmul(ps[CB][:, 0:127], lhsT=w2[0], rhs=geluT[:, g, 0:127], start=True, stop=True), [B_w2[0], B_gelu[g]], [PB[CB]])
        DVE(lambda e, g=g: e.tensor_copy(out=kcmpT[:, g, 0:127], in_=ps[CB][:, 0:127]), [PB[CB]], [B_kcmp[g]])
        PE(lambda e, g=g: e.matmul(ps[CB][0:127, 128:256], lhsT=geluT[:, 2 + g, 0:127], rhs=w2[1], start=True, stop=True), [B_w2[1], B_gelu[2 + g]], [PB[CB]])
        DVE(lambda e, g=g: e.tensor_copy(out=vcmp1[0:127, g, 0:128], in_=ps[CB][0:127, 128:256]), [PB[CB]], [B_vcmp[g]])
    dump("kcmpT", kcmpT.rearrange("p a b -> p (a b)"), B_kcmp)
    dump("vcmp1", vcmp1.rearrange("p a b -> p (a b)"), B_vcmp)

    cnt = {"S": 0, "PT": 0, "O": 0, "rs": 0, "ob": 0}
    SB = [0, 1, 2]

    def evac_branch(obanks, br, g, ot, oa, B_oa, first, imp=None):
        k = cnt["rs"] % 2
        cnt["rs"] += 1
        r_ = rs[k]
        Br = B_rs[k]
        for bi, bank in enumerate(obanks):
            DVE(lambda e, bi=bi, bank=bank: e.tensor_scalar(out=r_[:, 0, 2 * bi:2 * bi + 2], in0=ps[bank][:, 128:385:256], scalar1=1e-30, scalar2=None, op0=ALU.add),
                [PB[bank]], [Br])
        DVE(lambda e: e.reciprocal(out=r_[:, 1, :], in_=r_[:, 0, :]), [Br], [Br])
        s0 = g * 12 + br
        DVE(lambda e: e.tensor_tensor(out=r_[:, 2, :], in0=r_[:, 1, :], in1=gsig[:, ot, s0:s0 + 10:3], op=ALU.mult), [Br, B_gsig[ot]], [Br])
        for r in range(4):
            bank = obanks[r // 2]
            c0 = (r % 2) * 256
            if first:
                DVE(lambda e, bank=bank, c0=c0, r=r: e.tensor_scalar(out=oa[:, r, :], in0=ps[bank][:, c0:c0 + 128], scalar1=r_[:, 2, r:r + 1], scalar2=None, op0=ALU.mult),
                    [PB[bank], Br], [B_oa])
            else:
                DVE(lambda e, bank=bank, c0=c0, r=r: e.scalar_tensor_tensor(out=oa[:, r, :], in0=ps[bank][:, c0:c0 + 128], scalar=r_[:, 2, r:r + 1], in1=oa[:, r, :],
                                                                          op0=ALU.mult, op1=ALU.add), [PB[bank], Br], [B_oa])
            if imp is not None:
                ib, Bi = imp
                if r == 0:
                    DVE(lambda e, bank=bank, c0=c0, r=r: e.tensor_scalar(out=ib, in0=ps[bank][:, c0 + 129:c0 + 161], scalar1=r_[:, 1, r:r + 1], scalar2=None, op0=ALU.mult),
                        [PB[bank], Br], [Bi])
                else:
                    DVE(lambda e, bank=bank, c0=c0, r=r: e.scalar_tensor_tensor(out=ib, in0=ps[bank][:, c0 + 129:c0 + 161], scalar=r_[:, 1, r:r + 1], in1=ib, op0=ALU.mult, op1=ALU.add),
                        [PB[bank], Br], [Bi])

    def osets():
        i = cnt["O"] % 2
        cnt["O"] += 1
        return (3, 4) if i == 0 else (5, 6)

    def score_exp(lhsT_ap, B_l, g, ot, bias=None):
        sb = SB[cnt["S"] % 3]
        cnt["S"] += 1
        pi = cnt["PT"] % 4
        cnt["PT"] += 1
        rhs_q = qT[:, g, ot].rearrange("p r t -> p (r t)")
        PE(lambda e: e.matmul(ps[sb][:], lhsT=lhsT_ap, rhs=rhs_q, start=True, stop=(bias is None)), B_l + [B_qT[g][ot]], [PB[sb]])
        if bias is not None:
            kt = bias
            PE(lambda e: e.matmul(ps[sb][:], lhsT=Emat[:, kt, :], rhs=negT[:, g, ot].rearrange("p r t -> p (r t)"), start=False, stop=True),
               [B_const, B_negT[g][ot]], [PB[sb]])
        ACT(lambda e: e.activation(out=PT[pi].rearrange("p r t -> p (r t)"), in_=ps[sb][:], func=AF.Exp, scale=SCALE), [PB[sb]], [B_PT[pi]])
        return pi

    def pv(pi, rhs_ap, B_r, width, obanks, first, last):
        for r in range(4):
            bank = obanks[r // 2]
            c0 = (r % 2) * 256
            PE(lambda e, r=r, bank=bank, c0=c0: e.matmul(ps[bank][:, c0:c0 + width], lhsT=PT[pi][:, r, :], rhs=rhs_ap, start=(first and r % 2 == 0), stop=last,
                                                         skip_group_check=True),
               [B_PT[pi]] + B_r, [PB[bank]])

    def bcast4(m):
        return m.unsqueeze(1).to_broadcast([128, 4, 128])

    deferred = []

    def run_pipeline(steps, la):
        stt = {}
        n = len(steps)
        for i in range(n + la):
            if i < n:
                if steps[i][2] is not None:
                    steps[i][2]()
                stt[i] = steps[i][0]()
            j = i - la
            if j >= 0:
                steps[j][1](stt.pop(j))
            for dfr in list(deferred):
                dfr[0] -= 1
                if dfr[0] <= 0:
                    deferred.remove(dfr)
                    dfr[1]()

    def c1_stages(g, ot):
        oi = g * 8 + ot
        ii = oi % 2
        pt = PTc[ii]
        Bpt = B_PTc[ii]
        hold = {}

        def s0():
            sb = SB[cnt["S"] % 3]
            cnt["S"] += 1
            PE(lambda e: e.matmul(ps[sb][:], lhsT=kcmpT[:, g, :], rhs=qT[:, g, ot].rearrange("p r t -> p (r t)"), start=True, stop=True), [B_kcmp[g], B_qT[g][ot]], [PB[sb]])
            ACT(lambda e: e.activation(out=pt.rearrange("p r t -> p (r t)"), in_=ps[sb][:], func=AF.Exp, scale=SCALE), [PB[sb]], [Bpt])
            DVE(lambda e: e.tensor_tensor(out=pt, in0=pt, in1=bcast4(cvalid[:, ot * 128:(ot + 1) * 128]), op=ALU.mult), [Bpt, B_const], [Bpt])

        def s1():
            ob = (5, 6)
            for r in range(4):
                bank = ob[r // 2]
                c0 = (r % 2) * 256
                PE(lambda e, r=r, bank=bank, c0=c0: e.matmul(ps[bank][:, c0:c0 + 161], lhsT=pt[:, r, :], rhs=vcmp1[:, g, 0:161], start=(r % 2 == 0), stop=True, skip_group_check=True),
                   [Bpt, B_vcmp[g]], [PB[bank]])
            evac_branch(ob, 0, g, ot, oaccs[oi], B_oaccs[oi], True, imp=(impb[ii], B_imp[ii]))

        def s2():
            DVE(lambda e: e.tensor_tensor(out=impb[ii], in0=impb[ii], in1=selbias[:, ot, :], op=ALU.add), [B_imp[ii], B_const], [B_imp[ii]])
            DVE(lambda e: e.max(out=m16[ii][:, 0:8], in_=impb[ii]), [B_imp[ii]], [B_imp[ii]])
            DVE(lambda e: e.match_replace(out=impw[ii], in_to_replace=m16[ii][:, 0:8], in_values=impb[ii], imm_value=-3.0e30), [B_imp[ii]], [B_imp[ii]])
            DVE(lambda e: e.max(out=m16[ii][:, 8:16], in_=impw[ii]), [B_imp[ii]], [B_imp[ii]])
            DVE(lambda e: e.tensor_scalar(out=m16[ii][:, 15:16], in0=m16[ii][:, 15:16], scalar1=-1.0e29, scalar2=None, op0=ALU.max), [B_imp[ii]], [B_imp[ii]])
            DVE(lambda e: e.tensor_scalar(out=impw[ii], in0=impb[ii], scalar1=m16[ii][:, 15:16], scalar2=None, op0=ALU.is_ge), [B_imp[ii]], [B_imp[ii]])
            DVE(lambda e: e.tensor_scalar(out=selbf[ii], in0=impw[ii], scalar1=-1.0, scalar2=30000.0, op0=ALU.add, op1=ALU.mult), [B_imp[ii]], [B_selbf[ii]])

        def s3():
            PE(lambda e: e.transpose(out=psb[7][0:32, 0:128], in_=selbf[ii], identity=ident), [B_selbf[ii], B_const], [PB[7]])
            DVE(lambda e: e.tensor_copy(out=negT[0:32, g, ot], in_=psb[7][0:32, 0:128].unsqueeze(1).to_broadcast([32, 4, 128])), [PB[7]], [B_negT[g][ot]])
        return [s0, s1, s2, s3]

    chains = [(g, ot) for g in range(2) for ot in range(8)]
    c1s = [c1_stages(g, ot) for (g, ot) in chains]
    for f in c1s[0]:
        f()

    c2 = []
    for g in range(2):
        for ot in range(8):
            ltq = 8 + ot
            oi = g * 8 + ot
            lstep = [0]
            for br in (1, 2):
                nst = ltq + 1 if br == 1 else 5
                holder = {}
                for si in range(nst):
                    kt = si if br == 1 else ltq - 4 + si

                    def stepA(g=g, ot=ot, br=br, si=si, kt=kt, ltq=ltq):
                        if br == 1:
                            pi = score_exp(kT[:, 1, g, kt * 128:(kt + 1) * 128], [B_kT[1][g][kt]], g, ot, bias=kt)
                            if kt == ltq:
                                DVE(lambda e: e.tensor_tensor(out=PT[pi], in0=PT[pi], in1=bcast4(diag), op=ALU.mult), [B_PT[pi], B_const], [B_PT[pi]])
                        else:
                            pi = score_exp(kT[:, 2, g, kt * 128:(kt + 1) * 128], [B_kT[2][g][kt]], g, ot)
                            wv = wvalid[:, ot * 5 + si:ot * 5 + si + 1]
                            if si == 0:
                                DVE(lambda e: e.scalar_tensor_tensor(out=PT[pi], in0=PT[pi], scalar=wv, in1=bcast4(upst), op0=ALU.mult, op1=ALU.mult), [B_PT[pi], B_const], [B_PT[pi]])
                            elif si == 4:
                                DVE(lambda e: e.tensor_tensor(out=PT[pi], in0=PT[pi], in1=bcast4(diag), op=ALU.mult), [B_PT[pi], B_const], [B_PT[pi]])
                            elif kt < 8:
                                DVE(lambda e: e.tensor_scalar(out=PT[pi], in0=PT[pi], scalar1=wv, scalar2=None, op0=ALU.mult), [B_PT[pi], B_const], [B_PT[pi]])
                        return pi

                    def stepB(pi, g=g, ot=ot, br=br, si=si, kt=kt, nst=nst, oi=oi, holder=holder):
                        ob = (3, 4) if br == 1 else (5, 6)
                        vk = (0 if br == 1 else 2) + g
                        pv(pi, v1[:, vk, kt, 0:129], [B_v1[vk][kt], B_v1init], 129, ob, si == 0, si == nst - 1)
                        if si == nst - 1:
                            evac_branch(ob, br, g, ot, oaccs[oi], B_oaccs[oi], False)
                            if br == 2:
                                def combine(g=g, ot=ot, oi=oi):
                                    k = cnt["ob"] % 2
                                    cnt["ob"] += 1
                                    ACT(lambda e: e.activation(out=obf[k], in_=oaccs[oi], func=AF.Copy), [B_oaccs[oi]], [B_obf[k]])

                                    def tr():
                                        for r in range(4):
                                            PE(lambda e, r=r: e.transpose(out=psb[7][:, 256 + r * 128:256 + (r + 1) * 128], in_=obf[k][:, r, :], identity=ident), [B_obf[k], B_const], [PB[7]])
                                        DVE(lambda e: e.tensor_copy(out=ymixT[:, 8 + g * 4:12 + g * 4, ot * 128:(ot + 1) * 128], in_=psb[7][:, 256:768].rearrange("p (r t) -> p r t", r=4)),
                                            [PB[7]], [B_ymix[8 + g * 4 + r] for r in range(4)])
                                    deferred.append([6, tr])
                                deferred.append([4, combine])
                    pre = None
                    if oi + 1 < 16 and lstep[0] in (3, 5, 7, 10):
                        pre = c1s[oi + 1][(3, 5, 7, 10).index(lstep[0])]
                    lstep[0] += 1
                    c2.append((stepA, stepB, pre))
    run_pipeline(c2, 2)
    while deferred:
        deferred.pop(0)[1]()
    dump("ynsa", ymixT[:, 8:16, :].rearrange("p a b -> p (a b)"), B_ymix[8:16])

    if stop_after == "C":
        return _finish(nc, S, st, out_bufs)

    hres = A.view(CONST_END, [8, 2048], F32)
    assert CONST_END + 65536 <= YMIX_OFF
    D_END = CONST_END + 65536
    DC = Carver(A, TRANS0, ARENA_BYTES - 8192, "D")
    wo = [DC.take([16, 512], BF16) for _ in range(2)]
    B_h = [[S.buf(f"h{ot}_{c}") for c in range(4)] for ot in range(8)]
    B_wo = S.bufs(2, "wo")
    S.barrier([b for l in B_h for b in l] + B_wo)
    for ot in range(8):
        S.dma("sp", lambda e, ot=ot: e.dma_start(out=hres[:, ot, :], in_=x_d[1024 + ot * 128:1024 + (ot + 1) * 128, :]), f"hres{ot}", [], B_h[ot])
    pb8 = [0]

    def npb():
        b = pb8[0] % 8
        pb8[0] += 1
        return b

    for dmc in range(4):
        sl = dmc % 2
        S.dma("pool", lambda e, dmc=dmc, sl=sl: e.dma_start(out=wo[sl].rearrange("p a b -> p (a b)"), in_=wout_d[dmc]), f"wo{sl}", [], [B_wo[sl]])
        for ot in range(8):
            pb = npb()
            for c in range(16):
                PE(lambda e, c=c, ot=ot, sl=sl, pb=pb: e.matmul(ps[pb][:], lhsT=ymixT[:, c, ot * 128:(ot + 1) * 128], rhs=wo[sl][:, c, :], start=(c == 0), stop=(c == 15)),
                   [B_ymix[c], B_wo[sl]], [PB[pb]])
            DVE(lambda e, ot=ot, dmc=dmc, pb=pb: e.tensor_tensor(out=hres[:, ot, dmc * 512:(dmc + 1) * 512], in0=ps[pb][:], in1=hres[:, ot, dmc * 512:(dmc + 1) * 512], op=ALU.add),
                [PB[pb], B_h[ot][dmc]], [B_h[ot][dmc]])
    dump("h1", hres.rearrange("p a b -> p (a b)"), [b for l in B_h for b in l])
    if stop_after == "D":
        return _finish(nc, S, st, out_bufs)

    EC = Carver(A, D_END, ARENA_BYTES, "E")
    a2T = EC.take([16, 1024], BF16)
    E_A2T_END = EC.p
    hT = EC.take([12, 1024], BF16)
    wgu = [EC.take([16, 256], BF16) for _ in range(4)]
    wdn = [EC.take([12, 512], BF16) for _ in range(2)]
    sgt = [EC.take([512]) for _ in range(2)]
    EC2 = Carver(A, 768, 768 + 2048 + 4096 + 1024 + 256 + 2048 + 2048, "Ealias")
    abf2 = [EC2.take([2048], BF16), EC2.take([2048], BF16)]
    dbg_state[0] = EC2.p
    dbg_state[1] = 512
    S.barrier([B_dbgstg])
    B_a2T = [S.buf(f"a2T{i}") for i in range(8)]
    B_hT = [[S.buf(f"hT{c}_{t}") for t in range(2)] for c in range(12)]
    B_wgu = S.bufs(4, "wgu")
    B_wdn = S.bufs(2, "wdn")
    B_sgt = S.bufs(2, "sgt")
    B_abf2 = S.bufs(2, "abf2")
    S.barrier(B_a2T + [b for l in B_hT for b in l] + B_wgu + B_wdn + B_sgt + B_abf2)

    def rms_h(ot, dst, B_dst):
        k = nstat[0] % 16
        nstat[0] += 1
        s_ = stat[:, k, :]
        sb = S.buf()
        src = hres[:, ot, :]
        ACT(lambda e: e.activation(out=dst, in_=src, func=AF.Square, accum_out=s_[:, 0:1]), B_h[ot], [B_dst, sb])
        DVE(lambda e: e.tensor_scalar(out=s_[:, 1:2], in0=s_[:, 0:1], scalar1=1.0 / 2048, scalar2=1e-6, op0=ALU.mult, op1=ALU.add), [sb], [sb])
        ACT(lambda e: e.activation(out=s_[:, 2:3], in_=s_[:, 1:2], func=AF.Sqrt), [sb], [sb])
        DVE(lambda e: e.reciprocal(out=s_[:, 3:4], in_=s_[:, 2:3]), [sb], [sb])
        DVE(lambda e: e.scalar_tensor_tensor(out=dst, in0=src, scalar=s_[:, 3:4], in1=gfull, op0=ALU.mult, op1=ALU.mult),
            [sb, B_gfull] + B_h[ot], [B_dst])

    def norm_hres(g_d, dstT, B_dstT):
        load_g(g_d)
        rms_h(0, abf2[0], B_abf2[0])
        for ot in range(8):
            sl = ot % 2
            if ot + 1 < 8:
                rms_h(ot + 1, abf2[1 - sl], B_abf2[1 - sl])
            transpose_tile(abf2[sl], B_abf2[sl], dstT, B_dstT[ot], ot * 128)

    norm_hres(g_ffn_d, a2T, B_a2T)
    gcount = [0]
    dcount = [0]
    for fq, (c0, c1) in enumerate(NQ_FF):
        ncq = c1 - c0
        for grp in range(c0 // 2, c1 // 2):
            sl = (gcount[0] % 2) * 2
            gcount[0] += 1
            S.dma("pool", lambda e, grp=grp, sl=sl: e.dma_start(out=wgu[sl].rearrange("p a b -> p (a b)"), in_=wg_d[grp]), f"wgu{sl}", [], [B_wgu[sl]])
            S.dma("pool", lambda e, grp=grp, sl=sl: e.dma_start(out=wgu[sl + 1].rearrange("p a b -> p (a b)"), in_=wu_d[grp]), f"wgu{sl + 1}", [], [B_wgu[sl + 1]])
            for half in range(2):
                ci = grp * 2 + half - c0
                for th2 in range(2):
                    pg, pu = npb(), npb()
                    for dc in range(16):
                        PE(lambda e, dc=dc, sl=sl, half=half, th2=th2, pg=pg: e.matmul(ps[pg][:], lhsT=wgu[sl][:, dc, half * 128:(half + 1) * 128], rhs=a2T[:, dc, th2 * 512:(th2 + 1) * 512],
                                                                                     start=(dc == 0), stop=(dc == 15)), [B_wgu[sl]] + B_a2T[th2 * 4:th2 * 4 + 4], [PB[pg]])
                    for dc in range(16):
                        PE(lambda e, dc=dc, sl=sl, half=half, th2=th2, pu=pu: e.matmul(ps[pu][:], lhsT=wgu[sl + 1][:, dc, half * 128:(half + 1) * 128], rhs=a2T[:, dc, th2 * 512:(th2 + 1) * 512],
                                                                                     start=(dc == 0), stop=(dc == 15)), [B_wgu[sl + 1]] + B_a2T[th2 * 4:th2 * 4 + 4], [PB[pu]])
                    ss = th2
                    ACT(lambda e, pg=pg, ss=ss: e.activation(out=sgt[ss], in_=ps[pg][:], func=AF.Silu), [PB[pg]], [B_sgt[ss]])
                    DVE(lambda e, pu=pu, ss=ss, ci=ci, th2=th2: e.tensor_tensor(out=hT[:, ci, th2 * 512:(th2 + 1) * 512], in0=ps[pu][:], in1=sgt[ss], op=ALU.mult),
                        [PB[pu], B_sgt[ss]], [B_hT[ci][th2]])
        for dmc in range(4):
            sl = dcount[0] % 2
            dcount[0] += 1
            S.dma("pool", lambda e, fq=fq, dmc=dmc, sl=sl: e.dma_start(out=wdn[sl].rearrange("p a b -> p (a b)"), in_=wd_d[fq * 4 + dmc]), f"wdn{sl}", [], [B_wdn[sl]])
            for ot in range(8):
                pb = npb()
                for ci in range(ncq):
                    PE(lambda e, ci=ci, ot=ot, sl=sl, pb=pb: e.matmul(ps[pb][:], lhsT=hT[:, ci, ot * 128:(ot + 1) * 128], rhs=wdn[sl][:, ci, :], start=(ci == 0), stop=(ci == ncq - 1)),
                       [B_hT[ci][ot // 4], B_wdn[sl]], [PB[pb]])
                DVE(lambda e, ot=ot, dmc=dmc, pb=pb: e.tensor_tensor(out=hres[:, ot, dmc * 512:(dmc + 1) * 512], in0=ps[pb][:], in1=hres[:, ot, dmc * 512:(dmc + 1) * 512], op=ALU.add),
                    [PB[pb], B_h[ot][dmc]], [B_h[ot][dmc]])
    dump("h2", hres.rearrange("p a b -> p (a b)"), [b for l in B_h for b in l])
    if stop_after == "E":
        return _finish(nc, S, st, out_bufs)

    FC = Carver(A, E_A2T_END, ARENA_BYTES - 8192, "F")
    a3T = a2T
    B_a3T = B_a2T
    wpg = [FC.take([16, 512], BF16) for _ in range(2)]
    wpp = [FC.take([2, 512], BF16) for _ in range(2)]
    pT = FC.take([2, 1024], BF16)
    pf = [FC.take([256]) for _ in range(2)]
    pbf = [FC.take([256], BF16) for _ in range(2)]
    gt = [FC.take([512]) for _ in range(2)]
    outt = [FC.take([2048]) for _ in range(2)]
    B_wpg = S.bufs(2, "wpg")
    B_wpp = S.bufs(2, "wpp")
    B_pT = S.bufs(8, "pT")
    B_pf = S.bufs(2, "pf")
    B_pbf = S.bufs(2, "pbf")
    B_gt = S.bufs(2, "gt")
    B_outt = S.bufs(2, "outt")
    S.barrier(B_wpg + B_wpp + B_pT + B_pf + B_pbf + B_gt + B_outt)
    norm_hres(g_ple_d, a3T, B_a3T)
    for ot in range(8):
        sl = ot % 2
        S.dma("sp", lambda e, ot=ot, sl=sl: e.dma_start(out=pf[sl], in_=p_d[ot * 128:(ot + 1) * 128, :]), f"pf{sl}", [], [B_pf[sl]])
        ACT(lambda e, sl=sl: e.activation(out=pbf[sl], in_=pf[sl], func=AF.Copy), [B_pf[sl]], [B_pbf[sl]])
        for c2 in range(2):
            PE(lambda e, c2=c2, sl=sl: e.transpose(out=psb[7][:, c2 * 128:(c2 + 1) * 128], in_=pbf[sl][:, c2 * 128:(c2 + 1) * 128], identity=ident), [B_pbf[sl], B_const], [PB[7]])
        DVE(lambda e, ot=ot: e.tensor_copy(out=pT[:, :, ot * 128:(ot + 1) * 128], in_=psb[7][:, 0:256].rearrange("p (a b) -> p a b", a=2)), [PB[7]], [B_pT[ot]])
    pb7 = [0]

    def npb7():
        b = pb7[0] % 7
        pb7[0] += 1
        return b

    for dmc in range(4):
        sl = dmc % 2
        S.dma("pool", lambda e, dmc=dmc, sl=sl: e.dma_start(out=wpg[sl].rearrange("p a b -> p (a b)"), in_=wpg_d[dmc]), f"wpg{sl}", [], [B_wpg[sl]])
        S.dma("pool", lambda e, dmc=dmc, sl=sl: e.dma_start(out=wpp[sl].rearrange("p a b -> p (a b)"), in_=wpp_d[dmc]), f"wpp{sl}", [], [B_wpp[sl]])
        for ot in range(8):
            p1, p2 = npb7(), npb7()
            for dc in range(16):
                PE(lambda e, dc=dc, ot=ot, sl=sl, p1=p1: e.matmul(ps[p1][:], lhsT=a3T[:, dc, ot * 128:(ot + 1) * 128], rhs=wpg[sl][:, dc, :], start=(dc == 0), stop=(dc == 15)),
                   [B_a3T[ot], B_wpg[sl]], [PB[p1]])
            for c2 in range(2):
                PE(lambda e, c2=c2, ot=ot, sl=sl, p2=p2: e.matmul(ps[p2][:], lhsT=pT[:, c2, ot * 128:(ot + 1) * 128], rhs=wpp[sl][:, c2, :], start=(c2 == 0), stop=(c2 == 1)),
                   [B_pT[ot], B_wpp[sl]], [PB[p2]])
            gs = (dmc * 8 + ot) % 2
            ACT(lambda e, p1=p1, gs=gs: e.activation(out=gt[gs], in_=ps[p1][:], func=AF.Sigmoid), [PB[p1]], [B_gt[gs]])
            DVE(lambda e, p2=p2, gs=gs: e.tensor_tensor(out=gt[gs], in0=ps[p2][:], in1=gt[gs], op=ALU.mult), [PB[p2], B_gt[gs]], [B_gt[gs]])
            DVE(lambda e, ot=ot, dmc=dmc, gs=gs: e.tensor_tensor(out=hres[:, ot, dmc * 512:(dmc + 1) * 512], in0=gt[gs], in1=hres[:, ot, dmc * 512:(dmc + 1) * 512], op=ALU.add),
                [B_gt[gs], B_h[ot][dmc]], [B_h[ot][dmc]])
    load_g(g_fin_d)
    for ot in range(8):
        sl = ot % 2
        rms_h(ot, outt[sl], B_outt[sl])
        ob = S.buf()
        S.dma("sp", lambda e, ot=ot, sl=sl: e.dma_start(out=out_d[ot * 128:(ot + 1) * 128, :], in_=outt[sl]), f"ost{sl}", [B_outt[sl]], [ob])
        out_bufs.append(ob)
    return _finish(nc, S, st, out_bufs)


def _finish(nc, S, st, out_bufs):
    S.op("sp", lambda e: e.nop(), out_bufs, [])
    S.emit(nc, st)
    st.close()
    return nc, S


def _tile_cols(W, c0, n):
    return np.ascontiguousarray(W[:, c0:c0 + n].reshape(16, 128, n).transpose(1, 0, 2).reshape(128, 16 * n))


def _consts(th):
    f32 = np.float32
    L = np.arange(2048)
    tg = L - 1024 + 1024 * th
    pos = np.maximum(tg, 0).astype(f32)
    inv_freq = (f32(500000.0) ** (-np.arange(0, 32, 2, dtype=f32) / f32(32))).astype(f32)
    ang = (pos[:, None] * inv_freq[None, :]).astype(f32)
    cos, sin = np.cos(ang).astype(f32), np.sin(ang).astype(f32)
    cos2 = np.concatenate([cos, cos], 1)
    sin2 = np.concatenate([-sin, sin], 1)

    def tokmaj(a):
        w = a.shape[1]
        return np.ascontiguousarray(a.reshape(16, 128, w).transpose(1, 0, 2).reshape(128, 16 * w))

    n = np.arange(128)
    Lo = 1024 + np.arange(1024)
    cvalid = ((16 * n[:, None] + 31) <= Lo[None, :]) & ((th == 1) | (n[:, None] >= 64))
    s = np.arange(32)
    overlap = ((16 * n[:, None]) < (64 * s[None, :] + 64)) & ((16 * n[:, None] + 32) > 64 * s[None, :])
    overlap = overlap & (n[:, None] < 127)
    cur = Lo // 64
    valid = (s[None, :] <= cur[:, None]) & ((th == 1) | (s[None, :] >= 16))
    forced = (s[None, :] == cur[:, None]) | (s[None, :] == cur[:, None] - 1) | (s[None, :] == 16 * (1 - th))
    selbias = np.where(valid, np.where(forced, BIG, 0.0), -BIG).astype(f32)
    selbias = np.ascontiguousarray(selbias.reshape(8, 128, 32).transpose(1, 0, 2).reshape(128, 256))
    k = np.arange(128)
    E = np.zeros((128, 16, 128), f32)
    for kt in range(16):
        E[2 * kt + k // 64, kt, k] = 1.0
    ident = np.eye(128, dtype=f32)
    diag = (k[:, None] <= k[None, :]).astype(f32)
    upst = (k[:, None] > k[None, :]).astype(f32)
    tri = np.concatenate([ident, diag, upst], 1)
    wvalid = np.zeros((8, 5), f32)
    for ot in range(8):
        for off in range(5):
            kt = 8 + ot - 4 + off
            wvalid[ot, off] = 1.0 if (th == 1 or kt >= 8) else 0.0
    wvalid = np.broadcast_to(wvalid.reshape(1, 40), (128, 40))
    invc = np.zeros((4, 16), f32)
    for gi in range(4):
        w = 2 << gi
        for i in range(16):
            invc[gi, i] = 1.0 / w if th == 1 else 1.0 / min(i + 1, w)
    invc = np.broadcast_to(invc.reshape(1, 64), (128, 64))
    c = lambda a: np.ascontiguousarray(a, dtype=f32)
    return {
        "c_cos": tokmaj(cos2), "c_sin": tokmaj(sin2), "c_cvalid": c(cvalid), "c_overlap": c(overlap), "c_selbias": c(selbias),
        "c_E": c(E.reshape(128, 2048)), "c_tri": c(tri), "c_wvalid": c(wvalid), "c_invc": c(invc),
    }


def _prep_shared(w_in, pool_w, pool_scale, cmp_k_pe, cmp_k_w1, cmp_k_w2, cmp_v_pe, cmp_v_w1, cmp_v_w2, w_out,
                 w_gate, w_up, w_down, w_ple_gate, w_ple_proj, in_norm_g, ffn_norm_g, ple_norm_g, final_norm_g):
    c = lambda a: np.ascontiguousarray(a, dtype=np.float32)
    d = {}
    w_in = w_in[0]
    d["wU"] = np.stack([_tile_cols(w_in, i * 256, 256) for i in range(4)])
    d["wT"] = np.stack([_tile_cols(w_in, 1024 + i * 256, 256) for i in range(10)])
    d["wG"] = _tile_cols(w_in, 3584, 24)
    d["poolw"] = c(pool_w[0].reshape(4, 2, 128, 256).transpose(2, 0, 1, 3).reshape(128, 2048))
    d["pscale"] = c(pool_scale[0].reshape(8, 128).T)
    d["w1k"] = c(cmp_k_w1[0].reshape(32, 128, 128).transpose(1, 0, 2).reshape(128, 4096))
    d["w1v"] = c(cmp_v_w1[0].reshape(32, 128, 128).transpose(1, 0, 2).reshape(128, 4096))
    d["w2k"] = c(cmp_k_w2[0])
    d["w2v"] = c(cmp_v_w2[0])
    d["pekT"] = c(cmp_k_pe[0].T)
    d["pevT"] = c(cmp_v_pe[0].T)
    d["wout"] = np.stack([_tile_cols(w_out[0], i * 512, 512) for i in range(4)])
    d["wgate"] = np.stack([_tile_cols(w_gate[0], i * 256, 256) for i in range(22)])
    d["wup"] = np.stack([_tile_cols(w_up[0], i * 256, 256) for i in range(22)])
    wd = np.zeros((16, 128, 12 * 512), np.float32)
    for fq, (c0, c1) in enumerate(NQ_FF):
        for dmc in range(4):
            blk = w_down[0][c0 * 128:c1 * 128, dmc * 512:(dmc + 1) * 512].reshape(c1 - c0, 128, 512).transpose(1, 0, 2)
            wd[fq * 4 + dmc, :, :(c1 - c0) * 512] = blk.reshape(128, -1)
    d["wdown"] = wd
    d["wpg"] = np.stack([_tile_cols(w_ple_gate[0], i * 512, 512) for i in range(4)])
    wpp = w_ple_proj[0]
    d["wpp"] = np.stack([c(wpp[:, i * 512:(i + 1) * 512].reshape(2, 128, 512).transpose(1, 0, 2).reshape(128, 1024)) for i in range(4)])
    d["g_in"] = c(in_norm_g[0])
    d["g_ffn"] = c(ffn_norm_g[0])
    d["g_ple"] = c(ple_norm_g[0])
    d["g_fin"] = c(final_norm_g)
    return d


def make_in_maps(x, p, **w):
    shared = _prep_shared(**w)
    cst = [_consts(0), _consts(1)]
    in_maps = []
    for b in range(4):
        for th in range(2):
            m = dict(shared)
            m.update(cst[th])
            if th == 1:
                xl = x[b]
            else:
                xl = np.concatenate([np.zeros((1024, 2048), np.float32), x[b, :1024]], 0)
            m["x"] = np.ascontiguousarray(xl, dtype=np.float32)
            m["p"] = np.ascontiguousarray(p[0, b, th * 1024:(th + 1) * 1024], dtype=np.float32)
            in_maps.append(m)
    return in_maps


_NC_CACHE = {}


def kernel(x, p, in_norm_g, w_in, pool_w, pool_scale, cmp_k_pe, cmp_k_w1, cmp_k_w2, cmp_v_pe, cmp_v_w1, cmp_v_w2,
           w_out, ffn_norm_g, w_gate, w_up, w_down, ple_norm_g, w_ple_gate, w_ple_proj, final_norm_g):
    args = dict(locals())
    args = {k: np.asarray(v) for k, v in args.items()}
    x = args.pop("x")
    p = args.pop("p")
    in_maps = make_in_maps(x, p, **args)
    if "nc" not in _NC_CACHE:
        _NC_CACHE["nc"] = build()[0]
    nc = _NC_CACHE["nc"]
    res = run_bass_kernel_spmd(nc, in_maps, core_ids=list(range(8)))
    out = np.zeros((4, 2048, 2048), np.float32)
    for b in range(4):
        for th in range(2):
            out[b, th * 1024:(th + 1) * 1024] = res.results[b * 2 + th]["out"]
    return out
```

```python
import numpy as np
from contextlib import ExitStack
import concourse.bass as bass
import concourse.mybir as mybir
from concourse.bass_utils import run_bass_kernel_spmd

F32 = mybir.dt.float32
BF16 = mybir.dt.bfloat16
AF = mybir.ActivationFunctionType
ALU = mybir.AluOpType
AX = mybir.AxisListType

ENGS = ("pe", "act", "dve", "pool", "sp")
BIG = 1.0e30
SCALE = 128.0 ** -0.5


class Buf:
    __slots__ = ("name", "last_w", "readers", "excl")

    def __init__(self, name, excl=False):
        self.name = name
        self.last_w = None
        self.readers = []
        self.excl = excl


class Op:
    __slots__ = ("eng", "fn", "deps", "signal", "sigval", "dma_key", "dma_val", "idx")


class Sched:
    def __init__(self):
        self.ops = {e: [] for e in ENGS}
        self.dma_cnt = {}
        self.nbuf = 0

    def buf(self, name=None):
        self.nbuf += 1
        return Buf(name or f"b{self.nbuf}")

    def bufs(self, n, name="b"):
        return [self.buf(f"{name}{i}") for i in range(n)]

    def _add(self, eng, fn, reads, writes, dma_key=None):
        op = Op()
        op.eng = eng
        op.fn = fn
        op.signal = False
        op.sigval = None
        op.dma_key = dma_key
        op.dma_val = None
        op.idx = len(self.ops[eng])
        deps = []
        for b in reads:
            if b.last_w is not None:
                deps.append(b.last_w)
            if b.excl:
                deps.extend(t for t in b.readers if t[1] != eng)
        for b in writes:
            if b.last_w is not None:
                deps.append(b.last_w)
            deps.extend(b.readers)
        if dma_key is not None:
            self.dma_cnt[dma_key] = self.dma_cnt.get(dma_key, 0) + 16
            op.dma_val = self.dma_cnt[dma_key]
            tok = ("dma", dma_key, op.dma_val)
        else:
            tok = ("eng", eng, op.idx)
        op.deps = [d for d in set(deps)
                   if not (d[0] == "eng" and d[1] == "pe" and eng == "pe" and dma_key is None)]
        for b in reads:
            b.readers.append(tok)
        for b in writes:
            b.last_w = tok
            b.readers = []
        self.ops[eng].append(op)
        return op

    def op(self, eng, fn, reads=(), writes=()):
        return self._add(eng, fn, list(reads), list(writes))

    def dma(self, eng, fn, key, reads=(), writes=()):
        return self._add(eng, fn, list(reads), list(writes), dma_key=key)

    def barrier(self, bufs):
        toks = []
        for e in ENGS:
            for o in reversed(self.ops[e]):
                if o.dma_key is None:
                    toks.append(("eng", e, o.idx))
                    break
        for k, v in self.dma_cnt.items():
            toks.append(("dma", k, v))
        for b in bufs:
            b.readers.extend(toks)

    def emit(self, nc, stack):
        for e in ENGS:
            for o in self.ops[e]:
                for d in o.deps:
                    if d[0] == "eng":
                        self.ops[d[1]][d[2]].signal = True
        esem = {}
        for e in ENGS:
            c = 0
            for o in self.ops[e]:
                if o.signal and o.dma_key is None:
                    c += 1
                    o.sigval = c
            if c > 0:
                esem[e] = stack.enter_context(nc.semaphore(f"s_{e}"))
        dsem = {k: stack.enter_context(nc.semaphore(f"d_{k}")) for k in self.dma_cnt}
        self.nsem = len(esem) + len(dsem)
        block = stack.enter_context(nc.Block())
        ops = self.ops
        stats = {}

        def run(e, engobj):
            waited = {}
            nw = 0
            for o in ops[e]:
                need = {}
                for d in o.deps:
                    if d[0] == "eng":
                        sem = esem[d[1]]
                        val = ops[d[1]][d[2]].sigval
                        key = ("e", d[1])
                    else:
                        sem = dsem[d[1]]
                        val = d[2]
                        key = ("d", d[1])
                    if need.get(key, (None, -1))[1] < val:
                        need[key] = (sem, val)
                for key, (sem, val) in need.items():
                    if waited.get(key, -1) >= val:
                        continue
                    waited[key] = val
                    engobj.wait_ge(sem, val)
                    nw += 1
                ins = o.fn(engobj)
                if o.dma_key is not None:
                    ins.then_inc(dsem[o.dma_key], 16)
                elif o.signal:
                    ins.then_inc(esem[e], 1)
            stats[e] = (len(ops[e]), nw)

        if ops["sp"]:
            block.sync(lambda eng: run("sp", eng))
        if ops["pe"]:
            block.tensor(lambda eng: run("pe", eng))
        if ops["act"]:
            block.scalar(lambda eng: run("act", eng))
        if ops["dve"]:
            block.vector(lambda eng: run("dve", eng))
        if ops["pool"]:
            block.gpsimd(lambda eng: run("pool", eng))
        self.stats = stats


class Arena:
    def __init__(self, nc, nbytes):
        self.nbytes = nbytes
        self.t = nc.alloc_sbuf_tensor("arena", [128, nbytes // 4], F32)

    def view(self, off, shape, dtype=F32, parts=128):
        shape = list(shape)
        n = int(np.prod(shape))
        nb = n * (2 if dtype == BF16 else 4)
        assert off % 4 == 0 and nb % 4 == 0, (off, nb)
        assert off + nb <= self.nbytes, ("arena overflow", off, nb)
        ap = self.t[0:parts, off // 4:(off + nb) // 4]
        if dtype == BF16:
            ap = ap.bitcast(BF16)
        if len(shape) == 2:
            ap = ap.rearrange("p (a b) -> p a b", a=shape[0])
        elif len(shape) == 3:
            ap = ap.rearrange("p (a b c) -> p a b c", a=shape[0], b=shape[1])
        elif len(shape) == 4:
            ap = ap.rearrange("p (a b c d) -> p a b c d", a=shape[0], b=shape[1], c=shape[2])
        return ap


class Carver:
    def __init__(self, arena, start, end, name=""):
        self.a = arena
        self.p = start
        self.end = end
        self.name = name

    def take(self, shape, dtype=F32, parts=128):
        n = int(np.prod(shape)) * (2 if dtype == BF16 else 4)
        n = (n + 31) // 32 * 32
        off = self.p
        self.p += n
        assert self.p <= self.end, ("carver overflow", self.name, self.p, self.end)
        return self.a.view(off, shape, dtype, parts)


ARENA_BYTES = 212000
NQ_FF = [(0, 12), (12, 22), (22, 34), (34, 44)]


def build(stop_after=None, dbg=()):
    nc = bass.Bass("TRN2", target_bir_lowering=False)

    def din(name, shape):
        return nc.dram_tensor(name, list(shape), F32, kind="ExternalInput").ap()

    x_d = din("x", [2048, 2048])
    p_d = din("p", [1024, 256])
    g_in_d = din("g_in", [2048])
    g_ffn_d = din("g_ffn", [2048])
    g_ple_d = din("g_ple", [2048])
    g_fin_d = din("g_fin", [2048])
    wU_d = din("wU", [4, 128, 16 * 256])
    wT_d = din("wT", [10, 128, 16 * 256])
    wG_d = din("wG", [128, 16 * 24])
    poolw_d = din("poolw", [128, 4 * 2 * 256])
    pscale_d = din("pscale", [128, 8])
    w1k_d = din("w1k", [128, 32 * 128])
    w1v_d = din("w1v", [128, 32 * 128])
    w2k_d = din("w2k", [128, 128])
    w2v_d = din("w2v", [128, 128])
    pek_d = din("pekT", [128, 32])
    pev_d = din("pevT", [128, 32])
    wout_d = din("wout", [4, 128, 16 * 512])
    wg_d = din("wgate", [22, 128, 16 * 256])
    wu_d = din("wup", [22, 128, 16 * 256])
    wd_d = din("wdown", [16, 128, 12 * 512])
    wpg_d = din("wpg", [4, 128, 16 * 512])
    wpp_d = din("wpp", [4, 128, 2 * 512])
    c_cos_d = din("c_cos", [128, 16 * 32])
    c_sin_d = din("c_sin", [128, 16 * 32])
    c_cvalid_d = din("c_cvalid", [128, 1024])
    c_overlap_d = din("c_overlap", [128, 32])
    c_selbias_d = din("c_selbias", [128, 8 * 32])
    c_E_d = din("c_E", [128, 16 * 128])
    c_tri_d = din("c_tri", [128, 3 * 128])
    c_wvalid_d = din("c_wvalid", [128, 40])
    c_invc_d = din("c_invc", [128, 4 * 16])
    out_d = nc.dram_tensor("out", [1024, 2048], F32, kind="ExternalOutput").ap()
    dbg_d = {}
    for name, shape in dbg:
        dbg_d[name] = nc.dram_tensor("dbg_" + name, list(shape), F32, kind="ExternalOutput").ap()

    S = Sched()
    st = ExitStack()
    A = Arena(nc, ARENA_BYTES)
    ps = [st.enter_context(nc.psum_tensor(f"ps{i}", [128, 512], F32)) for i in range(8)]
    psb = [p[:].bitcast(BF16) for p in ps]
    PB = [Buf(f"psum{i}", excl=True) for i in range(8)]

    def PE(fn, r=(), w=()):
        return S.op("pe", fn, r, w)

    def ACT(fn, r=(), w=()):
        return S.op("act", fn, r, w)

    def DVE(fn, r=(), w=()):
        return S.op("dve", fn, r, w)

    def POOL(fn, r=(), w=()):
        return S.op("pool", fn, r, w)

    out_bufs = []

    dbg_state = [ARENA_BYTES - 8192, 2048]
    B_dbgstg = S.buf("dbgstg")

    def dump(name, ap, buf, parts=128):
        if name not in dbg_d:
            return
        d = dbg_d[name]
        n = ap.shape[1]
        stg = A.view(dbg_state[0], [dbg_state[1]], F32)
        bl = buf if isinstance(buf, list) else [buf]
        for c0 in range(0, n, dbg_state[1]):
            c1 = min(n, c0 + dbg_state[1])
            ACT(lambda e, c0=c0, c1=c1: e.activation(out=stg[0:parts, 0:c1 - c0], in_=ap[:, c0:c1], func=AF.Copy), bl, [B_dbgstg])
            db = S.buf("dbgd_" + name)
            S.dma("sp", lambda e, c0=c0, c1=c1: e.dma_start(out=d[0:parts, c0:c1], in_=stg[0:parts, 0:c1 - c0]), "dbg", [B_dbgstg], [db])
            out_bufs.append(db)

    CONST_END = 26624
    CC = Carver(A, 0, CONST_END, "const")
    tri = CC.take([3, 128], BF16)
    ident = tri[:, 0, :]
    diag = tri[:, 1, :]
    upst = tri[:, 2, :]
    cvalid = CC.take([1024], BF16)
    Emat = CC.take([16, 128], BF16)
    selbias = CC.take([8, 32])
    invc = CC.take([4, 16])
    cos2 = CC.take([16, 32])
    sin2 = CC.take([16, 32])
    wvalid = CC.take([40])
    gsig = CC.take([8, 24])
    pscale = CC.take([8])
    gfull = CC.take([2048])
    stat = CC.take([16, 8])
    B_const = S.buf("const")
    B_gfull = S.buf("gfull")
    B_gsig = S.bufs(8, "gsig")

    cbufs = S.bufs(9, "cst")
    S.dma("sp", lambda e: e.dma_start(out=selbias.rearrange("p a b -> p (a b)"), in_=c_selbias_d), "cst", [], [cbufs[0]])
    S.dma("sp", lambda e: e.dma_start(out=invc.rearrange("p a b -> p (a b)"), in_=c_invc_d), "cst", [], [cbufs[1]])
    S.dma("sp", lambda e: e.dma_start(out=cos2.rearrange("p a b -> p (a b)"), in_=c_cos_d), "cst", [], [cbufs[2]])
    S.dma("sp", lambda e: e.dma_start(out=sin2.rearrange("p a b -> p (a b)"), in_=c_sin_d), "cst", [], [cbufs[3]])
    S.dma("sp", lambda e: e.dma_start(out=wvalid, in_=c_wvalid_d), "cst", [], [cbufs[4]])
    S.dma("sp", lambda e: e.dma_start(out=pscale, in_=pscale_d), "cst", [], [cbufs[5]])
    S.dma("pool", lambda e: e.dma_start(out=tri.rearrange("p a b -> p (a b)"), in_=c_tri_d), "cstp", [], [cbufs[6]])
    S.dma("pool", lambda e: e.dma_start(out=cvalid, in_=c_cvalid_d), "cstp", [], [cbufs[7]])
    S.dma("pool", lambda e: e.dma_start(out=Emat.rearrange("p a b -> p (a b)"), in_=c_E_d), "cstp", [], [cbufs[8]])
    POOL(lambda e: e.memset(stat[:, 15, 7:8], 0.0), cbufs, [B_const])

    def load_g(g_d):
        S.dma("sp", lambda e: e.dma_start(out=gfull, in_=g_d.partition_broadcast(128)), "gld", [], [B_gfull])

    PC = Carver(A, CONST_END, ARENA_BYTES, "persist")
    qT = PC.take([2, 8, 4, 128], BF16)
    kT = PC.take([3, 2, 2048], BF16)
    vcT = PC.take([2, 2048], BF16)
    v1 = PC.take([4, 16, 132], BF16)
    YMIX_OFF = PC.p
    ymixT = PC.take([16, 1024], BF16)
    uhalo = PC.take([8, 16])
    TRANS0 = PC.p
    B_qT = [[S.buf(f"qT{g}_{o}") for o in range(8)] for g in range(2)]
    B_kT = [[[S.buf(f"kT{k}_{g}_{l}") for l in range(16)] for g in range(2)] for k in range(3)]
    B_vcT = [[S.buf(f"vcT{g}_{l}") for l in range(16)] for g in range(2)]
    B_v1 = [[S.buf(f"v1{k}_{l}") for l in range(16)] for k in range(4)]
    B_ymix = [S.buf(f"ymix{c}") for c in range(16)]
    B_uhalo = S.buf("uhalo")
    B_v1init = S.buf("v1init")
    POOL(lambda e: e.memset(v1[:, :, :, 128:132], 1.0), [], [B_v1init])

    nstat = [0]

    def rms_to_bf16(src, B_src, dst_bf, B_dst, junk, B_junk):
        k = nstat[0] % 16
        nstat[0] += 1
        s = stat[:, k, :]
        sb = S.buf()
        ACT(lambda e: e.activation(out=junk, in_=src, func=AF.Square, accum_out=s[:, 0:1]), [B_src], [B_junk, sb])
        DVE(lambda e: e.tensor_scalar(out=s[:, 1:2], in0=s[:, 0:1], scalar1=1.0 / 2048, scalar2=1e-6, op0=ALU.mult, op1=ALU.add), [sb], [sb])
        ACT(lambda e: e.activation(out=s[:, 2:3], in_=s[:, 1:2], func=AF.Sqrt), [sb], [sb])
        DVE(lambda e: e.reciprocal(out=s[:, 3:4], in_=s[:, 2:3]), [sb], [sb])
        DVE(lambda e: e.scalar_tensor_tensor(out=dst_bf, in0=src, scalar=s[:, 3:4], in1=gfull, op0=ALU.mult, op1=ALU.mult),
            [sb, B_src, B_gfull], [B_dst])

    def transpose_tile(a_bf, B_a, aT_dst, B_aT, col0):
        for hb in range(2):
            bank = 6 + hb
            for j in range(8):
                dc = hb * 8 + j
                PE(lambda e, dc=dc, j=j, bank=bank: e.transpose(out=psb[bank][:, j * 128:(j + 1) * 128], in_=a_bf[:, dc * 128:(dc + 1) * 128], identity=ident),
                   [B_a, B_const], [PB[bank]])
            eng = ACT if hb == 0 else DVE
            if hb == 0:
                ACT(lambda e, bank=bank, hb=hb: e.activation(out=aT_dst[:, hb * 8:(hb + 1) * 8, col0:col0 + 128],
                                                           in_=psb[bank].rearrange("p (a b) -> p a b", a=8), func=AF.Copy), [PB[bank]], [B_aT])
            else:
                DVE(lambda e, bank=bank, hb=hb: e.tensor_copy(out=aT_dst[:, hb * 8:(hb + 1) * 8, col0:col0 + 128],
                                                            in_=psb[bank].rearrange("p (a b) -> p a b", a=8)), [PB[bank]], [B_aT])

    TC = Carver(A, TRANS0, ARENA_BYTES - 8192, "AB")
    aT = TC.take([16, 1024], BF16)
    wsl = [TC.take([16, 256], BF16) for _ in range(2)]
    AB_SHARED = TC.p
    xt = [TC.take([2048]) for _ in range(2)]
    abf = [TC.take([2048], BF16) for _ in range(2)]
    stage = [TC.take([2, 128], BF16) for _ in range(2)]
    rtmp = [TC.take([2, 32]) for _ in range(2)]
    rtmp2 = [TC.take([2, 32]) for _ in range(2)]
    wgates = TC.take([16, 24], BF16)
    B_aT = [S.buf(f"aT{i}") for i in range(8)]
    B_xt = S.bufs(2, "xt")
    B_abf = S.bufs(2, "abf")
    B_wsl = S.bufs(2, "wsl")
    B_stage = S.bufs(2, "stage")
    B_rtmp = S.bufs(2, "rtmp")
    B_wgates = S.buf("wgates")
    wcount = [0]
    scount = [0]

    load_g(g_in_d)
    S.dma("pool", lambda e: e.dma_start(out=wgates.rearrange("p a b -> p (a b)"), in_=wG_d), "wgl", [], [B_wgates])

    def load_w(src_ap):
        i = wcount[0] % 2
        wcount[0] += 1
        S.dma("pool", lambda e: e.dma_start(out=wsl[i].rearrange("p a b -> p (a b)"), in_=src_ap), f"wsl{i}", [], [B_wsl[i]])
        return i

    def norm_pass(pas):
        def s1(i):
            lt = pas * 8 + i
            sl = lt % 2
            S.dma("sp", lambda e, lt=lt, sl=sl: e.dma_start(out=xt[sl], in_=x_d[lt * 128:(lt + 1) * 128, :]), f"xt{sl}", [], [B_xt[sl]])
            rms_to_bf16(xt[sl], B_xt[sl], abf[sl], B_abf[sl], abf[sl], B_abf[sl])

        def s2(i):
            sl = (pas * 8 + i) % 2
            transpose_tile(abf[sl], B_abf[sl], aT, B_aT[i], i * 128)
        s1(0)
        for i in range(8):
            if i + 1 < 8:
                s1(i + 1)
            s2(i)

    def rope(pbank, stg, B_stg, lt, sl):
        src = ps[pbank][:, 0:256].rearrange("p (h d) -> p h d", h=2)
        c2 = cos2[:, lt, :].unsqueeze(1).to_broadcast([128, 2, 32])
        sA = sin2[:, lt, 0:16].unsqueeze(1).to_broadcast([128, 2, 16])
        sB = sin2[:, lt, 16:32].unsqueeze(1).to_broadcast([128, 2, 16])
        t1 = rtmp[sl]
        t2 = rtmp2[sl]
        DVE(lambda e: e.tensor_tensor(out=t1, in0=src[:, :, 0:32], in1=c2, op=ALU.mult), [PB[pbank], B_const], [B_rtmp[sl]])
        DVE(lambda e: e.tensor_tensor(out=t2[:, :, 0:16], in0=src[:, :, 16:32], in1=sA, op=ALU.mult), [PB[pbank], B_const], [B_rtmp[sl]])
        DVE(lambda e: e.tensor_tensor(out=t2[:, :, 16:32], in0=src[:, :, 0:16], in1=sB, op=ALU.mult), [PB[pbank], B_const], [B_rtmp[sl]])
        DVE(lambda e: e.tensor_tensor(out=stg[:, :, 0:32], in0=t1, in1=t2, op=ALU.add), [B_rtmp[sl]], [B_stg])

    def tok_half(hg, pas, i, wi, pbank):
        lt = pas * 8 + i
        for dc in range(16):
            PE(lambda e, dc=dc: e.matmul(ps[pbank][:, 0:256], lhsT=aT[:, dc, i * 128:(i + 1) * 128], rhs=wsl[wi][:, dc, :], start=(dc == 0), stop=(dc == 15)),
               [B_aT[i], B_wsl[wi]], [PB[pbank]])

        def evac():
            src = ps[pbank][:, 0:256].rearrange("p (h d) -> p h d", h=2)
            if hg in (7, 9):
                vk = 0 if hg == 7 else 2
                ACT(lambda e: e.activation(out=v1[:, vk:vk + 2, lt, 0:128], in_=src, func=AF.Copy), [PB[pbank], B_v1init], [B_v1[vk][lt], B_v1[vk + 1][lt]])
                return
            sl = scount[0] % 2
            scount[0] += 1
            stg = stage[sl]
            tb = 6 + (scount[0] % 2)
            ACT(lambda e: e.activation(out=stg, in_=src, func=AF.Copy), [PB[pbank]], [B_stage[sl]])
            if hg != 5:
                rope(pbank, stg, B_stage[sl], lt, sl)
            for r in range(2):
                PE(lambda e, r=r: e.transpose(out=psb[tb][:, r * 128:(r + 1) * 128], in_=stg[:, r, :], identity=ident), [B_stage[sl], B_const], [PB[tb]])
            pin = psb[tb][:, 0:256].rearrange("p (g t) -> p g t", g=2)
            if hg < 4:
                g, r0, ot = hg // 2, (hg % 2) * 2, i
                DVE(lambda e: e.tensor_copy(out=qT[:, g, ot, r0:r0 + 2, :], in_=pin), [PB[tb]], [B_qT[g][ot]])
            elif hg == 5:
                DVE(lambda e: e.tensor_copy(out=vcT[:, :, lt * 128:(lt + 1) * 128], in_=pin), [PB[tb]], [B_vcT[0][lt], B_vcT[1][lt]])
            else:
                kind = (hg - 4) // 2
                DVE(lambda e: e.tensor_copy(out=kT[:, kind, :, lt * 128:(lt + 1) * 128], in_=pin), [PB[tb]], [B_kT[kind][0][lt], B_kT[kind][1][lt]])
        return evac

    pbrot = [0]

    def next_pb():
        b = pbrot[0] % 6
        pbrot[0] += 1
        return b

    if stop_after == "C0":
        dump("gfull", gfull, [B_gfull, B_const])
        return _finish(nc, S, st, out_bufs)
    for pas in range(2):
        norm_pass(pas)
        if stop_after == "N0":
            dump("aT", aT.rearrange("p a b -> p (a b)"), B_aT)
            return _finish(nc, S, st, out_bufs)
        hgs = [4, 5, 6, 7, 8, 9] if pas == 0 else list(range(10))
        pend = []
        for hg in hgs:
            wi = load_w(wT_d[hg])
            for i in range(8):
                pend.append(tok_half(hg, pas, i, wi, next_pb()))
                if len(pend) > 2:
                    pend.pop(0)()
        for ev in pend:
            ev()
        if pas == 0:
            for ug in range(4):
                wi = load_w(wU_d[ug])
                for cc in range(2):
                    c = ug * 2 + cc
                    pb = next_pb()
                    for dc in range(16):
                        PE(lambda e, dc=dc, cc=cc, wi=wi, pb=pb: e.matmul(ps[pb][:, 0:16], lhsT=wsl[wi][:, dc, cc * 128:(cc + 1) * 128], rhs=aT[:, dc, 1008:1024],
                                                                        start=(dc == 0), stop=(dc == 15)), [B_aT[7], B_wsl[wi]], [PB[pb]])
                    ACT(lambda e, c=c, pb=pb: e.activation(out=uhalo[:, c, :], in_=ps[pb][:, 0:16], func=AF.Copy), [PB[pb]], [B_uhalo])
            if stop_after == "P0":
                dump("kT", kT.rearrange("p a b c -> p (a b c)"), [B_kT[k][g][l] for k in range(3) for g in range(2) for l in range(16)])
                return _finish(nc, S, st, out_bufs)
        else:
            for i in range(8):
                pb = next_pb()
                for dc in range(16):
                    PE(lambda e, dc=dc, i=i, pb=pb: e.matmul(ps[pb][:, 0:24], lhsT=aT[:, dc, i * 128:(i + 1) * 128], rhs=wgates[:, dc, :],
                                                             start=(dc == 0), stop=(dc == 15)), [B_aT[i], B_wgates], [PB[pb]])
                ACT(lambda e, i=i, pb=pb: e.activation(out=gsig[:, i, :], in_=ps[pb][:, 0:24], func=AF.Sigmoid), [PB[pb]], [B_gsig[i]])

    dump("kT", kT.rearrange("p a b c -> p (a b c)"), [B_kT[k][g][l] for k in range(3) for g in range(2) for l in range(16)])
    dump("qT", qT.rearrange("p a b c d -> p (a b c d)"), [B_qT[g][o] for g in range(2) for o in range(8)])
    dump("v1", v1.rearrange("p a b c -> p (a b c)"), [B_v1[k][l] for k in range(4) for l in range(16)] + [B_v1init])
    dump("vcT", vcT.rearrange("p a b -> p (a b)"), [B_vcT[g][l] for g in range(2) for l in range(16)])
    dump("gsig", gsig.rearrange("p a b -> p (a b)"), B_gsig)

    if stop_after == "P1":
        return _finish(nc, S, st, out_bufs)
    UC = Carver(A, AB_SHARED, ARENA_BYTES - 8192, "pool")
    ubuf = UC.take([1040])
    sbufA = UC.take([1040])
    sbufB = UC.take([1040])
    ptmp = UC.take([16])
    pooled = [UC.take([2, 1024], BF16) for _ in range(2)]
    poolw = UC.take([4, 2, 256], BF16)
    B_ubuf = S.buf("ubuf")
    B_sA = S.buf("sA")
    B_sB = S.buf("sB")
    B_ptmp = S.buf("ptmp")
    B_pooled = S.bufs(2, "pooled")
    B_poolw = S.buf("poolw")
    S.barrier([B_ubuf, B_sA, B_sB, B_ptmp, B_poolw] + B_pooled)
    S.dma("pool", lambda e: e.dma_start(out=poolw.rearrange("p a b c -> p (a b c)"), in_=poolw_d), "poolw", [], [B_poolw])
    pend_y = []
    for gi in range(4):
        w = 2 << gi
        wi = load_w(wU_d[gi])
        psl = gi % 2
        for cc in range(2):
            c = gi * 2 + cc
            pbs = [next_pb(), next_pb()]
            for th2 in range(2):
                for dc in range(16):
                    PE(lambda e, dc=dc, cc=cc, wi=wi, th2=th2, pb=pbs[th2]: e.matmul(ps[pb][:], lhsT=wsl[wi][:, dc, cc * 128:(cc + 1) * 128],
                                                                                   rhs=aT[:, dc, th2 * 512:(th2 + 1) * 512], start=(dc == 0), stop=(dc == 15)),
                       [B_aT[th2 * 4 + k] for k in range(4)] + [B_wsl[wi]], [PB[pbs[th2]]])
            ACT(lambda e, c=c: e.activation(out=ubuf[:, 0:16], in_=uhalo[:, c, :], func=AF.Copy), [B_uhalo], [B_ubuf])
            ACT(lambda e, pb=pbs[0]: e.activation(out=ubuf[:, 16:528], in_=ps[pb][:], func=AF.Copy), [PB[pbs[0]]], [B_ubuf])
            ACT(lambda e, pb=pbs[1]: e.activation(out=ubuf[:, 528:1040], in_=ps[pb][:], func=AF.Copy), [PB[pbs[1]]], [B_ubuf])
            cur, Bcur = ubuf, B_ubuf
            nxt = [(sbufA, B_sA), (sbufB, B_sB)]
            sh = 1
            k = 0
            while sh < w:
                dst, Bd = nxt[k % 2]
                DVE(lambda e, cur=cur, dst=dst, sh=sh: e.tensor_tensor(out=dst[:, sh:1040], in0=cur[:, sh:1040], in1=cur[:, 0:1040 - sh], op=ALU.add), [Bcur], [Bd])
                cur, Bcur = dst, Bd
                sh *= 2
                k += 1
            pl = pooled[psl][:, cc, :]
            DVE(lambda e, cur=cur, pl=pl, w=w: e.scalar_tensor_tensor(out=pl[:, 16:1024], in0=cur[:, 32:1040], scalar=1.0 / w, in1=ubuf[:, 32:1040], op0=ALU.mult, op1=ALU.subtract),
                [Bcur, B_ubuf], [B_pooled[psl]])
            DVE(lambda e, cur=cur, gi=gi: e.tensor_tensor(out=ptmp, in0=cur[:, 16:32], in1=invc[:, gi, :], op=ALU.mult), [Bcur, B_const], [B_ptmp])
            DVE(lambda e, pl=pl: e.tensor_tensor(out=pl[:, 0:16], in0=ptmp, in1=ubuf[:, 16:32], op=ALU.subtract), [B_ptmp, B_ubuf], [B_pooled[psl]])
        def ypool(gi=gi, psl=psl):
          for dch in range(2):
              for th2 in range(2):
                  pb = next_pb()
                  for cc in range(2):
                      PE(lambda e, cc=cc, dch=dch, th2=th2, pb=pb, gi=gi, psl=psl: e.matmul(ps[pb][:], lhsT=poolw[:, gi, cc, dch * 128:(dch + 1) * 128],
                                                                                          rhs=pooled[psl][:, cc, th2 * 512:(th2 + 1) * 512], start=(cc == 0), stop=(cc == 1)),
                         [B_poolw, B_pooled[psl]], [PB[pb]])
                  ch = gi * 2 + dch
                  ACT(lambda e, ch=ch, th2=th2, pb=pb: e.activation(out=ymixT[:, ch, th2 * 512:(th2 + 1) * 512], in_=ps[pb][:], func=AF.Copy, scale=pscale[:, ch:ch + 1]),
                      [PB[pb], B_const], [B_ymix[ch]])
        if gi > 0:
            pend_y.pop(0)()
        pend_y.append(ypool)
    pend_y.pop(0)()
    dump("ypool", ymixT[:, 0:8, :].rearrange("p a b -> p (a b)"), B_ymix[0:8])

    if stop_after == "B":
        return _finish(nc, S, st, out_bufs)

    AC = Carver(A, TRANS0, ARENA_BYTES - 8192, "attn")
    w1 = [AC.take([32, 128], BF16) for _ in range(2)]
    w2 = [AC.take([128], BF16) for _ in range(2)]
    peT = [AC.take([32], BF16) for _ in range(2)]
    hb_ = AC.take([4, 128])
    hx = AC.take([4, 128])
    hy = AC.take([4, 128])
    cbias = AC.take([2, 2])
    geluT = AC.take([4, 128], BF16)
    kcmpT = AC.take([2, 128], BF16)
    vcmp1 = AC.take([2, 164], BF16)
    PT = [AC.take([4, 128], BF16) for _ in range(4)]
    oaccs = [AC.take([4, 128]) for _ in range(4)] * 4
    obf = [AC.take([4, 128], BF16) for _ in range(2)]
    etmp = [AC.take([2, 128]) for _ in range(2)]
    impb = [AC.take([32]) for _ in range(2)]
    impw = [AC.take([32]) for _ in range(2)]
    m16 = [AC.take([16]) for _ in range(2)]
    selbf = [AC.take([32], BF16) for _ in range(2)]
    negT = AC.take([2, 8, 4, 128], BF16)
    PTc = [AC.take([4, 128], BF16) for _ in range(2)]
    rs = [AC.take([3, 4]) for _ in range(2)]
    B_w1 = S.bufs(2, "w1")
    B_w2 = S.bufs(2, "w2")
    B_peT = S.bufs(2, "peT")
    B_h = S.bufs(4, "hid")
    B_cb = S.buf("cbias")
    B_gelu = S.bufs(4, "gelu")
    B_kcmp = S.bufs(2, "kcmp")
    B_vcmp = S.bufs(2, "vcmp")
    B_PT = S.bufs(4, "PT")
    B_PTc = S.bufs(2, "PTc")
    B_oaccs = S.bufs(4, "oacc") * 4
    B_obf = S.bufs(2, "obf")
    B_etmp = S.bufs(2, "etmp")
    B_imp = S.bufs(2, "imp")
    B_selbf = S.bufs(2, "selbf")
    B_negT = [[S.buf(f"negT{g}_{o}") for o in range(8)] for g in range(2)]
    B_rs = S.bufs(2, "rs")
    S.barrier(B_w1 + B_w2 + B_peT + B_h + [B_cb] + B_gelu + B_kcmp + B_vcmp + B_PT + B_PTc + B_oaccs + B_obf + B_etmp + B_imp + B_selbf
              + [b for l in B_negT for b in l] + B_rs)
    for kv, (w1d, w2d, ped) in enumerate([(w1k_d, w2k_d, pek_d), (w1v_d, w2v_d, pev_d)]):
        S.dma("pool", lambda e, kv=kv, w1d=w1d: e.dma_start(out=w1[kv].rearrange("p a b -> p (a b)"), in_=w1d), f"w1_{kv}", [], [B_w1[kv]])
        S.dma("pool", lambda e, kv=kv, w2d=w2d: e.dma_start(out=w2[kv], in_=w2d), f"w2_{kv}", [], [B_w2[kv]])
        S.dma("pool", lambda e, kv=kv, ped=ped: e.dma_start(out=peT[kv], in_=ped), f"pe_{kv}", [], [B_peT[kv]])
    POOL(lambda e: e.memset(vcmp1, 0.0), [], B_vcmp)
    POOL(lambda e: e.memset(negT.rearrange("p a b c d -> p (a b c d)"), 0.0), [], [b for l in B_negT for b in l])
    for g in range(2):
        S.dma("pool", lambda e, g=g: e.dma_start(out=vcmp1[:, g, 129:161], in_=c_overlap_d), f"ovl{g}", [], [B_vcmp[g]])
        POOL(lambda e, g=g: e.memset(vcmp1[0:127, g, 128:129], 1.0), [], [B_vcmp[g]])

    CB = 7
    for kv in range(2):
        for l in range(32):
            PE(lambda e, l=l, kv=kv: e.matmul(ps[CB][:, 500 + kv:501 + kv], lhsT=w1[kv][:, l, :], rhs=peT[kv][:, l:l + 1], start=(l == 0), stop=(l == 31)),
               [B_w1[kv], B_peT[kv]], [PB[CB]])
        DVE(lambda e, kv=kv: e.tensor_copy(out=cbias[:, kv, 0:1], in_=ps[CB][:, 500 + kv:501 + kv]), [PB[CB]], [B_cb])
    for kv in range(2):
        for g in range(2):
            idx = kv * 2 + g
            srcT = kT[:, 0, g, :] if kv == 0 else vcT[:, g, :]
            Bsrc = [B_kT[0][g][l] for l in range(16)] if kv == 0 else [B_vcT[g][l] for l in range(16)]
            for l in range(32):
                PE(lambda e, l=l, kv=kv, idx=idx, srcT=srcT: e.matmul(ps[CB][:, idx * 128:idx * 128 + 127], lhsT=w1[kv][:, l, :], rhs=srcT[:, l:l + 16 * 126 + 1:16],
                                                                  start=(l == 0), stop=(l == 31)), Bsrc + [B_w1[kv]], [PB[CB]])
            x_ = hb_[:, idx, 0:127]
            DVE(lambda e, idx=idx, kv=kv, x_=x_: e.tensor_scalar(out=x_, in0=ps[CB][:, idx * 128:idx * 128 + 127], scalar1=cbias[:, kv, 0:1], scalar2=None, op0=ALU.add),
                [PB[CB], B_cb], [B_h[idx]])
            DVE(lambda e, idx=idx, x_=x_: e.tensor_tensor(out=hx[:, idx, 0:127], in0=x_, in1=x_, op=ALU.mult), [B_h[idx]], [B_h[idx]])
            DVE(lambda e, idx=idx: e.tensor_scalar(out=hx[:, idx, 0:127], in0=hx[:, idx, 0:127], scalar1=0.044715, scalar2=1.0, op0=ALU.mult, op1=ALU.add), [B_h[idx]], [B_h[idx]])
            DVE(lambda e, idx=idx, x_=x_: e.tensor_tensor(out=hx[:, idx, 0:127], in0=hx[:, idx, 0:127], in1=x_, op=ALU.mult), [B_h[idx]], [B_h[idx]])
            ACT(lambda e, idx=idx: e.activation(out=hy[:, idx, 0:127], in_=hx[:, idx, 0:127], func=AF.Sigmoid, scale=1.5957691216057308), [B_h[idx]], [B_h[idx]])
            DVE(lambda e, idx=idx, x_=x_: e.tensor_tensor(out=geluT[:, idx, 0:127], in0=hy[:, idx, 0:127], in1=x_, op=ALU.mult), [B_h[idx]], [B_gelu[idx]])
    POOL(lambda e: e.memset(kcmpT, 0.0), [], B_kcmp)
    for g in range(2):
        PE(lambda e, g=g: e.matmul(ps[CB][:, 0:127], lhsT=w2[0], rhs=geluT[:, g, 0:127], start=True, stop=True), [B_w2[0], B_gelu[g]], [PB[CB]])
        DVE(lambda e, g=g: e.tensor_copy(out=kcmpT[:, g, 0:127], in_=ps[CB][:, 0:127]), [PB[CB]], [B_kcmp[g]])
        PE(lambda e, g=g: e.matmul(ps[CB][0:127, 128:256], lhsT=geluT[:, 2 + g, 0:127], rhs=w2[1], start=True, stop=True), [B_w2[1], B_gelu[2 + g]], [PB[CB]])
        DVE(lambda e, g=g: e.tensor_copy(out=vcmp1[0:127, g, 0:128], in_=ps[CB][0:127, 128:256]), [PB[CB]], [B_vcmp[g]])
    dump("kcmpT", kcmpT.rearrange("p a b -> p (a b)"), B_kcmp)
    dump("vcmp1", vcmp1.rearrange("p a b -> p (a b)"), B_vcmp)

    cnt = {"S": 0, "PT": 0, "O": 0, "rs": 0, "ob": 0}
    SB = [0, 1, 2]

    def evac_branch(obanks, br, g, ot, oa, B_oa, first, imp=None):
        k = cnt["rs"] % 2
        cnt["rs"] += 1
        r_ = rs[k]
        Br = B_rs[k]
        for bi, bank in enumerate(obanks):
            DVE(lambda e, bi=bi, bank=bank: e.tensor_scalar(out=r_[:, 0, 2 * bi:2 * bi + 2], in0=ps[bank][:, 128:385:256], scalar1=1e-30, scalar2=None, op0=ALU.add),
                [PB[bank]], [Br])
        DVE(lambda e: e.reciprocal(out=r_[:, 1, :], in_=r_[:, 0, :]), [Br], [Br])
        s0 = g * 12 + br
        DVE(lambda e: e.tensor_tensor(out=r_[:, 2, :], in0=r_[:, 1, :], in1=gsig[:, ot, s0:s0 + 10:3], op=ALU.mult), [Br, B_gsig[ot]], [Br])
        for r in range(4):
            bank = obanks[r // 2]
            c0 = (r % 2) * 256
            if first:
                DVE(lambda e, bank=bank, c0=c0, r=r: e.tensor_scalar(out=oa[:, r, :], in0=ps[bank][:, c0:c0 + 128], scalar1=r_[:, 2, r:r + 1], scalar2=None, op0=ALU.mult),
                    [PB[bank], Br], [B_oa])
            else:
                DVE(lambda e, bank=bank, c0=c0, r=r: e.scalar_tensor_tensor(out=oa[:, r, :], in0=ps[bank][:, c0:c0 + 128], scalar=r_[:, 2, r:r + 1], in1=oa[:, r, :],
                                                                          op0=ALU.mult, op1=ALU.add), [PB[bank], Br], [B_oa])
            if imp is not None:
                ib, Bi = imp
                if r == 0:
                    DVE(lambda e, bank=bank, c0=c0, r=r: e.tensor_scalar(out=ib, in0=ps[bank][:, c0 + 129:c0 + 161], scalar1=r_[:, 1, r:r + 1], scalar2=None, op0=ALU.mult),
                        [PB[bank], Br], [Bi])
                else:
                    DVE(lambda e, bank=bank, c0=c0, r=r: e.scalar_tensor_tensor(out=ib, in0=ps[bank][:, c0 + 129:c0 + 161], scalar=r_[:, 1, r:r + 1], in1=ib, op0=ALU.mult, op1=ALU.add),
                        [PB[bank], Br], [Bi])

    def osets():
        i = cnt["O"] % 2
        cnt["O"] += 1
        return (3, 4) if i == 0 else (5, 6)

    def score_exp(lhsT_ap, B_l, g, ot, bias=None):
        sb = SB[cnt["S"] % 3]
        cnt["S"] += 1
        pi = cnt["PT"] % 4
        cnt["PT"] += 1
        rhs_q = qT[:, g, ot].rearrange("p r t -> p (r t)")
        PE(lambda e: e.matmul(ps[sb][:], lhsT=lhsT_ap, rhs=rhs_q, start=True, stop=(bias is None)), B_l + [B_qT[g][ot]], [PB[sb]])
        if bias is not None:
            kt = bias
            PE(lambda e: e.matmul(ps[sb][:], lhsT=Emat[:, kt, :], rhs=negT[:, g, ot].rearrange("p r t -> p (r t)"), start=False, stop=True),
               [B_const, B_negT[g][ot]], [PB[sb]])
        ACT(lambda e: e.activation(out=PT[pi].rearrange("p r t -> p (r t)"), in_=ps[sb][:], func=AF.Exp, scale=SCALE), [PB[sb]], [B_PT[pi]])
        return pi

    def pv(pi, rhs_ap, B_r, width, obanks, first, last):
        for r in range(4):
            bank = obanks[r // 2]
            c0 = (r % 2) * 256
            PE(lambda e, r=r, bank=bank, c0=c0: e.matmul(ps[bank][:, c0:c0 + width], lhsT=PT[pi][:, r, :], rhs=rhs_ap, start=(first and r % 2 == 0), stop=last,
                                                         skip_group_check=True),
               [B_PT[pi]] + B_r, [PB[bank]])

    def bcast4(m):
        return m.unsqueeze(1).to_broadcast([128, 4, 128])

    deferred = []

    def run_pipeline(steps, la):
        stt = {}
        n = len(steps)
        for i in range(n + la):
            if i < n:
                if steps[i][2] is not None:
                    steps[i][2]()
                stt[i] = steps[i][0]()
            j = i - la
            if j >= 0:
                steps[j][1](stt.pop(j))
            for dfr in list(deferred):
                dfr[0] -= 1
                if dfr[0] <= 0:
                    deferred.remove(dfr)
                    dfr[1]()

    def c1_stages(g, ot):
        oi = g * 8 + ot
        ii = oi % 2
        pt = PTc[ii]
        Bpt = B_PTc[ii]
        hold = {}

        def s0():
            sb = SB[cnt["S"] % 3]
            cnt["S"] += 1
            PE(lambda e: e.matmul(ps[sb][:], lhsT=kcmpT[:, g, :], rhs=qT[:, g, ot].rearrange("p r t -> p (r t)"), start=True, stop=True), [B_kcmp[g], B_qT[g][ot]], [PB[sb]])
            ACT(lambda e: e.activation(out=pt.rearrange("p r t -> p (r t)"), in_=ps[sb][:], func=AF.Exp, scale=SCALE), [PB[sb]], [Bpt])
            DVE(lambda e: e.tensor_tensor(out=pt, in0=pt, in1=bcast4(cvalid[:, ot * 128:(ot + 1) * 128]), op=ALU.mult), [Bpt, B_const], [Bpt])

        def s1():
            ob = (5, 6)
            for r in range(4):
                bank = ob[r // 2]
                c0 = (r % 2) * 256
                PE(lambda e, r=r, bank=bank, c0=c0: e.matmul(ps[bank][:, c0:c0 + 161], lhsT=pt[:, r, :], rhs=vcmp1[:, g, 0:161], start=(r % 2 == 0), stop=True, skip_group_check=True),
                   [Bpt, B_vcmp[g]], [PB[bank]])
            evac_branch(ob, 0, g, ot, oaccs[oi], B_oaccs[oi], True, imp=(impb[ii], B_imp[ii]))

        def s2():
            DVE(lambda e: e.tensor_tensor(out=impb[ii], in0=impb[ii], in1=selbias[:, ot, :], op=ALU.add), [B_imp[ii], B_const], [B_imp[ii]])
            DVE(lambda e: e.max(out=m16[ii][:, 0:8], in_=impb[ii]), [B_imp[ii]], [B_imp[ii]])
            DVE(lambda e: e.match_replace(out=impw[ii], in_to_replace=m16[ii][:, 0:8], in_values=impb[ii], imm_value=-3.0e30), [B_imp[ii]], [B_imp[ii]])
            DVE(lambda e: e.max(out=m16[ii][:, 8:16], in_=impw[ii]), [B_imp[ii]], [B_imp[ii]])
            DVE(lambda e: e.tensor_scalar(out=m16[ii][:, 15:16], in0=m16[ii][:, 15:16], scalar1=-1.0e29, scalar2=None, op0=ALU.max), [B_imp[ii]], [B_imp[ii]])
            DVE(lambda e: e.tensor_scalar(out=selbf[ii], in0=impb[ii], scalar1=m16[ii][:, 15:16], scalar2=1.0, op0=ALU.is_ge, op1=ALU.subtract), [B_imp[ii]], [B_selbf[ii]])

        def s3():
            PE(lambda e: e.transpose(out=psb[7][0:32, 0:128], in_=selbf[ii], identity=ident), [B_selbf[ii], B_const], [PB[7]])
            DVE(lambda e: e.tensor_copy(out=negT[0:32, g, ot], in_=psb[7][0:32, 0:128].unsqueeze(1).to_broadcast([32, 4, 128])), [PB[7]], [B_negT[g][ot]])
        return [s0, s1, s2, s3]

    chains = [(g, ot) for g in range(2) for ot in range(8)]
    c1s = [c1_stages(g, ot) for (g, ot) in chains]
    for f in c1s[0]:
        f()

    c2 = []
    for g in range(2):
        for ot in range(8):
            ltq = 8 + ot
            oi = g * 8 + ot
            lstep = [0]
            for br in (1, 2):
                nst = ltq + 1 if br == 1 else 5
                holder = {}
                for si in range(nst):
                    kt = si if br == 1 else ltq - 4 + si

                    def stepA(g=g, ot=ot, br=br, si=si, kt=kt, ltq=ltq):
                        if br == 1:
                            pi = score_exp(kT[:, 1, g, kt * 128:(kt + 1) * 128], [B_kT[1][g][kt]], g, ot, bias=kt)
                            if kt == ltq:
                                DVE(lambda e: e.tensor_tensor(out=PT[pi], in0=PT[pi], in1=bcast4(diag), op=ALU.mult), [B_PT[pi], B_const], [B_PT[pi]])
                        else:
                            pi = score_exp(kT[:, 2, g, kt * 128:(kt + 1) * 128], [B_kT[2][g][kt]], g, ot)
                            wv = wvalid[:, ot * 5 + si:ot * 5 + si + 1]
                            if si == 0:
                                DVE(lambda e: e.scalar_tensor_tensor(out=PT[pi], in0=PT[pi], scalar=wv, in1=bcast4(upst), op0=ALU.mult, op1=ALU.mult), [B_PT[pi], B_const], [B_PT[pi]])
                            elif si == 4:
                                DVE(lambda e: e.tensor_tensor(out=PT[pi], in0=PT[pi], in1=bcast4(diag), op=ALU.mult), [B_PT[pi], B_const], [B_PT[pi]])
                            elif kt < 8:
                                DVE(lambda e: e.tensor_scalar(out=PT[pi], in0=PT[pi], scalar1=wv, scalar2=None, op0=ALU.mult), [B_PT[pi], B_const], [B_PT[pi]])
                        return pi

                    def stepB(pi, g=g, ot=ot, br=br, si=si, kt=kt, nst=nst, oi=oi, holder=holder):
                        ob = (3, 4) if br == 1 else (5, 6)
                        vk = (0 if br == 1 else 2) + g
                        pv(pi, v1[:, vk, kt, 0:129], [B_v1[vk][kt], B_v1init], 129, ob, si == 0, si == nst - 1)
                        if si == nst - 1:
                            evac_branch(ob, br, g, ot, oaccs[oi], B_oaccs[oi], False)
                            if br == 2:
                                def combine(g=g, ot=ot, oi=oi):
                                    k = cnt["ob"] % 2
                                    cnt["ob"] += 1
                                    ACT(lambda e: e.activation(out=obf[k], in_=oaccs[oi], func=AF.Copy), [B_oaccs[oi]], [B_obf[k]])

                                    def tr():
                                        for r in range(4):
                                            PE(lambda e, r=r: e.transpose(out=psb[7][:, 256 + r * 128:256 + (r + 1) * 128], in_=obf[k][:, r, :], identity=ident), [B_obf[k], B_const], [PB[7]])
                                        DVE(lambda e: e.tensor_copy(out=ymixT[:, 8 + g * 4:12 + g * 4, ot * 128:(ot + 1) * 128], in_=psb[7][:, 256:768].rearrange("p (r t) -> p r t", r=4)),
                                            [PB[7]], [B_ymix[8 + g * 4 + r] for r in range(4)])
                                    deferred.append([6, tr])
                                deferred.append([4, combine])
                    pre = None
                    ntot = ltq + 1 + 5
                    sched = (3, 5, 7, ntot - 2)
                    if oi + 1 < 16 and lstep[0] in sched:
                        pre = c1s[oi + 1][sched.index(lstep[0])]
                    lstep[0] += 1
                    c2.append((stepA, stepB, pre))
    run_pipeline(c2, 2)
    while deferred:
        deferred.pop(0)[1]()
    dump("ynsa", ymixT[:, 8:16, :].rearrange("p a b -> p (a b)"), B_ymix[8:16])

    if stop_after == "C":
        return _finish(nc, S, st, out_bufs)

    hres = A.view(CONST_END, [8, 2048], F32)
    assert CONST_END + 65536 <= YMIX_OFF
    D_END = CONST_END + 65536
    DC = Carver(A, TRANS0, ARENA_BYTES - 8192, "D")
    wo = [DC.take([16, 512], BF16) for _ in range(2)]
    B_h = [[S.buf(f"h{ot}_{c}") for c in range(4)] for ot in range(8)]
    B_wo = S.bufs(2, "wo")
    S.barrier([b for l in B_h for b in l] + B_wo)
    for ot in range(8):
        S.dma("sp", lambda e, ot=ot: e.dma_start(out=hres[:, ot, :], in_=x_d[1024 + ot * 128:1024 + (ot + 1) * 128, :]), f"hres{ot}", [], B_h[ot])
    pb8 = [0]

    def npb():
        b = pb8[0] % 8
        pb8[0] += 1
        return b

    for dmc in range(4):
        sl = dmc % 2
        S.dma("pool", lambda e, dmc=dmc, sl=sl: e.dma_start(out=wo[sl].rearrange("p a b -> p (a b)"), in_=wout_d[dmc]), f"wo{sl}", [], [B_wo[sl]])
        for ot in range(8):
            pb = npb()
            for c in range(16):
                PE(lambda e, c=c, ot=ot, sl=sl, pb=pb: e.matmul(ps[pb][:], lhsT=ymixT[:, c, ot * 128:(ot + 1) * 128], rhs=wo[sl][:, c, :], start=(c == 0), stop=(c == 15)),
                   [B_ymix[c], B_wo[sl]], [PB[pb]])
            DVE(lambda e, ot=ot, dmc=dmc, pb=pb: e.tensor_tensor(out=hres[:, ot, dmc * 512:(dmc + 1) * 512], in0=ps[pb][:], in1=hres[:, ot, dmc * 512:(dmc + 1) * 512], op=ALU.add),
                [PB[pb], B_h[ot][dmc]], [B_h[ot][dmc]])
    dump("h1", hres.rearrange("p a b -> p (a b)"), [b for l in B_h for b in l])
    if stop_after == "D":
        return _finish(nc, S, st, out_bufs)

    EC = Carver(A, D_END, ARENA_BYTES, "E")
    a2T = EC.take([16, 1024], BF16)
    E_A2T_END = EC.p
    hT = EC.take([12, 1024], BF16)
    wgu = [EC.take([16, 256], BF16) for _ in range(4)]
    wdn = [EC.take([12, 512], BF16) for _ in range(2)]
    sgt = [EC.take([512]) for _ in range(2)]
    EC2 = Carver(A, 768, 768 + 2048 + 4096 + 1024 + 256 + 2048 + 2048, "Ealias")
    abf2 = [EC2.take([2048], BF16), EC2.take([2048], BF16)]
    dbg_state[0] = EC2.p
    dbg_state[1] = 512
    S.barrier([B_dbgstg])
    B_a2T = [S.buf(f"a2T{i}") for i in range(8)]
    B_hT = [[S.buf(f"hT{c}_{t}") for t in range(2)] for c in range(12)]
    B_wgu = S.bufs(4, "wgu")
    B_wdn = S.bufs(2, "wdn")
    B_sgt = S.bufs(2, "sgt")
    B_abf2 = S.bufs(2, "abf2")
    S.barrier(B_a2T + [b for l in B_hT for b in l] + B_wgu + B_wdn + B_sgt + B_abf2)

    def rms_h(ot, dst, B_dst):
        k = nstat[0] % 16
        nstat[0] += 1
        s_ = stat[:, k, :]
        sb = S.buf()
        src = hres[:, ot, :]
        ACT(lambda e: e.activation(out=dst, in_=src, func=AF.Square, accum_out=s_[:, 0:1]), B_h[ot], [B_dst, sb])
        DVE(lambda e: e.tensor_scalar(out=s_[:, 1:2], in0=s_[:, 0:1], scalar1=1.0 / 2048, scalar2=1e-6, op0=ALU.mult, op1=ALU.add), [sb], [sb])
        ACT(lambda e: e.activation(out=s_[:, 2:3], in_=s_[:, 1:2], func=AF.Sqrt), [sb], [sb])
        DVE(lambda e: e.reciprocal(out=s_[:, 3:4], in_=s_[:, 2:3]), [sb], [sb])
        DVE(lambda e: e.scalar_tensor_tensor(out=dst, in0=src, scalar=s_[:, 3:4], in1=gfull, op0=ALU.mult, op1=ALU.mult),
            [sb, B_gfull] + B_h[ot], [B_dst])

    def norm_hres(g_d, dstT, B_dstT):
        load_g(g_d)
        rms_h(0, abf2[0], B_abf2[0])
        for ot in range(8):
            sl = ot % 2
            if ot + 1 < 8:
                rms_h(ot + 1, abf2[1 - sl], B_abf2[1 - sl])
            transpose_tile(abf2[sl], B_abf2[sl], dstT, B_dstT[ot], ot * 128)

    norm_hres(g_ffn_d, a2T, B_a2T)
    gcount = [0]
    dcount = [0]
    for fq, (c0, c1) in enumerate(NQ_FF):
        ncq = c1 - c0
        for grp in range(c0 // 2, c1 // 2):
            sl = (gcount[0] % 2) * 2
            gcount[0] += 1
            S.dma("pool", lambda e, grp=grp, sl=sl: e.dma_start(out=wgu[sl].rearrange("p a b -> p (a b)"), in_=wg_d[grp]), f"wgu{sl}", [], [B_wgu[sl]])
            S.dma("pool", lambda e, grp=grp, sl=sl: e.dma_start(out=wgu[sl + 1].rearrange("p a b -> p (a b)"), in_=wu_d[grp]), f"wgu{sl + 1}", [], [B_wgu[sl + 1]])
            for half in range(2):
                ci = grp * 2 + half - c0
                for th2 in range(2):
                    pg, pu = npb(), npb()
                    for dc in range(16):
                        PE(lambda e, dc=dc, sl=sl, half=half, th2=th2, pg=pg: e.matmul(ps[pg][:], lhsT=wgu[sl][:, dc, half * 128:(half + 1) * 128], rhs=a2T[:, dc, th2 * 512:(th2 + 1) * 512],
                                                                                     start=(dc == 0), stop=(dc == 15)), [B_wgu[sl]] + B_a2T[th2 * 4:th2 * 4 + 4], [PB[pg]])
                    for dc in range(16):
                        PE(lambda e, dc=dc, sl=sl, half=half, th2=th2, pu=pu: e.matmul(ps[pu][:], lhsT=wgu[sl + 1][:, dc, half * 128:(half + 1) * 128], rhs=a2T[:, dc, th2 * 512:(th2 + 1) * 512],
                                                                                     start=(dc == 0), stop=(dc == 15)), [B_wgu[sl + 1]] + B_a2T[th2 * 4:th2 * 4 + 4], [PB[pu]])
                    ss = th2
                    ACT(lambda e, pg=pg, ss=ss: e.activation(out=sgt[ss], in_=ps[pg][:], func=AF.Silu), [PB[pg]], [B_sgt[ss]])
                    DVE(lambda e, pu=pu, ss=ss, ci=ci, th2=th2: e.tensor_tensor(out=hT[:, ci, th2 * 512:(th2 + 1) * 512], in0=ps[pu][:], in1=sgt[ss], op=ALU.mult),
                        [PB[pu], B_sgt[ss]], [B_hT[ci][th2]])
        for dmc in range(4):
            sl = dcount[0] % 2
            dcount[0] += 1
            S.dma("pool", lambda e, fq=fq, dmc=dmc, sl=sl: e.dma_start(out=wdn[sl].rearrange("p a b -> p (a b)"), in_=wd_d[fq * 4 + dmc]), f"wdn{sl}", [], [B_wdn[sl]])
            for ot in range(8):
                pb = npb()
                for ci in range(ncq):
                    PE(lambda e, ci=ci, ot=ot, sl=sl, pb=pb, ncq=ncq: e.matmul(ps[pb][:], lhsT=hT[:, ci, ot * 128:(ot + 1) * 128], rhs=wdn[sl][:, ci, :], start=(ci == 0), stop=(ci == ncq - 1)),
                       [B_hT[ci][ot // 4], B_wdn[sl]], [PB[pb]])
                DVE(lambda e, ot=ot, dmc=dmc, pb=pb: e.tensor_tensor(out=hres[:, ot, dmc * 512:(dmc + 1) * 512], in0=ps[pb][:], in1=hres[:, ot, dmc * 512:(dmc + 1) * 512], op=ALU.add),
                    [PB[pb], B_h[ot][dmc]], [B_h[ot][dmc]])
    dump("h2", hres.rearrange("p a b -> p (a b)"), [b for l in B_h for b in l])
    if stop_after == "E":
        return _finish(nc, S, st, out_bufs)

    FC = Carver(A, E_A2T_END, ARENA_BYTES - 8192, "F")
    a3T = a2T
    B_a3T = B_a2T
    wpg = [FC.take([16, 512], BF16) for _ in range(2)]
    wpp = [FC.take([2, 512], BF16) for _ in range(2)]
    pT = FC.take([2, 1024], BF16)
    pf = [FC.take([256]) for _ in range(2)]
    pbf = [FC.take([256], BF16) for _ in range(2)]
    gt = [FC.take([512]) for _ in range(2)]
    outt = [FC.take([2048]) for _ in range(2)]
    B_wpg = S.bufs(2, "wpg")
    B_wpp = S.bufs(2, "wpp")
    B_pT = S.bufs(8, "pT")
    B_pf = S.bufs(2, "pf")
    B_pbf = S.bufs(2, "pbf")
    B_gt = S.bufs(2, "gt")
    B_outt = S.bufs(2, "outt")
    S.barrier(B_wpg + B_wpp + B_pT + B_pf + B_pbf + B_gt + B_outt)
    norm_hres(g_ple_d, a3T, B_a3T)
    for ot in range(8):
        sl = ot % 2
        S.dma("sp", lambda e, ot=ot, sl=sl: e.dma_start(out=pf[sl], in_=p_d[ot * 128:(ot + 1) * 128, :]), f"pf{sl}", [], [B_pf[sl]])
        ACT(lambda e, sl=sl: e.activation(out=pbf[sl], in_=pf[sl], func=AF.Copy), [B_pf[sl]], [B_pbf[sl]])
        for c2 in range(2):
            PE(lambda e, c2=c2, sl=sl: e.transpose(out=psb[7][:, c2 * 128:(c2 + 1) * 128], in_=pbf[sl][:, c2 * 128:(c2 + 1) * 128], identity=ident), [B_pbf[sl], B_const], [PB[7]])
        DVE(lambda e, ot=ot: e.tensor_copy(out=pT[:, :, ot * 128:(ot + 1) * 128], in_=psb[7][:, 0:256].rearrange("p (a b) -> p a b", a=2)), [PB[7]], [B_pT[ot]])
    pb7 = [0]

    def npb7():
        b = pb7[0] % 7
        pb7[0] += 1
        return b

    for dmc in range(4):
        sl = dmc % 2
        S.dma("pool", lambda e, dmc=dmc, sl=sl: e.dma_start(out=wpg[sl].rearrange("p a b -> p (a b)"), in_=wpg_d[dmc]), f"wpg{sl}", [], [B_wpg[sl]])
        S.dma("pool", lambda e, dmc=dmc, sl=sl: e.dma_start(out=wpp[sl].rearrange("p a b -> p (a b)"), in_=wpp_d[dmc]), f"wpp{sl}", [], [B_wpp[sl]])
        for ot in range(8):
            p1, p2 = npb7(), npb7()
            for dc in range(16):
                PE(lambda e, dc=dc, ot=ot, sl=sl, p1=p1: e.matmul(ps[p1][:], lhsT=a3T[:, dc, ot * 128:(ot + 1) * 128], rhs=wpg[sl][:, dc, :], start=(dc == 0), stop=(dc == 15)),
                   [B_a3T[ot], B_wpg[sl]], [PB[p1]])
            for c2 in range(2):
                PE(lambda e, c2=c2, ot=ot, sl=sl, p2=p2: e.matmul(ps[p2][:], lhsT=pT[:, c2, ot * 128:(ot + 1) * 128], rhs=wpp[sl][:, c2, :], start=(c2 == 0), stop=(c2 == 1)),
                   [B_pT[ot], B_wpp[sl]], [PB[p2]])
            gs = (dmc * 8 + ot) % 2
            ACT(lambda e, p1=p1, gs=gs: e.activation(out=gt[gs], in_=ps[p1][:], func=AF.Sigmoid), [PB[p1]], [B_gt[gs]])
            DVE(lambda e, p2=p2, gs=gs: e.tensor_tensor(out=gt[gs], in0=ps[p2][:], in1=gt[gs], op=ALU.mult), [PB[p2], B_gt[gs]], [B_gt[gs]])
            DVE(lambda e, ot=ot, dmc=dmc, gs=gs: e.tensor_tensor(out=hres[:, ot, dmc * 512:(dmc + 1) * 512], in0=gt[gs], in1=hres[:, ot, dmc * 512:(dmc + 1) * 512], op=ALU.add),
                [B_gt[gs], B_h[ot][dmc]], [B_h[ot][dmc]])
    load_g(g_fin_d)
    for ot in range(8):
        sl = ot % 2
        rms_h(ot, outt[sl], B_outt[sl])
        ob = S.buf()
        S.dma("sp", lambda e, ot=ot, sl=sl: e.dma_start(out=out_d[ot * 128:(ot + 1) * 128, :], in_=outt[sl]), f"ost{sl}", [B_outt[sl]], [ob])
        out_bufs.append(ob)
    return _finish(nc, S, st, out_bufs)


def _finish(nc, S, st, out_bufs):
    S.op("sp", lambda e: e.nop(), out_bufs, [])
    S.emit(nc, st)
    st.close()
    return nc, S


def _tile_cols(W, c0, n):
    return np.ascontiguousarray(W[:, c0:c0 + n].reshape(16, 128, n).transpose(1, 0, 2).reshape(128, 16 * n))


def _consts(th):
    f32 = np.float32
    L = np.arange(2048)
    tg = L - 1024 + 1024 * th
    pos = np.maximum(tg, 0).astype(f32)
    inv_freq = (f32(500000.0) ** (-np.arange(0, 32, 2, dtype=f32) / f32(32))).astype(f32)
    ang = (pos[:, None] * inv_freq[None, :]).astype(f32)
    cos, sin = np.cos(ang).astype(f32), np.sin(ang).astype(f32)
    cos2 = np.concatenate([cos, cos], 1)
    sin2 = np.concatenate([-sin, sin], 1)

    def tokmaj(a):
        w = a.shape[1]
        return np.ascontiguousarray(a.reshape(16, 128, w).transpose(1, 0, 2).reshape(128, 16 * w))

    n = np.arange(128)
    Lo = 1024 + np.arange(1024)
    cvalid = ((16 * n[:, None] + 31) <= Lo[None, :]) & ((th == 1) | (n[:, None] >= 64))
    s = np.arange(32)
    overlap = ((16 * n[:, None]) < (64 * s[None, :] + 64)) & ((16 * n[:, None] + 32) > 64 * s[None, :])
    overlap = overlap & (n[:, None] < 127)
    cur = Lo // 64
    valid = (s[None, :] <= cur[:, None]) & ((th == 1) | (s[None, :] >= 16))
    forced = (s[None, :] == cur[:, None]) | (s[None, :] == cur[:, None] - 1) | (s[None, :] == 16 * (1 - th))
    selbias = np.where(valid, np.where(forced, BIG, 0.0), -BIG).astype(f32)
    selbias = np.ascontiguousarray(selbias.reshape(8, 128, 32).transpose(1, 0, 2).reshape(128, 256))
    k = np.arange(128)
    E = np.zeros((128, 16, 128), f32)
    for kt in range(16):
        E[2 * kt + k // 64, kt, k] = 30000.0
    ident = np.eye(128, dtype=f32)
    diag = (k[:, None] <= k[None, :]).astype(f32)
    upst = (k[:, None] > k[None, :]).astype(f32)
    tri = np.concatenate([ident, diag, upst], 1)
    wvalid = np.zeros((8, 5), f32)
    for ot in range(8):
        for off in range(5):
            kt = 8 + ot - 4 + off
            wvalid[ot, off] = 1.0 if (th == 1 or kt >= 8) else 0.0
    wvalid = np.broadcast_to(wvalid.reshape(1, 40), (128, 40))
    invc = np.zeros((4, 16), f32)
    for gi in range(4):
        w = 2 << gi
        for i in range(16):
            invc[gi, i] = 1.0 / w if th == 1 else 1.0 / min(i + 1, w)
    invc = np.broadcast_to(invc.reshape(1, 64), (128, 64))
    c = lambda a: np.ascontiguousarray(a, dtype=f32)
    return {
        "c_cos": tokmaj(cos2), "c_sin": tokmaj(sin2), "c_cvalid": c(cvalid), "c_overlap": c(overlap), "c_selbias": c(selbias),
        "c_E": c(E.reshape(128, 2048)), "c_tri": c(tri), "c_wvalid": c(wvalid), "c_invc": c(invc),
    }


def _prep_shared(w_in, pool_w, pool_scale, cmp_k_pe, cmp_k_w1, cmp_k_w2, cmp_v_pe, cmp_v_w1, cmp_v_w2, w_out,
                 w_gate, w_up, w_down, w_ple_gate, w_ple_proj, in_norm_g, ffn_norm_g, ple_norm_g, final_norm_g):
    c = lambda a: np.ascontiguousarray(a, dtype=np.float32)
    d = {}
    w_in = w_in[0]
    d["wU"] = np.stack([_tile_cols(w_in, i * 256, 256) for i in range(4)])
    d["wT"] = np.stack([_tile_cols(w_in, 1024 + i * 256, 256) for i in range(10)])
    d["wG"] = _tile_cols(w_in, 3584, 24)
    d["poolw"] = c(pool_w[0].reshape(4, 2, 128, 256).transpose(2, 0, 1, 3).reshape(128, 2048))
    d["pscale"] = c(pool_scale[0].reshape(8, 128).T)
    d["w1k"] = c(cmp_k_w1[0].reshape(32, 128, 128).transpose(1, 0, 2).reshape(128, 4096))
    d["w1v"] = c(cmp_v_w1[0].reshape(32, 128, 128).transpose(1, 0, 2).reshape(128, 4096))
    d["w2k"] = c(cmp_k_w2[0])
    d["w2v"] = c(cmp_v_w2[0])
    d["pekT"] = c(cmp_k_pe[0].T)
    d["pevT"] = c(cmp_v_pe[0].T)
    d["wout"] = np.stack([_tile_cols(w_out[0], i * 512, 512) for i in range(4)])
    d["wgate"] = np.stack([_tile_cols(w_gate[0], i * 256, 256) for i in range(22)])
    d["wup"] = np.stack([_tile_cols(w_up[0], i * 256, 256) for i in range(22)])
    wd = np.zeros((16, 128, 12 * 512), np.float32)
    for fq, (c0, c1) in enumerate(NQ_FF):
        for dmc in range(4):
            blk = w_down[0][c0 * 128:c1 * 128, dmc * 512:(dmc + 1) * 512].reshape(c1 - c0, 128, 512).transpose(1, 0, 2)
            wd[fq * 4 + dmc, :, :(c1 - c0) * 512] = blk.reshape(128, -1)
    d["wdown"] = wd
    d["wpg"] = np.stack([_tile_cols(w_ple_gate[0], i * 512, 512) for i in range(4)])
    wpp = w_ple_proj[0]
    d["wpp"] = np.stack([c(wpp[:, i * 512:(i + 1) * 512].reshape(2, 128, 512).transpose(1, 0, 2).reshape(128, 1024)) for i in range(4)])
    d["g_in"] = c(in_norm_g[0])
    d["g_ffn"] = c(ffn_norm_g[0])
    d["g_ple"] = c(ple_norm_g[0])
    d["g_fin"] = c(final_norm_g)
    return d


def make_in_maps(x, p, **w):
    shared = _prep_shared(**w)
    cst = [_consts(0), _consts(1)]
    in_maps = []
    for b in range(4):
        for th in range(2):
            m = dict(shared)
            m.update(cst[th])
            if th == 1:
                xl = x[b]
            else:
                xl = np.concatenate([np.zeros((1024, 2048), np.float32), x[b, :1024]], 0)
            m["x"] = np.ascontiguousarray(xl, dtype=np.float32)
            m["p"] = np.ascontiguousarray(p[0, b, th * 1024:(th + 1) * 1024], dtype=np.float32)
            in_maps.append(m)
    return in_maps


_NC_CACHE = {}


def kernel(x, p, in_norm_g, w_in, pool_w, pool_scale, cmp_k_pe, cmp_k_w1, cmp_k_w2, cmp_v_pe, cmp_v_w1, cmp_v_w2,
           w_out, ffn_norm_g, w_gate, w_up, w_down, ple_norm_g, w_ple_gate, w_ple_proj, final_norm_g):
    args = dict(locals())
    args = {k: np.asarray(v) for k, v in args.items()}
    x = args.pop("x")
    p = args.pop("p")
    in_maps = make_in_maps(x, p, **args)
    if "nc" not in _NC_CACHE:
        _NC_CACHE["nc"] = build()[0]
    nc = _NC_CACHE["nc"]
    res = run_bass_kernel_spmd(nc, in_maps, core_ids=list(range(8)))
    out = np.zeros((4, 2048, 2048), np.float32)
    for b in range(4):
        for th in range(2):
            out[b, th * 1024:(th + 1) * 1024] = res.results[b * 2 + th]["out"]
    return out
```

```python
import numpy as np
from contextlib import ExitStack
import concourse.bass as bass
import concourse.mybir as mybir
from concourse.bass_utils import run_bass_kernel_spmd

F32 = mybir.dt.float32
BF16 = mybir.dt.bfloat16
AF = mybir.ActivationFunctionType
ALU = mybir.AluOpType
AX = mybir.AxisListType

ENGS = ("pe", "act", "dve", "pool", "sp")
BIG = 1.0e30
SCALE = 128.0 ** -0.5


class Buf:
    __slots__ = ("name", "last_w", "readers", "excl")

    def __init__(self, name, excl=False):
        self.name = name
        self.last_w = None
        self.readers = []
        self.excl = excl


class Op:
    __slots__ = ("eng", "fn", "deps", "signal", "sigval", "dma_key", "dma_val", "idx")


class Sched:
    def __init__(self):
        self.ops = {e: [] for e in ENGS}
        self.dma_cnt = {}
        self.nbuf = 0

    def buf(self, name=None):
        self.nbuf += 1
        return Buf(name or f"b{self.nbuf}")

    def bufs(self, n, name="b"):
        return [self.buf(f"{name}{i}") for i in range(n)]

    def _add(self, eng, fn, reads, writes, dma_key=None):
        op = Op()
        op.eng = eng
        op.fn = fn
        op.signal = False
        op.sigval = None
        op.dma_key = dma_key
        op.dma_val = None
        op.idx = len(self.ops[eng])
        deps = []
        for b in reads:
            if b.last_w is not None:
                deps.append(b.last_w)
            if b.excl:
                deps.extend(t for t in b.readers if t[1] != eng)
        for b in writes:
            if b.last_w is not None:
                deps.append(b.last_w)
            deps.extend(b.readers)
        if dma_key is not None:
            self.dma_cnt[dma_key] = self.dma_cnt.get(dma_key, 0) + 16
            op.dma_val = self.dma_cnt[dma_key]
            tok = ("dma", dma_key, op.dma_val)
        else:
            tok = ("eng", eng, op.idx)
        op.deps = [d for d in set(deps)
                   if not (d[0] == "eng" and d[1] == "pe" and eng == "pe" and dma_key is None)]
        for b in reads:
            b.readers.append(tok)
        for b in writes:
            b.last_w = tok
            b.readers = []
        self.ops[eng].append(op)
        return op

    def op(self, eng, fn, reads=(), writes=()):
        return self._add(eng, fn, list(reads), list(writes))

    def dma(self, eng, fn, key, reads=(), writes=()):
        return self._add(eng, fn, list(reads), list(writes), dma_key=key)

    def barrier(self, bufs):
        toks = []
        for e in ENGS:
            for o in reversed(self.ops[e]):
                if o.dma_key is None:
                    toks.append(("eng", e, o.idx))
                    break
        for k, v in self.dma_cnt.items():
            toks.append(("dma", k, v))
        for b in bufs:
            b.readers.extend(toks)

    def emit(self, nc, stack):
        for e in ENGS:
            for o in self.ops[e]:
                for d in o.deps:
                    if d[0] == "eng":
                        self.ops[d[1]][d[2]].signal = True
        esem = {}
        for e in ENGS:
            c = 0
            for o in self.ops[e]:
                if o.signal and o.dma_key is None:
                    c += 1
                    o.sigval = c
            if c > 0:
                esem[e] = stack.enter_context(nc.semaphore(f"s_{e}"))
        dsem = {k: stack.enter_context(nc.semaphore(f"d_{k}")) for k in self.dma_cnt}
        self.nsem = len(esem) + len(dsem)
        block = stack.enter_context(nc.Block())
        ops = self.ops
        stats = {}

        def run(e, engobj):
            waited = {}
            nw = 0
            for o in ops[e]:
                need = {}
                for d in o.deps:
                    if d[0] == "eng":
                        sem = esem[d[1]]
                        val = ops[d[1]][d[2]].sigval
                        key = ("e", d[1])
                    else:
                        sem = dsem[d[1]]
                        val = d[2]
                        key = ("d", d[1])
                    if need.get(key, (None, -1))[1] < val:
                        need[key] = (sem, val)
                for key, (sem, val) in need.items():
                    if waited.get(key, -1) >= val:
                        continue
                    waited[key] = val
                    engobj.wait_ge(sem, val)
                    nw += 1
                ins = o.fn(engobj)
                if o.dma_key is not None:
                    ins.then_inc(dsem[o.dma_key], 16)
                elif o.signal:
                    ins.then_inc(esem[e], 1)
            stats[e] = (len(ops[e]), nw)

        if ops["sp"]:
            block.sync(lambda eng: run("sp", eng))
        if ops["pe"]:
            block.tensor(lambda eng: run("pe", eng))
        if ops["act"]:
            block.scalar(lambda eng: run("act", eng))
        if ops["dve"]:
            block.vector(lambda eng: run("dve", eng))
        if ops["pool"]:
            block.gpsimd(lambda eng: run("pool", eng))
        self.stats = stats


class Arena:
    def __init__(self, nc, nbytes):
        self.nbytes = nbytes
        self.t = nc.alloc_sbuf_tensor("arena", [128, nbytes // 4], F32)

    def view(self, off, shape, dtype=F32, parts=128):
        shape = list(shape)
        n = int(np.prod(shape))
        nb = n * (2 if dtype == BF16 else 4)
        assert off % 4 == 0 and nb % 4 == 0, (off, nb)
        assert off + nb <= self.nbytes, ("arena overflow", off, nb)
        ap = self.t[0:parts, off // 4:(off + nb) // 4]
        if dtype == BF16:
            ap = ap.bitcast(BF16)
        if len(shape) == 2:
            ap = ap.rearrange("p (a b) -> p a b", a=shape[0])
        elif len(shape) == 3:
            ap = ap.rearrange("p (a b c) -> p a b c", a=shape[0], b=shape[1])
        elif len(shape) == 4:
            ap = ap.rearrange("p (a b c d) -> p a b c d", a=shape[0], b=shape[1], c=shape[2])
        return ap


class Carver:
    def __init__(self, arena, start, end, name=""):
        self.a = arena
        self.p = start
        self.end = end
        self.name = name

    def take(self, shape, dtype=F32, parts=128):
        n = int(np.prod(shape)) * (2 if dtype == BF16 else 4)
        n = (n + 31) // 32 * 32
        off = self.p
        self.p += n
        assert self.p <= self.end, ("carver overflow", self.name, self.p, self.end)
        return self.a.view(off, shape, dtype, parts)


ARENA_BYTES = 212000
NQ_FF = [(0, 12), (12, 22), (22, 34), (34, 44)]


def build(stop_after=None, dbg=()):
    nc = bass.Bass("TRN2", target_bir_lowering=False)

    def din(name, shape):
        return nc.dram_tensor(name, list(shape), F32, kind="ExternalInput").ap()

    x_d = din("x", [2048, 2048])
    p_d = din("p", [1024, 256])
    g_in_d = din("g_in", [2048])
    g_ffn_d = din("g_ffn", [2048])
    g_ple_d = din("g_ple", [2048])
    g_fin_d = din("g_fin", [2048])
    wU_d = din("wU", [4, 128, 16 * 256])
    wT_d = din("wT", [10, 128, 16 * 256])
    wG_d = din("wG", [128, 16 * 24])
    poolw_d = din("poolw", [128, 4 * 2 * 256])
    pscale_d = din("pscale", [128, 8])
    w1k_d = din("w1k", [128, 32 * 128])
    w1v_d = din("w1v", [128, 32 * 128])
    w2k_d = din("w2k", [128, 128])
    w2v_d = din("w2v", [128, 128])
    pek_d = din("pekT", [128, 32])
    pev_d = din("pevT", [128, 32])
    wout_d = din("wout", [4, 128, 16 * 512])
    wg_d = din("wgate", [22, 128, 16 * 256])
    wu_d = din("wup", [22, 128, 16 * 256])
    wd_d = din("wdown", [16, 128, 12 * 512])
    wpg_d = din("wpg", [4, 128, 16 * 512])
    wpp_d = din("wpp", [4, 128, 2 * 512])
    c_cos_d = din("c_cos", [128, 16 * 32])
    c_sin_d = din("c_sin", [128, 16 * 32])
    c_cvalid_d = din("c_cvalid", [128, 1024])
    c_overlap_d = din("c_overlap", [128, 32])
    c_selbias_d = din("c_selbias", [128, 8 * 32])
    c_E_d = din("c_E", [128, 16 * 128])
    c_tri_d = din("c_tri", [128, 3 * 128])
    c_wvalid_d = din("c_wvalid", [128, 40])
    c_invc_d = din("c_invc", [128, 4 * 16])
    out_d = nc.dram_tensor("out", [1024, 2048], F32, kind="ExternalOutput").ap()
    dbg_d = {}
    for name, shape in dbg:
        dbg_d[name] = nc.dram_tensor("dbg_" + name, list(shape), F32, kind="ExternalOutput").ap()

    S = Sched()
    st = ExitStack()
    A = Arena(nc, ARENA_BYTES)
    ps = [st.enter_context(nc.psum_tensor(f"ps{i}", [128, 512], F32)) for i in range(8)]
    psb = [p[:].bitcast(BF16) for p in ps]
    PB = [Buf(f"psum{i}", excl=True) for i in range(8)]

    def PE(fn, r=(), w=()):
        return S.op("pe", fn, r, w)

    def ACT(fn, r=(), w=()):
        return S.op("act", fn, r, w)

    def DVE(fn, r=(), w=()):
        return S.op("dve", fn, r, w)

    def POOL(fn, r=(), w=()):
        return S.op("pool", fn, r, w)

    out_bufs = []

    dbg_state = [ARENA_BYTES - 8192, 2048]
    B_dbgstg = S.buf("dbgstg")

    def dump(name, ap, buf, parts=128):
        if name not in dbg_d:
            return
        d = dbg_d[name]
        n = ap.shape[1]
        stg = A.view(dbg_state[0], [dbg_state[1]], F32)
        bl = buf if isinstance(buf, list) else [buf]
        for c0 in range(0, n, dbg_state[1]):
            c1 = min(n, c0 + dbg_state[1])
            ACT(lambda e, c0=c0, c1=c1: e.activation(out=stg[0:parts, 0:c1 - c0], in_=ap[:, c0:c1], func=AF.Copy), bl, [B_dbgstg])
            db = S.buf("dbgd_" + name)
            S.dma("sp", lambda e, c0=c0, c1=c1: e.dma_start(out=d[0:parts, c0:c1], in_=stg[0:parts, 0:c1 - c0]), "dbg", [B_dbgstg], [db])
            out_bufs.append(db)

    CONST_END = 26624
    CC = Carver(A, 0, CONST_END, "const")
    tri = CC.take([3, 128], BF16)
    ident = tri[:, 0, :]
    diag = tri[:, 1, :]
    upst = tri[:, 2, :]
    cvalid = CC.take([1024], BF16)
    Emat = CC.take([16, 128], BF16)
    selbias = CC.take([8, 32])
    invc = CC.take([4, 16])
    cos2 = CC.take([16, 32])
    sin2 = CC.take([16, 32])
    wvalid = CC.take([40])
    gsig = CC.take([8, 24])
    pscale = CC.take([8])
    gfull = CC.take([2048])
    stat = CC.take([16, 8])
    B_const = S.buf("const")
    B_gfull = S.buf("gfull")
    B_gsig = S.bufs(8, "gsig")

    cbufs = S.bufs(9, "cst")
    S.dma("sp", lambda e: e.dma_start(out=selbias.rearrange("p a b -> p (a b)"), in_=c_selbias_d), "cst", [], [cbufs[0]])
    S.dma("sp", lambda e: e.dma_start(out=invc.rearrange("p a b -> p (a b)"), in_=c_invc_d), "cst", [], [cbufs[1]])
    S.dma("sp", lambda e: e.dma_start(out=cos2.rearrange("p a b -> p (a b)"), in_=c_cos_d), "cst", [], [cbufs[2]])
    S.dma("sp", lambda e: e.dma_start(out=sin2.rearrange("p a b -> p (a b)"), in_=c_sin_d), "cst", [], [cbufs[3]])
    S.dma("sp", lambda e: e.dma_start(out=wvalid, in_=c_wvalid_d), "cst", [], [cbufs[4]])
    S.dma("sp", lambda e: e.dma_start(out=pscale, in_=pscale_d), "cst", [], [cbufs[5]])
    S.dma("pool", lambda e: e.dma_start(out=tri.rearrange("p a b -> p (a b)"), in_=c_tri_d), "cstp", [], [cbufs[6]])
    S.dma("pool", lambda e: e.dma_start(out=cvalid, in_=c_cvalid_d), "cstp", [], [cbufs[7]])
    S.dma("pool", lambda e: e.dma_start(out=Emat.rearrange("p a b -> p (a b)"), in_=c_E_d), "cstp", [], [cbufs[8]])
    POOL(lambda e: e.memset(stat[:, 15, 7:8], 0.0), cbufs, [B_const])

    def load_g(g_d):
        S.dma("sp", lambda e: e.dma_start(out=gfull, in_=g_d.partition_broadcast(128)), "gld", [], [B_gfull])

    PC = Carver(A, CONST_END, ARENA_BYTES, "persist")
    qT = PC.take([2, 8, 4, 128], BF16)
    kT = PC.take([3, 2, 2048], BF16)
    vcT = PC.take([2, 2048], BF16)
    v1 = PC.take([4, 16, 132], BF16)
    YMIX_OFF = PC.p
    ymixT = PC.take([16, 1024], BF16)
    uhalo = PC.take([8, 16])
    TRANS0 = PC.p
    B_qT = [[S.buf(f"qT{g}_{o}") for o in range(8)] for g in range(2)]
    B_kT = [[[S.buf(f"kT{k}_{g}_{l}") for l in range(16)] for g in range(2)] for k in range(3)]
    B_vcT = [[S.buf(f"vcT{g}_{l}") for l in range(16)] for g in range(2)]
    B_v1 = [[S.buf(f"v1{k}_{l}") for l in range(16)] for k in range(4)]
    B_ymix = [S.buf(f"ymix{c}") for c in range(16)]
    B_uhalo = S.buf("uhalo")
    B_v1init = S.buf("v1init")
    POOL(lambda e: e.memset(v1[:, :, :, 128:132], 1.0), [], [B_v1init])

    nstat = [0]

    def rms_to_bf16(src, B_src, dst_bf, B_dst, junk, B_junk):
        k = nstat[0] % 16
        nstat[0] += 1
        s = stat[:, k, :]
        sb = S.buf()
        ACT(lambda e: e.activation(out=junk, in_=src, func=AF.Square, accum_out=s[:, 0:1]), [B_src], [B_junk, sb])
        DVE(lambda e: e.tensor_scalar(out=s[:, 1:2], in0=s[:, 0:1], scalar1=1.0 / 2048, scalar2=1e-6, op0=ALU.mult, op1=ALU.add), [sb], [sb])
        ACT(lambda e: e.activation(out=s[:, 2:3], in_=s[:, 1:2], func=AF.Sqrt), [sb], [sb])
        DVE(lambda e: e.reciprocal(out=s[:, 3:4], in_=s[:, 2:3]), [sb], [sb])
        DVE(lambda e: e.scalar_tensor_tensor(out=dst_bf, in0=src, scalar=s[:, 3:4], in1=gfull, op0=ALU.mult, op1=ALU.mult),
            [sb, B_src, B_gfull], [B_dst])

    def transpose_tile(a_bf, B_a, aT_dst, B_aT, col0):
        for hb in range(2):
            bank = 6 + hb
            for j in range(8):
                dc = hb * 8 + j
                PE(lambda e, dc=dc, j=j, bank=bank: e.transpose(out=psb[bank][:, j * 128:(j + 1) * 128], in_=a_bf[:, dc * 128:(dc + 1) * 128], identity=ident),
                   [B_a, B_const], [PB[bank]])
            eng = ACT if hb == 0 else DVE
            if hb == 0:
                ACT(lambda e, bank=bank, hb=hb: e.activation(out=aT_dst[:, hb * 8:(hb + 1) * 8, col0:col0 + 128],
                                                           in_=psb[bank].rearrange("p (a b) -> p a b", a=8), func=AF.Copy), [PB[bank]], [B_aT])
            else:
                DVE(lambda e, bank=bank, hb=hb: e.tensor_copy(out=aT_dst[:, hb * 8:(hb + 1) * 8, col0:col0 + 128],
                                                            in_=psb[bank].rearrange("p (a b) -> p a b", a=8)), [PB[bank]], [B_aT])

    TC = Carver(A, TRANS0, ARENA_BYTES - 8192, "AB")
    aT = TC.take([16, 1024], BF16)
    wsl = [TC.take([16, 256], BF16) for _ in range(2)]
    AB_SHARED = TC.p
    xt = [TC.take([2048]) for _ in range(2)]
    if not dbg:
        xt.append(A.view(ARENA_BYTES - 8192, [2048], F32))
    NXT = len(xt)
    abf = [TC.take([2048], BF16) for _ in range(2)]
    stage = [TC.take([2, 128], BF16) for _ in range(2)]
    rtmp = [TC.take([2, 32]) for _ in range(2)]
    rtmp2 = [TC.take([2, 32]) for _ in range(2)]
    wgates = TC.take([16, 24], BF16)
    B_aT = [S.buf(f"aT{i}") for i in range(8)]
    B_xt = S.bufs(3, "xt")
    B_abf = S.bufs(2, "abf")
    B_wsl = S.bufs(2, "wsl")
    B_stage = S.bufs(2, "stage")
    B_rtmp = S.bufs(2, "rtmp")
    B_wgates = S.buf("wgates")
    wcount = [0]
    scount = [0]

    load_g(g_in_d)
    S.dma("pool", lambda e: e.dma_start(out=wgates.rearrange("p a b -> p (a b)"), in_=wG_d), "wgl", [], [B_wgates])

    def load_w(src_ap):
        i = wcount[0] % 2
        wcount[0] += 1
        S.dma("pool", lambda e: e.dma_start(out=wsl[i].rearrange("p a b -> p (a b)"), in_=src_ap), f"wsl{i}", [], [B_wsl[i]])
        return i

    def norm_pass(pas):
        def dma(i):
            lt = pas * 8 + i
            xs = lt % NXT
            S.dma("sp", lambda e, lt=lt, xs=xs: e.dma_start(out=xt[xs], in_=x_d[lt * 128:(lt + 1) * 128, :]), f"xt{xs}", [], [B_xt[xs]])

        def s1(i):
            lt = pas * 8 + i
            xs = lt % NXT
            sl = lt % 2
            rms_to_bf16(xt[xs], B_xt[xs], abf[sl], B_abf[sl], abf[sl], B_abf[sl])

        def s2(i):
            sl = (pas * 8 + i) % 2
            transpose_tile(abf[sl], B_abf[sl], aT, B_aT[i], i * 128)
        pre = NXT - 1
        for i in range(pre):
            dma(i)
        s1(0)
        for i in range(8):
            if i + pre < 8:
                dma(i + pre)
            if i + 1 < 8:
                s1(i + 1)
            s2(i)

    def rope(pbank, stg, B_stg, lt, sl):
        src = ps[pbank][:, 0:256].rearrange("p (h d) -> p h d", h=2)
        c2 = cos2[:, lt, :].unsqueeze(1).to_broadcast([128, 2, 32])
        sA = sin2[:, lt, 0:16].unsqueeze(1).to_broadcast([128, 2, 16])
        sB = sin2[:, lt, 16:32].unsqueeze(1).to_broadcast([128, 2, 16])
        t1 = rtmp[sl]
        t2 = rtmp2[sl]
        DVE(lambda e: e.tensor_tensor(out=t1, in0=src[:, :, 0:32], in1=c2, op=ALU.mult), [PB[pbank], B_const], [B_rtmp[sl]])
        DVE(lambda e: e.tensor_tensor(out=t2[:, :, 0:16], in0=src[:, :, 16:32], in1=sA, op=ALU.mult), [PB[pbank], B_const], [B_rtmp[sl]])
        DVE(lambda e: e.tensor_tensor(out=t2[:, :, 16:32], in0=src[:, :, 0:16], in1=sB, op=ALU.mult), [PB[pbank], B_const], [B_rtmp[sl]])
        DVE(lambda e: e.tensor_tensor(out=stg[:, :, 0:32], in0=t1, in1=t2, op=ALU.add), [B_rtmp[sl]], [B_stg])

    def tok_half(hg, pas, i, wi, pbank):
        lt = pas * 8 + i
        for dc in range(16):
            PE(lambda e, dc=dc: e.matmul(ps[pbank][:, 0:256], lhsT=aT[:, dc, i * 128:(i + 1) * 128], rhs=wsl[wi][:, dc, :], start=(dc == 0), stop=(dc == 15)),
               [B_aT[i], B_wsl[wi]], [PB[pbank]])

        def evac():
            src = ps[pbank][:, 0:256].rearrange("p (h d) -> p h d", h=2)
            if hg in (7, 9):
                vk = 0 if hg == 7 else 2
                ACT(lambda e: e.activation(out=v1[:, vk:vk + 2, lt, 0:128], in_=src, func=AF.Copy), [PB[pbank], B_v1init], [B_v1[vk][lt], B_v1[vk + 1][lt]])
                return
            sl = scount[0] % 2
            scount[0] += 1
            stg = stage[sl]
            tb = 6 + (scount[0] % 2)
            ACT(lambda e: e.activation(out=stg, in_=src, func=AF.Copy), [PB[pbank]], [B_stage[sl]])
            if hg != 5:
                rope(pbank, stg, B_stage[sl], lt, sl)
            for r in range(2):
                PE(lambda e, r=r: e.transpose(out=psb[tb][:, r * 128:(r + 1) * 128], in_=stg[:, r, :], identity=ident), [B_stage[sl], B_const], [PB[tb]])
            pin = psb[tb][:, 0:256].rearrange("p (g t) -> p g t", g=2)
            if hg < 4:
                g, r0, ot = hg // 2, (hg % 2) * 2, i
                DVE(lambda e: e.tensor_copy(out=qT[:, g, ot, r0:r0 + 2, :], in_=pin), [PB[tb]], [B_qT[g][ot]])
            elif hg == 5:
                DVE(lambda e: e.tensor_copy(out=vcT[:, :, lt * 128:(lt + 1) * 128], in_=pin), [PB[tb]], [B_vcT[0][lt], B_vcT[1][lt]])
            else:
                kind = (hg - 4) // 2
                DVE(lambda e: e.tensor_copy(out=kT[:, kind, :, lt * 128:(lt + 1) * 128], in_=pin), [PB[tb]], [B_kT[kind][0][lt], B_kT[kind][1][lt]])
        return evac

    pbrot = [0]

    def next_pb():
        b = pbrot[0] % 6
        pbrot[0] += 1
        return b

    if stop_after == "C0":
        dump("gfull", gfull, [B_gfull, B_const])
        return _finish(nc, S, st, out_bufs)
    for pas in range(2):
        norm_pass(pas)
        if stop_after == "N0":
            dump("aT", aT.rearrange("p a b -> p (a b)"), B_aT)
            return _finish(nc, S, st, out_bufs)
        hgs = [4, 5, 6, 7, 8, 9] if pas == 0 else list(range(10))
        pend = []
        for hg in hgs:
            wi = load_w(wT_d[hg])
            for i in range(8):
                pend.append(tok_half(hg, pas, i, wi, next_pb()))
                if len(pend) > 2:
                    pend.pop(0)()
        for ev in pend:
            ev()
        if pas == 0:
            for ug in range(4):
                wi = load_w(wU_d[ug])
                for cc in range(2):
                    c = ug * 2 + cc
                    pb = next_pb()
                    for dc in range(16):
                        PE(lambda e, dc=dc, cc=cc, wi=wi, pb=pb: e.matmul(ps[pb][:, 0:16], lhsT=wsl[wi][:, dc, cc * 128:(cc + 1) * 128], rhs=aT[:, dc, 1008:1024],
                                                                        start=(dc == 0), stop=(dc == 15)), [B_aT[7], B_wsl[wi]], [PB[pb]])
                    ACT(lambda e, c=c, pb=pb: e.activation(out=uhalo[:, c, :], in_=ps[pb][:, 0:16], func=AF.Copy), [PB[pb]], [B_uhalo])
            if stop_after == "P0":
                dump("kT", kT.rearrange("p a b c -> p (a b c)"), [B_kT[k][g][l] for k in range(3) for g in range(2) for l in range(16)])
                return _finish(nc, S, st, out_bufs)
        else:
            for i in range(8):
                pb = next_pb()
                for dc in range(16):
                    PE(lambda e, dc=dc, i=i, pb=pb: e.matmul(ps[pb][:, 0:24], lhsT=aT[:, dc, i * 128:(i + 1) * 128], rhs=wgates[:, dc, :],
                                                             start=(dc == 0), stop=(dc == 15)), [B_aT[i], B_wgates], [PB[pb]])
                ACT(lambda e, i=i, pb=pb: e.activation(out=gsig[:, i, :], in_=ps[pb][:, 0:24], func=AF.Sigmoid), [PB[pb]], [B_gsig[i]])

    dump("kT", kT.rearrange("p a b c -> p (a b c)"), [B_kT[k][g][l] for k in range(3) for g in range(2) for l in range(16)])
    dump("qT", qT.rearrange("p a b c d -> p (a b c d)"), [B_qT[g][o] for g in range(2) for o in range(8)])
    dump("v1", v1.rearrange("p a b c -> p (a b c)"), [B_v1[k][l] for k in range(4) for l in range(16)] + [B_v1init])
    dump("vcT", vcT.rearrange("p a b -> p (a b)"), [B_vcT[g][l] for g in range(2) for l in range(16)])
    dump("gsig", gsig.rearrange("p a b -> p (a b)"), B_gsig)

    if stop_after == "P1":
        return _finish(nc, S, st, out_bufs)
    UC = Carver(A, AB_SHARED, ARENA_BYTES - 8192, "pool")
    ubuf = UC.take([1040])
    sbufA = UC.take([1040])
    sbufB = UC.take([1040])
    ptmp = UC.take([16])
    pooled = [UC.take([2, 1024], BF16) for _ in range(2)]
    poolw = UC.take([4, 2, 256], BF16)
    B_ubuf = S.buf("ubuf")
    B_sA = S.buf("sA")
    B_sB = S.buf("sB")
    B_ptmp = S.buf("ptmp")
    B_pooled = S.bufs(2, "pooled")
    B_poolw = S.buf("poolw")
    S.barrier([B_ubuf, B_sA, B_sB, B_ptmp, B_poolw] + B_pooled)
    S.dma("pool", lambda e: e.dma_start(out=poolw.rearrange("p a b c -> p (a b c)"), in_=poolw_d), "poolw", [], [B_poolw])
    pend_y = []
    for gi in range(4):
        w = 2 << gi
        wi = load_w(wU_d[gi])
        psl = gi % 2
        for cc in range(2):
            c = gi * 2 + cc
            pbs = [next_pb(), next_pb()]
            for th2 in range(2):
                for dc in range(16):
                    PE(lambda e, dc=dc, cc=cc, wi=wi, th2=th2, pb=pbs[th2]: e.matmul(ps[pb][:], lhsT=wsl[wi][:, dc, cc * 128:(cc + 1) * 128],
                                                                                   rhs=aT[:, dc, th2 * 512:(th2 + 1) * 512], start=(dc == 0), stop=(dc == 15)),
                       [B_aT[th2 * 4 + k] for k in range(4)] + [B_wsl[wi]], [PB[pbs[th2]]])
            ACT(lambda e, c=c: e.activation(out=ubuf[:, 0:16], in_=uhalo[:, c, :], func=AF.Copy), [B_uhalo], [B_ubuf])
            ACT(lambda e, pb=pbs[0]: e.activation(out=ubuf[:, 16:528], in_=ps[pb][:], func=AF.Copy), [PB[pbs[0]]], [B_ubuf])
            ACT(lambda e, pb=pbs[1]: e.activation(out=ubuf[:, 528:1040], in_=ps[pb][:], func=AF.Copy), [PB[pbs[1]]], [B_ubuf])
            cur, Bcur = ubuf, B_ubuf
            nxt = [(sbufA, B_sA), (sbufB, B_sB)]
            sh = 1
            k = 0
            while sh < w:
                dst, Bd = nxt[k % 2]
                DVE(lambda e, cur=cur, dst=dst, sh=sh: e.tensor_tensor(out=dst[:, sh:1040], in0=cur[:, sh:1040], in1=cur[:, 0:1040 - sh], op=ALU.add), [Bcur], [Bd])
                cur, Bcur = dst, Bd
                sh *= 2
                k += 1
            pl = pooled[psl][:, cc, :]
            DVE(lambda e, cur=cur, pl=pl, w=w: e.scalar_tensor_tensor(out=pl[:, 16:1024], in0=cur[:, 32:1040], scalar=1.0 / w, in1=ubuf[:, 32:1040], op0=ALU.mult, op1=ALU.subtract),
                [Bcur, B_ubuf], [B_pooled[psl]])
            DVE(lambda e, cur=cur, gi=gi: e.tensor_tensor(out=ptmp, in0=cur[:, 16:32], in1=invc[:, gi, :], op=ALU.mult), [Bcur, B_const], [B_ptmp])
            DVE(lambda e, pl=pl: e.tensor_tensor(out=pl[:, 0:16], in0=ptmp, in1=ubuf[:, 16:32], op=ALU.subtract), [B_ptmp, B_ubuf], [B_pooled[psl]])
        def ypool(gi=gi, psl=psl):
          for dch in range(2):
              for th2 in range(2):
                  pb = next_pb()
                  for cc in range(2):
                      PE(lambda e, cc=cc, dch=dch, th2=th2, pb=pb, gi=gi, psl=psl: e.matmul(ps[pb][:], lhsT=poolw[:, gi, cc, dch * 128:(dch + 1) * 128],
                                                                                          rhs=pooled[psl][:, cc, th2 * 512:(th2 + 1) * 512], start=(cc == 0), stop=(cc == 1)),
                         [B_poolw, B_pooled[psl]], [PB[pb]])
                  ch = gi * 2 + dch
                  ACT(lambda e, ch=ch, th2=th2, pb=pb: e.activation(out=ymixT[:, ch, th2 * 512:(th2 + 1) * 512], in_=ps[pb][:], func=AF.Copy, scale=pscale[:, ch:ch + 1]),
                      [PB[pb], B_const], [B_ymix[ch]])
        if gi > 0:
            pend_y.pop(0)()
        pend_y.append(ypool)
    pend_y.pop(0)()
    dump("ypool", ymixT[:, 0:8, :].rearrange("p a b -> p (a b)"), B_ymix[0:8])

    if stop_after == "B":
        return _finish(nc, S, st, out_bufs)

    AC = Carver(A, TRANS0, ARENA_BYTES - 8192, "attn")
    w1 = [AC.take([32, 128], BF16) for _ in range(2)]
    w2 = [AC.take([128], BF16) for _ in range(2)]
    peT = [AC.take([32], BF16) for _ in range(2)]
    hb_ = AC.take([4, 128])
    hx = AC.take([4, 128])
    hy = AC.take([4, 128])
    cbias = AC.take([2, 2])
    geluT = AC.take([4, 128], BF16)
    kcmpT = AC.take([2, 128], BF16)
    vcmp1 = AC.take([2, 164], BF16)
    PT = [AC.take([4, 128], BF16) for _ in range(4)]
    oaccs = [AC.take([4, 128]) for _ in range(4)] * 4
    obf = [AC.take([4, 128], BF16) for _ in range(2)]
    etmp = [AC.take([2, 128]) for _ in range(2)]
    impb = [AC.take([32]) for _ in range(2)]
    impw = [AC.take([32]) for _ in range(2)]
    m16 = [AC.take([16]) for _ in range(2)]
    selbf = [AC.take([32], BF16) for _ in range(2)]
    negT = AC.take([2, 8, 4, 128], BF16)
    PTc = [AC.take([4, 128], BF16) for _ in range(2)]
    rs = [AC.take([3, 4]) for _ in range(2)]
    B_w1 = S.bufs(2, "w1")
    B_w2 = S.bufs(2, "w2")
    B_peT = S.bufs(2, "peT")
    B_h = S.bufs(4, "hid")
    B_cb = S.buf("cbias")
    B_gelu = S.bufs(4, "gelu")
    B_kcmp = S.bufs(2, "kcmp")
    B_vcmp = S.bufs(2, "vcmp")
    B_PT = S.bufs(4, "PT")
    B_PTc = S.bufs(2, "PTc")
    B_oaccs = S.bufs(4, "oacc") * 4
    B_obf = S.bufs(2, "obf")
    B_etmp = S.bufs(2, "etmp")
    B_imp = S.bufs(2, "imp")
    B_selbf = S.bufs(2, "selbf")
    B_negT = [[S.buf(f"negT{g}_{o}") for o in range(8)] for g in range(2)]
    B_rs = S.bufs(2, "rs")
    S.barrier(B_w1 + B_w2 + B_peT + B_h + [B_cb] + B_gelu + B_kcmp + B_vcmp + B_PT + B_PTc + B_oaccs + B_obf + B_etmp + B_imp + B_selbf
              + [b for l in B_negT for b in l] + B_rs)
    for kv, (w1d, w2d, ped) in enumerate([(w1k_d, w2k_d, pek_d), (w1v_d, w2v_d, pev_d)]):
        S.dma("pool", lambda e, kv=kv, w1d=w1d: e.dma_start(out=w1[kv].rearrange("p a b -> p (a b)"), in_=w1d), f"w1_{kv}", [], [B_w1[kv]])
        S.dma("pool", lambda e, kv=kv, w2d=w2d: e.dma_start(out=w2[kv], in_=w2d), f"w2_{kv}", [], [B_w2[kv]])
        S.dma("pool", lambda e, kv=kv, ped=ped: e.dma_start(out=peT[kv], in_=ped), f"pe_{kv}", [], [B_peT[kv]])
    POOL(lambda e: e.memset(vcmp1, 0.0), [], B_vcmp)
    POOL(lambda e: e.memset(negT.rearrange("p a b c d -> p (a b c d)"), 0.0), [], [b for l in B_negT for b in l])
    for g in range(2):
        S.dma("pool", lambda e, g=g: e.dma_start(out=vcmp1[:, g, 129:161], in_=c_overlap_d), f"ovl{g}", [], [B_vcmp[g]])
        POOL(lambda e, g=g: e.memset(vcmp1[0:127, g, 128:129], 1.0), [], [B_vcmp[g]])

    CB = 7
    for kv in range(2):
        for l in range(32):
            PE(lambda e, l=l, kv=kv: e.matmul(ps[CB][:, 500 + kv:501 + kv], lhsT=w1[kv][:, l, :], rhs=peT[kv][:, l:l + 1], start=(l == 0), stop=(l == 31)),
               [B_w1[kv], B_peT[kv]], [PB[CB]])
        DVE(lambda e, kv=kv: e.tensor_copy(out=cbias[:, kv, 0:1], in_=ps[CB][:, 500 + kv:501 + kv]), [PB[CB]], [B_cb])
    for kv in range(2):
        for g in range(2):
            idx = kv * 2 + g
            srcT = kT[:, 0, g, :] if kv == 0 else vcT[:, g, :]
            Bsrc = [B_kT[0][g][l] for l in range(16)] if kv == 0 else [B_vcT[g][l] for l in range(16)]
            for l in range(32):
                PE(lambda e, l=l, kv=kv, idx=idx, srcT=srcT: e.matmul(ps[CB][:, idx * 128:idx * 128 + 127], lhsT=w1[kv][:, l, :], rhs=srcT[:, l:l + 16 * 126 + 1:16],
                                                                  start=(l == 0), stop=(l == 31)), Bsrc + [B_w1[kv]], [PB[CB]])
            x_ = hb_[:, idx, 0:127]
            DVE(lambda e, idx=idx, kv=kv, x_=x_: e.tensor_scalar(out=x_, in0=ps[CB][:, idx * 128:idx * 128 + 127], scalar1=cbias[:, kv, 0:1], scalar2=None, op0=ALU.add),
                [PB[CB], B_cb], [B_h[idx]])
            DVE(lambda e, idx=idx, x_=x_: e.tensor_tensor(out=hx[:, idx, 0:127], in0=x_, in1=x_, op=ALU.mult), [B_h[idx]], [B_h[idx]])
            DVE(lambda e, idx=idx: e.tensor_scalar(out=hx[:, idx, 0:127], in0=hx[:, idx, 0:127], scalar1=0.044715, scalar2=1.0, op0=ALU.mult, op1=ALU.add), [B_h[idx]], [B_h[idx]])
            DVE(lambda e, idx=idx, x_=x_: e.tensor_tensor(out=hx[:, idx, 0:127], in0=hx[:, idx, 0:127], in1=x_, op=ALU.mult), [B_h[idx]], [B_h[idx]])
            ACT(lambda e, idx=idx: e.activation(out=hy[:, idx, 0:127], in_=hx[:, idx, 0:127], func=AF.Sigmoid, scale=1.5957691216057308), [B_h[idx]], [B_h[idx]])
            DVE(lambda e, idx=idx, x_=x_: e.tensor_tensor(out=geluT[:, idx, 0:127], in0=hy[:, idx, 0:127], in1=x_, op=ALU.mult), [B_h[idx]], [B_gelu[idx]])
    POOL(lambda e: e.memset(kcmpT, 0.0), [], B_kcmp)
    for g in range(2):
        PE(lambda e, g=g: e.matmul(ps[CB][:, 0:127], lhsT=w2[0], rhs=geluT[:, g, 0:127], start=True, stop=True), [B_w2[0], B_gelu[g]], [PB[CB]])
        DVE(lambda e, g=g: e.tensor_copy(out=kcmpT[:, g, 0:127], in_=ps[CB][:, 0:127]), [PB[CB]], [B_kcmp[g]])
        PE(lambda e, g=g: e.matmul(ps[CB][0:127, 128:256], lhsT=geluT[:, 2 + g, 0:127], rhs=w2[1], start=True, stop=True), [B_w2[1], B_gelu[2 + g]], [PB[CB]])
        DVE(lambda e, g=g: e.tensor_copy(out=vcmp1[0:127, g, 0:128], in_=ps[CB][0:127, 128:256]), [PB[CB]], [B_vcmp[g]])
    dump("kcmpT", kcmpT.rearrange("p a b -> p (a b)"), B_kcmp)
    dump("vcmp1", vcmp1.rearrange("p a b -> p (a b)"), B_vcmp)

    cnt = {"S": 0, "PT": 0, "O": 0, "rs": 0, "ob": 0}
    SB = [0, 1, 2]

    def evac_branch(obanks, br, g, ot, oa, B_oa, first, imp=None):
        k = cnt["rs"] % 2
        cnt["rs"] += 1
        r_ = rs[k]
        Br = B_rs[k]
        for bi, bank in enumerate(obanks):
            DVE(lambda e, bi=bi, bank=bank: e.tensor_scalar(out=r_[:, 0, 2 * bi:2 * bi + 2], in0=ps[bank][:, 128:385:256], scalar1=1e-30, scalar2=None, op0=ALU.add),
                [PB[bank]], [Br])
        DVE(lambda e: e.reciprocal(out=r_[:, 1, :], in_=r_[:, 0, :]), [Br], [Br])
        s0 = g * 12 + br
        DVE(lambda e: e.tensor_tensor(out=r_[:, 2, :], in0=r_[:, 1, :], in1=gsig[:, ot, s0:s0 + 10:3], op=ALU.mult), [Br, B_gsig[ot]], [Br])
        for r in range(4):
            bank = obanks[r // 2]
            c0 = (r % 2) * 256
            if first:
                DVE(lambda e, bank=bank, c0=c0, r=r: e.tensor_scalar(out=oa[:, r, :], in0=ps[bank][:, c0:c0 + 128], scalar1=r_[:, 2, r:r + 1], scalar2=None, op0=ALU.mult),
                    [PB[bank], Br], [B_oa])
            else:
                DVE(lambda e, bank=bank, c0=c0, r=r: e.scalar_tensor_tensor(out=oa[:, r, :], in0=ps[bank][:, c0:c0 + 128], scalar=r_[:, 2, r:r + 1], in1=oa[:, r, :],
                                                                          op0=ALU.mult, op1=ALU.add), [PB[bank], Br], [B_oa])
            if imp is not None:
                ib, Bi = imp
                if r == 0:
                    DVE(lambda e, bank=bank, c0=c0, r=r: e.tensor_scalar(out=ib, in0=ps[bank][:, c0 + 129:c0 + 161], scalar1=r_[:, 1, r:r + 1], scalar2=None, op0=ALU.mult),
                        [PB[bank], Br], [Bi])
                else:
                    DVE(lambda e, bank=bank, c0=c0, r=r: e.scalar_tensor_tensor(out=ib, in0=ps[bank][:, c0 + 129:c0 + 161], scalar=r_[:, 1, r:r + 1], in1=ib, op0=ALU.mult, op1=ALU.add),
                        [PB[bank], Br], [Bi])

    def osets():
        i = cnt["O"] % 2
        cnt["O"] += 1
        return (3, 4) if i == 0 else (5, 6)

    def score_exp(lhsT_ap, B_l, g, ot, bias=None):
        sb = SB[cnt["S"] % 3]
        cnt["S"] += 1
        pi = cnt["PT"] % 4
        cnt["PT"] += 1
        rhs_q = qT[:, g, ot].rearrange("p r t -> p (r t)")
        PE(lambda e: e.matmul(ps[sb][:], lhsT=lhsT_ap, rhs=rhs_q, start=True, stop=(bias is None)), B_l + [B_qT[g][ot]], [PB[sb]])
        if bias is not None:
            kt = bias
            PE(lambda e: e.matmul(ps[sb][:], lhsT=Emat[:, kt, :], rhs=negT[:, g, ot].rearrange("p r t -> p (r t)"), start=False, stop=True),
               [B_const, B_negT[g][ot]], [PB[sb]])
        ACT(lambda e: e.activation(out=PT[pi].rearrange("p r t -> p (r t)"), in_=ps[sb][:], func=AF.Exp, scale=SCALE), [PB[sb]], [B_PT[pi]])
        return pi

    def pv(pi, rhs_ap, B_r, width, obanks, first, last):
        for r in range(4):
            bank = obanks[r // 2]
            c0 = (r % 2) * 256
            PE(lambda e, r=r, bank=bank, c0=c0: e.matmul(ps[bank][:, c0:c0 + width], lhsT=PT[pi][:, r, :], rhs=rhs_ap, start=(first and r % 2 == 0), stop=last,
                                                         skip_group_check=True),
               [B_PT[pi]] + B_r, [PB[bank]])

    def bcast4(m):
        return m.unsqueeze(1).to_broadcast([128, 4, 128])

    deferred = []

    def run_pipeline(steps, la):
        stt = {}
        n = len(steps)
        for i in range(n + la):
            if i < n:
                if steps[i][2] is not None:
                    steps[i][2]()
                stt[i] = steps[i][0]()
            j = i - la
            if j >= 0:
                steps[j][1](stt.pop(j))
            for dfr in list(deferred):
                dfr[0] -= 1
                if dfr[0] <= 0:
                    deferred.remove(dfr)
                    dfr[1]()

    def c1_stages(g, ot):
        oi = g * 8 + ot
        ii = oi % 2
        pt = PTc[ii]
        Bpt = B_PTc[ii]
        hold = {}

        def s0():
            sb = SB[cnt["S"] % 3]
            cnt["S"] += 1
            PE(lambda e: e.matmul(ps[sb][:], lhsT=kcmpT[:, g, :], rhs=qT[:, g, ot].rearrange("p r t -> p (r t)"), start=True, stop=True), [B_kcmp[g], B_qT[g][ot]], [PB[sb]])
            ACT(lambda e: e.activation(out=pt.rearrange("p r t -> p (r t)"), in_=ps[sb][:], func=AF.Exp, scale=SCALE), [PB[sb]], [Bpt])
            DVE(lambda e: e.tensor_tensor(out=pt, in0=pt, in1=bcast4(cvalid[:, ot * 128:(ot + 1) * 128]), op=ALU.mult), [Bpt, B_const], [Bpt])

        def s1():
            ob = (5, 6)
            for r in range(4):
                bank = ob[r // 2]
                c0 = (r % 2) * 256
                PE(lambda e, r=r, bank=bank, c0=c0: e.matmul(ps[bank][:, c0:c0 + 161], lhsT=pt[:, r, :], rhs=vcmp1[:, g, 0:161], start=(r % 2 == 0), stop=True, skip_group_check=True),
                   [Bpt, B_vcmp[g]], [PB[bank]])
            evac_branch(ob, 0, g, ot, oaccs[oi], B_oaccs[oi], True, imp=(impb[ii], B_imp[ii]))

        def s2():
            DVE(lambda e: e.tensor_tensor(out=impb[ii], in0=impb[ii], in1=selbias[:, ot, :], op=ALU.add), [B_imp[ii], B_const], [B_imp[ii]])
            DVE(lambda e: e.max(out=m16[ii][:, 0:8], in_=impb[ii]), [B_imp[ii]], [B_imp[ii]])
            DVE(lambda e: e.match_replace(out=impw[ii], in_to_replace=m16[ii][:, 0:8], in_values=impb[ii], imm_value=-3.0e30), [B_imp[ii]], [B_imp[ii]])
            DVE(lambda e: e.max(out=m16[ii][:, 8:16], in_=impw[ii]), [B_imp[ii]], [B_imp[ii]])
            DVE(lambda e: e.tensor_scalar(out=m16[ii][:, 15:16], in0=m16[ii][:, 15:16], scalar1=-1.0e29, scalar2=None, op0=ALU.max), [B_imp[ii]], [B_imp[ii]])
            DVE(lambda e: e.tensor_scalar(out=selbf[ii], in0=impb[ii], scalar1=m16[ii][:, 15:16], scalar2=1.0, op0=ALU.is_ge, op1=ALU.subtract), [B_imp[ii]], [B_selbf[ii]])

        def s3():
            PE(lambda e: e.transpose(out=psb[7][0:32, 0:128], in_=selbf[ii], identity=ident), [B_selbf[ii], B_const], [PB[7]])
            DVE(lambda e: e.tensor_copy(out=negT[0:32, g, ot], in_=psb[7][0:32, 0:128].unsqueeze(1).to_broadcast([32, 4, 128])), [PB[7]], [B_negT[g][ot]])
        return [s0, s1, s2, s3]

    chains = [(g, ot) for g in range(2) for ot in range(8)]
    c1s = [c1_stages(g, ot) for (g, ot) in chains]
    for f in c1s[0]:
        f()

    c2 = []
    for g in range(2):
        for ot in range(8):
            ltq = 8 + ot
            oi = g * 8 + ot
            lstep = [0]
            for br in (1, 2):
                nst = ltq + 1 if br == 1 else 5
                holder = {}
                for si in range(nst):
                    kt = si if br == 1 else ltq - 4 + si

                    def stepA(g=g, ot=ot, br=br, si=si, kt=kt, ltq=ltq):
                        if br == 1:
                            pi = score_exp(kT[:, 1, g, kt * 128:(kt + 1) * 128], [B_kT[1][g][kt]], g, ot, bias=kt)
                            if kt == ltq:
                                DVE(lambda e: e.tensor_tensor(out=PT[pi], in0=PT[pi], in1=bcast4(diag), op=ALU.mult), [B_PT[pi], B_const], [B_PT[pi]])
                        else:
                            pi = score_exp(kT[:, 2, g, kt * 128:(kt + 1) * 128], [B_kT[2][g][kt]], g, ot)
                            wv = wvalid[:, ot * 5 + si:ot * 5 + si + 1]
                            if si == 0:
                                DVE(lambda e: e.scalar_tensor_tensor(out=PT[pi], in0=PT[pi], scalar=wv, in1=bcast4(upst), op0=ALU.mult, op1=ALU.mult), [B_PT[pi], B_const], [B_PT[pi]])
                            elif si == 4:
                                DVE(lambda e: e.tensor_tensor(out=PT[pi], in0=PT[pi], in1=bcast4(diag), op=ALU.mult), [B_PT[pi], B_const], [B_PT[pi]])
                            elif kt < 8:
                                DVE(lambda e: e.tensor_scalar(out=PT[pi], in0=PT[pi], scalar1=wv, scalar2=None, op0=ALU.mult), [B_PT[pi], B_const], [B_PT[pi]])
                        return pi

                    def stepB(pi, g=g, ot=ot, br=br, si=si, kt=kt, nst=nst, oi=oi, holder=holder):
                        ob = (3, 4) if br == 1 else (5, 6)
                        vk = (0 if br == 1 else 2) + g
                        pv(pi, v1[:, vk, kt, 0:129], [B_v1[vk][kt], B_v1init], 129, ob, si == 0, si == nst - 1)
                        if si == nst - 1:
                            evac_branch(ob, br, g, ot, oaccs[oi], B_oaccs[oi], False)
                            if br == 2:
                                def combine(g=g, ot=ot, oi=oi):
                                    k = cnt["ob"] % 2
                                    cnt["ob"] += 1
                                    ACT(lambda e: e.activation(out=obf[k], in_=oaccs[oi], func=AF.Copy), [B_oaccs[oi]], [B_obf[k]])

                                    def tr():
                                        for r in range(4):
                                            PE(lambda e, r=r: e.transpose(out=psb[7][:, 256 + r * 128:256 + (r + 1) * 128], in_=obf[k][:, r, :], identity=ident), [B_obf[k], B_const], [PB[7]])
                                        DVE(lambda e: e.tensor_copy(out=ymixT[:, 8 + g * 4:12 + g * 4, ot * 128:(ot + 1) * 128], in_=psb[7][:, 256:768].rearrange("p (r t) -> p r t", r=4)),
                                            [PB[7]], [B_ymix[8 + g * 4 + r] for r in range(4)])
                                    deferred.append([6, tr])
                                deferred.append([4, combine])
                    pre = None
                    ntot = ltq + 1 + 5
                    sched = (3, 5, 7, ntot - 2)
                    if oi + 1 < 16 and lstep[0] in sched:
                        pre = c1s[oi + 1][sched.index(lstep[0])]
                    lstep[0] += 1
                    c2.append((stepA, stepB, pre))
    run_pipeline(c2, 2)
    while deferred:
        deferred.pop(0)[1]()
    dump("ynsa", ymixT[:, 8:16, :].rearrange("p a b -> p (a b)"), B_ymix[8:16])

    if stop_after == "C":
        return _finish(nc, S, st, out_bufs)

    hres = A.view(CONST_END, [8, 2048], F32)
    assert CONST_END + 65536 <= YMIX_OFF
    D_END = CONST_END + 65536
    DC = Carver(A, TRANS0, ARENA_BYTES - 8192, "D")
    wo = [DC.take([16, 512], BF16) for _ in range(2)]
    B_h = [[S.buf(f"h{ot}_{c}") for c in range(4)] for ot in range(8)]
    B_wo = S.bufs(2, "wo")
    S.barrier([b for l in B_h for b in l] + B_wo)
    for ot in range(8):
        S.dma("sp", lambda e, ot=ot: e.dma_start(out=hres[:, ot, :], in_=x_d[1024 + ot * 128:1024 + (ot + 1) * 128, :]), f"hres{ot}", [], B_h[ot])
    pb8 = [0]

    def npb():
        b = pb8[0] % 8
        pb8[0] += 1
        return b

    for dmc in range(4):
        sl = dmc % 2
        S.dma("pool", lambda e, dmc=dmc, sl=sl: e.dma_start(out=wo[sl].rearrange("p a b -> p (a b)"), in_=wout_d[dmc]), f"wo{sl}", [], [B_wo[sl]])
        for ot in range(8):
            pb = npb()
            for c in range(16):
                PE(lambda e, c=c, ot=ot, sl=sl, pb=pb: e.matmul(ps[pb][:], lhsT=ymixT[:, c, ot * 128:(ot + 1) * 128], rhs=wo[sl][:, c, :], start=(c == 0), stop=(c == 15)),
                   [B_ymix[c], B_wo[sl]], [PB[pb]])
            DVE(lambda e, ot=ot, dmc=dmc, pb=pb: e.tensor_tensor(out=hres[:, ot, dmc * 512:(dmc + 1) * 512], in0=ps[pb][:], in1=hres[:, ot, dmc * 512:(dmc + 1) * 512], op=ALU.add),
                [PB[pb], B_h[ot][dmc]], [B_h[ot][dmc]])
    dump("h1", hres.rearrange("p a b -> p (a b)"), [b for l in B_h for b in l])
    if stop_after == "D":
        return _finish(nc, S, st, out_bufs)

    EC = Carver(A, D_END, ARENA_BYTES, "E")
    a2T = EC.take([16, 1024], BF16)
    E_A2T_END = EC.p
    hT = EC.take([12, 1024], BF16)
    wgu = [EC.take([16, 256], BF16) for _ in range(4)]
    wdn = [EC.take([12, 512], BF16) for _ in range(2)]
    sgt = [EC.take([512]) for _ in range(2)]
    EC2 = Carver(A, 768, 768 + 2048 + 4096 + 1024 + 256 + 2048 + 2048, "Ealias")
    abf2 = [EC2.take([2048], BF16), EC2.take([2048], BF16)]
    dbg_state[0] = EC2.p
    dbg_state[1] = 512
    S.barrier([B_dbgstg])
    B_a2T = [S.buf(f"a2T{i}") for i in range(8)]
    B_hT = [[S.buf(f"hT{c}_{t}") for t in range(2)] for c in range(12)]
    B_wgu = S.bufs(4, "wgu")
    B_wdn = S.bufs(2, "wdn")
    B_sgt = S.bufs(2, "sgt")
    B_abf2 = S.bufs(2, "abf2")
    S.barrier(B_a2T + [b for l in B_hT for b in l] + B_wgu + B_wdn + B_sgt + B_abf2)

    def rms_h(ot, dst, B_dst):
        k = nstat[0] % 16
        nstat[0] += 1
        s_ = stat[:, k, :]
        sb = S.buf()
        src = hres[:, ot, :]
        ACT(lambda e: e.activation(out=dst, in_=src, func=AF.Square, accum_out=s_[:, 0:1]), B_h[ot], [B_dst, sb])
        DVE(lambda e: e.tensor_scalar(out=s_[:, 1:2], in0=s_[:, 0:1], scalar1=1.0 / 2048, scalar2=1e-6, op0=ALU.mult, op1=ALU.add), [sb], [sb])
        ACT(lambda e: e.activation(out=s_[:, 2:3], in_=s_[:, 1:2], func=AF.Sqrt), [sb], [sb])
        DVE(lambda e: e.reciprocal(out=s_[:, 3:4], in_=s_[:, 2:3]), [sb], [sb])
        DVE(lambda e: e.scalar_tensor_tensor(out=dst, in0=src, scalar=s_[:, 3:4], in1=gfull, op0=ALU.mult, op1=ALU.mult),
            [sb, B_gfull] + B_h[ot], [B_dst])

    def norm_hres(g_d, dstT, B_dstT):
        load_g(g_d)
        rms_h(0, abf2[0], B_abf2[0])
        for ot in range(8):
            sl = ot % 2
            if ot + 1 < 8:
                rms_h(ot + 1, abf2[1 - sl], B_abf2[1 - sl])
            transpose_tile(abf2[sl], B_abf2[sl], dstT, B_dstT[ot], ot * 128)

    norm_hres(g_ffn_d, a2T, B_a2T)
    gcount = [0]
    dcount = [0]
    for fq, (c0, c1) in enumerate(NQ_FF):
        ncq = c1 - c0
        for grp in range(c0 // 2, c1 // 2):
            sl = (gcount[0] % 2) * 2
            gcount[0] += 1
            S.dma("pool", lambda e, grp=grp, sl=sl: e.dma_start(out=wgu[sl].rearrange("p a b -> p (a b)"), in_=wg_d[grp]), f"wgu{sl}", [], [B_wgu[sl]])
            S.dma("pool", lambda e, grp=grp, sl=sl: e.dma_start(out=wgu[sl + 1].rearrange("p a b -> p (a b)"), in_=wu_d[grp]), f"wgu{sl + 1}", [], [B_wgu[sl + 1]])
            for half in range(2):
                ci = grp * 2 + half - c0
                for th2 in range(2):
                    pg, pu = npb(), npb()
                    for dc in range(16):
                        PE(lambda e, dc=dc, sl=sl, half=half, th2=th2, pg=pg: e.matmul(ps[pg][:], lhsT=wgu[sl][:, dc, half * 128:(half + 1) * 128], rhs=a2T[:, dc, th2 * 512:(th2 + 1) * 512],
                                                                                     start=(dc == 0), stop=(dc == 15)), [B_wgu[sl]] + B_a2T[th2 * 4:th2 * 4 + 4], [PB[pg]])
                    for dc in range(16):
                        PE(lambda e, dc=dc, sl=sl, half=half, th2=th2, pu=pu: e.matmul(ps[pu][:], lhsT=wgu[sl + 1][:, dc, half * 128:(half + 1) * 128], rhs=a2T[:, dc, th2 * 512:(th2 + 1) * 512],
                                                                                     start=(dc == 0), stop=(dc == 15)), [B_wgu[sl + 1]] + B_a2T[th2 * 4:th2 * 4 + 4], [PB[pu]])
                    ss = th2
                    ACT(lambda e, pg=pg, ss=ss: e.activation(out=sgt[ss], in_=ps[pg][:], func=AF.Silu), [PB[pg]], [B_sgt[ss]])
                    DVE(lambda e, pu=pu, ss=ss, ci=ci, th2=th2: e.tensor_tensor(out=hT[:, ci, th2 * 512:(th2 + 1) * 512], in0=ps[pu][:], in1=sgt[ss], op=ALU.mult),
                        [PB[pu], B_sgt[ss]], [B_hT[ci][th2]])
        for dmc in range(4):
            sl = dcount[0] % 2
            dcount[0] += 1
            S.dma("pool", lambda e, fq=fq, dmc=dmc, sl=sl: e.dma_start(out=wdn[sl].rearrange("p a b -> p (a b)"), in_=wd_d[fq * 4 + dmc]), f"wdn{sl}", [], [B_wdn[sl]])
            for ot in range(8):
                pb = npb()
                for ci in range(ncq):
                    PE(lambda e, ci=ci, ot=ot, sl=sl, pb=pb, ncq=ncq: e.matmul(ps[pb][:], lhsT=hT[:, ci, ot * 128:(ot + 1) * 128], rhs=wdn[sl][:, ci, :], start=(ci == 0), stop=(ci == ncq - 1)),
                       [B_hT[ci][ot // 4], B_wdn[sl]], [PB[pb]])
                DVE(lambda e, ot=ot, dmc=dmc, pb=pb: e.tensor_tensor(out=hres[:, ot, dmc * 512:(dmc + 1) * 512], in0=ps[pb][:], in1=hres[:, ot, dmc * 512:(dmc + 1) * 512], op=ALU.add),
                    [PB[pb], B_h[ot][dmc]], [B_h[ot][dmc]])
    dump("h2", hres.rearrange("p a b -> p (a b)"), [b for l in B_h for b in l])
    if stop_after == "E":
        return _finish(nc, S, st, out_bufs)

    FC = Carver(A, E_A2T_END, ARENA_BYTES - 8192, "F")
    a3T = a2T
    B_a3T = B_a2T
    wpg = [FC.take([16, 512], BF16) for _ in range(2)]
    wpp = [FC.take([2, 512], BF16) for _ in range(2)]
    pT = FC.take([2, 1024], BF16)
    pf = [FC.take([256]) for _ in range(2)]
    pbf = [FC.take([256], BF16) for _ in range(2)]
    gt = [FC.take([512]) for _ in range(2)]
    outt = [FC.take([2048]) for _ in range(2)]
    B_wpg = S.bufs(2, "wpg")
    B_wpp = S.bufs(2, "wpp")
    B_pT = S.bufs(8, "pT")
    B_pf = S.bufs(2, "pf")
    B_pbf = S.bufs(2, "pbf")
    B_gt = S.bufs(2, "gt")
    B_outt = S.bufs(2, "outt")
    S.barrier(B_wpg + B_wpp + B_pT + B_pf + B_pbf + B_gt + B_outt)
    norm_hres(g_ple_d, a3T, B_a3T)
    for ot in range(8):
        sl = ot % 2
        S.dma("sp", lambda e, ot=ot, sl=sl: e.dma_start(out=pf[sl], in_=p_d[ot * 128:(ot + 1) * 128, :]), f"pf{sl}", [], [B_pf[sl]])
        ACT(lambda e, sl=sl: e.activation(out=pbf[sl], in_=pf[sl], func=AF.Copy), [B_pf[sl]], [B_pbf[sl]])
        for c2 in range(2):
            PE(lambda e, c2=c2, sl=sl: e.transpose(out=psb[7][:, c2 * 128:(c2 + 1) * 128], in_=pbf[sl][:, c2 * 128:(c2 + 1) * 128], identity=ident), [B_pbf[sl], B_const], [PB[7]])
        DVE(lambda e, ot=ot: e.tensor_copy(out=pT[:, :, ot * 128:(ot + 1) * 128], in_=psb[7][:, 0:256].rearrange("p (a b) -> p a b", a=2)), [PB[7]], [B_pT[ot]])
    pb7 = [0]

    def npb7():
        b = pb7[0] % 7
        pb7[0] += 1
        return b

    for dmc in range(4):
        sl = dmc % 2
        S.dma("pool", lambda e, dmc=dmc, sl=sl: e.dma_start(out=wpg[sl].rearrange("p a b -> p (a b)"), in_=wpg_d[dmc]), f"wpg{sl}", [], [B_wpg[sl]])
        S.dma("pool", lambda e, dmc=dmc, sl=sl: e.dma_start(out=wpp[sl].rearrange("p a b -> p (a b)"), in_=wpp_d[dmc]), f"wpp{sl}", [], [B_wpp[sl]])
        for ot in range(8):
            p1, p2 = npb7(), npb7()
            for dc in range(16):
                PE(lambda e, dc=dc, ot=ot, sl=sl, p1=p1: e.matmul(ps[p1][:], lhsT=a3T[:, dc, ot * 128:(ot + 1) * 128], rhs=wpg[sl][:, dc, :], start=(dc == 0), stop=(dc == 15)),
                   [B_a3T[ot], B_wpg[sl]], [PB[p1]])
            for c2 in range(2):
                PE(lambda e, c2=c2, ot=ot, sl=sl, p2=p2: e.matmul(ps[p2][:], lhsT=pT[:, c2, ot * 128:(ot + 1) * 128], rhs=wpp[sl][:, c2, :], start=(c2 == 0), stop=(c2 == 1)),
                   [B_pT[ot], B_wpp[sl]], [PB[p2]])
            gs = (dmc * 8 + ot) % 2
            ACT(lambda e, p1=p1, gs=gs: e.activation(out=gt[gs], in_=ps[p1][:], func=AF.Sigmoid), [PB[p1]], [B_gt[gs]])
            DVE(lambda e, p2=p2, gs=gs: e.tensor_tensor(out=gt[gs], in0=ps[p2][:], in1=gt[gs], op=ALU.mult), [PB[p2], B_gt[gs]], [B_gt[gs]])
            DVE(lambda e, ot=ot, dmc=dmc, gs=gs: e.tensor_tensor(out=hres[:, ot, dmc * 512:(dmc + 1) * 512], in0=gt[gs], in1=hres[:, ot, dmc * 512:(dmc + 1) * 512], op=ALU.add),
                [B_gt[gs], B_h[ot][dmc]], [B_h[ot][dmc]])
    load_g(g_fin_d)
    for ot in range(8):
        sl = ot % 2
        rms_h(ot, outt[sl], B_outt[sl])
        ob = S.buf()
        S.dma("sp", lambda e, ot=ot, sl=sl: e.dma_start(out=out_d[ot * 128:(ot + 1) * 128, :], in_=outt[sl]), f"ost{sl}", [B_outt[sl]], [ob])
        out_bufs.append(ob)
    return _finish(nc, S, st, out_bufs)


def _finish(nc, S, st, out_bufs):
    S.op("sp", lambda e: e.nop(), out_bufs, [])
    S.emit(nc, st)
    st.close()
    return nc, S


def _tile_cols(W, c0, n):
    return np.ascontiguousarray(W[:, c0:c0 + n].reshape(16, 128, n).transpose(1, 0, 2).reshape(128, 16 * n))


def _consts(th):
    f32 = np.float32
    L = np.arange(2048)
    tg = L - 1024 + 1024 * th
    pos = np.maximum(tg, 0).astype(f32)
    inv_freq = (f32(500000.0) ** (-np.arange(0, 32, 2, dtype=f32) / f32(32))).astype(f32)
    ang = (pos[:, None] * inv_freq[None, :]).astype(f32)
    cos, sin = np.cos(ang).astype(f32), np.sin(ang).astype(f32)
    cos2 = np.concatenate([cos, cos], 1)
    sin2 = np.concatenate([-sin, sin], 1)

    def tokmaj(a):
        w = a.shape[1]
        return np.ascontiguousarray(a.reshape(16, 128, w).transpose(1, 0, 2).reshape(128, 16 * w))

    n = np.arange(128)
    Lo = 1024 + np.arange(1024)
    cvalid = ((16 * n[:, None] + 31) <= Lo[None, :]) & ((th == 1) | (n[:, None] >= 64))
    s = np.arange(32)
    overlap = ((16 * n[:, None]) < (64 * s[None, :] + 64)) & ((16 * n[:, None] + 32) > 64 * s[None, :])
    overlap = overlap & (n[:, None] < 127)
    cur = Lo // 64
    valid = (s[None, :] <= cur[:, None]) & ((th == 1) | (s[None, :] >= 16))
    forced = (s[None, :] == cur[:, None]) | (s[None, :] == cur[:, None] - 1) | (s[None, :] == 16 * (1 - th))
    selbias = np.where(valid, np.where(forced, BIG, 0.0), -BIG).astype(f32)
    selbias = np.ascontiguousarray(selbias.reshape(8, 128, 32).transpose(1, 0, 2).reshape(128, 256))
    k = np.arange(128)
    E = np.zeros((128, 16, 128), f32)
    for kt in range(16):
        E[2 * kt + k // 64, kt, k] = 30000.0
    ident = np.eye(128, dtype=f32)
    diag = (k[:, None] <= k[None, :]).astype(f32)
    upst = (k[:, None] > k[None, :]).astype(f32)
    tri = np.concatenate([ident, diag, upst], 1)
    wvalid = np.zeros((8, 5), f32)
    for ot in range(8):
        for off in range(5):
            kt = 8 + ot - 4 + off
            wvalid[ot, off] = 1.0 if (th == 1 or kt >= 8) else 0.0
    wvalid = np.broadcast_to(wvalid.reshape(1, 40), (128, 40))
    invc = np.zeros((4, 16), f32)
    for gi in range(4):
        w = 2 << gi
        for i in range(16):
            invc[gi, i] = 1.0 / w if th == 1 else 1.0 / min(i + 1, w)
    invc = np.broadcast_to(invc.reshape(1, 64), (128, 64))
    c = lambda a: np.ascontiguousarray(a, dtype=f32)
    return {
        "c_cos": tokmaj(cos2), "c_sin": tokmaj(sin2), "c_cvalid": c(cvalid), "c_overlap": c(overlap), "c_selbias": c(selbias),
        "c_E": c(E.reshape(128, 2048)), "c_tri": c(tri), "c_wvalid": c(wvalid), "c_invc": c(invc),
    }


def _prep_shared(w_in, pool_w, pool_scale, cmp_k_pe, cmp_k_w1, cmp_k_w2, cmp_v_pe, cmp_v_w1, cmp_v_w2, w_out,
                 w_gate, w_up, w_down, w_ple_gate, w_ple_proj, in_norm_g, ffn_norm_g, ple_norm_g, final_norm_g):
    c = lambda a: np.ascontiguousarray(a, dtype=np.float32)
    d = {}
    w_in = w_in[0]
    d["wU"] = np.stack([_tile_cols(w_in, i * 256, 256) for i in range(4)])
    d["wT"] = np.stack([_tile_cols(w_in, 1024 + i * 256, 256) for i in range(10)])
    d["wG"] = _tile_cols(w_in, 3584, 24)
    d["poolw"] = c(pool_w[0].reshape(4, 2, 128, 256).transpose(2, 0, 1, 3).reshape(128, 2048))
    d["pscale"] = c(pool_scale[0].reshape(8, 128).T)
    d["w1k"] = c(cmp_k_w1[0].reshape(32, 128, 128).transpose(1, 0, 2).reshape(128, 4096))
    d["w1v"] = c(cmp_v_w1[0].reshape(32, 128, 128).transpose(1, 0, 2).reshape(128, 4096))
    d["w2k"] = c(cmp_k_w2[0])
    d["w2v"] = c(cmp_v_w2[0])
    d["pekT"] = c(cmp_k_pe[0].T)
    d["pevT"] = c(cmp_v_pe[0].T)
    d["wout"] = np.stack([_tile_cols(w_out[0], i * 512, 512) for i in range(4)])
    d["wgate"] = np.stack([_tile_cols(w_gate[0], i * 256, 256) for i in range(22)])
    d["wup"] = np.stack([_tile_cols(w_up[0], i * 256, 256) for i in range(22)])
    wd = np.zeros((16, 128, 12 * 512), np.float32)
    for fq, (c0, c1) in enumerate(NQ_FF):
        for dmc in range(4):
            blk = w_down[0][c0 * 128:c1 * 128, dmc * 512:(dmc + 1) * 512].reshape(c1 - c0, 128, 512).transpose(1, 0, 2)
            wd[fq * 4 + dmc, :, :(c1 - c0) * 512] = blk.reshape(128, -1)
    d["wdown"] = wd
    d["wpg"] = np.stack([_tile_cols(w_ple_gate[0], i * 512, 512) for i in range(4)])
    wpp = w_ple_proj[0]
    d["wpp"] = np.stack([c(wpp[:, i * 512:(i + 1) * 512].reshape(2, 128, 512).transpose(1, 0, 2).reshape(128, 1024)) for i in range(4)])
    d["g_in"] = c(in_norm_g[0])
    d["g_ffn"] = c(ffn_norm_g[0])
    d["g_ple"] = c(ple_norm_g[0])
    d["g_fin"] = c(final_norm_g)
    return d


def make_in_maps(x, p, **w):
    shared = _prep_shared(**w)
    cst = [_consts(0), _consts(1)]
    in_maps = []
    for b in range(4):
        for th in range(2):
            m = dict(shared)
            m.update(cst[th])
            if th == 1:
                xl = x[b]
            else:
                xl = np.concatenate([np.zeros((1024, 2048), np.float32), x[b, :1024]], 0)
            m["x"] = np.ascontiguousarray(xl, dtype=np.float32)
            m["p"] = np.ascontiguousarray(p[0, b, th * 1024:(th + 1) * 1024], dtype=np.float32)
            in_maps.append(m)
    return in_maps


_NC_CACHE = {}


def kernel(x, p, in_norm_g, w_in, pool_w, pool_scale, cmp_k_pe, cmp_k_w1, cmp_k_w2, cmp_v_pe, cmp_v_w1, cmp_v_w2,
           w_out, ffn_norm_g, w_gate, w_up, w_down, ple_norm_g, w_ple_gate, w_ple_proj, final_norm_g):
    args = dict(locals())
    args = {k: np.asarray(v) for k, v in args.items()}
    x = args.pop("x")
    p = args.pop("p")
    in_maps = make_in_maps(x, p, **args)
    if "nc" not in _NC_CACHE:
        _NC_CACHE["nc"] = build()[0]
    nc = _NC_CACHE["nc"]
    res = run_bass_kernel_spmd(nc, in_maps, core_ids=list(range(8)))
    out = np.zeros((4, 2048, 2048), np.float32)
    for b in range(4):
        for th in range(2):
            out[b, th * 1024:(th + 1) * 1024] = res.results[b * 2 + th]["out"]
    return out
```
